# Optimizing a Trainium2 kernel written in Bass

```python
import math
import jax, jax.numpy as jnp
from jax import lax
import numpy as np

D_MODEL = 1024
BATCH = 8
SEQ = 4096
DEPTH = 1

CHUNK = 64
Q_BLOCK = 128
ATTN_WIDTH = D_MODEL // 2
ATTN_HEADS = 4
ATTN_SUB_DIM = ATTN_WIDTH // (2 * ATTN_HEADS)
ATTN_V_DIM = 2 * ATTN_SUB_DIM
SSM_WIDTH = D_MODEL // 2
SSM_GROUP = 16
SSM_GROUPS = SSM_WIDTH // SSM_GROUP
SSM_STATE = 64
REL_BUCKETS = 32
REL_MAX_DIST = 128
EPS = 1e-6
DT_MIN = 1e-3
DT_MAX = 1e-1
IN_SIZES = [ATTN_WIDTH, ATTN_WIDTH, ATTN_HEADS * ATTN_V_DIM, ATTN_WIDTH,
            SSM_WIDTH, SSM_WIDTH, 2 * D_MODEL]
IN_COLS = sum(IN_SIZES)
IN_SPLITS = [sum(IN_SIZES[:i + 1]) for i in range(len(IN_SIZES) - 1)]

kernel_name = "hybrid_diffattn_s5_gated_merge"


def rms_norm(x, gain):
    xf = x.astype(jnp.float32)
    y = xf * lax.rsqrt(jnp.mean(xf * xf, axis=-1, keepdims=True) + EPS)
    return (y * gain.astype(jnp.float32)).astype(x.dtype)


def t5_relative_bucket(rel):
    nb = REL_BUCKETS // 2
    max_exact = nb // 2
    side = jnp.where(rel > 0, nb, 0)
    n = jnp.abs(rel)
    nf = jnp.maximum(n, 1).astype(jnp.float32)
    large = max_exact + (jnp.log(nf / max_exact) / math.log(REL_MAX_DIST / max_exact)
                         * (nb - max_exact)).astype(jnp.int32)
    large = jnp.minimum(large, nb - 1)
    return side + jnp.where(n < max_exact, n, large)


def diff_attention(q, k, v, lam, rel_table, subln_gain, lam_init):
    B, L = q.shape[0], q.shape[1]
    nb = L // Q_BLOCK
    scale = ATTN_SUB_DIM ** -0.5
    q_blocks = q.reshape(B, nb, Q_BLOCK, ATTN_HEADS, 2, ATTN_SUB_DIM).transpose(1, 0, 2, 3, 4, 5)
    k_pos = jnp.arange(L)

    def one_block(args):
        q_blk, blk = args
        q_pos = blk * Q_BLOCK + jnp.arange(Q_BLOCK)
        s = jnp.einsum('bqhsd,bkhsd->bhsqk', q_blk, k).astype(jnp.float32) * scale
        bias = rel_table[t5_relative_bucket(k_pos[None, :] - q_pos[:, None])]
        bias = jnp.transpose(bias, (2, 0, 1)).astype(jnp.float32)[None, :, None]
        allowed = (k_pos[None, :] // CHUNK) <= (q_pos[:, None] // CHUNK)
        s = jnp.where(allowed, s + bias, -jnp.inf)
        p = jax.nn.softmax(s, axis=-1)
        attn = p[:, :, 0] - lam * p[:, :, 1]
        return jnp.einsum('bhqk,bkhd->bqhd', attn.astype(v.dtype), v)

    out = lax.map(one_block, (q_blocks, jnp.arange(nb)))
    out = out.transpose(1, 0, 2, 3, 4).reshape(B, L, ATTN_HEADS, ATTN_V_DIM)
    out = rms_norm(out, subln_gain) * (1.0 - lam_init)
    return out.reshape(B, L, ATTN_HEADS * ATTN_V_DIM)


def ssm_combine(left, right):
    ar_l, ai_l, br_l, bi_l = left
    ar_r, ai_r, br_r, bi_r = right
    ar = ar_r * ar_l - ai_r * ai_l
    ai = ar_r * ai_l + ai_r * ar_l
    br = ar_r * br_l - ai_r * bi_l + br_r
    bi = ar_r * bi_l + ai_r * br_l + bi_r
    return (ar, ai, br, bi)


def s5_branch(u, A_re, A_im, log_dt, B_re, B_im, C_re, C_im, D, glu_w, glu_b):
    f32 = jnp.float32
    Bsz, L = u.shape[0], u.shape[1]
    uf = u.astype(f32).reshape(Bsz, L, SSM_GROUPS, SSM_GROUP)
    a_re = A_re.astype(f32)
    a_im = A_im.astype(f32)
    dt = jnp.exp(log_dt.astype(f32))[:, None]
    decay = jnp.exp(a_re * dt)
    lb_re = decay * jnp.cos(a_im * dt)
    lb_im = decay * jnp.sin(a_im * dt)
    nr = lb_re - 1.0
    ni = lb_im
    den = a_re * a_re + a_im * a_im
    q_re = (nr * a_re + ni * a_im) / den
    q_im = (ni * a_re - nr * a_im) / den
    b_re = B_re.astype(f32)
    b_im = B_im.astype(f32)
    bb_re = q_re[..., None] * b_re - q_im[..., None] * b_im
    bb_im = q_re[..., None] * b_im + q_im[..., None] * b_re
    bu_re = jnp.einsum('blgh,gph->blgp', uf, bb_re)
    bu_im = jnp.einsum('blgh,gph->blgp', uf, bb_im)
    shape_a = (1, L, SSM_GROUPS, SSM_STATE)
    a_el_re = jnp.broadcast_to(lb_re[None, None], shape_a)
    a_el_im = jnp.broadcast_to(lb_im[None, None], shape_a)
    _, _, x_re, x_im = lax.associative_scan(ssm_combine, (a_el_re, a_el_im, bu_re, bu_im), axis=1)
    y = (jnp.einsum('blgp,ghp->blgh', x_re, C_re.astype(f32))
         - jnp.einsum('blgp,ghp->blgh', x_im, C_im.astype(f32))
         + D.astype(f32) * uf)
    y = y.reshape(Bsz, L, SSM_WIDTH)
    g = jax.nn.gelu(y)
    y = g * jax.nn.sigmoid(g @ glu_w.astype(f32) + glu_b.astype(f32))
    return y.astype(u.dtype)


def setup_inputs(seed: int = 0) -> dict:
    key = jax.random.key(seed)
    ks = jax.random.split(key, 24)
    f32 = jnp.float32
    nrm = lambda k, shape, s: (jax.random.normal(k, shape, f32) * s)
    x = jax.random.normal(ks[0], (BATCH, SEQ, D_MODEL), f32)
    norm_gain = 1.0 + nrm(ks[1], (DEPTH, D_MODEL), 0.01)
    w_in = nrm(ks[2], (DEPTH, D_MODEL, IN_COLS), D_MODEL ** -0.5)
    merge_gate_b = nrm(ks[3], (DEPTH, 2 * D_MODEL), 0.01)
    q_norm_gain = 1.0 + nrm(ks[4], (DEPTH, ATTN_SUB_DIM), 0.01)
    k_norm_gain = 1.0 + nrm(ks[5], (DEPTH, ATTN_SUB_DIM), 0.01)
    lambda_q1 = nrm(ks[6], (DEPTH, ATTN_SUB_DIM), 0.1)
    lambda_k1 = nrm(ks[7], (DEPTH, ATTN_SUB_DIM), 0.1)
    lambda_q2 = nrm(ks[8], (DEPTH, ATTN_SUB_DIM), 0.1)
    lambda_k2 = nrm(ks[9], (DEPTH, ATTN_SUB_DIM), 0.1)
    diff_subln_gain = 1.0 + nrm(ks[10], (DEPTH, ATTN_V_DIM), 0.01)
    rel_bias_table = nrm(ks[11], (REL_BUCKETS, ATTN_HEADS), 0.5)
    n_idx = jnp.arange(SSM_STATE, dtype=f32)
    ssm_A_re = -0.5 + nrm(ks[12], (DEPTH, SSM_GROUPS, SSM_STATE), 0.01)
    ssm_A_im = math.pi * n_idx[None, None, :] + nrm(ks[13], (DEPTH, SSM_GROUPS, SSM_STATE), 0.01)
    ssm_log_dt = jax.random.uniform(ks[14], (DEPTH, SSM_GROUPS), f32,
                                    math.log(DT_MIN), math.log(DT_MAX))
    ssm_B_re = nrm(ks[15], (DEPTH, SSM_GROUPS, SSM_STATE, SSM_GROUP), (2 * SSM_GROUP) ** -0.5)
    ssm_B_im = nrm(ks[16], (DEPTH, SSM_GROUPS, SSM_STATE, SSM_GROUP), (2 * SSM_GROUP) ** -0.5)
    ssm_C_re = nrm(ks[17], (DEPTH, SSM_GROUPS, SSM_GROUP, SSM_STATE), (2 * SSM_STATE) ** -0.5 * 4.0)
    ssm_C_im = nrm(ks[18], (DEPTH, SSM_GROUPS, SSM_GROUP, SSM_STATE), (2 * SSM_STATE) ** -0.5 * 4.0)
    ssm_D = nrm(ks[19], (DEPTH, SSM_GROUPS, SSM_GROUP), 1.0)
    ssm_glu_w = nrm(ks[20], (DEPTH, SSM_WIDTH, SSM_WIDTH), SSM_WIDTH ** -0.5)
    ssm_glu_b = nrm(ks[21], (DEPTH, SSM_WIDTH), 0.01)
    kk = jax.random.split(ks[22], 3)
    proj_attn = nrm(kk[0], (DEPTH, ATTN_HEADS * ATTN_V_DIM, D_MODEL), (ATTN_HEADS * ATTN_V_DIM) ** -0.5)
    proj_ssm = nrm(kk[1], (DEPTH, SSM_WIDTH, D_MODEL), SSM_WIDTH ** -0.5)
    w_out = nrm(kk[2], (DEPTH, D_MODEL, D_MODEL), D_MODEL ** -0.5)
    return {"x": x, "norm_gain": norm_gain, "w_in": w_in, "merge_gate_b": merge_gate_b,
            "q_norm_gain": q_norm_gain, "k_norm_gain": k_norm_gain,
            "lambda_q1": lambda_q1, "lambda_k1": lambda_k1, "lambda_q2": lambda_q2, "lambda_k2": lambda_k2,
            "diff_subln_gain": diff_subln_gain, "rel_bias_table": rel_bias_table,
            "ssm_A_re": ssm_A_re, "ssm_A_im": ssm_A_im, "ssm_log_dt": ssm_log_dt,
            "ssm_B_re": ssm_B_re, "ssm_B_im": ssm_B_im, "ssm_C_re": ssm_C_re, "ssm_C_im": ssm_C_im,
            "ssm_D": ssm_D, "ssm_glu_w": ssm_glu_w, "ssm_glu_b": ssm_glu_b,
            "proj_attn": proj_attn, "proj_ssm": proj_ssm, "w_out": w_out}


def reference(x, norm_gain, w_in, merge_gate_b, q_norm_gain, k_norm_gain,
              lambda_q1, lambda_k1, lambda_q2, lambda_k2, diff_subln_gain, rel_bias_table,
              ssm_A_re, ssm_A_im, ssm_log_dt, ssm_B_re, ssm_B_im, ssm_C_re, ssm_C_im,
              ssm_D, ssm_glu_w, ssm_glu_b, proj_attn, proj_ssm, w_out):
    f32 = jnp.float32
    B, L = x.shape[0], x.shape[1]
    for layer in range(DEPTH):
        lam_init = 0.8 - 0.6 * math.exp(-0.3 * layer)
        h = rms_norm(x, norm_gain[layer])
        z = h @ w_in[layer]
        q, k, v, gate_a, u_s, gate_s, merge_logits = jnp.split(z, IN_SPLITS, axis=-1)
        q = rms_norm(q.reshape(B, L, ATTN_HEADS, 2, ATTN_SUB_DIM), q_norm_gain[layer])
        k = rms_norm(k.reshape(B, L, ATTN_HEADS, 2, ATTN_SUB_DIM), k_norm_gain[layer])
        v = v.reshape(B, L, ATTN_HEADS, ATTN_V_DIM)
        lam = (jnp.exp(jnp.sum(lambda_q1[layer].astype(f32) * lambda_k1[layer].astype(f32)))
               - jnp.exp(jnp.sum(lambda_q2[layer].astype(f32) * lambda_k2[layer].astype(f32)))
               + lam_init)
        o_a = diff_attention(q, k, v, lam, rel_bias_table, diff_subln_gain[layer], lam_init)
        o_a = o_a * jax.nn.silu(gate_a)
        o_s = s5_branch(u_s, ssm_A_re[layer], ssm_A_im[layer], ssm_log_dt[layer],
                        ssm_B_re[layer], ssm_B_im[layer], ssm_C_re[layer], ssm_C_im[layer],
                        ssm_D[layer], ssm_glu_w[layer], ssm_glu_b[layer])
        o_s = o_s * jax.nn.silu(gate_s)
        g = jax.nn.sigmoid((merge_logits + merge_gate_b[layer]).astype(f32)).astype(x.dtype)
        g_a, g_s = jnp.split(g, 2, axis=-1)
        merged = g_a * (o_a @ proj_attn[layer]) + g_s * (o_s @ proj_ssm[layer])
        x = x + merged @ w_out[layer]
    return x
```

```python
import math
import numpy as np
import ml_dtypes
import concourse.bass as bass
import concourse.mybir as mybir
from concourse.bass_utils import run_bass_kernel_spmd

F32 = mybir.dt.float32
BF16 = mybir.dt.bfloat16
AF = mybir.ActivationFunctionType
ALU = mybir.AluOpType
AX = mybir.AxisListType


class _Op:
    __slots__ = ("eng", "fn", "idx", "dma", "dkey", "deps", "signal", "sigval", "sem")

    def __init__(self, eng, fn, idx, dma, dkey):
        self.eng = eng
        self.fn = fn
        self.idx = idx
        self.dma = dma
        self.dkey = dkey
        self.deps = {}
        self.signal = False
        self.sigval = 0
        self.sem = None


class Sched:
    ENGS = ("pe", "act", "dve", "pool", "sp")

    def __init__(self, nc):
        self.nc = nc
        self.ops = {e: [] for e in self.ENGS}
        self.last_w = {}
        self.readers = {}
        self.dma_count = {}

    def add(self, eng, fn, reads=(), writes=(), dma=False, dkey=None):
        op = _Op(eng, fn, len(self.ops[eng]), dma, dkey)
        reads = tuple(reads) + ("__B__",)
        for k in reads:
            w = self.last_w.get(k)
            if w is not None:
                op.deps[w] = True
        for k in writes:
            w = self.last_w.get(k)
            if w is not None and w not in op.deps:
                op.deps[w] = False
            rd = self.readers.get(k)
            if rd:
                for r in rd.get("eng", {}).values():
                    if r is not op and r not in op.deps:
                        op.deps[r] = False
                for r in rd.get("dma", []):
                    if r is not op and r not in op.deps:
                        op.deps[r] = False
        for k in reads:
            rd = self.readers.setdefault(k, {"eng": {}, "dma": []})
            if dma:
                rd["dma"].append(op)
            else:
                rd["eng"][eng] = op
        for k in writes:
            self.last_w[k] = op
            self.readers[k] = {"eng": {}, "dma": []}
        if dma:
            assert dkey is not None
            n = self.dma_count.get(dkey, 0) + 1
            self.dma_count[dkey] = n
            op.sigval = 16 * n
        self.ops[eng].append(op)
        return op

    def barrier(self):
        t = self._bar_tile
        self.add("dve", lambda e: e.memset(t[:, 0:1], 0.0), writes=("__B__",))

    def _needs_wait(self, op, dep, is_raw):
        if dep.dma:
            return True
        if dep.eng != op.eng:
            return True
        if op.dma:
            return True
        if op.eng == "pe":
            return False
        return is_raw and (op.idx - dep.idx) <= 3

    def emit(self, block, extra_final=None):
        nc = self.nc
        for e in self.ENGS:
            for op in self.ops[e]:
                for dep, is_raw in op.deps.items():
                    if self._needs_wait(op, dep, is_raw) and not dep.dma:
                        dep.signal = True
        sems = {e: self._sem_ctx[e] for e in self.ENGS}
        dsems = self._dsem
        for e in self.ENGS:
            n = 0
            for op in self.ops[e]:
                if op.dma:
                    op.sem = dsems[op.dkey]
                else:
                    op.sem = sems[e]
                    if op.signal:
                        n += 1
                        op.sigval = n

        def run_engine(ename, eng):
            known = {}
            for op in self.ops[ename]:
                need = {}
                for dep, is_raw in op.deps.items():
                    if not self._needs_wait(op, dep, is_raw):
                        continue
                    s = dep.sem
                    v = dep.sigval
                    if need.get(s, 0) < v:
                        need[s] = v
                for s, v in need.items():
                    if known.get(s, 0) < v:
                        eng.wait_ge(s, v)
                        known[s] = v
                ins = op.fn(eng)
                if op.dma:
                    ins.then_inc(op.sem, 16)
                elif op.signal:
                    ins.then_inc(op.sem, 1)
            if extra_final is not None:
                extra_final(ename, eng, known)

        @block.tensor
        def _(eng):
            run_engine("pe", eng)

        @block.scalar
        def _(eng):
            run_engine("act", eng)

        @block.vector
        def _(eng):
            run_engine("dve", eng)

        @block.gpsimd
        def _(eng):
            run_engine("pool", eng)

        @block.sync
        def _(eng):
            run_engine("sp", eng)


L = 4096
D = 1024
NST = 8
EPS = 1e-6
LAM_INIT = 0.8 - 0.6 * math.exp(-0.3 * 0)
TWO_PI = 2.0 * math.pi
NEG = -30000.0


def _bucket_np(rel):
    nb = 16
    me = 8
    side = np.where(rel > 0, nb, 0)
    n = np.abs(rel)
    nf = np.maximum(n, 1).astype(np.float32)
    large = me + (np.log(nf / np.float32(me)).astype(np.float32) / np.float32(math.log(128 / 8))
                  * np.float32(nb - me)).astype(np.int32)
    large = np.minimum(large, nb - 1)
    return side + np.where(n < me, n, large)


def host_consts():
    c = {}
    c["c_ident"] = np.eye(128, dtype=np.float32).astype(ml_dtypes.bfloat16)
    bo = np.zeros((128, 128), np.float32)
    bo[:64, :64] = 1.0 / 64
    bo[64:, 64:] = 1.0 / 64
    c["c_bones"] = bo.astype(ml_dtypes.bfloat16)
    c["c_J"] = np.ascontiguousarray(np.eye(128, dtype=np.float32)[::-1])
    rel = np.arange(-255, 128)
    b = _bucket_np(rel)
    oh = np.zeros((32, 384), np.float32)
    oh[b, np.arange(383)] = 1.0
    c["c_oh"] = oh
    k = np.arange(128)[:, None]
    q = np.arange(128)[None, :]
    c["c_maskD"] = np.where((k // 64) <= (q // 64), 0.0, NEG).astype(np.float32)
    c["c_iotap"] = np.stack([np.arange(128), -np.arange(128)], 1).astype(np.float32)
    c["c_iotat"] = np.tile(np.arange(128, dtype=np.float32)[None, :], (128, 1))
    c["c_tri"] = (np.arange(128)[:, None] <= np.arange(128)[None, :]).astype(np.float32).astype(ml_dtypes.bfloat16)
    return c


class Mem:
    def __init__(self, nc, start=16640, end=229376):
        self.nc = nc
        self.p = start
        self.end = end
        self.n = 0

    def alloc(self, name, shape, dtype):
        esz = 4 if dtype in (F32, mybir.dt.int32) else 2
        size = int(np.prod(shape[1:])) * esz
        size = (size + 63) // 64 * 64
        assert self.p + size <= self.end, f"SBUF overflow allocating {name}: {self.p}+{size} > {self.end}"
        self.n += 1
        t = self.nc.alloc_sbuf_tensor_at(f"{name}_{self.n}", list(shape), dtype, offset=self.p)
        self.p += size
        return t

    def mark(self):
        return self.p

    def release(self, m):
        self.p = m


def build_nc(dbg=None):
    nc = bass.Bass("TRN2", target_bir_lowering=False)
    S = Sched(nc)
    M = Mem(nc)

    def din(name, shape, dt=F32):
        return nc.dram_tensor(name, list(shape), dt, kind="ExternalInput").ap()

    x_d = din("x", [L, D])
    w_in_d = din("w_in", [D, 5120])
    ng_d = din("norm_gain", [D])
    mb_d = din("merge_gate_b", [2048])
    qg_d = din("q_norm_gain", [64])
    kg_d = din("k_norm_gain", [64])
    lq1_d = din("lambda_q1", [64]); lk1_d = din("lambda_k1", [64])
    lq2_d = din("lambda_q2", [64]); lk2_d = din("lambda_k2", [64])
    sg_d = din("diff_subln_gain", [128])
    rb_d = din("rel_bias_table", [32, 4])
    are_d = din("ssm_A_re", [2048]); aim_d = din("ssm_A_im", [2048])
    ldt_d = din("ssm_log_dt", [32])
    bre_d = din("ssm_B_re", [32 * 64 * 16]); bim_d = din("ssm_B_im", [32 * 64 * 16])
    cre_d = din("ssm_C_re", [32 * 16 * 64]); cim_d = din("ssm_C_im", [32 * 16 * 64])
    dd_d = din("ssm_D", [512])
    gw_d = din("ssm_glu_w", [512, 512])
    gb_d = din("ssm_glu_b", [512])
    pa_d = din("proj_attn", [512, 1024])
    ps_d = din("proj_ssm", [512, 1024])
    wo_d = din("w_out", [1024, 1024])
    c_ident_d = din("c_ident", [128, 128], BF16)
    c_bones_d = din("c_bones", [128, 128], BF16)
    c_J_d = din("c_J", [128, 128])
    c_oh_d = din("c_oh", [32, 384])
    c_maskD_d = din("c_maskD", [128, 128])
    c_iotap_d = din("c_iotap", [128, 2])
    c_iotat_d = din("c_iotat", [128, 128])
    c_tri_d = din("c_tri", [128, 128], BF16)
    out_d = nc.dram_tensor("out", [L, D], F32, kind="ExternalOutput").ap()
    fsc_d = nc.dram_tensor("fscratch", [4, 384], F32).ap()
    dbg_d = None
    if dbg:
        dbg_d = nc.dram_tensor("dbg", [128, 4 * L], BF16, kind="ExternalOutput").ap()

    def dap(t, off, pat):
        return bass.AP(t.tensor, off, [list(p) for p in pat])

    PS = [nc.alloc_psum_tensor(f"psb{i}", [128, 512], F32) for i in range(8)]

    def MM(out, lhsT, rhs, start, stop, r, w):
        S.add("pe", lambda e: e.matmul(out, lhsT=lhsT, rhs=rhs, start=start, stop=stop), r, w)

    def TR(out, in_, r, w):
        S.add("pe", lambda e: e.transpose(out=out, in_=in_, identity=ident[:]), tuple(r) + ("ident",), w)

    def ACT(out, in_, func, r, w, bias=0.0, scale=1.0, accum=None):
        if accum is None:
            S.add("act", lambda e: e.activation(out=out, in_=in_, func=func, bias=bias, scale=scale), r, w)
        else:
            S.add("act", lambda e: e.activation(out=out, in_=in_, func=func, bias=bias, scale=scale, accum_out=accum), r, w)

    def TS(eng, out, in0, s1, s2, op0, op1, r, w):
        if s2 is None:
            S.add(eng, lambda e: e.tensor_scalar(out=out, in0=in0, scalar1=s1, scalar2=None, op0=op0), r, w)
        else:
            S.add(eng, lambda e: e.tensor_scalar(out=out, in0=in0, scalar1=s1, scalar2=s2, op0=op0, op1=op1), r, w)

    def TT(eng, out, in0, in1, op, r, w):
        S.add(eng, lambda e: e.tensor_tensor(out=out, in0=in0, in1=in1, op=op), r, w)

    def STT(eng, out, in0, scalar, in1, op0, op1, r, w):
        S.add(eng, lambda e: e.scalar_tensor_tensor(out=out, in0=in0, scalar=scalar, in1=in1, op0=op0, op1=op1), r, w)

    def CP(eng, out, in_, r, w):
        if eng == "act":
            S.add("act", lambda e: e.copy(out=out, in_=in_), r, w)
        else:
            S.add(eng, lambda e: e.tensor_copy(out=out, in_=in_), r, w)

    def RSUM(out, in_, r, w):
        S.add("dve", lambda e: e.reduce_sum(out=out, in_=in_, axis=AX.X), r, w)

    def RECIP(out, in_, r, w):
        S.add("dve", lambda e: e.reciprocal(out=out, in_=in_), r, w)

    def NOP(r):
        S.add("sp", lambda e: e.nop(), r, [])

    def MS(eng, ap, val, w):
        S.add(eng, lambda e: e.memset(ap, val), (), w)

    def DMA(eng, out, in_, r, w, dkey, slow=False):
        if slow:
            S.add(eng, lambda e: e.dma_start(out=out, in_=in_, allow_slow_non_contiguous=True), r, w, dma=True, dkey=dkey)
        else:
            S.add(eng, lambda e: e.dma_start(out=out, in_=in_), r, w, dma=True, dkey=dkey)

    bar = M.alloc("bar", [128, 16], F32)
    S._bar_tile = bar
    ident = M.alloc("ident", [128, 128], BF16)
    bones = M.alloc("bones", [128, 128], BF16)
    tri = M.alloc("tri", [128, 128], BF16)
    iotap = M.alloc("iotap", [128, 2], F32)
    iotat = M.alloc("iotat", [128, 128], F32)
    gcol = M.alloc("gcol", [128, 8], F32)
    small = M.alloc("small", [128, 64], F32)
    rs_all = M.alloc("rs_all", [128, 3 * 32 * 2], F32)
    OAT = M.alloc("OAT", [128, 4, L], BF16)
    OSTm = M.alloc("OSTm", [128, 4 * L], BF16)
    OST = OSTm[:, :].rearrange("p (c t) -> p c t", c=4)
    stag32 = OSTm[:, :].bitcast(F32)
    Bst = stag32[:, 0:4096].rearrange("p (r c g q) -> p r c g q", r=2, c=4, g=8)
    Cst = stag32[:, 4096:8192].rearrange("p (r a c) -> p r a c", r=2, a=16)

    DMA("sp", ident[:], c_ident_d[:, :], (), ["ident"], "ident")
    DMA("sp", bones[:], c_bones_d[:, :], (), ["bones"], "bones")
    DMA("sp", tri[:], c_tri_d[:, :], (), ["tri"], "tri")
    DMA("sp", iotap[:], c_iotap_d[:, :], (), ["iotap"], "iotap")
    DMA("sp", iotat[:], c_iotat_d[:, :], (), ["iotat"], "iotat")
    DMA("sp", gcol[:], dap(ng_d, 0, [[1, 128], [128, 8]]), (), ["gcol"], "gcol", slow=True)

    MS("dve", stag32[:, 0:4096], 0.0, ["Bst"])
    MS("pool", stag32[:, 4096:8192], 0.0, ["Cst"])
    for g in range(32):
        ct, g8 = g // 8, g % 8
        pair, g2 = g // 2, g % 2
        for ri, (bd, cd) in enumerate(((bre_d, cre_d), (bim_d, cim_d))):
            DMA("act", Bst[16 * g8:16 * g8 + 16, ri, ct, g8, :], dap(bd, g * 1024, [[1, 16], [16, 64]]),
                (), ["Bst"], "Bst", slow=True)
            DMA("act", Cst[64 * g2:64 * g2 + 64, ri, pair, (g % 8) * 16:(g % 8) * 16 + 16],
                dap(cd, g * 1024, [[1, 64], [64, 16]]), (), ["Cst"], "Cst", slow=True)

    def load_weight(dst, src_d, row0, nrows_tiles, col0, ncols, stg, fold_gain, keyw):
        i = 0
        for kt in range(nrows_tiles):
            for c0 in range(0, ncols, 2048):
                cw = min(2048, ncols - c0)
                sb = stg[i % 2]
                sk = f"stg{i % 2}"
                DMA("sp", sb[:, 0:cw], src_d[row0 + kt * 128: row0 + (kt + 1) * 128, col0 + c0: col0 + c0 + cw],
                    (), [sk], sk)
                if fold_gain:
                    if i % 2 == 0:
                        ACT(dst[:, kt, c0:c0 + cw], sb[:, 0:cw], AF.Copy, [sk, "gcol"], [keyw], scale=gcol[:, kt:kt + 1])
                    else:
                        TS("dve", dst[:, kt, c0:c0 + cw], sb[:, 0:cw], gcol[:, kt:kt + 1], None, ALU.mult, None, [sk, "gcol"], [keyw])
                else:
                    if i % 2 == 0:
                        CP("act", dst[:, kt, c0:c0 + cw], sb[:, 0:cw], [sk], [keyw])
                    else:
                        CP("dve", dst[:, kt, c0:c0 + cw], sb[:, 0:cw], [sk], [keyw])
                i += 1

    def make_hT(st, phase, hT, xt, xn, junk):
        psT = PS[0][:, 0:512].bitcast(BF16)
        for tt in range(4):
            i = st * 4 + tt
            b = i % len(xt)
            row = st * 512 + tt * 128
            col = (phase * 32 + i) * 2
            ss = rs_all[:, col:col + 1]
            rs = rs_all[:, col + 1:col + 2]
            DMA("sp", xt[b][:], x_d[row:row + 128, :], (), [f"xt{b}"], f"xt{b}")
            ACT(junk[:], xt[b][:], AF.Square, [f"xt{b}"], ["junk", f"ss{col}"], accum=ss)
            ACT(rs, ss, AF.Ln, [f"ss{col}"], [f"rs{col}"], bias=EPS, scale=1.0 / D)
            ACT(rs, rs, AF.Exp, [f"rs{col}"], [f"rs{col}"], scale=-0.5)
            TS("dve", xn[:], xt[b][:], rs, None, ALU.mult, None, [f"xt{b}", f"rs{col}"], ["xn"])
            for kt in range(8):
                TR(psT[:, kt * 128:(kt + 1) * 128], xn[:, kt * 128:(kt + 1) * 128], ["xn"], ["ps0"])
            CP("dve", hT[:, :, tt * 128:(tt + 1) * 128], psT[:, :].rearrange("p (k t) -> p k t", k=8), ["ps0"], ["hT"])

    m_glob = M.mark()
    wq = M.alloc("wq", [128, 8, 1536], BF16)
    KT = M.alloc("KT", [128, 4, L], BF16)
    V = M.alloc("V", [128, 32, 4, 129], BF16)
    c15 = M.alloc("c15", [128, 4], F32)
    biasP = M.alloc("biasP", [128, 4, 128], F32)
    biasD = M.alloc("biasD", [128, 4, 128], F32)
    gq = M.alloc("gq", [128, 2], F32)
    sgrow = M.alloc("sgrow", [128, 128], F32)
    lamt = M.alloc("lamt", [128, 8], F32)
    m_work = M.mark()
    stg = [M.alloc("stg", [128, 2048], F32) for _ in range(2)]
    lamv = M.alloc("lamv", [128, 4, 64], F32)
    tbl = M.alloc("tbl", [32, 4], F32)
    oh = M.alloc("oh", [32, 384], F32)
    fsb = M.alloc("fsb", [4, 384], F32)
    Jt = M.alloc("Jt", [128, 128], F32)
    hank = M.alloc("hank", [128, 4, 256], F32)
    maskD = M.alloc("maskD", [128, 128], F32)

    load_weight(wq, w_in_d, 0, 8, 0, 1536, stg, True, "wq")
    MS("pool", V[:, :, :, 128:129], 1.0, ["Vones"])
    for hlf in range(2):
        DMA("sp", gq[64 * hlf:64 * hlf + 64, 0:1], dap(qg_d, 0, [[1, 64], [1, 1]]), (), ["gq"], "gq")
        DMA("sp", gq[64 * hlf:64 * hlf + 64, 1:2], dap(kg_d, 0, [[1, 64], [1, 1]]), (), ["gq"], "gq")
    TS("dve", gq[:, 0:1], gq[:, 0:1], 0.125, None, ALU.mult, None, ["gq"], ["gq"])
    DMA("sp", sgrow[:], dap(sg_d, 0, [[0, 128], [1, 128]]), (), ["sgrow"], "sgrow")
    TS("dve", sgrow[:], sgrow[:], 1.0 - LAM_INIT, None, ALU.mult, None, ["sgrow"], ["sgrow"])
    for i, dd in enumerate((lq1_d, lk1_d, lq2_d, lk2_d)):
        DMA("sp", lamv[:, i, :], dap(dd, 0, [[0, 128], [1, 64]]), (), ["lamv"], "lamv")
    TT("dve", lamv[:, 0, :], lamv[:, 0, :], lamv[:, 1, :], ALU.mult, ["lamv"], ["lamv"])
    TT("dve", lamv[:, 2, :], lamv[:, 2, :], lamv[:, 3, :], ALU.mult, ["lamv"], ["lamv"])
    RSUM(lamt[:, 0:1], lamv[:, 0, :], ["lamv"], ["lamt"])
    RSUM(lamt[:, 1:2], lamv[:, 2, :], ["lamv"], ["lamt"])
    ACT(lamt[:, 2:4], lamt[:, 0:2], AF.Exp, ["lamt"], ["lamt"])
    TT("dve", lamt[:, 4:5], lamt[:, 3:4], lamt[:, 2:3], ALU.subtract, ["lamt"], ["lamt"])
    TS("dve", lamt[:, 5:6], lamt[:, 4:5], -LAM_INIT, None, ALU.add, None, ["lamt"], ["lamt"])
    neglam = lamt[:, 5:6]
    DMA("sp", tbl[:], rb_d[:, :], (), ["tbl"], "tbl")
    DMA("sp", oh[:], c_oh_d[:, :], (), ["oh"], "oh")
    DMA("sp", Jt[:], c_J_d[:, :], (), ["Jt"], "Jt")
    DMA("sp", maskD[:], c_maskD_d[:, :], (), ["maskD"], "maskD")
    DMA("sp", c15[:], dap(rb_d, 15 * 4, [[0, 128], [1, 4]]), (), ["c15"], "c15")
    MM(PS[1][0:4, 0:384], tbl[:], oh[:], True, True, ["tbl", "oh"], ["ps1"])
    CP("dve", fsb[:], PS[1][0:4, 0:384], ["ps1"], ["fsb"])
    DMA("sp", fsc_d[:, :], fsb[:], ["fsb"], ["fsc"], "fsb")
    for h in range(4):
        DMA("sp", hank[:, h, :], dap(fsc_d, h * 384, [[1, 128], [1, 256]]), ["fsc"], ["hank"], "hank")
    for h in range(4):
        MM(PS[2][:, h * 128:(h + 1) * 128], hank[:, h, 0:128], Jt[:], True, True, ["hank", "Jt"], ["ps2"])
        MM(PS[3][:, h * 128:(h + 1) * 128], hank[:, h, 128:256], Jt[:], True, True, ["hank", "Jt"], ["ps3"])
    CP("dve", biasP[:], PS[2][:, :].rearrange("p (h q) -> p h q", h=4), ["ps2"], ["biasP"])
    TT("dve", biasD[:], PS[3][:, :].rearrange("p (h q) -> p h q", h=4),
       maskD[:].unsqueeze(1).broadcast_to([128, 4, 128]), ALU.add, ["ps3", "maskD"], ["biasD"])
    S.barrier()
    M.release(m_work)
    hT = M.alloc("hT", [128, 8, 512], BF16)
    xt = [M.alloc("xt", [128, 1024], F32)]
    xn = M.alloc("xn", [128, 1024], BF16)
    junk = M.alloc("junk", [128, 1024], BF16)
    qT = M.alloc("qT", [128, 4, 512], BF16)
    sq = M.alloc("sq", [128, 512], BF16)
    rstd = M.alloc("rstd", [128, 512], F32)
    PT = [M.alloc("PT", [128, 512], BF16) for _ in range(2)]
    tmpb = M.alloc("tmpb", [128, 128], F32)
    oacc = M.alloc("oacc", [128, 2, 4, 129], F32)
    t0 = M.alloc("t0", [128, 4, 128], F32)
    t1 = M.alloc("t1", [128, 4, 128], F32)
    fst = M.alloc("fst", [128, 16], F32)
    og = M.alloc("og", [128, 4, 128], BF16)

    ACCB = [5, 6, 7, 1]
    for st in range(NST):
        make_hT(st, 0, hT, xt, xn, junk)
        for c in range(8):
            for kt in range(8):
                MM(PS[1][:, :], wq[:, kt, c * 128:(c + 1) * 128], hT[:, kt, :], kt == 0, kt == 7, ["wq", "hT"], ["ps1"])
            ACT(sq[:], PS[1][:, :], AF.Square, ["ps1"], ["sq"])
            MM(PS[2][:, :], bones[:], sq[:], True, True, ["bones", "sq"], ["ps2"])
            ACT(rstd[:], PS[2][:, :], AF.Ln, ["ps2"], ["rstd"], bias=EPS)
            ACT(rstd[:], rstd[:], AF.Exp, ["rstd"], ["rstd"], scale=-0.5)
            if c < 4:
                STT("dve", qT[:, c, :], PS[1][:, :], gq[:, 0:1], rstd[:], ALU.mult, ALU.mult, ["ps1", "gq", "rstd"], ["qT"])
            else:
                STT("dve", KT[:, c - 4, st * 512:(st + 1) * 512], PS[1][:, :], gq[:, 1:2], rstd[:], ALU.mult, ALU.mult,
                    ["ps1", "gq", "rstd"], [f"KT{st}"])
        for tt in range(4):
            blk = st * 4 + tt
            for kt in range(8):
                MM(PS[2][:, :], hT[:, kt, tt * 128:(tt + 1) * 128], wq[:, kt, 1024:1536], kt == 0, kt == 7, ["wq", "hT"], ["ps2"])
            CP("dve", V[:, blk, :, 0:128], PS[2][:, :].rearrange("p (h d) -> p h d", h=4), ["ps2"], [f"V{st}"])
        kv_keys = [f"KT{i}" for i in range(st + 1)] + [f"V{i}" for i in range(st + 1)] + ["Vones"]
        it = 0
        for h in range(4):
            for s in range(2):
                for j in range(4 * st + 4):
                    lo = max(0, j - 4 * st)
                    cols = slice(lo * 128, 512)
                    pb = it % 2
                    it += 1
                    psS = PS[3 + pb]
                    MM(psS[:, cols], KT[64 * s:64 * s + 64, h, j * 128:(j + 1) * 128], qT[64 * s:64 * s + 64, h, cols],
                       True, True, kv_keys + ["qT"], [f"ps{3 + pb}"])
                    far_lo = lo
                    for qb in range(lo, 4):
                        d = 4 * st + qb - j
                        if d >= 2:
                            break
                        bt = biasD if d == 0 else biasP
                        TT("dve", tmpb[:], psS[:, qb * 128:(qb + 1) * 128], bt[:, h, :], ALU.add,
                           [f"ps{3 + pb}", "biasD", "biasP"], ["tmpb"])
                        ACT(PT[pb][:, qb * 128:(qb + 1) * 128], tmpb[:], AF.Exp, ["tmpb"], [f"PT{pb}"])
                        far_lo = qb + 1
                    if far_lo < 4:
                        ACT(PT[pb][:, far_lo * 128:512], psS[:, far_lo * 128:512], AF.Exp, [f"ps{3 + pb}", "c15"], [f"PT{pb}"],
                            bias=c15[:, h:h + 1])
                    for qb in range(lo, 4):
                        ab = ACCB[qb]
                        MM(PS[ab][:, 0:129], PT[pb][:, qb * 128:(qb + 1) * 128], V[:, j, h, :], j == 0, j == 4 * st + qb,
                           [f"PT{pb}"] + kv_keys, [f"ps{ab}"])
                for qb in range(4):
                    ab = ACCB[qb]
                    CP("dve", oacc[:, s, qb, :], PS[ab][:, 0:129], [f"ps{ab}"], ["oacc"])
            RECIP(fst[:, 0:8], oacc[:, :, :, 128:129].rearrange("p s q o -> p (s q o)"), ["oacc"], ["fst"])
            TT("dve", t0[:], oacc[:, 0, :, 0:128], fst[:, 0:4].unsqueeze(2).broadcast_to([128, 4, 128]), ALU.mult, ["oacc", "fst"], ["t0"])
            TT("dve", t1[:], oacc[:, 1, :, 0:128], fst[:, 4:8].unsqueeze(2).broadcast_to([128, 4, 128]), ALU.mult, ["oacc", "fst"], ["t1"])
            STT("dve", t0[:], t1[:], neglam, t0[:], ALU.mult, ALU.add, ["t0", "t1", "lamt"], ["t0"])
            TT("dve", t1[:], t0[:], t0[:], ALU.mult, ["t0"], ["t1"])
            RSUM(fst[:, 8:12], t1[:], ["t1"], ["fst2"])
            ACT(fst[:, 12:16], fst[:, 8:12], AF.Ln, ["fst2"], ["fst3"], bias=EPS, scale=1.0 / 128)
            ACT(fst[:, 12:16], fst[:, 12:16], AF.Exp, ["fst3"], ["fst3"], scale=-0.5)
            TT("dve", t0[:], t0[:], fst[:, 12:16].unsqueeze(2).broadcast_to([128, 4, 128]), ALU.mult, ["t0", "fst3"], ["t0"])
            TT("dve", t0[:], t0[:], sgrow[:].unsqueeze(1).broadcast_to([128, 4, 128]), ALU.mult, ["t0", "sgrow"], ["t0"])
            CP("dve", og[:], t0[:], ["t0"], ["og"])
            psT = PS[0][:, 0:512].bitcast(BF16)
            for qb in range(4):
                TR(psT[:, qb * 128:(qb + 1) * 128], og[:, qb, :], ["og"], ["ps0"])
            CP("dve", OAT[:, h, st * 512:(st + 1) * 512], psT[:, 0:512], ["ps0"], ["OAT"])

    if dbg == "p1":
        DMA("sp", dbg_d[:, :], OAT[:, :, :].rearrange("p h t -> p (h t)"), ["OAT"], ["outd"], "OAT")
        NOP(["outd"])
        return finish(nc, S)
    I32 = mybir.dt.int32
    S.barrier()
    M.release(m_glob)
    w2 = M.alloc("w2", [128, 8, 1024], BF16)
    gluw = M.alloc("gluw", [128, 4, 512], BF16)
    glub = M.alloc("glub", [128, 4], F32)
    ta = M.alloc("ta", [128, 2048], F32)
    tb = M.alloc("tb", [128, 2048], F32)
    te = M.alloc("te", [128, 16, 128], F32)
    tf = M.alloc("tf", [128, 16, 128], F32)
    Bbd = M.alloc("Bbd", [128, 4, 2, 512], BF16)
    Ccat = M.alloc("Ccat", [128, 2, 16, 128], BF16)
    Ddiag = M.alloc("Ddiag", [128, 4, 128], BF16)
    dcol = M.alloc("dcol", [128, 4], F32)
    cre = [M.alloc("cre", [128, 16], F32) for _ in range(2)]
    cim = [M.alloc("cim", [128, 16], F32) for _ in range(2)]
    m_work2 = M.mark()
    stg = [M.alloc("stg", [128, 2048], F32) for _ in range(2)]
    T1 = M.alloc("T1", [128, 2048], F32)
    T2 = M.alloc("T2", [128, 2048], F32)
    T3 = M.alloc("T3", [128, 2048], F32)
    T4 = M.alloc("T4", [128, 2048], F32)
    TI = M.alloc("TI", [128, 2048], I32)
    dtb = M.alloc("dtb", [128, 32], F32)
    s2 = M.alloc("s2", [128, 8, 16], F32)
    s3 = M.alloc("s3", [128, 14, 256], F32)
    s3i = M.alloc("s3i", [128, 256], I32)
    dt3 = M.alloc("dt3", [128, 4], F32)

    load_weight(w2, w_in_d, 0, 8, 2048, 1024, stg, True, "w2")
    load_weight(gluw, gw_d, 0, 4, 0, 512, stg, False, "gluw")
    DMA("sp", glub[:], dap(gb_d, 0, [[1, 128], [128, 4]]), (), ["glub"], "glub", slow=True)

    def frac_sincos(y, tmp, ti, sin_out, cos_out, key):
        CP("dve", ti, y, [key + "y"], [key + "ti"])
        CP("dve", tmp, ti, [key + "ti"], [key + "tmp"])
        TT("dve", tmp, y, tmp, ALU.subtract, [key + "y", key + "tmp"], [key + "tmp"])
        ACT(sin_out, tmp, AF.Sin, [key + "tmp"], [key + "sin"], scale=TWO_PI)
        TS("dve", y, y, 0.25, None, ALU.add, None, [key + "y"], [key + "y"])
        CP("dve", ti, y, [key + "y"], [key + "ti"])
        CP("dve", tmp, ti, [key + "ti"], [key + "tmp"])
        TT("dve", tmp, y, tmp, ALU.subtract, [key + "y", key + "tmp"], [key + "tmp"])
        ACT(cos_out, tmp, AF.Sin, [key + "tmp"], [key + "cos"], scale=TWO_PI)

    DMA("sp", T1[:], dap(are_d, 0, [[0, 128], [1, 2048]]), (), ["T1"], "T1")
    DMA("sp", T2[:], dap(aim_d, 0, [[0, 128], [1, 2048]]), (), ["T2"], "T2")
    DMA("sp", dtb[:], dap(ldt_d, 0, [[0, 128], [1, 32]]), (), ["dtb"], "dtb")
    ACT(dtb[:], dtb[:], AF.Exp, ["dtb"], ["dtb"])
    dtb_b = dtb[:, :].unsqueeze(2).broadcast_to([128, 32, 64])
    TT("dve", T1[:, :].rearrange("p (g q) -> p g q", g=32), T1[:, :].rearrange("p (g q) -> p g q", g=32), dtb_b, ALU.mult, ["T1", "dtb"], ["T1"])
    TT("dve", T2[:, :].rearrange("p (g q) -> p g q", g=32), T2[:, :].rearrange("p (g q) -> p g q", g=32), dtb_b, ALU.mult, ["T2", "dtb"], ["T2"])
    ACT(ta[:], T1[:], AF.Exp, ["T1", "iotap"], ["ta"], scale=iotap[:, 1:2])
    TS("dve", T3[:], T2[:], iotap[:, 0:1], 1.0 / TWO_PI, ALU.mult, ALU.mult, ["T2", "iotap"], ["L1y"])
    frac_sincos(T3[:], T4[:], TI[:], tb[:], T1[:], "L1")
    STT("dve", tb[:], tb[:], -1.0, ta[:], ALU.mult, ALU.mult, ["L1sin", "ta"], ["tb", "L1sin"])
    TT("dve", ta[:], ta[:], T1[:], ALU.mult, ["ta", "L1cos", "tb"], ["ta"])
    A2re, A2im, dt2, m2, th2 = (s2[:, i, :] for i in range(5))
    DMA("sp", A2re, dap(are_d, 0, [[1, 128], [128, 16]]), (), ["A2re"], "A2re", slow=True)
    DMA("sp", A2im, dap(aim_d, 0, [[1, 128], [128, 16]]), (), ["A2im"], "A2im", slow=True)
    for g2 in range(2):
        DMA("sp", s2[64 * g2:64 * g2 + 64, 2, :], dap(ldt_d, g2, [[0, 64], [2, 16]]), (), ["dt2"], "dt2", slow=True)
    ACT(dt2, dt2, AF.Exp, ["dt2"], ["dt2"])
    TT("dve", m2, A2re, dt2, ALU.mult, ["A2re", "dt2"], ["m2"])
    STT("dve", th2, A2im, 1.0 / TWO_PI, dt2, ALU.mult, ALU.mult, ["A2im", "dt2"], ["th2"])
    T3v = T3[:, :].rearrange("p (a t) -> p a t", a=16)
    for pair in range(16):
        ACT(te[:, pair, :], iotat[:], AF.Exp, ["iotat", "m2"], ["te"], scale=s2[:, 3, pair:pair + 1])
        TS("dve", T3v[:, pair, :], iotat[:], s2[:, 4, pair:pair + 1], None, ALU.mult, None, ["iotat", "th2", "L1y", "L1tmp"], ["L2y"])
    frac_sincos(T3[:], T4[:], TI[:], tf[:, :, :].rearrange("p a t -> p (a t)"), T1[:], "L2")
    TT("dve", tf[:, :, :].rearrange("p a t -> p (a t)"), tf[:, :, :].rearrange("p a t -> p (a t)"), te[:, :, :].rearrange("p a t -> p (a t)"),
       ALU.mult, ["L2sin", "te"], ["tf", "L2sin"])
    TT("dve", te[:, :, :].rearrange("p a t -> p (a t)"), te[:, :, :].rearrange("p a t -> p (a t)"), T1[:], ALU.mult, ["te", "L2cos", "tf"], ["te"])
    A3re, A3im, m3, y3, dec3, sin3, cos3, nr3, den3, qre3, qim3, u3a, u3b, tmp3 = (s3[:, i, :] for i in range(14))
    for g8 in range(8):
        DMA("sp", s3[16 * g8:16 * g8 + 16, 0, :].rearrange("p (c q) -> p c q", c=4), dap(are_d, g8 * 64, [[0, 16], [512, 4], [1, 64]]), (), ["A3re"], "A3re")
        DMA("sp", s3[16 * g8:16 * g8 + 16, 1, :].rearrange("p (c q) -> p c q", c=4), dap(aim_d, g8 * 64, [[0, 16], [512, 4], [1, 64]]), (), ["A3im"], "A3im")
        DMA("sp", dt3[16 * g8:16 * g8 + 16, :], dap(ldt_d, g8, [[0, 16], [8, 4]]), (), ["dt3"], "dt3", slow=True)
    ACT(dt3[:], dt3[:], AF.Exp, ["dt3"], ["dt3"])
    dt3_b = dt3[:, :].unsqueeze(2).broadcast_to([128, 4, 64])
    v3 = lambda a: a.rearrange("p (c q) -> p c q", c=4)
    TT("dve", v3(m3), v3(A3re), dt3_b, ALU.mult, ["A3re", "dt3"], ["m3"])
    TT("dve", v3(y3), v3(A3im), dt3_b, ALU.mult, ["A3im", "dt3"], ["L3y"])
    TS("dve", y3, y3, 1.0 / TWO_PI, None, ALU.mult, None, ["L3y"], ["L3y"])
    ACT(dec3, m3, AF.Exp, ["m3"], ["dec3"])
    frac_sincos(y3, tmp3, s3i[:], sin3, cos3, "L3")
    TT("dve", cos3, cos3, dec3, ALU.mult, ["L3cos", "dec3"], ["lbr"])
    TT("dve", sin3, sin3, dec3, ALU.mult, ["L3sin", "dec3"], ["lbi"])
    TS("dve", nr3, cos3, -1.0, None, ALU.add, None, ["lbr"], ["nr3"])
    TT("dve", den3, A3re, A3re, ALU.mult, ["A3re"], ["den3"])
    TT("dve", u3a, A3im, A3im, ALU.mult, ["A3im"], ["u3a"])
    TT("dve", den3, den3, u3a, ALU.add, ["den3", "u3a"], ["den3"])
    RECIP(den3, den3, ["den3"], ["den3"])
    TT("dve", u3a, nr3, A3re, ALU.mult, ["nr3", "A3re", "den3"], ["u3a"])
    TT("dve", u3b, sin3, A3im, ALU.mult, ["lbi", "A3im"], ["u3b"])
    TT("dve", qre3, u3a, u3b, ALU.add, ["u3a", "u3b"], ["qre3"])
    TT("dve", qre3, qre3, den3, ALU.mult, ["qre3", "den3"], ["qre3"])
    TT("dve", u3a, sin3, A3re, ALU.mult, ["lbi", "A3re", "qre3"], ["u3a"])
    TT("dve", u3b, nr3, A3im, ALU.mult, ["nr3", "A3im", "qre3"], ["u3b"])
    TT("dve", qim3, u3a, u3b, ALU.subtract, ["u3a", "u3b"], ["qim3"])
    TT("dve", qim3, qim3, den3, ALU.mult, ["qim3", "den3"], ["qim3"])
    qre_b = v3(qre3).unsqueeze(2).broadcast_to([128, 4, 8, 64])
    qim_b = v3(qim3).unsqueeze(2).broadcast_to([128, 4, 8, 64])
    T1v4 = T1[:, :].rearrange("p (c g q) -> p c g q", c=4, g=8)
    T2v4 = T2[:, :].rearrange("p (c g q) -> p c g q", c=4, g=8)
    TT("dve", T1v4, Bst[:, 0], qre_b, ALU.mult, ["Bst", "qre3", "ta", "te"], ["T1"])
    TT("dve", T2v4, Bst[:, 1], qim_b, ALU.mult, ["Bst", "qim3", "L1y", "L2y"], ["T2"])
    TT("dve", Bbd[:, :, 0, :], T1[:, :].rearrange("p (c x) -> p c x", c=4), T2[:, :].rearrange("p (c x) -> p c x", c=4), ALU.subtract, ["T1", "T2"], ["Bbd"])
    TT("dve", T1v4, Bst[:, 1], qre_b, ALU.mult, ["Bst", "qre3", "Bbd"], ["T1"])
    TT("dve", T2v4, Bst[:, 0], qim_b, ALU.mult, ["Bst", "qim3", "Bbd"], ["T2"])
    TT("dve", Bbd[:, :, 1, :], T1[:, :].rearrange("p (c x) -> p c x", c=4), T2[:, :].rearrange("p (c x) -> p c x", c=4), ALU.add, ["T1", "T2"], ["Bbd"])
    CP("dve", Ccat[:, 0], Cst[:, 0], ["Cst"], ["Ccat"])
    TS("dve", Ccat[:, 1], Cst[:, 1], -1.0, None, ALU.mult, None, ["Cst"], ["Ccat"])
    DMA("sp", dcol[:], dap(dd_d, 0, [[1, 128], [128, 4]]), (), ["dcol"], "dcol", slow=True)
    for ct in range(4):
        TS("dve", Ddiag[:, ct, :], ident[:], dcol[:, ct:ct + 1], None, ALU.mult, None, ["ident", "dcol"], ["Ddiag"])
    MS("dve", cre[0][:], 0.0, ["c0"])
    MS("dve", cim[0][:], 0.0, ["c0"])
    S.barrier()
    M.release(m_work2)
    hT = M.alloc("hT", [128, 8, 512], BF16)
    xt = [M.alloc("xt", [128, 1024], F32)]
    xn = M.alloc("xn", [128, 1024], BF16)
    junk = M.alloc("junk", [128, 1024], BF16)
    uT = M.alloc("uT", [128, 4, 512], BF16)
    gsT = M.alloc("gsT", [128, 4, 512], BF16)
    gT = M.alloc("gT", [128, 4, 512], BF16)
    Vt = M.alloc("Vt", [128, 2, 2048], BF16)
    tA = M.alloc("tA", [128, 8, 128], F32)
    tB = M.alloc("tB", [128, 8, 128], F32)
    tC = M.alloc("tC", [128, 8, 128], F32)
    tD = M.alloc("tD", [128, 8, 128], F32)
    sr = M.alloc("sr", [128, 8, 128], F32)
    si = M.alloc("si", [128, 8, 128], F32)
    Xre = [M.alloc("Xre", [128, 8, 128], BF16) for _ in range(2)]
    Xim = [M.alloc("Xim", [128, 8, 128], BF16) for _ in range(2)]
    sg = M.alloc("sg", [128, 512], F32)
    xl = M.alloc("xl", [128, 6, 8], F32)
    fl = lambda a: a[:, :, :].rearrange("p a t -> p (a t)")

    for st in range(NST):
        make_hT(st, 1, hT, xt, xn, junk)
        for ct in range(4):
            for kt in range(8):
                MM(PS[1][:, :], w2[:, kt, ct * 128:(ct + 1) * 128], hT[:, kt, :], kt == 0, kt == 7, ["w2", "hT"], ["ps1"])
            CP("act", uT[:, ct, :], PS[1][:, :], ["ps1"], ["uT"])
        for ct in range(4):
            for kt in range(8):
                MM(PS[1][:, :], w2[:, kt, 512 + ct * 128:512 + (ct + 1) * 128], hT[:, kt, :], kt == 0, kt == 7, ["w2", "hT"], ["ps1"])
            ACT(gsT[:, ct, :], PS[1][:, :], AF.Silu, ["ps1"], ["gsT"])
        for cc in range(4):
            ci = st * 4 + cc
            tok = slice(cc * 128, (cc + 1) * 128)
            cr_in, ci_in = cre[ci % 2], cim[ci % 2]
            cr_out, ci_out = cre[(ci + 1) % 2], cim[(ci + 1) % 2]
            kin, kout = f"c{ci % 2}", f"c{(ci + 1) % 2}"
            for half in range(2):
                for ct in (2 * half, 2 * half + 1):
                    MM(PS[2][:, :], uT[:, ct, tok], Bbd[:, ct, 0, :], True, True, ["uT", "Bbd"], ["ps2"])
                    MM(PS[3][:, :], uT[:, ct, tok], Bbd[:, ct, 1, :], True, True, ["uT", "Bbd"], ["ps3"])
                    a_ = ta[:, ct * 512:(ct + 1) * 512]
                    b_ = tb[:, ct * 512:(ct + 1) * 512]
                    t1, t2, t3, t4 = fl(tA)[:, 0:512], fl(tB)[:, 0:512], fl(tC)[:, 0:512], fl(tD)[:, 0:512]
                    TT("dve", t1, PS[2][:, :], a_, ALU.mult, ["ps2", "ta"], ["tA"])
                    TT("dve", t2, PS[3][:, :], b_, ALU.mult, ["ps3", "tb"], ["tB"])
                    TT("pool", Vt[:, 0, ct * 512:(ct + 1) * 512], t1, t2, ALU.subtract, ["tA", "tB"], ["Vt"])
                    TT("dve", t3, PS[3][:, :], a_, ALU.mult, ["ps3", "ta"], ["tC"])
                    TT("dve", t4, PS[2][:, :], b_, ALU.mult, ["ps2", "tb"], ["tD"])
                    TT("pool", Vt[:, 1, ct * 512:(ct + 1) * 512], t3, t4, ALU.add, ["tC", "tD"], ["Vt"])
                for p8 in range(8):
                    pair = 8 * half + p8
                    sl = slice((p8 % 4) * 128, (p8 % 4) * 128 + 128)
                    MM(PS[4 + p8 // 4][:, sl], Vt[:, 0, pair * 128:(pair + 1) * 128], tri[:], True, True, ["Vt", "tri"], [f"ps{4 + p8 // 4}"])
                    MM(PS[6 + p8 // 4][:, sl], Vt[:, 1, pair * 128:(pair + 1) * 128], tri[:], True, True, ["Vt", "tri"], [f"ps{6 + p8 // 4}"])
                for p8 in range(8):
                    pair = 8 * half + p8
                    sl = slice((p8 % 4) * 128, (p8 % 4) * 128 + 128)
                    ACT(sr[:, p8, :], PS[4 + p8 // 4][:, sl], AF.Identity, [f"ps{4 + p8 // 4}", kin], ["sr"], bias=cr_in[:, pair:pair + 1])
                    ACT(si[:, p8, :], PS[6 + p8 // 4][:, sl], AF.Identity, [f"ps{6 + p8 // 4}", kin], ["si"], bias=ci_in[:, pair:pair + 1])
                eh = te[:, 8 * half:8 * half + 8, :]
                fh = tf[:, 8 * half:8 * half + 8, :]
                TT("dve", tA[:], sr[:], eh, ALU.mult, ["sr", "te"], ["tA"])
                TT("dve", tB[:], si[:], fh, ALU.mult, ["si", "tf"], ["tB"])
                TT("pool", Xre[half][:], tA[:], tB[:], ALU.subtract, ["tA", "tB"], [f"Xre{half}"])
                TT("pool", xl[:, 0, :], tA[:, :, 127], tB[:, :, 127], ALU.subtract, ["tA", "tB"], ["xl0"])
                TT("dve", tC[:], si[:], eh, ALU.mult, ["si", "te"], ["tC"])
                TT("dve", tD[:], sr[:], fh, ALU.mult, ["sr", "tf"], ["tD"])
                TT("pool", Xim[half][:], tC[:], tD[:], ALU.add, ["tC", "tD"], [f"Xim{half}"])
                TT("pool", xl[:, 1, :], tC[:, :, 127], tD[:, :, 127], ALU.add, ["tC", "tD"], ["xl1"])
                lr = te[:, 8 * half:8 * half + 8, 1]
                li = tf[:, 8 * half:8 * half + 8, 1]
                ps_ = slice(8 * half, 8 * half + 8)
                TT("pool", xl[:, 2, :], lr, xl[:, 0, :], ALU.mult, ["te", "xl0"], ["xl2"])
                TT("pool", xl[:, 3, :], li, xl[:, 1, :], ALU.mult, ["tf", "xl1"], ["xl3"])
                TT("pool", cr_out[:, ps_], xl[:, 2, :], xl[:, 3, :], ALU.subtract, ["xl2", "xl3"], [kout])
                TT("pool", xl[:, 4, :], lr, xl[:, 1, :], ALU.mult, ["te", "xl1"], ["xl4"])
                TT("pool", xl[:, 5, :], li, xl[:, 0, :], ALU.mult, ["tf", "xl0"], ["xl5"])
                TT("pool", ci_out[:, ps_], xl[:, 4, :], xl[:, 5, :], ALU.add, ["xl4", "xl5"], [kout])
                for ct in (2 * half, 2 * half + 1):
                    ysl = slice(ct * 128, (ct + 1) * 128)
                    for pr in range(4):
                        pair = 4 * ct + pr
                        p8 = pair - 8 * half
                        MM(PS[1][:, ysl], Ccat[:, 0, pair, :], Xre[half][:, p8, :], pr == 0, False, ["Ccat", f"Xre{half}"], ["ps1"])
                        MM(PS[1][:, ysl], Ccat[:, 1, pair, :], Xim[half][:, p8, :], False, False, ["Ccat", f"Xim{half}"], ["ps1"])
                    MM(PS[1][:, ysl], Ddiag[:, ct, :], uT[:, ct, tok], False, True, ["Ddiag", "uT"], ["ps1"])
            ACT(gT[:, :, tok], PS[1][:, :].rearrange("p (c t) -> p c t", c=4), AF.Gelu, ["ps1"], ["gT"])
        for c in range(4):
            for kt in range(4):
                MM(PS[2][:, :], gluw[:, kt, c * 128:(c + 1) * 128], gT[:, kt, :], kt == 0, kt == 3, ["gluw", "gT"], ["ps2"])
            ACT(sg[:], PS[2][:, :], AF.Sigmoid, ["ps2", "glub"], ["sg"], bias=glub[:, c:c + 1])
            TT("dve", sg[:], sg[:], gT[:, c, :], ALU.mult, ["sg", "gT"], ["sg"])
            TT("dve", OST[:, c, st * 512:(st + 1) * 512], sg[:], gsT[:, c, :], ALU.mult, ["sg", "gsT"], ["OST", "Bst", "Cst"])
    if dbg == "p2":
        DMA("sp", dbg_d[:, :], OSTm[:, :], ["OST"], ["outd"], "OSTm")
        NOP(["outd"])
        return finish(nc, S)

    S.barrier()
    M.release(m_glob)
    w3 = M.alloc("w3", [128, 8, 2560], BF16)
    pa = M.alloc("pa", [128, 4, 1024], BF16)
    psw = M.alloc("psw", [128, 4, 1024], BF16)
    wo = M.alloc("wo", [128, 8, 1024], BF16)
    mb = M.alloc("mb", [128, 16], F32)
    m_work3 = M.mark()
    stg = [M.alloc("stg", [128, 2048], F32) for _ in range(2)]
    load_weight(w3[:, :, 0:512], w_in_d, 0, 8, 1536, 512, stg, True, "w3")
    load_weight(w3[:, :, 512:2560], w_in_d, 0, 8, 3072, 2048, stg, True, "w3")
    load_weight(pa, pa_d, 0, 4, 0, 1024, stg, False, "pa")
    load_weight(psw, ps_d, 0, 4, 0, 1024, stg, False, "psw")
    load_weight(wo, wo_d, 0, 8, 0, 1024, stg, False, "wo")
    DMA("sp", mb[:], dap(mb_d, 0, [[1, 128], [128, 16]]), (), ["mb"], "mb", slow=True)
    S.barrier()
    M.release(m_work3)
    hT = M.alloc("hT", [128, 8, 512], BF16)
    xt = [M.alloc("xt", [128, 1024], F32)]
    xn = M.alloc("xn", [128, 1024], BF16)
    junk = M.alloc("junk", [128, 1024], BF16)
    mT = M.alloc("mT", [128, 8, 512], BF16)
    g1 = M.alloc("g1", [128, 512], F32)
    g2t = M.alloc("g2t", [128, 512], F32)
    m1 = M.alloc("m1", [128, 512], F32)
    xres = M.alloc("xres", [128, 1024], F32)
    ot = [M.alloc("ot", [128, 512], F32) for _ in range(2)]
    outkeys = []
    for st in range(NST):
        stc = slice(st * 512, (st + 1) * 512)
        make_hT(st, 2, hT, xt, xn, junk)
        for h in range(4):
            for kt in range(8):
                MM(PS[1][:, :], w3[:, kt, h * 128:(h + 1) * 128], hT[:, kt, :], kt == 0, kt == 7, ["w3", "hT"], ["ps1"])
            ACT(g1[:], PS[1][:, :], AF.Silu, ["ps1"], ["g1"])
            TT("dve", OAT[:, h, stc], OAT[:, h, stc], g1[:], ALU.mult, ["OAT", "g1"], ["OAT"])
        for c in range(8):
            for h in range(4):
                MM(PS[1][:, :], pa[:, h, c * 128:(c + 1) * 128], OAT[:, h, stc], h == 0, h == 3, ["pa", "OAT"], ["ps1"])
            for kt in range(8):
                MM(PS[2][:, :], w3[:, kt, 512 + c * 128:512 + (c + 1) * 128], hT[:, kt, :], kt == 0, kt == 7, ["w3", "hT"], ["ps2"])
            ACT(g1[:], PS[2][:, :], AF.Sigmoid, ["ps2", "mb"], ["g1"], bias=mb[:, c:c + 1])
            TT("dve", m1[:], PS[1][:, :], g1[:], ALU.mult, ["ps1", "g1"], ["m1"])
            for k in range(4):
                MM(PS[3][:, :], psw[:, k, c * 128:(c + 1) * 128], OST[:, k, stc], k == 0, k == 3, ["psw", "OST"], ["ps3"])
            for kt in range(8):
                MM(PS[4][:, :], w3[:, kt, 1536 + c * 128:1536 + (c + 1) * 128], hT[:, kt, :], kt == 0, kt == 7, ["w3", "hT"], ["ps4"])
            ACT(g2t[:], PS[4][:, :], AF.Sigmoid, ["ps4", "mb"], ["g2t"], bias=mb[:, 8 + c:9 + c])
            TT("dve", g2t[:], PS[3][:, :], g2t[:], ALU.mult, ["ps3", "g2t"], ["g2t"])
            TT("pool", mT[:, c, :], m1[:], g2t[:], ALU.add, ["m1", "g2t"], ["mT"])
        for tt in range(4):
            row = st * 512 + tt * 128
            DMA("sp", xres[:], x_d[row:row + 128, :], (), ["xres"], "xres")
            for half in range(2):
                for kt in range(8):
                    MM(PS[5 + half][:, :], mT[:, kt, tt * 128:(tt + 1) * 128], wo[:, kt, half * 512:(half + 1) * 512], kt == 0, kt == 7,
                       ["mT", "wo"], [f"ps{5 + half}"])
                TT("dve", ot[half][:], PS[5 + half][:, :], xres[:, half * 512:(half + 1) * 512], ALU.add, [f"ps{5 + half}", "xres"], [f"ot{half}"])
                ok = f"outd{st}_{tt}_{half}"
                DMA("sp", out_d[row:row + 128, half * 512:(half + 1) * 512], ot[half][:], [f"ot{half}"], [ok], f"ot{half}")
                outkeys.append(ok)
    NOP(outkeys)
    return finish(nc, S)


def finish(nc, S):
    import contextlib
    with contextlib.ExitStack() as stk:
        S._sem_ctx = {e: stk.enter_context(nc.semaphore(f"s_{e}")) for e in S.ENGS}
        S._dsem = {k: stk.enter_context(nc.semaphore(f"d_{k}")) for k in S.dma_count}
        block = stk.enter_context(nc.Block())
        S.emit(block)
    return nc


_NC_CACHE = {}


def _core_inputs(inputs, b, consts):
    m = {"x": np.ascontiguousarray(inputs["x"][b], dtype=np.float32)}
    for k, v in inputs.items():
        if k == "x":
            continue
        v = np.asarray(v, dtype=np.float32)
        if k == "rel_bias_table":
            m[k] = np.ascontiguousarray(v)
            continue
        v0 = v[0]
        if k in ("w_in", "ssm_glu_w", "proj_attn", "proj_ssm", "w_out"):
            m[k] = np.ascontiguousarray(v0)
        else:
            m[k] = np.ascontiguousarray(v0).reshape(-1)
    m.update(consts)
    return m


def kernel(**inputs):
    if "nc" not in _NC_CACHE:
        _NC_CACHE["nc"] = build_nc()
    nc = _NC_CACHE["nc"]
    consts = host_consts()
    in_maps = [_core_inputs(inputs, b, consts) for b in range(8)]
    res = run_bass_kernel_spmd(nc, in_maps, core_ids=list(range(8)))
    out = np.stack([np.asarray(res.results[b]["out"], dtype=np.float32) for b in range(8)], axis=0)
    return out
```

```python
import math
import numpy as np
import ml_dtypes
import concourse.bass as bass
import concourse.mybir as mybir
from concourse.bass_utils import run_bass_kernel_spmd

F32 = mybir.dt.float32
BF16 = mybir.dt.bfloat16
AF = mybir.ActivationFunctionType
ALU = mybir.AluOpType
AX = mybir.AxisListType


class _Op:
    __slots__ = ("eng", "fn", "idx", "dma", "dkey", "deps", "signal", "sigval", "sem")

    def __init__(self, eng, fn, idx, dma, dkey):
        self.eng = eng
        self.fn = fn
        self.idx = idx
        self.dma = dma
        self.dkey = dkey
        self.deps = {}
        self.signal = False
        self.sigval = 0
        self.sem = None


class Sched:
    ENGS = ("pe", "act", "dve", "pool", "sp")

    def __init__(self, nc):
        self.nc = nc
        self.ops = {e: [] for e in self.ENGS}
        self.last_w = {}
        self.readers = {}
        self.dma_count = {}

    def add(self, eng, fn, reads=(), writes=(), dma=False, dkey=None):
        op = _Op(eng, fn, len(self.ops[eng]), dma, dkey)
        reads = tuple(reads) + ("__B__",)
        for k in reads:
            w = self.last_w.get(k)
            if w is not None:
                op.deps[w] = True
        for k in writes:
            w = self.last_w.get(k)
            if w is not None and w not in op.deps:
                op.deps[w] = False
            rd = self.readers.get(k)
            if rd:
                for r in rd.get("eng", {}).values():
                    if r is not op and r not in op.deps:
                        op.deps[r] = False
                for r in rd.get("dma", []):
                    if r is not op and r not in op.deps:
                        op.deps[r] = False
        for k in reads:
            rd = self.readers.setdefault(k, {"eng": {}, "dma": []})
            if dma:
                rd["dma"].append(op)
            else:
                rd["eng"][eng] = op
        for k in writes:
            self.last_w[k] = op
            self.readers[k] = {"eng": {}, "dma": []}
        if dma:
            assert dkey is not None
            n = self.dma_count.get(dkey, 0) + 1
            self.dma_count[dkey] = n
            op.sigval = 16 * n
        self.ops[eng].append(op)
        return op

    def barrier(self):
        t = self._bar_tile
        self.add("dve", lambda e: e.memset(t[:, 0:1], 0.0), writes=("__B__",))

    def _needs_wait(self, op, dep, is_raw):
        if dep.dma:
            return True
        if dep.eng != op.eng:
            return True
        if op.dma:
            return True
        if op.eng == "pe":
            return False
        return is_raw and (op.idx - dep.idx) <= 3

    def emit(self, block, extra_final=None):
        nc = self.nc
        for e in self.ENGS:
            for op in self.ops[e]:
                for dep, is_raw in op.deps.items():
                    if self._needs_wait(op, dep, is_raw) and not dep.dma:
                        dep.signal = True
        sems = {e: self._sem_ctx[e] for e in self.ENGS}
        dsems = self._dsem
        for e in self.ENGS:
            n = 0
            for op in self.ops[e]:
                if op.dma:
                    op.sem = dsems[op.dkey]
                else:
                    op.sem = sems[e]
                    if op.signal:
                        n += 1
                        op.sigval = n

        def run_engine(ename, eng):
            known = {}
            for op in self.ops[ename]:
                need = {}
                for dep, is_raw in op.deps.items():
                    if not self._needs_wait(op, dep, is_raw):
                        continue
                    s = dep.sem
                    v = dep.sigval
                    if need.get(s, 0) < v:
                        need[s] = v
                for s, v in need.items():
                    if known.get(s, 0) < v:
                        eng.wait_ge(s, v)
                        known[s] = v
                ins = op.fn(eng)
                if op.dma:
                    ins.then_inc(op.sem, 16)
                elif op.signal:
                    ins.then_inc(op.sem, 1)
            if extra_final is not None:
                extra_final(ename, eng, known)

        @block.tensor
        def _(eng):
            run_engine("pe", eng)

        @block.scalar
        def _(eng):
            run_engine("act", eng)

        @block.vector
        def _(eng):
            run_engine("dve", eng)

        @block.gpsimd
        def _(eng):
            run_engine("pool", eng)

        @block.sync
        def _(eng):
            run_engine("sp", eng)


L = 4096
D = 1024
NST = 8
EPS = 1e-6
LAM_INIT = 0.8 - 0.6 * math.exp(-0.3 * 0)
TWO_PI = 2.0 * math.pi
NEG = -30000.0


def _bucket_np(rel):
    nb = 16
    me = 8
    side = np.where(rel > 0, nb, 0)
    n = np.abs(rel)
    nf = np.maximum(n, 1).astype(np.float32)
    large = me + (np.log(nf / np.float32(me)).astype(np.float32) / np.float32(math.log(128 / 8))
                  * np.float32(nb - me)).astype(np.int32)
    large = np.minimum(large, nb - 1)
    return side + np.where(n < me, n, large)


def host_consts():
    c = {}
    c["c_ident"] = np.eye(128, dtype=np.float32).astype(ml_dtypes.bfloat16)
    bo = np.zeros((128, 128), np.float32)
    bo[:64, :64] = 1.0 / 64
    bo[64:, 64:] = 1.0 / 64
    c["c_bones"] = bo.astype(ml_dtypes.bfloat16)
    c["c_J"] = np.ascontiguousarray(np.eye(128, dtype=np.float32)[::-1])
    rel = np.arange(-255, 128)
    b = _bucket_np(rel)
    oh = np.zeros((32, 384), np.float32)
    oh[b, np.arange(383)] = 1.0
    c["c_oh"] = oh
    k = np.arange(128)[:, None]
    q = np.arange(128)[None, :]
    c["c_maskD"] = np.where((k // 64) <= (q // 64), 0.0, NEG).astype(np.float32)
    c["c_iotap"] = np.stack([np.arange(128), -np.arange(128)], 1).astype(np.float32)
    c["c_iotat"] = np.tile(np.arange(128, dtype=np.float32)[None, :], (128, 1))
    c["c_tri"] = (np.arange(128)[:, None] <= np.arange(128)[None, :]).astype(np.float32).astype(ml_dtypes.bfloat16)
    return c


class Mem:
    def __init__(self, nc, start=16640, end=229376):
        self.nc = nc
        self.p = start
        self.end = end
        self.n = 0

    def alloc(self, name, shape, dtype):
        esz = 4 if dtype in (F32, mybir.dt.int32) else 2
        size = int(np.prod(shape[1:])) * esz
        size = (size + 63) // 64 * 64
        assert self.p + size <= self.end, f"SBUF overflow allocating {name}: {self.p}+{size} > {self.end}"
        self.n += 1
        t = self.nc.alloc_sbuf_tensor_at(f"{name}_{self.n}", list(shape), dtype, offset=self.p)
        self.p += size
        return t

    def mark(self):
        return self.p

    def release(self, m):
        self.p = m


def build_nc(dbg=None):
    nc = bass.Bass("TRN2", target_bir_lowering=False)
    S = Sched(nc)
    M = Mem(nc)

    def din(name, shape, dt=F32):
        return nc.dram_tensor(name, list(shape), dt, kind="ExternalInput").ap()

    x_d = din("x", [L, D])
    w_in_d = din("w_in", [D, 5120])
    ng_d = din("norm_gain", [D])
    mb_d = din("merge_gate_b", [2048])
    qg_d = din("q_norm_gain", [64])
    kg_d = din("k_norm_gain", [64])
    lq1_d = din("lambda_q1", [64]); lk1_d = din("lambda_k1", [64])
    lq2_d = din("lambda_q2", [64]); lk2_d = din("lambda_k2", [64])
    sg_d = din("diff_subln_gain", [128])
    rb_d = din("rel_bias_table", [32, 4])
    are_d = din("ssm_A_re", [2048]); aim_d = din("ssm_A_im", [2048])
    ldt_d = din("ssm_log_dt", [32])
    bre_d = din("ssm_B_re", [32 * 64 * 16]); bim_d = din("ssm_B_im", [32 * 64 * 16])
    cre_d = din("ssm_C_re", [32 * 16 * 64]); cim_d = din("ssm_C_im", [32 * 16 * 64])
    dd_d = din("ssm_D", [512])
    gw_d = din("ssm_glu_w", [512, 512])
    gb_d = din("ssm_glu_b", [512])
    pa_d = din("proj_attn", [512, 1024])
    ps_d = din("proj_ssm", [512, 1024])
    wo_d = din("w_out", [1024, 1024])
    c_ident_d = din("c_ident", [128, 128], BF16)
    c_bones_d = din("c_bones", [128, 128], BF16)
    c_J_d = din("c_J", [128, 128])
    c_oh_d = din("c_oh", [32, 384])
    c_maskD_d = din("c_maskD", [128, 128])
    c_iotap_d = din("c_iotap", [128, 2])
    c_iotat_d = din("c_iotat", [128, 128])
    c_tri_d = din("c_tri", [128, 128], BF16)
    out_d = nc.dram_tensor("out", [L, D], F32, kind="ExternalOutput").ap()
    fsc_d = nc.dram_tensor("fscratch", [4, 384], F32).ap()
    dbg_d = None
    if dbg:
        dbg_d = nc.dram_tensor("dbg", [128, 4 * L], BF16, kind="ExternalOutput").ap()

    def dap(t, off, pat):
        return bass.AP(t.tensor, off, [list(p) for p in pat])

    PS = [nc.alloc_psum_tensor(f"psb{i}", [128, 512], F32) for i in range(8)]

    def MM(out, lhsT, rhs, start, stop, r, w):
        S.add("pe", lambda e: e.matmul(out, lhsT=lhsT, rhs=rhs, start=start, stop=stop), r, w)

    def TR(out, in_, r, w):
        S.add("pe", lambda e: e.transpose(out=out, in_=in_, identity=ident[:]), tuple(r) + ("ident",), w)

    def ACT(out, in_, func, r, w, bias=0.0, scale=1.0, accum=None):
        if accum is None:
            S.add("act", lambda e: e.activation(out=out, in_=in_, func=func, bias=bias, scale=scale), r, w)
        else:
            S.add("act", lambda e: e.activation(out=out, in_=in_, func=func, bias=bias, scale=scale, accum_out=accum), r, w)

    def TS(eng, out, in0, s1, s2, op0, op1, r, w):
        if s2 is None:
            S.add(eng, lambda e: e.tensor_scalar(out=out, in0=in0, scalar1=s1, scalar2=None, op0=op0), r, w)
        else:
            S.add(eng, lambda e: e.tensor_scalar(out=out, in0=in0, scalar1=s1, scalar2=s2, op0=op0, op1=op1), r, w)

    def TT(eng, out, in0, in1, op, r, w):
        S.add(eng, lambda e: e.tensor_tensor(out=out, in0=in0, in1=in1, op=op), r, w)

    def STT(eng, out, in0, scalar, in1, op0, op1, r, w):
        S.add(eng, lambda e: e.scalar_tensor_tensor(out=out, in0=in0, scalar=scalar, in1=in1, op0=op0, op1=op1), r, w)

    def CP(eng, out, in_, r, w):
        if eng == "act":
            S.add("act", lambda e: e.copy(out=out, in_=in_), r, w)
        else:
            S.add(eng, lambda e: e.tensor_copy(out=out, in_=in_), r, w)

    def RSUM(out, in_, r, w):
        S.add("dve", lambda e: e.reduce_sum(out=out, in_=in_, axis=AX.X), r, w)

    def RECIP(out, in_, r, w):
        S.add("dve", lambda e: e.reciprocal(out=out, in_=in_), r, w)

    def NOP(r):
        S.add("sp", lambda e: e.nop(), r, [])

    def MS(eng, ap, val, w):
        S.add(eng, lambda e: e.memset(ap, val), (), w)

    def DMA(eng, out, in_, r, w, dkey, slow=False):
        if slow:
            S.add(eng, lambda e: e.dma_start(out=out, in_=in_, allow_slow_non_contiguous=True), r, w, dma=True, dkey=dkey)
        else:
            S.add(eng, lambda e: e.dma_start(out=out, in_=in_), r, w, dma=True, dkey=dkey)

    bar = M.alloc("bar", [128, 16], F32)
    S._bar_tile = bar
    ident = M.alloc("ident", [128, 128], BF16)
    bones = M.alloc("bones", [128, 128], BF16)
    tri = M.alloc("tri", [128, 128], BF16)
    iotap = M.alloc("iotap", [128, 2], F32)
    iotat = M.alloc("iotat", [128, 128], F32)
    gcol = M.alloc("gcol", [128, 8], F32)
    small = M.alloc("small", [128, 64], F32)
    rs_all = M.alloc("rs_all", [128, 3 * 32 * 2], F32)
    OAT = M.alloc("OAT", [128, 4, L], BF16)
    OSTm = M.alloc("OSTm", [128, 4 * L], BF16)
    OST = OSTm[:, :].rearrange("p (c t) -> p c t", c=4)
    stag32 = OSTm[:, :].bitcast(F32)
    Bst = stag32[:, 0:4096].rearrange("p (r c g q) -> p r c g q", r=2, c=4, g=8)
    Cst = stag32[:, 4096:8192].rearrange("p (r a c) -> p r a c", r=2, a=16)

    DMA("sp", ident[:], c_ident_d[:, :], (), ["ident"], "ident")
    DMA("sp", bones[:], c_bones_d[:, :], (), ["bones"], "bones")
    DMA("sp", tri[:], c_tri_d[:, :], (), ["tri"], "tri")
    DMA("sp", iotap[:], c_iotap_d[:, :], (), ["iotap"], "iotap")
    DMA("sp", iotat[:], c_iotat_d[:, :], (), ["iotat"], "iotat")
    DMA("sp", gcol[:], dap(ng_d, 0, [[1, 128], [128, 8]]), (), ["gcol"], "gcol", slow=True)

    MS("dve", stag32[:, 0:4096], 0.0, ["Bst"])
    MS("pool", stag32[:, 4096:8192], 0.0, ["Cst"])
    bc_dmas = []
    for g in range(32):
        for ri in range(2):
            bc_dmas.append((g, ri))

    def emit_bc(n):
        for _ in range(n):
            if not bc_dmas:
                return
            g, ri = bc_dmas.pop(0)
            ct, g8 = g // 8, g % 8
            pair, g2 = g // 2, g % 2
            bd = (bre_d, bim_d)[ri]
            cd = (cre_d, cim_d)[ri]
            DMA("sp", Bst[16 * g8:16 * g8 + 16, ri, ct, g8, :], dap(bd, g * 1024, [[1, 16], [16, 64]]),
                (), ["Bst"], "Bst", slow=True)
            DMA("sp", Cst[64 * g2:64 * g2 + 64, ri, pair, (g % 8) * 16:(g % 8) * 16 + 16],
                dap(cd, g * 1024, [[1, 64], [64, 16]]), (), ["Cst"], "Cst", slow=True)

    def load_weight(dst, src_d, row0, nrows_tiles, col0, ncols, stg, fold_gain, keyw):
        i = 0
        for kt in range(nrows_tiles):
            for c0 in range(0, ncols, 2048):
                cw = min(2048, ncols - c0)
                sb = stg[i % 2]
                sk = f"stg{i % 2}"
                DMA("sp", sb[:, 0:cw], src_d[row0 + kt * 128: row0 + (kt + 1) * 128, col0 + c0: col0 + c0 + cw],
                    (), [sk], sk)
                if fold_gain:
                    if i % 2 == 0:
                        ACT(dst[:, kt, c0:c0 + cw], sb[:, 0:cw], AF.Copy, [sk, "gcol"], [keyw], scale=gcol[:, kt:kt + 1])
                    else:
                        TS("dve", dst[:, kt, c0:c0 + cw], sb[:, 0:cw], gcol[:, kt:kt + 1], None, ALU.mult, None, [sk, "gcol"], [keyw])
                else:
                    if i % 2 == 0:
                        CP("act", dst[:, kt, c0:c0 + cw], sb[:, 0:cw], [sk], [keyw])
                    else:
                        CP("dve", dst[:, kt, c0:c0 + cw], sb[:, 0:cw], [sk], [keyw])
                i += 1

    def make_hT(st, phase, hT, xt, xn, junk):
        psT = PS[0][:, 0:512].bitcast(BF16)
        for tt in range(4):
            i = st * 4 + tt
            b = i % len(xt)
            row = st * 512 + tt * 128
            col = (phase * 32 + i) * 2
            ss = rs_all[:, col:col + 1]
            rs = rs_all[:, col + 1:col + 2]
            DMA("sp", xt[b][:], x_d[row:row + 128, :], (), [f"xt{b}"], f"xt{b}")
            jk = "xn" if junk is xn else "junk"
            ACT(junk[:], xt[b][:], AF.Square, [f"xt{b}"], [jk, f"ss{col}"], accum=ss)
            ACT(rs, ss, AF.Ln, [f"ss{col}"], [f"rs{col}"], bias=EPS, scale=1.0 / D)
            ACT(rs, rs, AF.Exp, [f"rs{col}"], [f"rs{col}"], scale=-0.5)
            TS("dve", xn[:], xt[b][:], rs, None, ALU.mult, None, [f"xt{b}", f"rs{col}"], ["xn"])
            for kt in range(8):
                TR(psT[:, kt * 128:(kt + 1) * 128], xn[:, kt * 128:(kt + 1) * 128], ["xn"], ["ps0"])
            CP("dve", hT[:, :, tt * 128:(tt + 1) * 128], psT[:, :].rearrange("p (k t) -> p k t", k=8), ["ps0"], ["hT"])

    m_glob = M.mark()
    wq = M.alloc("wq", [128, 8, 1536], BF16)
    KT = M.alloc("KT", [128, 4, L], BF16)
    V = M.alloc("V", [128, 32, 4, 128], BF16)
    ones128 = M.alloc("ones128", [128, 128], BF16)
    sgcol = M.alloc("sgcol", [128, 1], F32)
    c15 = M.alloc("c15", [128, 4], F32)
    biasP = M.alloc("biasP", [128, 4, 128], F32)
    biasD = M.alloc("biasD", [128, 4, 128], F32)
    gq = M.alloc("gq", [128, 2], F32)
    lamt = M.alloc("lamt", [128, 8], F32)
    m_work = M.mark()
    stg = [M.alloc("stg", [128, 2048], F32) for _ in range(2)]
    lamv = M.alloc("lamv", [128, 4, 64], F32)
    tbl = M.alloc("tbl", [32, 4], F32)
    oh = M.alloc("oh", [32, 384], F32)
    fsb = M.alloc("fsb", [4, 384], F32)
    Jt = M.alloc("Jt", [128, 128], F32)
    hank = M.alloc("hank", [128, 4, 256], F32)
    maskD = M.alloc("maskD", [128, 128], F32)

    load_weight(wq, w_in_d, 0, 8, 0, 1536, stg, True, "wq")
    MS("dve", ones128[:], 1.0, ["ones128"])
    DMA("sp", sgcol[:], dap(sg_d, 0, [[1, 128], [1, 1]]), (), ["sgcol"], "sgcol")
    TS("dve", sgcol[:], sgcol[:], 1.0 - LAM_INIT, None, ALU.mult, None, ["sgcol"], ["sgcol"])
    for hlf in range(2):
        DMA("sp", gq[64 * hlf:64 * hlf + 64, 0:1], dap(qg_d, 0, [[1, 64], [1, 1]]), (), ["gq"], "gq")
        DMA("sp", gq[64 * hlf:64 * hlf + 64, 1:2], dap(kg_d, 0, [[1, 64], [1, 1]]), (), ["gq"], "gq")
    TS("dve", gq[:, 0:1], gq[:, 0:1], 0.125, None, ALU.mult, None, ["gq"], ["gq"])
    for i, dd in enumerate((lq1_d, lk1_d, lq2_d, lk2_d)):
        DMA("sp", lamv[:, i, :], dap(dd, 0, [[0, 128], [1, 64]]), (), ["lamv"], "lamv")
    TT("dve", lamv[:, 0, :], lamv[:, 0, :], lamv[:, 1, :], ALU.mult, ["lamv"], ["lamv"])
    TT("dve", lamv[:, 2, :], lamv[:, 2, :], lamv[:, 3, :], ALU.mult, ["lamv"], ["lamv"])
    RSUM(lamt[:, 0:1], lamv[:, 0, :], ["lamv"], ["lamt"])
    RSUM(lamt[:, 1:2], lamv[:, 2, :], ["lamv"], ["lamt"])
    ACT(lamt[:, 2:4], lamt[:, 0:2], AF.Exp, ["lamt"], ["lamt"])
    TT("dve", lamt[:, 4:5], lamt[:, 3:4], lamt[:, 2:3], ALU.subtract, ["lamt"], ["lamt"])
    TS("dve", lamt[:, 5:6], lamt[:, 4:5], -LAM_INIT, None, ALU.add, None, ["lamt"], ["lamt"])
    neglam = lamt[:, 5:6]
    DMA("sp", tbl[:], rb_d[:, :], (), ["tbl"], "tbl")
    DMA("sp", oh[:], c_oh_d[:, :], (), ["oh"], "oh")
    DMA("sp", Jt[:], c_J_d[:, :], (), ["Jt"], "Jt")
    DMA("sp", maskD[:], c_maskD_d[:, :], (), ["maskD"], "maskD")
    DMA("sp", c15[:], dap(rb_d, 15 * 4, [[0, 128], [1, 4]]), (), ["c15"], "c15")
    MM(PS[1][0:4, 0:384], tbl[:], oh[:], True, True, ["tbl", "oh"], ["ps1"])
    CP("dve", fsb[:], PS[1][0:4, 0:384], ["ps1"], ["fsb"])
    DMA("sp", fsc_d[:, :], fsb[:], ["fsb"], ["fsc"], "fsb")
    for h in range(4):
        DMA("sp", hank[:, h, :], dap(fsc_d, h * 384, [[1, 128], [1, 256]]), ["fsc"], ["hank"], "hank")
    for h in range(4):
        MM(PS[2][:, h * 128:(h + 1) * 128], hank[:, h, 0:128], Jt[:], True, True, ["hank", "Jt"], ["ps2"])
        MM(PS[3][:, h * 128:(h + 1) * 128], hank[:, h, 128:256], Jt[:], True, True, ["hank", "Jt"], ["ps3"])
    CP("dve", biasP[:], PS[2][:, :].rearrange("p (h q) -> p h q", h=4), ["ps2"], ["biasP"])
    TT("dve", biasD[:], PS[3][:, :].rearrange("p (h q) -> p h q", h=4),
       maskD[:].unsqueeze(1).broadcast_to([128, 4, 128]), ALU.add, ["ps3", "maskD"], ["biasD"])
    S.barrier()
    M.release(m_work)
    hT = M.alloc("hT", [128, 8, 512], BF16)
    xt = [M.alloc("xt", [128, 1024], F32)]
    xn = M.alloc("xn", [128, 1024], BF16)
    qT = M.alloc("qT", [128, 4, 512], BF16)
    sq = M.alloc("sq", [128, 512], BF16)
    rstd = M.alloc("rstd", [128, 512], F32)
    PT = [M.alloc("PT", [128, 512], BF16) for _ in range(4)]
    tmpb = [M.alloc("tmpb", [128, 128], F32) for _ in range(2)]
    rcp = M.alloc("rcp", [128, 512], F32)
    oraw = [M.alloc("oraw", [128, 512], F32) for _ in range(2)]
    racc = [[M.alloc("racc", [128, 512], F32) for _ in range(2)] for _ in range(2)]
    tob = M.alloc("tob", [128, 512], F32)
    rsb = M.alloc("rsb", [128, 512], BF16)
    sqb = M.alloc("sqb", [128, 512], BF16)
    rs2 = M.alloc("rs2", [128, 512], F32)
    sq2 = [sq, M.alloc("sq2", [128, 512], BF16)]
    rstd2 = [rstd, rs2]
    rkeys = ["rstd0", "rs2"]

    PSB = [3, 4, 7, 6]
    NB = 4
    SKEW = 3
    FILL = 0
    deferred = []

    def qk_mm(c, st):
        zb = 1 + (c % 2)
        for kt in range(8):
            MM(PS[zb][:, :], wq[:, kt, c * 128:(c + 1) * 128], hT[:, kt, :], kt == 0, kt == 7, ["wq", "hT"], [f"ps{zb}"])

    def qk_chain(c, st):
        zb = 1 + (c % 2)
        sqc, rsc = sq2[c % 2], rstd2[c % 2]
        ACT(sqc[:], PS[zb][:, :], AF.Square, [f"ps{zb}"], [f"sq{c % 2}"])
        MM(PS[0][:, :], bones[:], sqc[:], True, True, ["bones", f"sq{c % 2}"], ["ps0"])
        ACT(rsc[:], PS[0][:, :], AF.Ln, ["ps0"], [rkeys[c % 2]], bias=EPS)
        ACT(rsc[:], rsc[:], AF.Exp, [rkeys[c % 2]], [rkeys[c % 2]], scale=-0.5)
        if c < 4:
            STT("dve", qT[:, c, :], PS[zb][:, :], gq[:, 0:1], rsc[:], ALU.mult, ALU.mult, [f"ps{zb}", "gq", rkeys[c % 2]], ["qT"])
        else:
            STT("dve", KT[:, c - 4, st * 512:(st + 1) * 512], PS[zb][:, :], gq[:, 1:2], rsc[:], ALU.mult, ALU.mult,
                [f"ps{zb}", "gq", rkeys[c % 2]], [f"KT{st}"])

    for st in range(NST):
        make_hT(st, 0, hT, xt, xn, xn)
        emit_bc(8)
        qk_mm(0, st)
        for c in range(8):
            if c + 1 < 8:
                qk_mm(c + 1, st)
            qk_chain(c, st)
        for tt in range(4):
            blk = st * 4 + tt
            zb = 1 + (tt % 2)
            for kt in range(8):
                MM(PS[zb][:, :], hT[:, kt, tt * 128:(tt + 1) * 128], wq[:, kt, 1024:1536], kt == 0, kt == 7, ["wq", "hT"], [f"ps{zb}"])
            CP("dve", V[:, blk, :, :], PS[zb][:, :].rearrange("p (h d) -> p h d", h=4), [f"ps{zb}"], [f"V{st}"])
        kv_keys = [f"KT{i}" for i in range(st + 1)] + [f"V{i}" for i in range(st + 1)]
        iters = [(h, s, j) for h in range(4) for s in range(2) for j in range(4 * st + 4)]
        NI = len(iters)

        def emit_S(i, st=st, kv_keys=kv_keys, iters=iters):
            h, s, j = iters[i]
            lo = max(0, j - 4 * st)
            cols = slice(lo * 128, 512)
            pb = i % NB
            psS = PS[PSB[pb]]
            MM(psS[:, cols], KT[64 * s:64 * s + 64, h, j * 128:(j + 1) * 128], qT[64 * s:64 * s + 64, h, cols],
               True, True, kv_keys + ["qT"], [f"ps{PSB[pb]}"])

        def fin_A(h, s, st):
            par = (h * 2 + s) % 2
            TT("dve", rsb[:], racc[par][0][:], racc[par][1][:], ALU.add, [f"racc{par}0", f"racc{par}1"], ["rsb"])
            MM(PS[0][:, :], ones128[:], rsb[:], True, True, ["ones128", "rsb"], ["ps0"])
            RECIP(rcp[:], PS[0][:, :], ["ps0"], ["rcp"])
            TT("dve", oraw[s][:], oraw[s][:], rcp[:], ALU.mult, [f"oraw{s}", "rcp"], [f"oraw{s}"])
            if s == 1:
                STT("dve", tob[:], oraw[1][:], neglam, oraw[0][:], ALU.mult, ALU.add, ["oraw0", "oraw1", "lamt"], ["tob"])
                ACT(sqb[:], tob[:], AF.Square, ["tob"], ["sqb"])

        def fin_B(h, st):
            stc = slice(st * 512, (st + 1) * 512)
            MM(PS[0][:, :], ones128[:], sqb[:], True, True, ["ones128", "sqb"], ["ps0"])
            ACT(rs2[:], PS[0][:, :], AF.Ln, ["ps0"], ["rs2"], bias=EPS, scale=1.0 / 128)
            ACT(rs2[:], rs2[:], AF.Exp, ["rs2"], ["rs2"], scale=-0.5)
            STT("dve", OAT[:, h, stc], tob[:], sgcol[:, 0:1], rs2[:], ALU.mult, ALU.mult, ["tob", "sgcol", "rs2"], ["OAT"])

        def emit_rest(i, st=st, kv_keys=kv_keys, iters=iters):
            h, s, j = iters[i]
            lo = max(0, j - 4 * st)
            pb = i % NB
            pk = f"ps{PSB[pb]}"
            psS = PS[PSB[pb]]
            par = (h * 2 + s) % 2
            if j == 0:
                MS("dve", racc[par][0][:], 0.0, [f"racc{par}0"])
                MS("pool", racc[par][1][:], 0.0, [f"racc{par}1"])
            far_lo = lo
            for qb in range(lo, 4):
                d = 4 * st + qb - j
                if d >= 2:
                    break
                bt = biasD if d == 0 else biasP
                tb_ = tmpb[d]
                TT("dve", tb_[:], psS[:, qb * 128:(qb + 1) * 128], bt[:, h, :], ALU.add, [pk, "biasD", "biasP"], [f"tmpb{d}"])
                ACT(PT[pb][:, qb * 128:(qb + 1) * 128], tb_[:], AF.Exp, [f"tmpb{d}"], [f"PT{pb}"])
                far_lo = qb + 1
            if far_lo < 4:
                ACT(PT[pb][:, far_lo * 128:512], psS[:, far_lo * 128:512], AF.Exp, [pk, "c15"], [f"PT{pb}"], bias=c15[:, h:h + 1])
            qs = slice(lo * 128, 512)
            last = (j == 4 * st + 3)
            MM(PS[5][:, qs], V[:, j, h, :], PT[pb][:, qs], j == 0, last, [f"PT{pb}"] + kv_keys, ["ps5"])
            for _f in range(FILL):
                MM(PS[6][:, :], ones128[:], wq[:, 0, 0:512], True, True, ["ones128", "wq"], ["ps6"])
            e_ = j % 2
            TT(("dve", "pool")[e_], racc[par][e_][:, qs], racc[par][e_][:, qs], PT[pb][:, qs], ALU.add,
               [f"racc{par}{e_}", f"PT{pb}"], [f"racc{par}{e_}"])
            if not last:
                return
            CP("act", oraw[s][:], PS[5][:, :], ["ps5"], [f"oraw{s}"])
            deferred.append((i + 3, "A", h, s, st))
            if s == 1:
                deferred.append((i + 6, "B", h, s, st))

        def run_deferred(k):
            while deferred and deferred[0][0] <= k:
                _, kind, hh, ss, sst = deferred.pop(0)
                if kind == "A":
                    fin_A(hh, ss, sst)
                else:
                    fin_B(hh, sst)

        for i in range(NI + SKEW):
            if i < NI:
                emit_S(i)
            k = i - SKEW
            if k >= 0:
                emit_rest(k)
                run_deferred(k)
        run_deferred(10 ** 9)

    if dbg == "p1":
        DMA("sp", dbg_d[:, :], OAT[:, :, :].rearrange("p h t -> p (h t)"), ["OAT"], ["outd"], "OAT")
        NOP(["outd"])
        return finish(nc, S)
    I32 = mybir.dt.int32
    S.barrier()
    M.release(m_glob)
    w2 = M.alloc("w2", [128, 8, 1024], BF16)
    gluw = M.alloc("gluw", [128, 4, 512], BF16)
    glub = M.alloc("glub", [128, 4], F32)
    ta = M.alloc("ta", [128, 2048], F32)
    tb = M.alloc("tb", [128, 2048], F32)
    te = M.alloc("te", [128, 16, 128], F32)
    tf = M.alloc("tf", [128, 16, 128], F32)
    Bbd = M.alloc("Bbd", [128, 4, 2, 512], BF16)
    Ccat = M.alloc("Ccat", [128, 2, 16, 128], BF16)
    Ddiag = M.alloc("Ddiag", [128, 4, 128], BF16)
    dcol = M.alloc("dcol", [128, 4], F32)
    cre = [M.alloc("cre", [128, 16], F32) for _ in range(2)]
    cim = [M.alloc("cim", [128, 16], F32) for _ in range(2)]
    m_work2 = M.mark()
    stg = [M.alloc("stg", [128, 2048], F32) for _ in range(2)]
    T1 = M.alloc("T1", [128, 2048], F32)
    T2 = M.alloc("T2", [128, 2048], F32)
    T3 = M.alloc("T3", [128, 2048], F32)
    T4 = M.alloc("T4", [128, 2048], F32)
    TI = M.alloc("TI", [128, 2048], I32)
    dtb = M.alloc("dtb", [128, 32], F32)
    s2 = M.alloc("s2", [128, 8, 16], F32)
    s3 = M.alloc("s3", [128, 14, 256], F32)
    s3i = M.alloc("s3i", [128, 256], I32)
    dt3 = M.alloc("dt3", [128, 4], F32)

    load_weight(w2, w_in_d, 0, 8, 2048, 1024, stg, True, "w2")
    load_weight(gluw, gw_d, 0, 4, 0, 512, stg, False, "gluw")
    DMA("sp", glub[:], dap(gb_d, 0, [[1, 128], [128, 4]]), (), ["glub"], "glub", slow=True)

    def frac_sincos(y, tmp, ti, sin_out, cos_out, key):
        CP("dve", ti, y, [key + "y"], [key + "ti"])
        CP("dve", tmp, ti, [key + "ti"], [key + "tmp"])
        TT("dve", tmp, y, tmp, ALU.subtract, [key + "y", key + "tmp"], [key + "tmp"])
        ACT(sin_out, tmp, AF.Sin, [key + "tmp"], [key + "sin"], scale=TWO_PI)
        TS("dve", y, y, 0.25, None, ALU.add, None, [key + "y"], [key + "y"])
        CP("dve", ti, y, [key + "y"], [key + "ti"])
        CP("dve", tmp, ti, [key + "ti"], [key + "tmp"])
        TT("dve", tmp, y, tmp, ALU.subtract, [key + "y", key + "tmp"], [key + "tmp"])
        ACT(cos_out, tmp, AF.Sin, [key + "tmp"], [key + "cos"], scale=TWO_PI)

    DMA("sp", T1[:], dap(are_d, 0, [[0, 128], [1, 2048]]), (), ["T1"], "T1")
    DMA("sp", T2[:], dap(aim_d, 0, [[0, 128], [1, 2048]]), (), ["T2"], "T2")
    DMA("sp", dtb[:], dap(ldt_d, 0, [[0, 128], [1, 32]]), (), ["dtb"], "dtb")
    ACT(dtb[:], dtb[:], AF.Exp, ["dtb"], ["dtb"])
    dtb_b = dtb[:, :].unsqueeze(2).broadcast_to([128, 32, 64])
    TT("dve", T1[:, :].rearrange("p (g q) -> p g q", g=32), T1[:, :].rearrange("p (g q) -> p g q", g=32), dtb_b, ALU.mult, ["T1", "dtb"], ["T1"])
    TT("dve", T2[:, :].rearrange("p (g q) -> p g q", g=32), T2[:, :].rearrange("p (g q) -> p g q", g=32), dtb_b, ALU.mult, ["T2", "dtb"], ["T2"])
    ACT(ta[:], T1[:], AF.Exp, ["T1", "iotap"], ["ta"], scale=iotap[:, 1:2])
    TS("dve", T3[:], T2[:], iotap[:, 0:1], 1.0 / TWO_PI, ALU.mult, ALU.mult, ["T2", "iotap"], ["L1y"])
    frac_sincos(T3[:], T4[:], TI[:], tb[:], T1[:], "L1")
    STT("dve", tb[:], tb[:], -1.0, ta[:], ALU.mult, ALU.mult, ["L1sin", "ta"], ["tb", "L1sin"])
    TT("dve", ta[:], ta[:], T1[:], ALU.mult, ["ta", "L1cos", "tb"], ["ta"])
    A2re, A2im, dt2, m2, th2 = (s2[:, i, :] for i in range(5))
    DMA("sp", A2re, dap(are_d, 0, [[1, 128], [128, 16]]), (), ["A2re"], "A2re", slow=True)
    DMA("sp", A2im, dap(aim_d, 0, [[1, 128], [128, 16]]), (), ["A2im"], "A2im", slow=True)
    for g2 in range(2):
        DMA("sp", s2[64 * g2:64 * g2 + 64, 2, :], dap(ldt_d, g2, [[0, 64], [2, 16]]), (), ["dt2"], "dt2", slow=True)
    ACT(dt2, dt2, AF.Exp, ["dt2"], ["dt2"])
    TT("dve", m2, A2re, dt2, ALU.mult, ["A2re", "dt2"], ["m2"])
    STT("dve", th2, A2im, 1.0 / TWO_PI, dt2, ALU.mult, ALU.mult, ["A2im", "dt2"], ["th2"])
    T3v = T3[:, :].rearrange("p (a t) -> p a t", a=16)
    for pair in range(16):
        ACT(te[:, pair, :], iotat[:], AF.Exp, ["iotat", "m2"], ["te"], scale=s2[:, 3, pair:pair + 1])
        TS("dve", T3v[:, pair, :], iotat[:], s2[:, 4, pair:pair + 1], None, ALU.mult, None, ["iotat", "th2", "L1y", "L1tmp"], ["L2y"])
    frac_sincos(T3[:], T4[:], TI[:], tf[:, :, :].rearrange("p a t -> p (a t)"), T1[:], "L2")
    TT("dve", tf[:, :, :].rearrange("p a t -> p (a t)"), tf[:, :, :].rearrange("p a t -> p (a t)"), te[:, :, :].rearrange("p a t -> p (a t)"),
       ALU.mult, ["L2sin", "te"], ["tf", "L2sin"])
    TT("dve", te[:, :, :].rearrange("p a t -> p (a t)"), te[:, :, :].rearrange("p a t -> p (a t)"), T1[:], ALU.mult, ["te", "L2cos", "tf"], ["te"])
    A3re, A3im, m3, y3, dec3, sin3, cos3, nr3, den3, qre3, qim3, u3a, u3b, tmp3 = (s3[:, i, :] for i in range(14))
    for g8 in range(8):
        DMA("sp", s3[16 * g8:16 * g8 + 16, 0, :].rearrange("p (c q) -> p c q", c=4), dap(are_d, g8 * 64, [[0, 16], [512, 4], [1, 64]]), (), ["A3re"], "A3re")
        DMA("sp", s3[16 * g8:16 * g8 + 16, 1, :].rearrange("p (c q) -> p c q", c=4), dap(aim_d, g8 * 64, [[0, 16], [512, 4], [1, 64]]), (), ["A3im"], "A3im")
        DMA("sp", dt3[16 * g8:16 * g8 + 16, :], dap(ldt_d, g8, [[0, 16], [8, 4]]), (), ["dt3"], "dt3", slow=True)
    ACT(dt3[:], dt3[:], AF.Exp, ["dt3"], ["dt3"])
    dt3_b = dt3[:, :].unsqueeze(2).broadcast_to([128, 4, 64])
    v3 = lambda a: a.rearrange("p (c q) -> p c q", c=4)
    TT("dve", v3(m3), v3(A3re), dt3_b, ALU.mult, ["A3re", "dt3"], ["m3"])
    TT("dve", v3(y3), v3(A3im), dt3_b, ALU.mult, ["A3im", "dt3"], ["L3y"])
    TS("dve", y3, y3, 1.0 / TWO_PI, None, ALU.mult, None, ["L3y"], ["L3y"])
    ACT(dec3, m3, AF.Exp, ["m3"], ["dec3"])
    frac_sincos(y3, tmp3, s3i[:], sin3, cos3, "L3")
    TT("dve", cos3, cos3, dec3, ALU.mult, ["L3cos", "dec3"], ["lbr"])
    TT("dve", sin3, sin3, dec3, ALU.mult, ["L3sin", "dec3"], ["lbi"])
    TS("dve", nr3, cos3, -1.0, None, ALU.add, None, ["lbr"], ["nr3"])
    TT("dve", den3, A3re, A3re, ALU.mult, ["A3re"], ["den3"])
    TT("dve", u3a, A3im, A3im, ALU.mult, ["A3im"], ["u3a"])
    TT("dve", den3, den3, u3a, ALU.add, ["den3", "u3a"], ["den3"])
    RECIP(den3, den3, ["den3"], ["den3"])
    TT("dve", u3a, nr3, A3re, ALU.mult, ["nr3", "A3re", "den3"], ["u3a"])
    TT("dve", u3b, sin3, A3im, ALU.mult, ["lbi", "A3im"], ["u3b"])
    TT("dve", qre3, u3a, u3b, ALU.add, ["u3a", "u3b"], ["qre3"])
    TT("dve", qre3, qre3, den3, ALU.mult, ["qre3", "den3"], ["qre3"])
    TT("dve", u3a, sin3, A3re, ALU.mult, ["lbi", "A3re", "qre3"], ["u3a"])
    TT("dve", u3b, nr3, A3im, ALU.mult, ["nr3", "A3im", "qre3"], ["u3b"])
    TT("dve", qim3, u3a, u3b, ALU.subtract, ["u3a", "u3b"], ["qim3"])
    TT("dve", qim3, qim3, den3, ALU.mult, ["qim3", "den3"], ["qim3"])
    qre_b = v3(qre3).unsqueeze(2).broadcast_to([128, 4, 8, 64])
    qim_b = v3(qim3).unsqueeze(2).broadcast_to([128, 4, 8, 64])
    T1v4 = T1[:, :].rearrange("p (c g q) -> p c g q", c=4, g=8)
    T2v4 = T2[:, :].rearrange("p (c g q) -> p c g q", c=4, g=8)
    TT("dve", T1v4, Bst[:, 0], qre_b, ALU.mult, ["Bst", "qre3", "ta", "te"], ["T1"])
    TT("dve", T2v4, Bst[:, 1], qim_b, ALU.mult, ["Bst", "qim3", "L1y", "L2y"], ["T2"])
    TT("dve", Bbd[:, :, 0, :], T1[:, :].rearrange("p (c x) -> p c x", c=4), T2[:, :].rearrange("p (c x) -> p c x", c=4), ALU.subtract, ["T1", "T2"], ["Bbd"])
    TT("dve", T1v4, Bst[:, 1], qre_b, ALU.mult, ["Bst", "qre3", "Bbd"], ["T1"])
    TT("dve", T2v4, Bst[:, 0], qim_b, ALU.mult, ["Bst", "qim3", "Bbd"], ["T2"])
    TT("dve", Bbd[:, :, 1, :], T1[:, :].rearrange("p (c x) -> p c x", c=4), T2[:, :].rearrange("p (c x) -> p c x", c=4), ALU.add, ["T1", "T2"], ["Bbd"])
    CP("dve", Ccat[:, 0], Cst[:, 0], ["Cst"], ["Ccat"])
    TS("dve", Ccat[:, 1], Cst[:, 1], -1.0, None, ALU.mult, None, ["Cst"], ["Ccat"])
    DMA("sp", dcol[:], dap(dd_d, 0, [[1, 128], [128, 4]]), (), ["dcol"], "dcol", slow=True)
    for ct in range(4):
        TS("dve", Ddiag[:, ct, :], ident[:], dcol[:, ct:ct + 1], None, ALU.mult, None, ["ident", "dcol"], ["Ddiag"])
    MS("dve", cre[0][:], 0.0, ["c0"])
    MS("dve", cim[0][:], 0.0, ["c0"])
    S.barrier()
    M.release(m_work2)
    hT = M.alloc("hT", [128, 8, 512], BF16)
    xt = [M.alloc("xt", [128, 1024], F32)]
    xn = M.alloc("xn", [128, 1024], BF16)
    junk = M.alloc("junk", [128, 1024], BF16)
    uT = M.alloc("uT", [128, 4, 512], BF16)
    gsT = M.alloc("gsT", [128, 4, 512], BF16)
    gT = M.alloc("gT", [128, 4, 512], BF16)
    Vt = M.alloc("Vt", [128, 2, 2048], BF16)
    tA = M.alloc("tA", [128, 8, 128], F32)
    tB = M.alloc("tB", [128, 8, 128], F32)
    tC = M.alloc("tC", [128, 8, 128], F32)
    tD = M.alloc("tD", [128, 8, 128], F32)
    sr = M.alloc("sr", [128, 8, 128], F32)
    si = M.alloc("si", [128, 8, 128], F32)
    Xre = [M.alloc("Xre", [128, 8, 128], BF16) for _ in range(2)]
    Xim = [M.alloc("Xim", [128, 8, 128], BF16) for _ in range(2)]
    sg = M.alloc("sg", [128, 512], F32)
    xl = M.alloc("xl", [128, 6, 8], F32)
    fl = lambda a: a[:, :, :].rearrange("p a t -> p (a t)")

    for st in range(NST):
        make_hT(st, 1, hT, xt, xn, junk)
        for ct in range(4):
            for kt in range(8):
                MM(PS[1][:, :], w2[:, kt, ct * 128:(ct + 1) * 128], hT[:, kt, :], kt == 0, kt == 7, ["w2", "hT"], ["ps1"])
            CP("act", uT[:, ct, :], PS[1][:, :], ["ps1"], ["uT"])
        for ct in range(4):
            for kt in range(8):
                MM(PS[1][:, :], w2[:, kt, 512 + ct * 128:512 + (ct + 1) * 128], hT[:, kt, :], kt == 0, kt == 7, ["w2", "hT"], ["ps1"])
            ACT(gsT[:, ct, :], PS[1][:, :], AF.Silu, ["ps1"], ["gsT"])
        for cc in range(4):
            ci = st * 4 + cc
            tok = slice(cc * 128, (cc + 1) * 128)
            cr_in, ci_in = cre[ci % 2], cim[ci % 2]
            cr_out, ci_out = cre[(ci + 1) % 2], cim[(ci + 1) % 2]
            kin, kout = f"c{ci % 2}", f"c{(ci + 1) % 2}"
            for half in range(2):
                for ct in (2 * half, 2 * half + 1):
                    MM(PS[2][:, :], uT[:, ct, tok], Bbd[:, ct, 0, :], True, True, ["uT", "Bbd"], ["ps2"])
                    MM(PS[3][:, :], uT[:, ct, tok], Bbd[:, ct, 1, :], True, True, ["uT", "Bbd"], ["ps3"])
                    a_ = ta[:, ct * 512:(ct + 1) * 512]
                    b_ = tb[:, ct * 512:(ct + 1) * 512]
                    t1, t2, t3, t4 = fl(tA)[:, 0:512], fl(tB)[:, 0:512], fl(tC)[:, 0:512], fl(tD)[:, 0:512]
                    TT("dve", t1, PS[2][:, :], a_, ALU.mult, ["ps2", "ta"], ["tA"])
                    TT("dve", t2, PS[3][:, :], b_, ALU.mult, ["ps3", "tb"], ["tB"])
                    TT("pool", Vt[:, 0, ct * 512:(ct + 1) * 512], t1, t2, ALU.subtract, ["tA", "tB"], ["Vt"])
                    TT("dve", t3, PS[3][:, :], a_, ALU.mult, ["ps3", "ta"], ["tC"])
                    TT("dve", t4, PS[2][:, :], b_, ALU.mult, ["ps2", "tb"], ["tD"])
                    TT("pool", Vt[:, 1, ct * 512:(ct + 1) * 512], t3, t4, ALU.add, ["tC", "tD"], ["Vt"])
                for p8 in range(8):
                    pair = 8 * half + p8
                    sl = slice((p8 % 4) * 128, (p8 % 4) * 128 + 128)
                    MM(PS[4 + p8 // 4][:, sl], Vt[:, 0, pair * 128:(pair + 1) * 128], tri[:], True, True, ["Vt", "tri"], [f"ps{4 + p8 // 4}"])
                    MM(PS[6 + p8 // 4][:, sl], Vt[:, 1, pair * 128:(pair + 1) * 128], tri[:], True, True, ["Vt", "tri"], [f"ps{6 + p8 // 4}"])
                for p8 in range(8):
                    pair = 8 * half + p8
                    sl = slice((p8 % 4) * 128, (p8 % 4) * 128 + 128)
                    ACT(sr[:, p8, :], PS[4 + p8 // 4][:, sl], AF.Identity, [f"ps{4 + p8 // 4}", kin], ["sr"], bias=cr_in[:, pair:pair + 1])
                    ACT(si[:, p8, :], PS[6 + p8 // 4][:, sl], AF.Identity, [f"ps{6 + p8 // 4}", kin], ["si"], bias=ci_in[:, pair:pair + 1])
                eh = te[:, 8 * half:8 * half + 8, :]
                fh = tf[:, 8 * half:8 * half + 8, :]
                TT("dve", tA[:], sr[:], eh, ALU.mult, ["sr", "te"], ["tA"])
                TT("dve", tB[:], si[:], fh, ALU.mult, ["si", "tf"], ["tB"])
                TT("pool", Xre[half][:], tA[:], tB[:], ALU.subtract, ["tA", "tB"], [f"Xre{half}"])
                TT("pool", xl[:, 0, :], tA[:, :, 127], tB[:, :, 127], ALU.subtract, ["tA", "tB"], ["xl0"])
                TT("dve", tC[:], si[:], eh, ALU.mult, ["si", "te"], ["tC"])
                TT("dve", tD[:], sr[:], fh, ALU.mult, ["sr", "tf"], ["tD"])
                TT("pool", Xim[half][:], tC[:], tD[:], ALU.add, ["tC", "tD"], [f"Xim{half}"])
                TT("pool", xl[:, 1, :], tC[:, :, 127], tD[:, :, 127], ALU.add, ["tC", "tD"], ["xl1"])
                lr = te[:, 8 * half:8 * half + 8, 1]
                li = tf[:, 8 * half:8 * half + 8, 1]
                ps_ = slice(8 * half, 8 * half + 8)
                TT("pool", xl[:, 2, :], lr, xl[:, 0, :], ALU.mult, ["te", "xl0"], ["xl2"])
                TT("pool", xl[:, 3, :], li, xl[:, 1, :], ALU.mult, ["tf", "xl1"], ["xl3"])
                TT("pool", cr_out[:, ps_], xl[:, 2, :], xl[:, 3, :], ALU.subtract, ["xl2", "xl3"], [kout])
                TT("pool", xl[:, 4, :], lr, xl[:, 1, :], ALU.mult, ["te", "xl1"], ["xl4"])
                TT("pool", xl[:, 5, :], li, xl[:, 0, :], ALU.mult, ["tf", "xl0"], ["xl5"])
                TT("pool", ci_out[:, ps_], xl[:, 4, :], xl[:, 5, :], ALU.add, ["xl4", "xl5"], [kout])
                for ct in (2 * half, 2 * half + 1):
                    ysl = slice(ct * 128, (ct + 1) * 128)
                    for pr in range(4):
                        pair = 4 * ct + pr
                        p8 = pair - 8 * half
                        MM(PS[1][:, ysl], Ccat[:, 0, pair, :], Xre[half][:, p8, :], pr == 0, False, ["Ccat", f"Xre{half}"], ["ps1"])
                        MM(PS[1][:, ysl], Ccat[:, 1, pair, :], Xim[half][:, p8, :], False, False, ["Ccat", f"Xim{half}"], ["ps1"])
                    MM(PS[1][:, ysl], Ddiag[:, ct, :], uT[:, ct, tok], False, True, ["Ddiag", "uT"], ["ps1"])
            ACT(gT[:, :, tok], PS[1][:, :].rearrange("p (c t) -> p c t", c=4), AF.Gelu, ["ps1"], ["gT"])
        for c in range(4):
            for kt in range(4):
                MM(PS[2][:, :], gluw[:, kt, c * 128:(c + 1) * 128], gT[:, kt, :], kt == 0, kt == 3, ["gluw", "gT"], ["ps2"])
            ACT(sg[:], PS[2][:, :], AF.Sigmoid, ["ps2", "glub"], ["sg"], bias=glub[:, c:c + 1])
            TT("dve", sg[:], sg[:], gT[:, c, :], ALU.mult, ["sg", "gT"], ["sg"])
            TT("dve", OST[:, c, st * 512:(st + 1) * 512], sg[:], gsT[:, c, :], ALU.mult, ["sg", "gsT"], ["OST", "Bst", "Cst"])
    if dbg == "p2":
        DMA("sp", dbg_d[:, :], OSTm[:, :], ["OST"], ["outd"], "OSTm")
        NOP(["outd"])
        return finish(nc, S)

    S.barrier()
    M.release(m_glob)
    w3 = M.alloc("w3", [128, 8, 2560], BF16)
    pa = M.alloc("pa", [128, 4, 1024], BF16)
    psw = M.alloc("psw", [128, 4, 1024], BF16)
    wo = M.alloc("wo", [128, 8, 1024], BF16)
    mb = M.alloc("mb", [128, 16], F32)
    m_work3 = M.mark()
    stg = [M.alloc("stg", [128, 2048], F32) for _ in range(2)]
    load_weight(w3[:, :, 0:512], w_in_d, 0, 8, 1536, 512, stg, True, "w3")
    load_weight(w3[:, :, 512:2560], w_in_d, 0, 8, 3072, 2048, stg, True, "w3")
    load_weight(pa, pa_d, 0, 4, 0, 1024, stg, False, "pa")
    load_weight(psw, ps_d, 0, 4, 0, 1024, stg, False, "psw")
    load_weight(wo, wo_d, 0, 8, 0, 1024, stg, False, "wo")
    DMA("sp", mb[:], dap(mb_d, 0, [[1, 128], [128, 16]]), (), ["mb"], "mb", slow=True)
    S.barrier()
    M.release(m_work3)
    hT = M.alloc("hT", [128, 8, 512], BF16)
    xt = [M.alloc("xt", [128, 1024], F32)]
    xn = M.alloc("xn", [128, 1024], BF16)
    junk = M.alloc("junk", [128, 1024], BF16)
    mT = M.alloc("mT", [128, 8, 512], BF16)
    g1 = M.alloc("g1", [128, 512], F32)
    g2t = M.alloc("g2t", [128, 512], F32)
    m1 = M.alloc("m1", [128, 512], F32)
    xres = M.alloc("xres", [128, 1024], F32)
    ot = [M.alloc("ot", [128, 512], F32) for _ in range(2)]
    outkeys = []
    for st in range(NST):
        stc = slice(st * 512, (st + 1) * 512)
        make_hT(st, 2, hT, xt, xn, junk)
        for h in range(4):
            for kt in range(8):
                MM(PS[1][:, :], w3[:, kt, h * 128:(h + 1) * 128], hT[:, kt, :], kt == 0, kt == 7, ["w3", "hT"], ["ps1"])
            ACT(g1[:], PS[1][:, :], AF.Silu, ["ps1"], ["g1"])
            TT("dve", OAT[:, h, stc], OAT[:, h, stc], g1[:], ALU.mult, ["OAT", "g1"], ["OAT"])
        for c in range(8):
            for h in range(4):
                MM(PS[1][:, :], pa[:, h, c * 128:(c + 1) * 128], OAT[:, h, stc], h == 0, h == 3, ["pa", "OAT"], ["ps1"])
            for kt in range(8):
                MM(PS[2][:, :], w3[:, kt, 512 + c * 128:512 + (c + 1) * 128], hT[:, kt, :], kt == 0, kt == 7, ["w3", "hT"], ["ps2"])
            ACT(g1[:], PS[2][:, :], AF.Sigmoid, ["ps2", "mb"], ["g1"], bias=mb[:, c:c + 1])
            TT("dve", m1[:], PS[1][:, :], g1[:], ALU.mult, ["ps1", "g1"], ["m1"])
            for k in range(4):
                MM(PS[3][:, :], psw[:, k, c * 128:(c + 1) * 128], OST[:, k, stc], k == 0, k == 3, ["psw", "OST"], ["ps3"])
            for kt in range(8):
                MM(PS[4][:, :], w3[:, kt, 1536 + c * 128:1536 + (c + 1) * 128], hT[:, kt, :], kt == 0, kt == 7, ["w3", "hT"], ["ps4"])
            ACT(g2t[:], PS[4][:, :], AF.Sigmoid, ["ps4", "mb"], ["g2t"], bias=mb[:, 8 + c:9 + c])
            TT("dve", g2t[:], PS[3][:, :], g2t[:], ALU.mult, ["ps3", "g2t"], ["g2t"])
            TT("pool", mT[:, c, :], m1[:], g2t[:], ALU.add, ["m1", "g2t"], ["mT"])
        for tt in range(4):
            row = st * 512 + tt * 128
            DMA("sp", xres[:], x_d[row:row + 128, :], (), ["xres"], "xres")
            for half in range(2):
                for kt in range(8):
                    MM(PS[5 + half][:, :], mT[:, kt, tt * 128:(tt + 1) * 128], wo[:, kt, half * 512:(half + 1) * 512], kt == 0, kt == 7,
                       ["mT", "wo"], [f"ps{5 + half}"])
                TT("dve", ot[half][:], PS[5 + half][:, :], xres[:, half * 512:(half + 1) * 512], ALU.add, [f"ps{5 + half}", "xres"], [f"ot{half}"])
                ok = f"outd{st}_{tt}_{half}"
                DMA("sp", out_d[row:row + 128, half * 512:(half + 1) * 512], ot[half][:], [f"ot{half}"], [ok], f"ot{half}")
                outkeys.append(ok)
    NOP(outkeys)
    return finish(nc, S)


def finish(nc, S):
    import contextlib
    with contextlib.ExitStack() as stk:
        S._sem_ctx = {e: stk.enter_context(nc.semaphore(f"s_{e}")) for e in S.ENGS}
        S._dsem = {k: stk.enter_context(nc.semaphore(f"d_{k}")) for k in S.dma_count}
        block = stk.enter_context(nc.Block())
        S.emit(block)
    return nc


_NC_CACHE = {}


def _core_inputs(inputs, b, consts):
    m = {"x": np.ascontiguousarray(inputs["x"][b], dtype=np.float32)}
    for k, v in inputs.items():
        if k == "x":
            continue
        v = np.asarray(v, dtype=np.float32)
        if k == "rel_bias_table":
            m[k] = np.ascontiguousarray(v)
            continue
        v0 = v[0]
        if k in ("w_in", "ssm_glu_w", "proj_attn", "proj_ssm", "w_out"):
            m[k] = np.ascontiguousarray(v0)
        else:
            m[k] = np.ascontiguousarray(v0).reshape(-1)
    m.update(consts)
    return m


def kernel(**inputs):
    if "nc" not in _NC_CACHE:
        _NC_CACHE["nc"] = build_nc()
    nc = _NC_CACHE["nc"]
    consts = host_consts()
    in_maps = [_core_inputs(inputs, b, consts) for b in range(8)]
    res = run_bass_kernel_spmd(nc, in_maps, core_ids=list(range(8)))
    out = np.stack([np.asarray(res.results[b]["out"], dtype=np.float32) for b in range(8)], axis=0)
    return out
```

```python
import math
import numpy as np
import ml_dtypes
import concourse.bass as bass
import concourse.mybir as mybir
from concourse.bass_utils import run_bass_kernel_spmd

F32 = mybir.dt.float32
BF16 = mybir.dt.bfloat16
AF = mybir.ActivationFunctionType
ALU = mybir.AluOpType
AX = mybir.AxisListType


class _Op:
    __slots__ = ("eng", "fn", "idx", "dma", "dkey", "deps", "signal", "sigval", "sem")

    def __init__(self, eng, fn, idx, dma, dkey):
        self.eng = eng
        self.fn = fn
        self.idx = idx
        self.dma = dma
        self.dkey = dkey
        self.deps = {}
        self.signal = False
        self.sigval = 0
        self.sem = None


class Sched:
    ENGS = ("pe", "act", "dve", "pool", "sp")

    def __init__(self, nc):
        self.nc = nc
        self.ops = {e: [] for e in self.ENGS}
        self.last_w = {}
        self.readers = {}
        self.dma_count = {}

    def add(self, eng, fn, reads=(), writes=(), dma=False, dkey=None):
        op = _Op(eng, fn, len(self.ops[eng]), dma, dkey)
        reads = tuple(reads) + ("__B__",)
        for k in reads:
            w = self.last_w.get(k)
            if w is not None:
                op.deps[w] = True
        for k in writes:
            w = self.last_w.get(k)
            if w is not None and w not in op.deps:
                op.deps[w] = False
            rd = self.readers.get(k)
            if rd:
                for r in rd.get("eng", {}).values():
                    if r is not op and r not in op.deps:
                        op.deps[r] = False
                for r in rd.get("dma", []):
                    if r is not op and r not in op.deps:
                        op.deps[r] = False
        for k in reads:
            rd = self.readers.setdefault(k, {"eng": {}, "dma": []})
            if dma:
                rd["dma"].append(op)
            else:
                rd["eng"][eng] = op
        for k in writes:
            self.last_w[k] = op
            self.readers[k] = {"eng": {}, "dma": []}
        if dma:
            assert dkey is not None
            n = self.dma_count.get(dkey, 0) + 1
            self.dma_count[dkey] = n
            op.sigval = 16 * n
        self.ops[eng].append(op)
        return op

    def barrier(self):
        t = self._bar_tile
        self.add("dve", lambda e: e.memset(t[:, 0:1], 0.0), writes=("__B__",))

    def _needs_wait(self, op, dep, is_raw):
        if dep.dma:
            return True
        if dep.eng != op.eng:
            return True
        if op.dma:
            return True
        if op.eng == "pe":
            return False
        return is_raw and (op.idx - dep.idx) <= 3

    def emit(self, block, extra_final=None):
        nc = self.nc
        for e in self.ENGS:
            for op in self.ops[e]:
                for dep, is_raw in op.deps.items():
                    if self._needs_wait(op, dep, is_raw) and not dep.dma:
                        dep.signal = True
        sems = {e: self._sem_ctx[e] for e in self.ENGS}
        dsems = self._dsem
        for e in self.ENGS:
            n = 0
            for op in self.ops[e]:
                if op.dma:
                    op.sem = dsems[op.dkey]
                else:
                    op.sem = sems[e]
                    if op.signal:
                        n += 1
                        op.sigval = n

        def run_engine(ename, eng):
            known = {}
            for op in self.ops[ename]:
                need = {}
                for dep, is_raw in op.deps.items():
                    if not self._needs_wait(op, dep, is_raw):
                        continue
                    s = dep.sem
                    v = dep.sigval
                    if need.get(s, 0) < v:
                        need[s] = v
                for s, v in need.items():
                    if known.get(s, 0) < v:
                        eng.wait_ge(s, v)
                        known[s] = v
                ins = op.fn(eng)
                if op.dma:
                    ins.then_inc(op.sem, 16)
                elif op.signal:
                    ins.then_inc(op.sem, 1)
            if extra_final is not None:
                extra_final(ename, eng, known)

        @block.tensor
        def _(eng):
            run_engine("pe", eng)

        @block.scalar
        def _(eng):
            run_engine("act", eng)

        @block.vector
        def _(eng):
            run_engine("dve", eng)

        @block.gpsimd
        def _(eng):
            run_engine("pool", eng)

        @block.sync
        def _(eng):
            run_engine("sp", eng)


L = 4096
D = 1024
NST = 8
EPS = 1e-6
LAM_INIT = 0.8 - 0.6 * math.exp(-0.3 * 0)
TWO_PI = 2.0 * math.pi
NEG = -30000.0


def _bucket_np(rel):
    nb = 16
    me = 8
    side = np.where(rel > 0, nb, 0)
    n = np.abs(rel)
    nf = np.maximum(n, 1).astype(np.float32)
    large = me + (np.log(nf / np.float32(me)).astype(np.float32) / np.float32(math.log(128 / 8))
                  * np.float32(nb - me)).astype(np.int32)
    large = np.minimum(large, nb - 1)
    return side + np.where(n < me, n, large)


def host_consts():
    c = {}
    c["c_ident"] = np.eye(128, dtype=np.float32).astype(ml_dtypes.bfloat16)
    bo = np.zeros((128, 128), np.float32)
    bo[:64, :64] = 1.0 / 64
    bo[64:, 64:] = 1.0 / 64
    c["c_bones"] = bo.astype(ml_dtypes.bfloat16)
    c["c_J"] = np.ascontiguousarray(np.eye(128, dtype=np.float32)[::-1])
    rel = np.arange(-255, 128)
    b = _bucket_np(rel)
    oh = np.zeros((32, 384), np.float32)
    oh[b, np.arange(383)] = 1.0
    c["c_oh"] = oh
    k = np.arange(128)[:, None]
    q = np.arange(128)[None, :]
    c["c_maskD"] = np.where((k // 64) <= (q // 64), 0.0, NEG).astype(np.float32)
    c["c_iotap"] = np.stack([np.arange(128), -np.arange(128)], 1).astype(np.float32)
    c["c_iotat"] = np.tile(np.arange(128, dtype=np.float32)[None, :], (128, 1))
    c["c_tri"] = (np.arange(128)[:, None] <= np.arange(128)[None, :]).astype(np.float32).astype(ml_dtypes.bfloat16)
    return c


class Mem:
    def __init__(self, nc, start=16640, end=229376):
        self.nc = nc
        self.p = start
        self.end = end
        self.n = 0

    def alloc(self, name, shape, dtype):
        esz = 4 if dtype in (F32, mybir.dt.int32) else 2
        size = int(np.prod(shape[1:])) * esz
        size = (size + 63) // 64 * 64
        assert self.p + size <= self.end, f"SBUF overflow allocating {name}: {self.p}+{size} > {self.end}"
        self.n += 1
        t = self.nc.alloc_sbuf_tensor_at(f"{name}_{self.n}", list(shape), dtype, offset=self.p)
        self.p += size
        return t

    def mark(self):
        return self.p

    def release(self, m):
        self.p = m


def build_nc(dbg=None):
    nc = bass.Bass("TRN2", target_bir_lowering=False)
    S = Sched(nc)
    M = Mem(nc)

    def din(name, shape, dt=F32):
        return nc.dram_tensor(name, list(shape), dt, kind="ExternalInput").ap()

    x_d = din("x", [L, D])
    w_in_d = din("w_in", [D, 5120])
    ng_d = din("norm_gain", [D])
    mb_d = din("merge_gate_b", [2048])
    qg_d = din("q_norm_gain", [64])
    kg_d = din("k_norm_gain", [64])
    lq1_d = din("lambda_q1", [64]); lk1_d = din("lambda_k1", [64])
    lq2_d = din("lambda_q2", [64]); lk2_d = din("lambda_k2", [64])
    sg_d = din("diff_subln_gain", [128])
    rb_d = din("rel_bias_table", [32, 4])
    are_d = din("ssm_A_re", [2048]); aim_d = din("ssm_A_im", [2048])
    ldt_d = din("ssm_log_dt", [32])
    bre_d = din("ssm_B_re", [32 * 64 * 16]); bim_d = din("ssm_B_im", [32 * 64 * 16])
    cre_d = din("ssm_C_re", [32 * 16 * 64]); cim_d = din("ssm_C_im", [32 * 16 * 64])
    dd_d = din("ssm_D", [512])
    gw_d = din("ssm_glu_w", [512, 512])
    gb_d = din("ssm_glu_b", [512])
    pa_d = din("proj_attn", [512, 1024])
    ps_d = din("proj_ssm", [512, 1024])
    wo_d = din("w_out", [1024, 1024])
    c_ident_d = din("c_ident", [128, 128], BF16)
    c_bones_d = din("c_bones", [128, 128], BF16)
    c_J_d = din("c_J", [128, 128])
    c_oh_d = din("c_oh", [32, 384])
    c_maskD_d = din("c_maskD", [128, 128])
    c_iotap_d = din("c_iotap", [128, 2])
    c_iotat_d = din("c_iotat", [128, 128])
    c_tri_d = din("c_tri", [128, 128], BF16)
    out_d = nc.dram_tensor("out", [L, D], F32, kind="ExternalOutput").ap()
    fsc_d = nc.dram_tensor("fscratch", [4, 384], F32).ap()
    dbg_d = None
    if dbg:
        dbg_d = nc.dram_tensor("dbg", [128, 4 * L], BF16, kind="ExternalOutput").ap()

    def dap(t, off, pat):
        return bass.AP(t.tensor, off, [list(p) for p in pat])

    PS = [nc.alloc_psum_tensor(f"psb{i}", [128, 512], F32) for i in range(8)]

    def MM(out, lhsT, rhs, start, stop, r, w):
        S.add("pe", lambda e: e.matmul(out, lhsT=lhsT, rhs=rhs, start=start, stop=stop), r, w)

    def TR(out, in_, r, w):
        S.add("pe", lambda e: e.transpose(out=out, in_=in_, identity=ident[:]), tuple(r) + ("ident",), w)

    def ACT(out, in_, func, r, w, bias=0.0, scale=1.0, accum=None):
        if accum is None:
            S.add("act", lambda e: e.activation(out=out, in_=in_, func=func, bias=bias, scale=scale), r, w)
        else:
            S.add("act", lambda e: e.activation(out=out, in_=in_, func=func, bias=bias, scale=scale, accum_out=accum), r, w)

    def TS(eng, out, in0, s1, s2, op0, op1, r, w):
        if s2 is None:
            S.add(eng, lambda e: e.tensor_scalar(out=out, in0=in0, scalar1=s1, scalar2=None, op0=op0), r, w)
        else:
            S.add(eng, lambda e: e.tensor_scalar(out=out, in0=in0, scalar1=s1, scalar2=s2, op0=op0, op1=op1), r, w)

    def TT(eng, out, in0, in1, op, r, w):
        S.add(eng, lambda e: e.tensor_tensor(out=out, in0=in0, in1=in1, op=op), r, w)

    def STT(eng, out, in0, scalar, in1, op0, op1, r, w):
        S.add(eng, lambda e: e.scalar_tensor_tensor(out=out, in0=in0, scalar=scalar, in1=in1, op0=op0, op1=op1), r, w)

    def CP(eng, out, in_, r, w):
        if eng == "act":
            S.add("act", lambda e: e.copy(out=out, in_=in_), r, w)
        else:
            S.add(eng, lambda e: e.tensor_copy(out=out, in_=in_), r, w)

    def RSUM(out, in_, r, w):
        S.add("dve", lambda e: e.reduce_sum(out=out, in_=in_, axis=AX.X), r, w)

    def RECIP(out, in_, r, w):
        S.add("dve", lambda e: e.reciprocal(out=out, in_=in_), r, w)

    def NOP(r):
        S.add("sp", lambda e: e.nop(), r, [])

    def MS(eng, ap, val, w):
        S.add(eng, lambda e: e.memset(ap, val), (), w)

    def DMA(eng, out, in_, r, w, dkey, slow=False):
        if slow:
            S.add(eng, lambda e: e.dma_start(out=out, in_=in_, allow_slow_non_contiguous=True), r, w, dma=True, dkey=dkey)
        else:
            S.add(eng, lambda e: e.dma_start(out=out, in_=in_), r, w, dma=True, dkey=dkey)

    bar = M.alloc("bar", [128, 16], F32)
    S._bar_tile = bar
    ident = M.alloc("ident", [128, 128], BF16)
    bones = M.alloc("bones", [128, 128], BF16)
    tri = M.alloc("tri", [128, 128], BF16)
    iotap = M.alloc("iotap", [128, 2], F32)
    iotat = M.alloc("iotat", [128, 128], F32)
    gcol = M.alloc("gcol", [128, 8], F32)
    small = M.alloc("small", [128, 64], F32)
    rs_all = M.alloc("rs_all", [128, 3 * 32 * 2], F32)
    OAT = M.alloc("OAT", [128, 4, L], BF16)
    OSTm = M.alloc("OSTm", [128, 4 * L], BF16)
    OST = OSTm[:, :].rearrange("p (c t) -> p c t", c=4)
    stag32 = OSTm[:, :].bitcast(F32)
    Bst = stag32[:, 0:4096].rearrange("p (r c g q) -> p r c g q", r=2, c=4, g=8)
    Cst = stag32[:, 4096:8192].rearrange("p (r a c) -> p r a c", r=2, a=16)

    DMA("sp", ident[:], c_ident_d[:, :], (), ["ident"], "ident")
    DMA("sp", bones[:], c_bones_d[:, :], (), ["bones"], "bones")
    DMA("sp", tri[:], c_tri_d[:, :], (), ["tri"], "tri")
    DMA("sp", iotap[:], c_iotap_d[:, :], (), ["iotap"], "iotap")
    DMA("sp", iotat[:], c_iotat_d[:, :], (), ["iotat"], "iotat")
    DMA("sp", gcol[:], dap(ng_d, 0, [[1, 128], [128, 8]]), (), ["gcol"], "gcol", slow=True)

    MS("dve", stag32[:, 0:4096], 0.0, ["Bst"])
    MS("pool", stag32[:, 4096:8192], 0.0, ["Cst"])
    bc_dmas = []
    for g in range(32):
        for ri in range(2):
            bc_dmas.append((g, ri))

    def emit_bc(n):
        for _ in range(n):
            if not bc_dmas:
                return
            g, ri = bc_dmas.pop(0)
            ct, g8 = g // 8, g % 8
            pair, g2 = g // 2, g % 2
            bd = (bre_d, bim_d)[ri]
            cd = (cre_d, cim_d)[ri]
            DMA("sp", Bst[16 * g8:16 * g8 + 16, ri, ct, g8, :], dap(bd, g * 1024, [[1, 16], [16, 64]]),
                (), ["Bst"], "Bst", slow=True)
            DMA("sp", Cst[64 * g2:64 * g2 + 64, ri, pair, (g % 8) * 16:(g % 8) * 16 + 16],
                dap(cd, g * 1024, [[1, 64], [64, 16]]), (), ["Cst"], "Cst", slow=True)

    def load_weight(dst, src_d, row0, nrows_tiles, col0, ncols, stg, fold_gain, keyw):
        i = 0
        for kt in range(nrows_tiles):
            for c0 in range(0, ncols, 2048):
                cw = min(2048, ncols - c0)
                sb = stg[i % 2]
                sk = f"stg{i % 2}"
                DMA("sp", sb[:, 0:cw], src_d[row0 + kt * 128: row0 + (kt + 1) * 128, col0 + c0: col0 + c0 + cw],
                    (), [sk], sk)
                if fold_gain:
                    if i % 2 == 0:
                        ACT(dst[:, kt, c0:c0 + cw], sb[:, 0:cw], AF.Copy, [sk, "gcol"], [keyw], scale=gcol[:, kt:kt + 1])
                    else:
                        TS("dve", dst[:, kt, c0:c0 + cw], sb[:, 0:cw], gcol[:, kt:kt + 1], None, ALU.mult, None, [sk, "gcol"], [keyw])
                else:
                    if i % 2 == 0:
                        CP("act", dst[:, kt, c0:c0 + cw], sb[:, 0:cw], [sk], [keyw])
                    else:
                        CP("dve", dst[:, kt, c0:c0 + cw], sb[:, 0:cw], [sk], [keyw])
                i += 1

    def make_hT(st, phase, hT, xt, xn, junk):
        psT = PS[0][:, 0:512].bitcast(BF16)
        for tt in range(4):
            i = st * 4 + tt
            b = i % len(xt)
            row = st * 512 + tt * 128
            col = (phase * 32 + i) * 2
            ss = rs_all[:, col:col + 1]
            rs = rs_all[:, col + 1:col + 2]
            DMA("sp", xt[b][:], x_d[row:row + 128, :], (), [f"xt{b}"], f"xt{b}")
            jk = "xn" if junk is xn else "junk"
            ACT(junk[:], xt[b][:], AF.Square, [f"xt{b}"], [jk, f"ss{col}"], accum=ss)
            ACT(rs, ss, AF.Ln, [f"ss{col}"], [f"rs{col}"], bias=EPS, scale=1.0 / D)
            ACT(rs, rs, AF.Exp, [f"rs{col}"], [f"rs{col}"], scale=-0.5)
            TS("dve", xn[:], xt[b][:], rs, None, ALU.mult, None, [f"xt{b}", f"rs{col}"], ["xn"])
            for kt in range(8):
                TR(psT[:, kt * 128:(kt + 1) * 128], xn[:, kt * 128:(kt + 1) * 128], ["xn"], ["ps0"])
            CP("dve", hT[:, :, tt * 128:(tt + 1) * 128], psT[:, :].rearrange("p (k t) -> p k t", k=8), ["ps0"], ["hT"])

    m_glob = M.mark()
    wq = M.alloc("wq", [128, 8, 1536], BF16)
    KT = M.alloc("KT", [128, 4, L], BF16)
    V = M.alloc("V", [128, 32, 4, 128], BF16)
    ones128 = M.alloc("ones128", [128, 128], BF16)
    sgcol = M.alloc("sgcol", [128, 1], F32)
    c15 = M.alloc("c15", [128, 4], F32)
    biasP = M.alloc("biasP", [128, 4, 128], F32)
    biasD = M.alloc("biasD", [128, 4, 128], F32)
    gq = M.alloc("gq", [128, 2], F32)
    lamt = M.alloc("lamt", [128, 8], F32)
    m_work = M.mark()
    stg = [M.alloc("stg", [128, 2048], F32) for _ in range(2)]
    lamv = M.alloc("lamv", [128, 4, 64], F32)
    tbl = M.alloc("tbl", [32, 4], F32)
    oh = M.alloc("oh", [32, 384], F32)
    fsb = M.alloc("fsb", [4, 384], F32)
    Jt = M.alloc("Jt", [128, 128], F32)
    hank = M.alloc("hank", [128, 4, 256], F32)
    maskD = M.alloc("maskD", [128, 128], F32)

    load_weight(wq, w_in_d, 0, 8, 0, 1536, stg, True, "wq")
    MS("dve", ones128[:], 1.0, ["ones128"])
    DMA("sp", sgcol[:], dap(sg_d, 0, [[1, 128], [1, 1]]), (), ["sgcol"], "sgcol")
    TS("dve", sgcol[:], sgcol[:], 1.0 - LAM_INIT, None, ALU.mult, None, ["sgcol"], ["sgcol"])
    for hlf in range(2):
        DMA("sp", gq[64 * hlf:64 * hlf + 64, 0:1], dap(qg_d, 0, [[1, 64], [1, 1]]), (), ["gq"], "gq")
        DMA("sp", gq[64 * hlf:64 * hlf + 64, 1:2], dap(kg_d, 0, [[1, 64], [1, 1]]), (), ["gq"], "gq")
    TS("dve", gq[:, 0:1], gq[:, 0:1], 0.125, None, ALU.mult, None, ["gq"], ["gq"])
    for i, dd in enumerate((lq1_d, lk1_d, lq2_d, lk2_d)):
        DMA("sp", lamv[:, i, :], dap(dd, 0, [[0, 128], [1, 64]]), (), ["lamv"], "lamv")
    TT("dve", lamv[:, 0, :], lamv[:, 0, :], lamv[:, 1, :], ALU.mult, ["lamv"], ["lamv"])
    TT("dve", lamv[:, 2, :], lamv[:, 2, :], lamv[:, 3, :], ALU.mult, ["lamv"], ["lamv"])
    RSUM(lamt[:, 0:1], lamv[:, 0, :], ["lamv"], ["lamt"])
    RSUM(lamt[:, 1:2], lamv[:, 2, :], ["lamv"], ["lamt"])
    ACT(lamt[:, 2:4], lamt[:, 0:2], AF.Exp, ["lamt"], ["lamt"])
    TT("dve", lamt[:, 4:5], lamt[:, 3:4], lamt[:, 2:3], ALU.subtract, ["lamt"], ["lamt"])
    TS("dve", lamt[:, 5:6], lamt[:, 4:5], -LAM_INIT, None, ALU.add, None, ["lamt"], ["lamt"])
    neglam = lamt[:, 5:6]
    DMA("sp", tbl[:], rb_d[:, :], (), ["tbl"], "tbl")
    DMA("sp", oh[:], c_oh_d[:, :], (), ["oh"], "oh")
    DMA("sp", Jt[:], c_J_d[:, :], (), ["Jt"], "Jt")
    DMA("sp", maskD[:], c_maskD_d[:, :], (), ["maskD"], "maskD")
    DMA("sp", c15[:], dap(rb_d, 15 * 4, [[0, 128], [1, 4]]), (), ["c15"], "c15")
    MM(PS[1][0:4, 0:384], tbl[:], oh[:], True, True, ["tbl", "oh"], ["ps1"])
    CP("dve", fsb[:], PS[1][0:4, 0:384], ["ps1"], ["fsb"])
    DMA("sp", fsc_d[:, :], fsb[:], ["fsb"], ["fsc"], "fsb")
    for h in range(4):
        DMA("sp", hank[:, h, :], dap(fsc_d, h * 384, [[1, 128], [1, 256]]), ["fsc"], ["hank"], "hank")
    for h in range(4):
        MM(PS[2][:, h * 128:(h + 1) * 128], hank[:, h, 0:128], Jt[:], True, True, ["hank", "Jt"], ["ps2"])
        MM(PS[3][:, h * 128:(h + 1) * 128], hank[:, h, 128:256], Jt[:], True, True, ["hank", "Jt"], ["ps3"])
    CP("dve", biasP[:], PS[2][:, :].rearrange("p (h q) -> p h q", h=4), ["ps2"], ["biasP"])
    TT("dve", biasD[:], PS[3][:, :].rearrange("p (h q) -> p h q", h=4),
       maskD[:].unsqueeze(1).broadcast_to([128, 4, 128]), ALU.add, ["ps3", "maskD"], ["biasD"])
    S.barrier()
    M.release(m_work)
    hT = M.alloc("hT", [128, 8, 512], BF16)
    xt = [M.alloc("xt", [128, 1024], F32)]
    xn = M.alloc("xn", [128, 1024], BF16)
    qT = M.alloc("qT", [128, 4, 512], BF16)
    sq = M.alloc("sq", [128, 512], BF16)
    rstd = M.alloc("rstd", [128, 512], F32)
    PT = [M.alloc("PT", [128, 512], BF16) for _ in range(4)]
    tmpb = [M.alloc("tmpb", [128, 128], F32) for _ in range(2)]
    rcp = M.alloc("rcp", [128, 512], F32)
    oraw = [M.alloc("oraw", [128, 512], F32) for _ in range(2)]
    racc = [[M.alloc("racc", [128, 512], F32) for _ in range(2)] for _ in range(2)]
    tob = M.alloc("tob", [128, 512], F32)
    rsb = M.alloc("rsb", [128, 512], BF16)
    sqb = M.alloc("sqb", [128, 512], BF16)
    rs2 = M.alloc("rs2", [128, 512], F32)
    sq2 = [sq, M.alloc("sq2", [128, 512], BF16)]
    rstd2 = [rstd, rs2]
    rkeys = ["rstd0", "rs2"]

    PSB = [3, 4, 7, 6]
    NB = 4
    SKEW = 3
    FILL = 0
    deferred = []

    def qk_mm(c, st):
        zb = 1 + (c % 2)
        for kt in range(8):
            MM(PS[zb][:, :], wq[:, kt, c * 128:(c + 1) * 128], hT[:, kt, :], kt == 0, kt == 7, ["wq", "hT"], [f"ps{zb}"])

    def qk_chain(c, st):
        zb = 1 + (c % 2)
        sqc, rsc = sq2[c % 2], rstd2[c % 2]
        ACT(sqc[:], PS[zb][:, :], AF.Square, [f"ps{zb}"], [f"sq{c % 2}"])
        MM(PS[0][:, :], bones[:], sqc[:], True, True, ["bones", f"sq{c % 2}"], ["ps0"])
        ACT(rsc[:], PS[0][:, :], AF.Ln, ["ps0"], [rkeys[c % 2]], bias=EPS)
        ACT(rsc[:], rsc[:], AF.Exp, [rkeys[c % 2]], [rkeys[c % 2]], scale=-0.5)
        if c < 4:
            STT("dve", qT[:, c, :], PS[zb][:, :], gq[:, 0:1], rsc[:], ALU.mult, ALU.mult, [f"ps{zb}", "gq", rkeys[c % 2]], ["qT"])
        else:
            STT("dve", KT[:, c - 4, st * 512:(st + 1) * 512], PS[zb][:, :], gq[:, 1:2], rsc[:], ALU.mult, ALU.mult,
                [f"ps{zb}", "gq", rkeys[c % 2]], [f"KT{st}"])

    for st in range(NST):
        make_hT(st, 0, hT, xt, xn, xn)
        emit_bc(8)
        qk_mm(0, st)
        for c in range(8):
            if c + 1 < 8:
                qk_mm(c + 1, st)
            qk_chain(c, st)
        for tt in range(4):
            blk = st * 4 + tt
            zb = 1 + (tt % 2)
            for kt in range(8):
                MM(PS[zb][:, :], hT[:, kt, tt * 128:(tt + 1) * 128], wq[:, kt, 1024:1536], kt == 0, kt == 7, ["wq", "hT"], [f"ps{zb}"])
            CP("dve", V[:, blk, :, :], PS[zb][:, :].rearrange("p (h d) -> p h d", h=4), [f"ps{zb}"], [f"V{st}"])
        kv_keys = [f"KT{i}" for i in range(st + 1)] + [f"V{i}" for i in range(st + 1)]
        iters = [(h, s, j) for h in range(4) for s in range(2) for j in range(4 * st + 4)]
        NI = len(iters)

        def emit_S(i, st=st, kv_keys=kv_keys, iters=iters):
            h, s, j = iters[i]
            lo = max(0, j - 4 * st)
            cols = slice(lo * 128, 512)
            pb = i % NB
            psS = PS[PSB[pb]]
            MM(psS[:, cols], KT[64 * s:64 * s + 64, h, j * 128:(j + 1) * 128], qT[64 * s:64 * s + 64, h, cols],
               True, True, kv_keys + ["qT"], [f"ps{PSB[pb]}"])

        def fin_A(h, s, st):
            par = (h * 2 + s) % 2
            TT("dve", rsb[:], racc[par][0][:], racc[par][1][:], ALU.add, [f"racc{par}0", f"racc{par}1"], ["rsb"])
            MM(PS[0][:, :], ones128[:], rsb[:], True, True, ["ones128", "rsb"], ["ps0"])
            RECIP(rcp[:], PS[0][:, :], ["ps0"], ["rcp"])
            TT("dve", oraw[s][:], oraw[s][:], rcp[:], ALU.mult, [f"oraw{s}", "rcp"], [f"oraw{s}"])
            if s == 1:
                STT("dve", tob[:], oraw[1][:], neglam, oraw[0][:], ALU.mult, ALU.add, ["oraw0", "oraw1", "lamt"], ["tob"])
                ACT(sqb[:], tob[:], AF.Square, ["tob"], ["sqb"])

        def fin_B(h, st):
            stc = slice(st * 512, (st + 1) * 512)
            MM(PS[0][:, :], ones128[:], sqb[:], True, True, ["ones128", "sqb"], ["ps0"])
            ACT(rs2[:], PS[0][:, :], AF.Ln, ["ps0"], ["rs2"], bias=EPS, scale=1.0 / 128)
            ACT(rs2[:], rs2[:], AF.Exp, ["rs2"], ["rs2"], scale=-0.5)
            STT("dve", OAT[:, h, stc], tob[:], sgcol[:, 0:1], rs2[:], ALU.mult, ALU.mult, ["tob", "sgcol", "rs2"], ["OAT"])

        def emit_rest(i, st=st, kv_keys=kv_keys, iters=iters):
            h, s, j = iters[i]
            lo = max(0, j - 4 * st)
            pb = i % NB
            pk = f"ps{PSB[pb]}"
            psS = PS[PSB[pb]]
            par = (h * 2 + s) % 2
            if j == 0:
                MS("dve", racc[par][0][:], 0.0, [f"racc{par}0"])
                MS("pool", racc[par][1][:], 0.0, [f"racc{par}1"])
            far_lo = lo
            for qb in range(lo, 4):
                d = 4 * st + qb - j
                if d >= 2:
                    break
                bt = biasD if d == 0 else biasP
                tb_ = tmpb[d]
                TT("dve", tb_[:], psS[:, qb * 128:(qb + 1) * 128], bt[:, h, :], ALU.add, [pk, "biasD", "biasP"], [f"tmpb{d}"])
                ACT(PT[pb][:, qb * 128:(qb + 1) * 128], tb_[:], AF.Exp, [f"tmpb{d}"], [f"PT{pb}"])
                far_lo = qb + 1
            if far_lo < 4:
                ACT(PT[pb][:, far_lo * 128:512], psS[:, far_lo * 128:512], AF.Exp, [pk, "c15"], [f"PT{pb}"], bias=c15[:, h:h + 1])
            qs = slice(lo * 128, 512)
            last = (j == 4 * st + 3)
            MM(PS[5][:, qs], V[:, j, h, :], PT[pb][:, qs], j == 0, last, [f"PT{pb}"] + kv_keys, ["ps5"])
            for _f in range(FILL):
                MM(PS[6][:, :], ones128[:], wq[:, 0, 0:512], True, True, ["ones128", "wq"], ["ps6"])
            e_ = j % 2
            TT(("dve", "pool")[e_], racc[par][e_][:, qs], racc[par][e_][:, qs], PT[pb][:, qs], ALU.add,
               [f"racc{par}{e_}", f"PT{pb}"], [f"racc{par}{e_}"])
            if not last:
                return
            CP("act", oraw[s][:], PS[5][:, :], ["ps5"], [f"oraw{s}"])
            deferred.append((i + 3, "A", h, s, st))
            if s == 1:
                deferred.append((i + 6, "B", h, s, st))

        def run_deferred(k):
            while deferred and deferred[0][0] <= k:
                _, kind, hh, ss, sst = deferred.pop(0)
                if kind == "A":
                    fin_A(hh, ss, sst)
                else:
                    fin_B(hh, sst)

        for i in range(NI + SKEW):
            if i < NI:
                emit_S(i)
            k = i - SKEW
            if k >= 0:
                emit_rest(k)
                run_deferred(k)
        run_deferred(10 ** 9)

    if dbg == "p1":
        DMA("sp", dbg_d[:, :], OAT[:, :, :].rearrange("p h t -> p (h t)"), ["OAT"], ["outd"], "OAT")
        NOP(["outd"])
        return finish(nc, S)
    I32 = mybir.dt.int32
    S.barrier()
    M.release(m_glob)
    w2 = M.alloc("w2", [128, 8, 1024], BF16)
    gluw = M.alloc("gluw", [128, 4, 512], BF16)
    glub = M.alloc("glub", [128, 4], F32)
    ta = M.alloc("ta", [128, 2048], F32)
    tb = M.alloc("tb", [128, 2048], F32)
    te = M.alloc("te", [128, 16, 128], F32)
    tf = M.alloc("tf", [128, 16, 128], F32)
    Bbd = M.alloc("Bbd", [128, 4, 2, 512], BF16)
    Ccat = M.alloc("Ccat", [128, 2, 16, 128], BF16)
    Ddiag = M.alloc("Ddiag", [128, 4, 128], BF16)
    dcol = M.alloc("dcol", [128, 4], F32)
    negtri = M.alloc("negtri", [128, 128], BF16)
    CcatN = M.alloc("CcatN", [128, 16, 128], BF16)
    l128 = M.alloc("l128", [128, 6, 16], F32)
    cre = [M.alloc("cre", [128, 16], F32) for _ in range(2)]
    cim = [M.alloc("cim", [128, 16], F32) for _ in range(2)]
    m_work2 = M.mark()
    stg = [M.alloc("stg", [128, 2048], F32) for _ in range(2)]
    load_weight(w2, w_in_d, 0, 8, 2048, 1024, stg, True, "w2")
    load_weight(gluw, gw_d, 0, 4, 0, 512, stg, False, "gluw")
    S.barrier()
    M.release(m_work2)
    T1 = M.alloc("T1", [128, 2048], F32)
    T2 = M.alloc("T2", [128, 2048], F32)
    T3 = M.alloc("T3", [128, 2048], F32)
    T4 = M.alloc("T4", [128, 2048], F32)
    TI = M.alloc("TI", [128, 2048], I32)
    dtb = M.alloc("dtb", [128, 32], F32)
    s2 = M.alloc("s2", [128, 8, 16], F32)
    s3 = M.alloc("s3", [128, 14, 256], F32)
    s3i = M.alloc("s3i", [128, 256], I32)
    dt3 = M.alloc("dt3", [128, 4], F32)

    DMA("sp", glub[:], dap(gb_d, 0, [[1, 128], [128, 4]]), (), ["glub"], "glub", slow=True)

    def frac_sincos(y, tmp, ti, sin_out, cos_out, key):
        CP("dve", ti, y, [key + "y"], [key + "ti"])
        CP("dve", tmp, ti, [key + "ti"], [key + "tmp"])
        TT("dve", tmp, y, tmp, ALU.subtract, [key + "y", key + "tmp"], [key + "tmp"])
        ACT(sin_out, tmp, AF.Sin, [key + "tmp"], [key + "sin"], scale=TWO_PI)
        TS("dve", y, y, 0.25, None, ALU.add, None, [key + "y"], [key + "y"])
        CP("dve", ti, y, [key + "y"], [key + "ti"])
        CP("dve", tmp, ti, [key + "ti"], [key + "tmp"])
        TT("dve", tmp, y, tmp, ALU.subtract, [key + "y", key + "tmp"], [key + "tmp"])
        ACT(cos_out, tmp, AF.Sin, [key + "tmp"], [key + "cos"], scale=TWO_PI)

    DMA("sp", T1[:], dap(are_d, 0, [[0, 128], [1, 2048]]), (), ["T1"], "T1")
    DMA("sp", T2[:], dap(aim_d, 0, [[0, 128], [1, 2048]]), (), ["T2"], "T2")
    DMA("sp", dtb[:], dap(ldt_d, 0, [[0, 128], [1, 32]]), (), ["dtb"], "dtb")
    ACT(dtb[:], dtb[:], AF.Exp, ["dtb"], ["dtb"])
    dtb_b = dtb[:, :].unsqueeze(2).broadcast_to([128, 32, 64])
    TT("dve", T1[:, :].rearrange("p (g q) -> p g q", g=32), T1[:, :].rearrange("p (g q) -> p g q", g=32), dtb_b, ALU.mult, ["T1", "dtb"], ["T1"])
    TT("dve", T2[:, :].rearrange("p (g q) -> p g q", g=32), T2[:, :].rearrange("p (g q) -> p g q", g=32), dtb_b, ALU.mult, ["T2", "dtb"], ["T2"])
    ACT(ta[:], T1[:], AF.Exp, ["T1", "iotap"], ["ta"], scale=iotap[:, 1:2])
    TS("dve", T3[:], T2[:], iotap[:, 0:1], 1.0 / TWO_PI, ALU.mult, ALU.mult, ["T2", "iotap"], ["L1y"])
    frac_sincos(T3[:], T4[:], TI[:], tb[:], T1[:], "L1")
    STT("dve", tb[:], tb[:], -1.0, ta[:], ALU.mult, ALU.mult, ["L1sin", "ta"], ["tb", "L1sin"])
    TT("dve", ta[:], ta[:], T1[:], ALU.mult, ["ta", "L1cos", "tb"], ["ta"])
    A2re, A2im, dt2, m2, th2 = (s2[:, i, :] for i in range(5))
    DMA("sp", A2re, dap(are_d, 0, [[1, 128], [128, 16]]), (), ["A2re"], "A2re", slow=True)
    DMA("sp", A2im, dap(aim_d, 0, [[1, 128], [128, 16]]), (), ["A2im"], "A2im", slow=True)
    for g2 in range(2):
        DMA("sp", s2[64 * g2:64 * g2 + 64, 2, :], dap(ldt_d, g2, [[0, 64], [2, 16]]), (), ["dt2"], "dt2", slow=True)
    ACT(dt2, dt2, AF.Exp, ["dt2"], ["dt2"])
    TT("dve", m2, A2re, dt2, ALU.mult, ["A2re", "dt2"], ["m2"])
    STT("dve", th2, A2im, 1.0 / TWO_PI, dt2, ALU.mult, ALU.mult, ["A2im", "dt2"], ["th2"])
    T3v = T3[:, :].rearrange("p (a t) -> p a t", a=16)
    for pair in range(16):
        ACT(te[:, pair, :], iotat[:], AF.Exp, ["iotat", "m2"], ["te"], scale=s2[:, 3, pair:pair + 1])
        TS("dve", T3v[:, pair, :], iotat[:], s2[:, 4, pair:pair + 1], None, ALU.mult, None, ["iotat", "th2", "L1y", "L1tmp"], ["L2y"])
    frac_sincos(T3[:], T4[:], TI[:], tf[:, :, :].rearrange("p a t -> p (a t)"), T1[:], "L2")
    TT("dve", tf[:, :, :].rearrange("p a t -> p (a t)"), tf[:, :, :].rearrange("p a t -> p (a t)"), te[:, :, :].rearrange("p a t -> p (a t)"),
       ALU.mult, ["L2sin", "te"], ["tf", "L2sin"])
    TT("dve", te[:, :, :].rearrange("p a t -> p (a t)"), te[:, :, :].rearrange("p a t -> p (a t)"), T1[:], ALU.mult, ["te", "L2cos", "tf"], ["te"])
    A3re, A3im, m3, y3, dec3, sin3, cos3, nr3, den3, qre3, qim3, u3a, u3b, tmp3 = (s3[:, i, :] for i in range(14))
    for g8 in range(8):
        DMA("sp", s3[16 * g8:16 * g8 + 16, 0, :].rearrange("p (c q) -> p c q", c=4), dap(are_d, g8 * 64, [[0, 16], [512, 4], [1, 64]]), (), ["A3re"], "A3re")
        DMA("sp", s3[16 * g8:16 * g8 + 16, 1, :].rearrange("p (c q) -> p c q", c=4), dap(aim_d, g8 * 64, [[0, 16], [512, 4], [1, 64]]), (), ["A3im"], "A3im")
        DMA("sp", dt3[16 * g8:16 * g8 + 16, :], dap(ldt_d, g8, [[0, 16], [8, 4]]), (), ["dt3"], "dt3", slow=True)
    ACT(dt3[:], dt3[:], AF.Exp, ["dt3"], ["dt3"])
    dt3_b = dt3[:, :].unsqueeze(2).broadcast_to([128, 4, 64])
    v3 = lambda a: a.rearrange("p (c q) -> p c q", c=4)
    TT("dve", v3(m3), v3(A3re), dt3_b, ALU.mult, ["A3re", "dt3"], ["m3"])
    TT("dve", v3(y3), v3(A3im), dt3_b, ALU.mult, ["A3im", "dt3"], ["L3y"])
    TS("dve", y3, y3, 1.0 / TWO_PI, None, ALU.mult, None, ["L3y"], ["L3y"])
    ACT(dec3, m3, AF.Exp, ["m3"], ["dec3"])
    frac_sincos(y3, tmp3, s3i[:], sin3, cos3, "L3")
    TT("dve", cos3, cos3, dec3, ALU.mult, ["L3cos", "dec3"], ["lbr"])
    TT("dve", sin3, sin3, dec3, ALU.mult, ["L3sin", "dec3"], ["lbi"])
    TS("dve", nr3, cos3, -1.0, None, ALU.add, None, ["lbr"], ["nr3"])
    TT("dve", den3, A3re, A3re, ALU.mult, ["A3re"], ["den3"])
    TT("dve", u3a, A3im, A3im, ALU.mult, ["A3im"], ["u3a"])
    TT("dve", den3, den3, u3a, ALU.add, ["den3", "u3a"], ["den3"])
    RECIP(den3, den3, ["den3"], ["den3"])
    TT("dve", u3a, nr3, A3re, ALU.mult, ["nr3", "A3re", "den3"], ["u3a"])
    TT("dve", u3b, sin3, A3im, ALU.mult, ["lbi", "A3im"], ["u3b"])
    TT("dve", qre3, u3a, u3b, ALU.add, ["u3a", "u3b"], ["qre3"])
    TT("dve", qre3, qre3, den3, ALU.mult, ["qre3", "den3"], ["qre3"])
    TT("dve", u3a, sin3, A3re, ALU.mult, ["lbi", "A3re", "qre3"], ["u3a"])
    TT("dve", u3b, nr3, A3im, ALU.mult, ["nr3", "A3im", "qre3"], ["u3b"])
    TT("dve", qim3, u3a, u3b, ALU.subtract, ["u3a", "u3b"], ["qim3"])
    TT("dve", qim3, qim3, den3, ALU.mult, ["qim3", "den3"], ["qim3"])
    qre_b = v3(qre3).unsqueeze(2).broadcast_to([128, 4, 8, 64])
    qim_b = v3(qim3).unsqueeze(2).broadcast_to([128, 4, 8, 64])
    T1v4 = T1[:, :].rearrange("p (c g q) -> p c g q", c=4, g=8)
    T2v4 = T2[:, :].rearrange("p (c g q) -> p c g q", c=4, g=8)
    TT("dve", T1v4, Bst[:, 0], qre_b, ALU.mult, ["Bst", "qre3", "ta", "te"], ["T1"])
    TT("dve", T2v4, Bst[:, 1], qim_b, ALU.mult, ["Bst", "qim3", "L1y", "L2y"], ["T2"])
    TT("dve", Bbd[:, :, 0, :], T1[:, :].rearrange("p (c x) -> p c x", c=4), T2[:, :].rearrange("p (c x) -> p c x", c=4), ALU.subtract, ["T1", "T2"], ["Bbd"])
    TT("dve", T1v4, Bst[:, 1], qre_b, ALU.mult, ["Bst", "qre3", "Bbd"], ["T1"])
    TT("dve", T2v4, Bst[:, 0], qim_b, ALU.mult, ["Bst", "qim3", "Bbd"], ["T2"])
    TT("dve", Bbd[:, :, 1, :], T1[:, :].rearrange("p (c x) -> p c x", c=4), T2[:, :].rearrange("p (c x) -> p c x", c=4), ALU.add, ["T1", "T2"], ["Bbd"])
    CP("dve", Ccat[:, 0], Cst[:, 0], ["Cst"], ["Ccat"])
    TS("dve", CcatN[:], Cst[:, 0], -1.0, None, ALU.mult, None, ["Cst"], ["CcatN"])
    TS("dve", negtri[:], tri[:], -1.0, None, ALU.mult, None, ["tri"], ["negtri"])
    TT("dve", l128[:, 0, :], te[:, :, 127], te[:, :, 1], ALU.mult, ["te"], ["l128a"])
    TT("dve", l128[:, 1, :], tf[:, :, 127], tf[:, :, 1], ALU.mult, ["tf"], ["l128b"])
    TT("dve", l128[:, 4, :], l128[:, 0, :], l128[:, 1, :], ALU.subtract, ["l128a", "l128b"], ["l128"])
    TT("dve", l128[:, 2, :], te[:, :, 127], tf[:, :, 1], ALU.mult, ["te", "tf"], ["l128c"])
    TT("dve", l128[:, 3, :], tf[:, :, 127], te[:, :, 1], ALU.mult, ["te", "tf"], ["l128d"])
    TT("dve", l128[:, 5, :], l128[:, 2, :], l128[:, 3, :], ALU.add, ["l128c", "l128d"], ["l128"])
    TS("dve", Ccat[:, 1], Cst[:, 1], -1.0, None, ALU.mult, None, ["Cst"], ["Ccat"])
    DMA("sp", dcol[:], dap(dd_d, 0, [[1, 128], [128, 4]]), (), ["dcol"], "dcol", slow=True)
    for ct in range(4):
        TS("dve", Ddiag[:, ct, :], ident[:], dcol[:, ct:ct + 1], None, ALU.mult, None, ["ident", "dcol"], ["Ddiag"])
    MS("dve", cre[0][:], 0.0, ["c0_0", "c0_1", "c0_2", "c0_3"])
    MS("dve", cim[0][:], 0.0, ["c0_0", "c0_1", "c0_2", "c0_3"])
    S.barrier()
    M.release(m_work2)
    hT = M.alloc("hT", [128, 8, 512], BF16)
    xt = [M.alloc("xt", [128, 1024], F32)]
    xn = M.alloc("xn", [128, 1024], BF16)
    uT = M.alloc("uT", [128, 4, 512], BF16)
    gsT = M.alloc("gsT", [128, 4, 512], BF16)
    gT = M.alloc("gT", [128, 4, 512], BF16)
    xs = [M.alloc("xs", [128, 512], F32) for _ in range(2)]
    ys = [M.alloc("ys", [128, 512], F32) for _ in range(2)]
    pa_ = [[M.alloc("pa_", [128, 512], BF16) for _ in range(4)] for _ in range(2)]
    pb_ = [[M.alloc("pb_", [128, 4, 128], BF16) for _ in range(4)] for _ in range(2)]
    sr = [M.alloc("sr", [128, 4, 128], F32) for _ in range(2)]
    si = [M.alloc("si", [128, 4, 128], F32) for _ in range(2)]
    sg = M.alloc("sg", [128, 512], F32)
    xl = M.alloc("xl", [128, 4, 4], F32)

    def ssm_A(st, cc, ct, u):
        tok = slice(cc * 128, (cc + 1) * 128)
        ub = u % 2
        MM(PS[2][:, :], uT[:, ct, tok], Bbd[:, ct, 0, :], True, True, ["uT", "Bbd"], ["ps2"])
        MM(PS[3][:, :], uT[:, ct, tok], Bbd[:, ct, 1, :], True, True, ["uT", "Bbd"], ["ps3"])
        CP("act", xs[ub][:], PS[2][:, :], ["ps2"], [f"xs{ub}"])
        CP("act", ys[ub][:], PS[3][:, :], ["ps3"], [f"ys{ub}"])
        a_ = ta[:, ct * 512:(ct + 1) * 512]
        b_ = tb[:, ct * 512:(ct + 1) * 512]
        p = pa_[ub]
        TT("dve", p[0][:], xs[ub][:], a_, ALU.mult, [f"xs{ub}", "ta"], [f"pa{ub}0"])
        TT("pool", p[1][:], ys[ub][:], b_, ALU.mult, [f"ys{ub}", "tb"], [f"pa{ub}1"])
        TT("dve", p[2][:], ys[ub][:], a_, ALU.mult, [f"ys{ub}", "ta"], [f"pa{ub}2"])
        TT("pool", p[3][:], xs[ub][:], b_, ALU.mult, [f"xs{ub}", "tb"], [f"pa{ub}3"])

    def ssm_A2(st, cc, ct, u):
        ub = u % 2
        p = pa_[ub]
        for pr in range(4):
            cs = slice(pr * 128, (pr + 1) * 128)
            MM(PS[4 + 2 * ub][:, cs], p[0][:, cs], tri[:], True, False, [f"pa{ub}0", "tri"], [f"ps{4 + 2 * ub}"])
            MM(PS[4 + 2 * ub][:, cs], p[1][:, cs], negtri[:], False, True, [f"pa{ub}1", "negtri"], [f"ps{4 + 2 * ub}"])
            MM(PS[5 + 2 * ub][:, cs], p[2][:, cs], tri[:], True, False, [f"pa{ub}2", "tri"], [f"ps{5 + 2 * ub}"])
            MM(PS[5 + 2 * ub][:, cs], p[3][:, cs], tri[:], False, True, [f"pa{ub}3", "tri"], [f"ps{5 + 2 * ub}"])

    def ssm_B(st, cc, ct, u):
        tok = slice(cc * 128, (cc + 1) * 128)
        ci = st * 4 + cc
        ub = u % 2
        cr_in, ci_in = cre[ci % 2], cim[ci % 2]
        cr_out, ci_out = cre[(ci + 1) % 2], cim[(ci + 1) % 2]
        kin, kout = f"c{ci % 2}_{ct}", f"c{(ci + 1) % 2}_{ct}"
        for pr in range(4):
            pair = 4 * ct + pr
            cs = slice(pr * 128, (pr + 1) * 128)
            ACT(sr[ub][:, pr, :], PS[4 + 2 * ub][:, cs], AF.Identity, [f"ps{4 + 2 * ub}", kin], [f"sr{ub}"], bias=cr_in[:, pair:pair + 1])
            ACT(si[ub][:, pr, :], PS[5 + 2 * ub][:, cs], AF.Identity, [f"ps{5 + 2 * ub}", kin], [f"si{ub}"], bias=ci_in[:, pair:pair + 1])
        eh = te[:, 4 * ct:4 * ct + 4, :]
        fh = tf[:, 4 * ct:4 * ct + 4, :]
        q = pb_[ub]
        TT("dve", q[0][:], sr[ub][:], eh, ALU.mult, [f"sr{ub}", "te"], [f"pb{ub}0"])
        TT("pool", q[1][:], si[ub][:], fh, ALU.mult, [f"si{ub}", "tf"], [f"pb{ub}1"])
        TT("dve", q[2][:], si[ub][:], eh, ALU.mult, [f"si{ub}", "te"], [f"pb{ub}2"])
        TT("pool", q[3][:], sr[ub][:], fh, ALU.mult, [f"sr{ub}", "tf"], [f"pb{ub}3"])
        E = l128[:, 4, 4 * ct:4 * ct + 4]
        F_ = l128[:, 5, 4 * ct:4 * ct + 4]
        ps_ = slice(4 * ct, 4 * ct + 4)
        TT("pool", xl[:, 0, :], E, sr[ub][:, :, 127], ALU.mult, ["l128", f"sr{ub}"], ["xl0"])
        TT("pool", xl[:, 1, :], F_, si[ub][:, :, 127], ALU.mult, ["l128", f"si{ub}"], ["xl1"])
        TT("pool", cr_out[:, ps_], xl[:, 0, :], xl[:, 1, :], ALU.subtract, ["xl0", "xl1"], [kout])
        TT("pool", xl[:, 2, :], E, si[ub][:, :, 127], ALU.mult, ["l128", f"si{ub}"], ["xl2"])
        TT("pool", xl[:, 3, :], F_, sr[ub][:, :, 127], ALU.mult, ["l128", f"sr{ub}"], ["xl3"])
        TT("pool", ci_out[:, ps_], xl[:, 2, :], xl[:, 3, :], ALU.add, ["xl2", "xl3"], [kout])

    def ssm_C(st, cc, ct, u):
        tok = slice(cc * 128, (cc + 1) * 128)
        ub = u % 2
        q = pb_[ub]
        ysl = slice(ct * 128, (ct + 1) * 128)
        for pr in range(4):
            pair = 4 * ct + pr
            MM(PS[0][:, ysl], Ccat[:, 0, pair, :], q[0][:, pr, :], pr == 0, False, ["Ccat", f"pb{ub}0"], ["ps0"])
            MM(PS[0][:, ysl], CcatN[:, pair, :], q[1][:, pr, :], False, False, ["CcatN", f"pb{ub}1"], ["ps0"])
            MM(PS[0][:, ysl], Ccat[:, 1, pair, :], q[2][:, pr, :], False, False, ["Ccat", f"pb{ub}2"], ["ps0"])
            MM(PS[0][:, ysl], Ccat[:, 1, pair, :], q[3][:, pr, :], False, False, ["Ccat", f"pb{ub}3"], ["ps0"])
        MM(PS[0][:, ysl], Ddiag[:, ct, :], uT[:, ct, tok], False, True, ["Ddiag", "uT"], ["ps0"])
        if ct == 3:
            ACT(gT[:, :, tok], PS[0][:, :].rearrange("p (c t) -> p c t", c=4), AF.Gelu, ["ps0"], ["gT"])

    for st in range(NST):
        make_hT(st, 1, hT, xt, xn, xn)
        def us_mm(c):
            zb = 1 + (c % 2)
            for kt in range(8):
                MM(PS[zb][:, :], w2[:, kt, c * 128:(c + 1) * 128], hT[:, kt, :], kt == 0, kt == 7, ["w2", "hT"], [f"ps{zb}"])
        us_mm(0)
        for c in range(8):
            if c + 1 < 8:
                us_mm(c + 1)
            zb = 1 + (c % 2)
            if c < 4:
                CP("act", uT[:, c, :], PS[zb][:, :], [f"ps{zb}"], ["uT"])
            else:
                ACT(gsT[:, c - 4, :], PS[zb][:, :], AF.Silu, [f"ps{zb}"], ["gsT"])
        units = [(cc, ct) for cc in range(4) for ct in range(4)]
        NU = len(units)
        for k in range(-2, NU):
            if 0 <= k + 2 < NU:
                ssm_A(st, units[k + 2][0], units[k + 2][1], k + 2)
            if 0 <= k + 1 < NU:
                ssm_A2(st, units[k + 1][0], units[k + 1][1], k + 1)
                ssm_B(st, units[k + 1][0], units[k + 1][1], k + 1)
            if 0 <= k < NU:
                ssm_C(st, units[k][0], units[k][1], k)
        for c in range(4):
            zb = 1 + (c % 2)
            for kt in range(4):
                MM(PS[zb][:, :], gluw[:, kt, c * 128:(c + 1) * 128], gT[:, kt, :], kt == 0, kt == 3, ["gluw", "gT"], [f"ps{zb}"])
            ACT(sg[:], PS[zb][:, :], AF.Sigmoid, [f"ps{zb}", "glub"], ["sg"], bias=glub[:, c:c + 1])
            TT("dve", sg[:], sg[:], gT[:, c, :], ALU.mult, ["sg", "gT"], ["sg"])
            TT("dve", OST[:, c, st * 512:(st + 1) * 512], sg[:], gsT[:, c, :], ALU.mult, ["sg", "gsT"], ["OST", "Bst", "Cst"])
    if dbg == "p2":
        DMA("sp", dbg_d[:, :], OSTm[:, :], ["OST"], ["outd"], "OSTm")
        NOP(["outd"])
        return finish(nc, S)

    S.barrier()
    M.release(m_glob)
    w3 = M.alloc("w3", [128, 8, 2560], BF16)
    pa = M.alloc("pa", [128, 4, 1024], BF16)
    psw = M.alloc("psw", [128, 4, 1024], BF16)
    wo = M.alloc("wo", [128, 8, 1024], BF16)
    mb = M.alloc("mb", [128, 16], F32)
    m_work3 = M.mark()
    stg = [M.alloc("stg", [128, 2048], F32) for _ in range(2)]
    load_weight(w3[:, :, 0:512], w_in_d, 0, 8, 1536, 512, stg, True, "w3")
    load_weight(w3[:, :, 512:2560], w_in_d, 0, 8, 3072, 2048, stg, True, "w3")
    load_weight(pa, pa_d, 0, 4, 0, 1024, stg, False, "pa")
    load_weight(psw, ps_d, 0, 4, 0, 1024, stg, False, "psw")
    load_weight(wo, wo_d, 0, 8, 0, 1024, stg, False, "wo")
    DMA("sp", mb[:], dap(mb_d, 0, [[1, 128], [128, 16]]), (), ["mb"], "mb", slow=True)
    S.barrier()
    M.release(m_work3)
    hT = M.alloc("hT", [128, 8, 512], BF16)
    xt = [M.alloc("xt", [128, 1024], F32)]
    xn = M.alloc("xn", [128, 1024], BF16)
    junk = M.alloc("junk", [128, 1024], BF16)
    mT = M.alloc("mT", [128, 8, 512], BF16)
    g1 = M.alloc("g1", [128, 512], F32)
    g2t = M.alloc("g2t", [128, 512], F32)
    m1 = M.alloc("m1", [128, 512], F32)
    xres = M.alloc("xres", [128, 1024], F32)
    ot = [M.alloc("ot", [128, 512], F32) for _ in range(2)]
    outkeys = []
    for st in range(NST):
        stc = slice(st * 512, (st + 1) * 512)
        make_hT(st, 2, hT, xt, xn, junk)
        for h in range(4):
            for kt in range(8):
                MM(PS[1][:, :], w3[:, kt, h * 128:(h + 1) * 128], hT[:, kt, :], kt == 0, kt == 7, ["w3", "hT"], ["ps1"])
            ACT(g1[:], PS[1][:, :], AF.Silu, ["ps1"], ["g1"])
            TT("dve", OAT[:, h, stc], OAT[:, h, stc], g1[:], ALU.mult, ["OAT", "g1"], ["OAT"])
        for c in range(8):
            for h in range(4):
                MM(PS[1][:, :], pa[:, h, c * 128:(c + 1) * 128], OAT[:, h, stc], h == 0, h == 3, ["pa", "OAT"], ["ps1"])
            for kt in range(8):
                MM(PS[2][:, :], w3[:, kt, 512 + c * 128:512 + (c + 1) * 128], hT[:, kt, :], kt == 0, kt == 7, ["w3", "hT"], ["ps2"])
            ACT(g1[:], PS[2][:, :], AF.Sigmoid, ["ps2", "mb"], ["g1"], bias=mb[:, c:c + 1])
            TT("dve", m1[:], PS[1][:, :], g1[:], ALU.mult, ["ps1", "g1"], ["m1"])
            for k in range(4):
                MM(PS[3][:, :], psw[:, k, c * 128:(c + 1) * 128], OST[:, k, stc], k == 0, k == 3, ["psw", "OST"], ["ps3"])
            for kt in range(8):
                MM(PS[4][:, :], w3[:, kt, 1536 + c * 128:1536 + (c + 1) * 128], hT[:, kt, :], kt == 0, kt == 7, ["w3", "hT"], ["ps4"])
            ACT(g2t[:], PS[4][:, :], AF.Sigmoid, ["ps4", "mb"], ["g2t"], bias=mb[:, 8 + c:9 + c])
            TT("dve", g2t[:], PS[3][:, :], g2t[:], ALU.mult, ["ps3", "g2t"], ["g2t"])
            TT("pool", mT[:, c, :], m1[:], g2t[:], ALU.add, ["m1", "g2t"], ["mT"])
        for tt in range(4):
            row = st * 512 + tt * 128
            DMA("sp", xres[:], x_d[row:row + 128, :], (), ["xres"], "xres")
            for half in range(2):
                for kt in range(8):
                    MM(PS[5 + half][:, :], mT[:, kt, tt * 128:(tt + 1) * 128], wo[:, kt, half * 512:(half + 1) * 512], kt == 0, kt == 7,
                       ["mT", "wo"], [f"ps{5 + half}"])
                TT("dve", ot[half][:], PS[5 + half][:, :], xres[:, half * 512:(half + 1) * 512], ALU.add, [f"ps{5 + half}", "xres"], [f"ot{half}"])
                ok = f"outd{st}_{tt}_{half}"
                DMA("sp", out_d[row:row + 128, half * 512:(half + 1) * 512], ot[half][:], [f"ot{half}"], [ok], f"ot{half}")
                outkeys.append(ok)
    NOP(outkeys)
    return finish(nc, S)


def finish(nc, S):
    import contextlib
    with contextlib.ExitStack() as stk:
        S._sem_ctx = {e: stk.enter_context(nc.semaphore(f"s_{e}")) for e in S.ENGS}
        S._dsem = {k: stk.enter_context(nc.semaphore(f"d_{k}")) for k in S.dma_count}
        block = stk.enter_context(nc.Block())
        S.emit(block)
    return nc


_NC_CACHE = {}


def _core_inputs(inputs, b, consts):
    m = {"x": np.ascontiguousarray(inputs["x"][b], dtype=np.float32)}
    for k, v in inputs.items():
        if k == "x":
            continue
        v = np.asarray(v, dtype=np.float32)
        if k == "rel_bias_table":
            m[k] = np.ascontiguousarray(v)
            continue
        v0 = v[0]
        if k in ("w_in", "ssm_glu_w", "proj_attn", "proj_ssm", "w_out"):
            m[k] = np.ascontiguousarray(v0)
        else:
            m[k] = np.ascontiguousarray(v0).reshape(-1)
    m.update(consts)
    return m


def kernel(**inputs):
    if "nc" not in _NC_CACHE:
        _NC_CACHE["nc"] = build_nc()
    nc = _NC_CACHE["nc"]
    consts = host_consts()
    in_maps = [_core_inputs(inputs, b, consts) for b in range(8)]
    res = run_bass_kernel_spmd(nc, in_maps, core_ids=list(range(8)))
    out = np.stack([np.asarray(res.results[b]["out"], dtype=np.float32) for b in range(8)], axis=0)
    return out
```

```python
import math
import numpy as np
import ml_dtypes
import concourse.bass as bass
import concourse.mybir as mybir
from concourse.bass_utils import run_bass_kernel_spmd

F32 = mybir.dt.float32
BF16 = mybir.dt.bfloat16
AF = mybir.ActivationFunctionType
ALU = mybir.AluOpType
AX = mybir.AxisListType


class _Op:
    __slots__ = ("eng", "fn", "idx", "dma", "dkey", "deps", "signal", "sigval", "sem", "fuse")

    def __init__(self, eng, fn, idx, dma, dkey):
        self.eng = eng
        self.fn = fn
        self.idx = idx
        self.dma = dma
        self.dkey = dkey
        self.deps = {}
        self.signal = False
        self.sigval = 0
        self.sem = None
        self.fuse = False


class Sched:
    ENGS = ("pe", "act", "dve", "pool", "sp")
    FUSE_WAITS = True

    def __init__(self, nc):
        self.nc = nc
        self.ops = {e: [] for e in self.ENGS}
        self.last_w = {}
        self.readers = {}
        self.dma_count = {}

    def add(self, eng, fn, reads=(), writes=(), dma=False, dkey=None, fuse=None):
        op = _Op(eng, fn, len(self.ops[eng]), dma, dkey)
        if fuse is None:
            fuse = (eng in ("act", "dve", "pool")) and not dma
        op.fuse = fuse and self.FUSE_WAITS
        reads = tuple(reads) + ("__B__",)
        for k in reads:
            w = self.last_w.get(k)
            if w is not None:
                op.deps[w] = True
        for k in writes:
            w = self.last_w.get(k)
            if w is not None and w not in op.deps:
                op.deps[w] = False
            rd = self.readers.get(k)
            if rd:
                for r in rd.get("eng", {}).values():
                    if r is not op and r not in op.deps:
                        op.deps[r] = False
                for r in rd.get("dma", []):
                    if r is not op and r not in op.deps:
                        op.deps[r] = False
        for k in reads:
            rd = self.readers.setdefault(k, {"eng": {}, "dma": []})
            if dma:
                rd["dma"].append(op)
            else:
                rd["eng"][eng] = op
        for k in writes:
            self.last_w[k] = op
            self.readers[k] = {"eng": {}, "dma": []}
        if dma:
            assert dkey is not None
            n = self.dma_count.get(dkey, 0) + 1
            self.dma_count[dkey] = n
            op.sigval = 16 * n
        self.ops[eng].append(op)
        return op

    def barrier(self):
        t = self._bar_tile
        self.add("dve", lambda e: e.memset(t[:, 0:1], 0.0), writes=("__B__",))

    def _needs_wait(self, op, dep, is_raw):
        if dep.dma:
            return True
        if dep.eng != op.eng:
            return True
        if op.dma:
            return True
        if op.eng == "pe":
            return False
        return is_raw and (op.idx - dep.idx) <= 3

    def emit(self, block, extra_final=None):
        nc = self.nc
        for e in self.ENGS:
            for op in self.ops[e]:
                for dep, is_raw in op.deps.items():
                    if self._needs_wait(op, dep, is_raw) and not dep.dma:
                        dep.signal = True
        sems = {e: self._sem_ctx[e] for e in self.ENGS}
        dsems = self._dsem
        for e in self.ENGS:
            n = 0
            for op in self.ops[e]:
                if op.dma:
                    op.sem = dsems[op.dkey]
                else:
                    op.sem = sems[e]
                    if op.signal:
                        n += 1
                        op.sigval = n

        def run_engine(ename, eng):
            known = {}
            for op in self.ops[ename]:
                need = {}
                for dep, is_raw in op.deps.items():
                    if not self._needs_wait(op, dep, is_raw):
                        continue
                    s = dep.sem
                    v = dep.sigval
                    if need.get(s, 0) < v:
                        need[s] = v
                pend = [(s, v) for s, v in need.items() if known.get(s, 0) < v]
                fused = None
                if op.fuse and pend:
                    fused = pend.pop()
                for s, v in pend:
                    eng.wait_ge(s, v)
                    known[s] = v
                ins = op.fn(eng)
                if fused is not None:
                    ins._wait_ge(fused[0], fused[1])
                    known[fused[0]] = fused[1]
                if op.dma:
                    ins.then_inc(op.sem, 16)
                elif op.signal:
                    ins.then_inc(op.sem, 1)
            if extra_final is not None:
                extra_final(ename, eng, known)

        @block.tensor
        def _(eng):
            run_engine("pe", eng)

        @block.scalar
        def _(eng):
            run_engine("act", eng)

        @block.vector
        def _(eng):
            run_engine("dve", eng)

        @block.gpsimd
        def _(eng):
            run_engine("pool", eng)

        @block.sync
        def _(eng):
            run_engine("sp", eng)


L = 4096
D = 1024
NST = 8
EPS = 1e-6
LAM_INIT = 0.8 - 0.6 * math.exp(-0.3 * 0)
TWO_PI = 2.0 * math.pi
NEG = -30000.0


def _bucket_np(rel):
    nb = 16
    me = 8
    side = np.where(rel > 0, nb, 0)
    n = np.abs(rel)
    nf = np.maximum(n, 1).astype(np.float32)
    large = me + (np.log(nf / np.float32(me)).astype(np.float32) / np.float32(math.log(128 / 8))
                  * np.float32(nb - me)).astype(np.int32)
    large = np.minimum(large, nb - 1)
    return side + np.where(n < me, n, large)


def host_consts():
    c = {}
    c["c_ident"] = np.eye(128, dtype=np.float32).astype(ml_dtypes.bfloat16)
    bo = np.zeros((128, 128), np.float32)
    bo[:64, :64] = 1.0 / 64
    bo[64:, 64:] = 1.0 / 64
    c["c_bones"] = bo.astype(ml_dtypes.bfloat16)
    c["c_J"] = np.ascontiguousarray(np.eye(128, dtype=np.float32)[::-1])
    rel = np.arange(-255, 128)
    b = _bucket_np(rel)
    oh = np.zeros((32, 384), np.float32)
    oh[b, np.arange(383)] = 1.0
    c["c_oh"] = oh
    k = np.arange(128)[:, None]
    q = np.arange(128)[None, :]
    c["c_maskD"] = np.where((k // 64) <= (q // 64), 0.0, NEG).astype(np.float32)
    c["c_iotap"] = np.stack([np.arange(128), -np.arange(128)], 1).astype(np.float32)
    c["c_iotat"] = np.tile(np.arange(128, dtype=np.float32)[None, :], (128, 1))
    c["c_tri"] = (np.arange(128)[:, None] <= np.arange(128)[None, :]).astype(np.float32).astype(ml_dtypes.bfloat16)
    return c


class Mem:
    def __init__(self, nc, start=16640, end=229376):
        self.nc = nc
        self.p = start
        self.end = end
        self.n = 0

    def alloc(self, name, shape, dtype):
        esz = 4 if dtype in (F32, mybir.dt.int32) else 2
        size = int(np.prod(shape[1:])) * esz
        size = (size + 63) // 64 * 64
        assert self.p + size <= self.end, f"SBUF overflow allocating {name}: {self.p}+{size} > {self.end}"
        self.n += 1
        t = self.nc.alloc_sbuf_tensor_at(f"{name}_{self.n}", list(shape), dtype, offset=self.p)
        self.p += size
        return t

    def mark(self):
        return self.p

    def release(self, m):
        self.p = m


def build_nc(dbg=None):
    nc = bass.Bass("TRN2", target_bir_lowering=False)
    S = Sched(nc)
    M = Mem(nc)

    def din(name, shape, dt=F32):
        return nc.dram_tensor(name, list(shape), dt, kind="ExternalInput").ap()

    x_d = din("x", [L, D])
    w_in_d = din("w_in", [D, 5120])
    ng_d = din("norm_gain", [D])
    mb_d = din("merge_gate_b", [2048])
    qg_d = din("q_norm_gain", [64])
    kg_d = din("k_norm_gain", [64])
    lq1_d = din("lambda_q1", [64]); lk1_d = din("lambda_k1", [64])
    lq2_d = din("lambda_q2", [64]); lk2_d = din("lambda_k2", [64])
    sg_d = din("diff_subln_gain", [128])
    rb_d = din("rel_bias_table", [32, 4])
    are_d = din("ssm_A_re", [2048]); aim_d = din("ssm_A_im", [2048])
    ldt_d = din("ssm_log_dt", [32])
    bre_d = din("ssm_B_re", [32 * 64 * 16]); bim_d = din("ssm_B_im", [32 * 64 * 16])
    cre_d = din("ssm_C_re", [32 * 16 * 64]); cim_d = din("ssm_C_im", [32 * 16 * 64])
    dd_d = din("ssm_D", [512])
    gw_d = din("ssm_glu_w", [512, 512])
    gb_d = din("ssm_glu_b", [512])
    pa_d = din("proj_attn", [512, 1024])
    ps_d = din("proj_ssm", [512, 1024])
    wo_d = din("w_out", [1024, 1024])
    c_ident_d = din("c_ident", [128, 128], BF16)
    c_bones_d = din("c_bones", [128, 128], BF16)
    c_J_d = din("c_J", [128, 128])
    c_oh_d = din("c_oh", [32, 384])
    c_maskD_d = din("c_maskD", [128, 128])
    c_iotap_d = din("c_iotap", [128, 2])
    c_iotat_d = din("c_iotat", [128, 128])
    c_tri_d = din("c_tri", [128, 128], BF16)
    out_d = nc.dram_tensor("out", [L, D], F32, kind="ExternalOutput").ap()
    fsc_d = nc.dram_tensor("fscratch", [4, 384], F32).ap()
    dbg_d = None
    if dbg:
        dbg_d = nc.dram_tensor("dbg", [128, 4 * L], BF16, kind="ExternalOutput").ap()

    def dap(t, off, pat):
        return bass.AP(t.tensor, off, [list(p) for p in pat])

    PS = [nc.alloc_psum_tensor(f"psb{i}", [128, 512], F32) for i in range(8)]

    def MM(out, lhsT, rhs, start, stop, r, w, fuse=False):
        S.add("pe", lambda e: e.matmul(out, lhsT=lhsT, rhs=rhs, start=start, stop=stop), r, w, fuse=fuse)

    def TR(out, in_, r, w):
        S.add("pe", lambda e: e.transpose(out=out, in_=in_, identity=ident[:]), tuple(r) + ("ident",), w)

    def ACT(out, in_, func, r, w, bias=0.0, scale=1.0, accum=None):
        if accum is None:
            S.add("act", lambda e: e.activation(out=out, in_=in_, func=func, bias=bias, scale=scale), r, w)
        else:
            S.add("act", lambda e: e.activation(out=out, in_=in_, func=func, bias=bias, scale=scale, accum_out=accum), r, w)

    def TS(eng, out, in0, s1, s2, op0, op1, r, w):
        if s2 is None:
            S.add(eng, lambda e: e.tensor_scalar(out=out, in0=in0, scalar1=s1, scalar2=None, op0=op0), r, w)
        else:
            S.add(eng, lambda e: e.tensor_scalar(out=out, in0=in0, scalar1=s1, scalar2=s2, op0=op0, op1=op1), r, w)

    def TT(eng, out, in0, in1, op, r, w):
        S.add(eng, lambda e: e.tensor_tensor(out=out, in0=in0, in1=in1, op=op), r, w)

    def STT(eng, out, in0, scalar, in1, op0, op1, r, w):
        S.add(eng, lambda e: e.scalar_tensor_tensor(out=out, in0=in0, scalar=scalar, in1=in1, op0=op0, op1=op1), r, w)

    def CP(eng, out, in_, r, w):
        if eng == "act":
            S.add("act", lambda e: e.copy(out=out, in_=in_), r, w)
        else:
            S.add(eng, lambda e: e.tensor_copy(out=out, in_=in_), r, w)

    def RSUM(out, in_, r, w):
        S.add("dve", lambda e: e.reduce_sum(out=out, in_=in_, axis=AX.X), r, w)

    def RECIP(out, in_, r, w):
        S.add("dve", lambda e: e.reciprocal(out=out, in_=in_), r, w)

    def NOP(r):
        S.add("sp", lambda e: e.nop(), r, [])

    def MS(eng, ap, val, w):
        S.add(eng, lambda e: e.memset(ap, val), (), w)

    def DMA(eng, out, in_, r, w, dkey, slow=False):
        if slow:
            S.add(eng, lambda e: e.dma_start(out=out, in_=in_, allow_slow_non_contiguous=True), r, w, dma=True, dkey=dkey)
        else:
            S.add(eng, lambda e: e.dma_start(out=out, in_=in_), r, w, dma=True, dkey=dkey)

    bar = M.alloc("bar", [128, 16], F32)
    S._bar_tile = bar
    ident = M.alloc("ident", [128, 128], BF16)
    bones = M.alloc("bones", [128, 128], BF16)
    tri = M.alloc("tri", [128, 128], BF16)
    iotap = M.alloc("iotap", [128, 2], F32)
    iotat = M.alloc("iotat", [128, 128], F32)
    gcol = M.alloc("gcol", [128, 8], F32)
    small = M.alloc("small", [128, 64], F32)
    rs_all = M.alloc("rs_all", [128, 3 * 32 * 2], F32)
    OAT = M.alloc("OAT", [128, 4, L], BF16)
    OSTm = M.alloc("OSTm", [128, 4 * L], BF16)
    OST = OSTm[:, :].rearrange("p (c t) -> p c t", c=4)
    stag32 = OSTm[:, :].bitcast(F32)
    Bst = stag32[:, 0:4096].rearrange("p (r c g q) -> p r c g q", r=2, c=4, g=8)
    Cst = stag32[:, 4096:8192].rearrange("p (r a c) -> p r a c", r=2, a=16)

    DMA("sp", ident[:], c_ident_d[:, :], (), ["ident"], "ident")
    DMA("sp", bones[:], c_bones_d[:, :], (), ["bones"], "bones")
    DMA("sp", tri[:], c_tri_d[:, :], (), ["tri"], "tri")
    DMA("sp", iotap[:], c_iotap_d[:, :], (), ["iotap"], "iotap")
    DMA("sp", iotat[:], c_iotat_d[:, :], (), ["iotat"], "iotat")
    DMA("sp", gcol[:], dap(ng_d, 0, [[1, 128], [128, 8]]), (), ["gcol"], "gcol", slow=True)

    MS("dve", stag32[:, 0:4096], 0.0, ["Bst"])
    MS("pool", stag32[:, 4096:8192], 0.0, ["Cst"])
    bc_dmas = []
    for g in range(32):
        for ri in range(2):
            bc_dmas.append((g, ri))

    def emit_bc(n):
        for _ in range(n):
            if not bc_dmas:
                return
            g, ri = bc_dmas.pop(0)
            ct, g8 = g // 8, g % 8
            pair, g2 = g // 2, g % 2
            bd = (bre_d, bim_d)[ri]
            cd = (cre_d, cim_d)[ri]
            DMA("sp", Bst[16 * g8:16 * g8 + 16, ri, ct, g8, :], dap(bd, g * 1024, [[1, 16], [16, 64]]),
                (), ["Bst"], "Bst", slow=True)
            DMA("sp", Cst[64 * g2:64 * g2 + 64, ri, pair, (g % 8) * 16:(g % 8) * 16 + 16],
                dap(cd, g * 1024, [[1, 64], [64, 16]]), (), ["Cst"], "Cst", slow=True)

    def load_weight(dst, src_d, row0, nrows_tiles, col0, ncols, stg, fold_gain, keyw):
        i = 0
        for kt in range(nrows_tiles):
            for c0 in range(0, ncols, 2048):
                cw = min(2048, ncols - c0)
                sb = stg[i % 2]
                sk = f"stg{i % 2}"
                DMA("sp", sb[:, 0:cw], src_d[row0 + kt * 128: row0 + (kt + 1) * 128, col0 + c0: col0 + c0 + cw],
                    (), [sk], sk)
                if fold_gain:
                    if i % 2 == 0:
                        ACT(dst[:, kt, c0:c0 + cw], sb[:, 0:cw], AF.Copy, [sk, "gcol"], [keyw], scale=gcol[:, kt:kt + 1])
                    else:
                        TS("dve", dst[:, kt, c0:c0 + cw], sb[:, 0:cw], gcol[:, kt:kt + 1], None, ALU.mult, None, [sk, "gcol"], [keyw])
                else:
                    if i % 2 == 0:
                        CP("act", dst[:, kt, c0:c0 + cw], sb[:, 0:cw], [sk], [keyw])
                    else:
                        CP("dve", dst[:, kt, c0:c0 + cw], sb[:, 0:cw], [sk], [keyw])
                i += 1

    def make_hT(st, phase, hT, xt, xn, junk):
        psT = PS[0][:, 0:512].bitcast(BF16)
        for tt in range(4):
            i = st * 4 + tt
            b = i % len(xt)
            row = st * 512 + tt * 128
            col = (phase * 32 + i) * 2
            ss = rs_all[:, col:col + 1]
            rs = rs_all[:, col + 1:col + 2]
            DMA("sp", xt[b][:], x_d[row:row + 128, :], (), [f"xt{b}"], f"xt{b}")
            jk = "xn" if junk is xn else "junk"
            ACT(junk[:], xt[b][:], AF.Square, [f"xt{b}"], [jk, f"ss{col}"], accum=ss)
            ACT(rs, ss, AF.Ln, [f"ss{col}"], [f"rs{col}"], bias=EPS, scale=1.0 / D)
            ACT(rs, rs, AF.Exp, [f"rs{col}"], [f"rs{col}"], scale=-0.5)
            TS("dve", xn[:], xt[b][:], rs, None, ALU.mult, None, [f"xt{b}", f"rs{col}"], ["xn"])
            for kt in range(8):
                TR(psT[:, kt * 128:(kt + 1) * 128], xn[:, kt * 128:(kt + 1) * 128], ["xn"], ["ps0"])
            CP("dve", hT[:, :, tt * 128:(tt + 1) * 128], psT[:, :].rearrange("p (k t) -> p k t", k=8), ["ps0"], ["hT"])

    m_glob = M.mark()
    wq = M.alloc("wq", [128, 8, 1536], BF16)
    KT = M.alloc("KT", [128, 4, L], BF16)
    V = M.alloc("V", [128, 32, 4, 128], BF16)
    ones128 = M.alloc("ones128", [128, 128], BF16)
    sgcol = M.alloc("sgcol", [128, 1], F32)
    c15 = M.alloc("c15", [128, 4], F32)
    biasP = M.alloc("biasP", [128, 4, 128], F32)
    biasD = M.alloc("biasD", [128, 4, 128], F32)
    gq = M.alloc("gq", [128, 2], F32)
    lamt = M.alloc("lamt", [128, 8], F32)
    m_work = M.mark()
    stg = [M.alloc("stg", [128, 2048], F32) for _ in range(2)]
    lamv = M.alloc("lamv", [128, 4, 64], F32)
    tbl = M.alloc("tbl", [32, 4], F32)
    oh = M.alloc("oh", [32, 384], F32)
    fsb = M.alloc("fsb", [4, 384], F32)
    Jt = M.alloc("Jt", [128, 128], F32)
    hank = M.alloc("hank", [128, 4, 256], F32)
    maskD = M.alloc("maskD", [128, 128], F32)

    load_weight(wq, w_in_d, 0, 8, 0, 1536, stg, True, "wq")
    MS("dve", ones128[:], 1.0, ["ones128"])
    DMA("sp", sgcol[:], dap(sg_d, 0, [[1, 128], [1, 1]]), (), ["sgcol"], "sgcol")
    TS("dve", sgcol[:], sgcol[:], 1.0 - LAM_INIT, None, ALU.mult, None, ["sgcol"], ["sgcol"])
    for hlf in range(2):
        DMA("sp", gq[64 * hlf:64 * hlf + 64, 0:1], dap(qg_d, 0, [[1, 64], [1, 1]]), (), ["gq"], "gq")
        DMA("sp", gq[64 * hlf:64 * hlf + 64, 1:2], dap(kg_d, 0, [[1, 64], [1, 1]]), (), ["gq"], "gq")
    TS("dve", gq[:, 0:1], gq[:, 0:1], 0.125, None, ALU.mult, None, ["gq"], ["gq"])
    for i, dd in enumerate((lq1_d, lk1_d, lq2_d, lk2_d)):
        DMA("sp", lamv[:, i, :], dap(dd, 0, [[0, 128], [1, 64]]), (), ["lamv"], "lamv")
    TT("dve", lamv[:, 0, :], lamv[:, 0, :], lamv[:, 1, :], ALU.mult, ["lamv"], ["lamv"])
    TT("dve", lamv[:, 2, :], lamv[:, 2, :], lamv[:, 3, :], ALU.mult, ["lamv"], ["lamv"])
    RSUM(lamt[:, 0:1], lamv[:, 0, :], ["lamv"], ["lamt"])
    RSUM(lamt[:, 1:2], lamv[:, 2, :], ["lamv"], ["lamt"])
    ACT(lamt[:, 2:4], lamt[:, 0:2], AF.Exp, ["lamt"], ["lamt"])
    TT("dve", lamt[:, 4:5], lamt[:, 3:4], lamt[:, 2:3], ALU.subtract, ["lamt"], ["lamt"])
    TS("dve", lamt[:, 5:6], lamt[:, 4:5], -LAM_INIT, None, ALU.add, None, ["lamt"], ["lamt"])
    neglam = lamt[:, 5:6]
    DMA("sp", tbl[:], rb_d[:, :], (), ["tbl"], "tbl")
    DMA("sp", oh[:], c_oh_d[:, :], (), ["oh"], "oh")
    DMA("sp", Jt[:], c_J_d[:, :], (), ["Jt"], "Jt")
    DMA("sp", maskD[:], c_maskD_d[:, :], (), ["maskD"], "maskD")
    DMA("sp", c15[:], dap(rb_d, 15 * 4, [[0, 128], [1, 4]]), (), ["c15"], "c15")
    MM(PS[1][0:4, 0:384], tbl[:], oh[:], True, True, ["tbl", "oh"], ["ps1"])
    CP("dve", fsb[:], PS[1][0:4, 0:384], ["ps1"], ["fsb"])
    DMA("sp", fsc_d[:, :], fsb[:], ["fsb"], ["fsc"], "fsb")
    for h in range(4):
        DMA("sp", hank[:, h, :], dap(fsc_d, h * 384, [[1, 128], [1, 256]]), ["fsc"], ["hank"], "hank")
    for h in range(4):
        MM(PS[2][:, h * 128:(h + 1) * 128], hank[:, h, 0:128], Jt[:], True, True, ["hank", "Jt"], ["ps2"])
        MM(PS[3][:, h * 128:(h + 1) * 128], hank[:, h, 128:256], Jt[:], True, True, ["hank", "Jt"], ["ps3"])
    CP("dve", biasP[:], PS[2][:, :].rearrange("p (h q) -> p h q", h=4), ["ps2"], ["biasP"])
    TT("dve", biasD[:], PS[3][:, :].rearrange("p (h q) -> p h q", h=4),
       maskD[:].unsqueeze(1).broadcast_to([128, 4, 128]), ALU.add, ["ps3", "maskD"], ["biasD"])
    S.barrier()
    M.release(m_work)
    hT = M.alloc("hT", [128, 8, 512], BF16)
    xt = [M.alloc("xt", [128, 1024], F32)]
    xn = M.alloc("xn", [128, 1024], BF16)
    qT = M.alloc("qT", [128, 4, 512], BF16)
    sq = M.alloc("sq", [128, 512], BF16)
    rstd = M.alloc("rstd", [128, 512], F32)
    PT = [M.alloc("PT", [128, 512], BF16) for _ in range(4)]
    tmpb = [M.alloc("tmpb", [128, 128], F32) for _ in range(2)]
    rcp = M.alloc("rcp", [128, 512], F32)
    oraw = [M.alloc("oraw", [128, 512], F32) for _ in range(2)]
    racc = [[M.alloc("racc", [128, 512], F32) for _ in range(2)] for _ in range(2)]
    tob = M.alloc("tob", [128, 512], F32)
    rsb = [M.alloc("rsb", [128, 512], BF16) for _ in range(2)]
    sqb = M.alloc("sqb", [128, 512], BF16)
    rs2 = M.alloc("rs2", [128, 512], F32)
    sq2 = [sq, M.alloc("sq2", [128, 512], BF16)]
    rstd2 = [rstd, rs2]
    rkeys = ["rstd0", "rs2"]

    PSB = [3, 4, 7, 6]
    NB = 4
    SKEW = 2
    PAIRED = True
    FILL = 0
    deferred = []

    def qk_mm(c, st):
        zb = 1 + (c % 2)
        for kt in range(8):
            MM(PS[zb][:, :], wq[:, kt, c * 128:(c + 1) * 128], hT[:, kt, :], kt == 0, kt == 7, ["wq", "hT"], [f"ps{zb}"], fuse=True)

    def qk_chain(c, st):
        zb = 1 + (c % 2)
        sqc, rsc = sq2[c % 2], rstd2[c % 2]
        ACT(sqc[:], PS[zb][:, :], AF.Square, [f"ps{zb}"], [f"sq{c % 2}"])
        MM(PS[0][:, :], bones[:], sqc[:], True, True, ["bones", f"sq{c % 2}"], ["ps0"])
        ACT(rsc[:], PS[0][:, :], AF.Ln, ["ps0"], [rkeys[c % 2]], bias=EPS)
        ACT(rsc[:], rsc[:], AF.Exp, [rkeys[c % 2]], [rkeys[c % 2]], scale=-0.5)
        if c < 4:
            STT("dve", qT[:, c, :], PS[zb][:, :], gq[:, 0:1], rsc[:], ALU.mult, ALU.mult, [f"ps{zb}", "gq", rkeys[c % 2]], ["qT"])
        else:
            STT("dve", KT[:, c - 4, st * 512:(st + 1) * 512], PS[zb][:, :], gq[:, 1:2], rsc[:], ALU.mult, ALU.mult,
                [f"ps{zb}", "gq", rkeys[c % 2]], [f"KT{st}"])

    for st in range(NST):
        make_hT(st, 0, hT, xt, xn, xn)
        emit_bc(8)
        qk_mm(0, st)
        for c in range(8):
            if c + 1 < 8:
                qk_mm(c + 1, st)
            qk_chain(c, st)
        for tt in range(4):
            blk = st * 4 + tt
            zb = 1 + (tt % 2)
            for kt in range(8):
                MM(PS[zb][:, :], hT[:, kt, tt * 128:(tt + 1) * 128], wq[:, kt, 1024:1536], kt == 0, kt == 7, ["wq", "hT"], [f"ps{zb}"])
            CP("dve", V[:, blk, :, :], PS[zb][:, :].rearrange("p (h d) -> p h d", h=4), [f"ps{zb}"], [f"V{st}"])
        kv_keys = [f"KT{i}" for i in range(st + 1)] + [f"V{i}" for i in range(st + 1)]
        iters = [(h, s, j) for h in range(4) for j in range(4 * st + 4) for s in range(2)]
        NI = len(iters)

        def emit_S(i, st=st, kv_keys=kv_keys, iters=iters):
            h, s, j = iters[i]
            lo = max(0, j - 4 * st)
            cols = slice(lo * 128, 512)
            pb = i % NB
            psS = PS[PSB[pb]]
            MM(psS[:, cols], KT[64 * s:64 * s + 64, h, j * 128:(j + 1) * 128], qT[64 * s:64 * s + 64, h, cols],
               True, True, kv_keys + ["qT"], [f"ps{PSB[pb]}"], fuse=(j < 4 * st))

        def fin_A(h, s, st):
            MM(PS[0][:, :], ones128[:], rsb[s][:], True, True, ["ones128", f"rsb{s}"], ["ps0"])
            RECIP(rcp[:], PS[0][:, :], ["ps0"], ["rcp"])
            TT("dve", oraw[s][:], oraw[s][:], rcp[:], ALU.mult, [f"oraw{s}", "rcp"], [f"oraw{s}"])
            if s == 1:
                STT("dve", tob[:], oraw[1][:], neglam, oraw[0][:], ALU.mult, ALU.add, ["oraw0", "oraw1", "lamt"], ["tob"])
                ACT(sqb[:], tob[:], AF.Square, ["tob"], ["sqb"])

        def fin_B(h, st):
            stc = slice(st * 512, (st + 1) * 512)
            MM(PS[0][:, :], ones128[:], sqb[:], True, True, ["ones128", "sqb"], ["ps0"])
            ACT(rs2[:], PS[0][:, :], AF.Ln, ["ps0"], ["rs2"], bias=EPS, scale=1.0 / 128)
            ACT(rs2[:], rs2[:], AF.Exp, ["rs2"], ["rs2"], scale=-0.5)
            STT("dve", OAT[:, h, stc], tob[:], sgcol[:, 0:1], rs2[:], ALU.mult, ALU.mult, ["tob", "sgcol", "rs2"], ["OAT"])

        def emit_rest(i, st=st, kv_keys=kv_keys, iters=iters):
            h, s, j = iters[i]
            lo = max(0, j - 4 * st)
            pb = i % NB
            pk = f"ps{PSB[pb]}"
            psS = PS[PSB[pb]]
            par = s
            accb = 5 if s == 0 else 2
            if j == 0:
                MS("dve", racc[par][0][:], 0.0, [f"racc{par}0"])
                MS("pool", racc[par][1][:], 0.0, [f"racc{par}1"])
            far_lo = lo
            for qb in range(lo, 4):
                d = 4 * st + qb - j
                if d >= 2:
                    break
                bt = biasD if d == 0 else biasP
                tb_ = tmpb[d]
                TT("dve", tb_[:], psS[:, qb * 128:(qb + 1) * 128], bt[:, h, :], ALU.add, [pk, "biasD", "biasP"], [f"tmpb{d}"])
                ACT(PT[pb][:, qb * 128:(qb + 1) * 128], tb_[:], AF.Exp, [f"tmpb{d}"], [f"PT{pb}"])
                far_lo = qb + 1
            if far_lo < 4:
                ACT(PT[pb][:, far_lo * 128:512], psS[:, far_lo * 128:512], AF.Exp, [pk, "c15"], [f"PT{pb}"], bias=c15[:, h:h + 1])
            qs = slice(lo * 128, 512)
            last = (j == 4 * st + 3)
            MM(PS[accb][:, qs], V[:, j, h, :], PT[pb][:, qs], j == 0, last, [f"PT{pb}"] + kv_keys, [f"ps{accb}"], fuse=(j < 4 * st))
            for _f in range(FILL):
                MM(PS[6][:, :], ones128[:], wq[:, 0, 0:512], True, True, ["ones128", "wq"], ["ps6"])
            e_ = 0 if (j % 3 == 0) else 1
            TT(("dve", "pool")[e_], racc[par][e_][:, qs], racc[par][e_][:, qs], PT[pb][:, qs], ALU.add,
               [f"racc{par}{e_}", f"PT{pb}"], [f"racc{par}{e_}"])
            if not last:
                return
            CP("act", oraw[s][:], PS[accb][:, :], [f"ps{accb}"], [f"oraw{s}"])
            TT("dve", rsb[s][:], racc[par][0][:], racc[par][1][:], ALU.add, [f"racc{par}0", f"racc{par}1"], [f"rsb{s}"])
            deferred.append((i + 4, "A", h, s, st))
            if s == 1:
                deferred.append((i + 7, "B", h, s, st))

        def run_deferred(k):
            while deferred and deferred[0][0] <= k:
                _, kind, hh, ss, sst = deferred.pop(0)
                if kind == "A":
                    fin_A(hh, ss, sst)
                else:
                    fin_B(hh, sst)

        for i in range(0, NI + SKEW, 2):
            if PAIRED:
                for ii in (i, i + 1):
                    if ii < NI:
                        emit_S(ii)
                for ii in (i, i + 1):
                    k = ii - SKEW
                    if 0 <= k < NI:
                        emit_rest(k)
                        run_deferred(k)
            else:
                for ii in (i, i + 1):
                    if ii < NI:
                        emit_S(ii)
                    k = ii - SKEW
                    if 0 <= k < NI:
                        emit_rest(k)
                        run_deferred(k)
        run_deferred(10 ** 9)

    if dbg == "p1":
        DMA("sp", dbg_d[:, :], OAT[:, :, :].rearrange("p h t -> p (h t)"), ["OAT"], ["outd"], "OAT")
        NOP(["outd"])
        return finish(nc, S)
    I32 = mybir.dt.int32
    S.barrier()
    M.release(m_glob)
    w2 = M.alloc("w2", [128, 8, 1024], BF16)
    gluw = M.alloc("gluw", [128, 4, 512], BF16)
    glub = M.alloc("glub", [128, 4], F32)
    ta = M.alloc("ta", [128, 2048], F32)
    tb = M.alloc("tb", [128, 2048], F32)
    te = M.alloc("te", [128, 16, 128], F32)
    tf = M.alloc("tf", [128, 16, 128], F32)
    Bbd = M.alloc("Bbd", [128, 4, 2, 512], BF16)
    Ccat = M.alloc("Ccat", [128, 2, 16, 128], BF16)
    Ddiag = M.alloc("Ddiag", [128, 4, 128], BF16)
    dcol = M.alloc("dcol", [128, 4], F32)
    negtri = M.alloc("negtri", [128, 128], BF16)
    CcatN = M.alloc("CcatN", [128, 16, 128], BF16)
    l128 = M.alloc("l128", [128, 6, 16], F32)
    cre = [M.alloc("cre", [128, 16], F32) for _ in range(2)]
    cim = [M.alloc("cim", [128, 16], F32) for _ in range(2)]
    m_work2 = M.mark()
    stg = [M.alloc("stg", [128, 2048], F32) for _ in range(2)]
    load_weight(w2, w_in_d, 0, 8, 2048, 1024, stg, True, "w2")
    load_weight(gluw, gw_d, 0, 4, 0, 512, stg, False, "gluw")
    S.barrier()
    M.release(m_work2)
    T1 = M.alloc("T1", [128, 2048], F32)
    T2 = M.alloc("T2", [128, 2048], F32)
    T3 = M.alloc("T3", [128, 2048], F32)
    T4 = M.alloc("T4", [128, 2048], F32)
    TI = M.alloc("TI", [128, 2048], I32)
    dtb = M.alloc("dtb", [128, 32], F32)
    s2 = M.alloc("s2", [128, 8, 16], F32)
    s3 = M.alloc("s3", [128, 14, 256], F32)
    s3i = M.alloc("s3i", [128, 256], I32)
    dt3 = M.alloc("dt3", [128, 4], F32)

    DMA("sp", glub[:], dap(gb_d, 0, [[1, 128], [128, 4]]), (), ["glub"], "glub", slow=True)

    def frac_sincos(y, tmp, ti, sin_out, cos_out, key):
        CP("dve", ti, y, [key + "y"], [key + "ti"])
        CP("dve", tmp, ti, [key + "ti"], [key + "tmp"])
        TT("dve", tmp, y, tmp, ALU.subtract, [key + "y", key + "tmp"], [key + "tmp"])
        ACT(sin_out, tmp, AF.Sin, [key + "tmp"], [key + "sin"], scale=TWO_PI)
        TS("dve", y, y, 0.25, None, ALU.add, None, [key + "y"], [key + "y"])
        CP("dve", ti, y, [key + "y"], [key + "ti"])
        CP("dve", tmp, ti, [key + "ti"], [key + "tmp"])
        TT("dve", tmp, y, tmp, ALU.subtract, [key + "y", key + "tmp"], [key + "tmp"])
        ACT(cos_out, tmp, AF.Sin, [key + "tmp"], [key + "cos"], scale=TWO_PI)

    DMA("sp", T1[:], dap(are_d, 0, [[0, 128], [1, 2048]]), (), ["T1"], "T1")
    DMA("sp", T2[:], dap(aim_d, 0, [[0, 128], [1, 2048]]), (), ["T2"], "T2")
    DMA("sp", dtb[:], dap(ldt_d, 0, [[0, 128], [1, 32]]), (), ["dtb"], "dtb")
    ACT(dtb[:], dtb[:], AF.Exp, ["dtb"], ["dtb"])
    dtb_b = dtb[:, :].unsqueeze(2).broadcast_to([128, 32, 64])
    TT("dve", T1[:, :].rearrange("p (g q) -> p g q", g=32), T1[:, :].rearrange("p (g q) -> p g q", g=32), dtb_b, ALU.mult, ["T1", "dtb"], ["T1"])
    TT("dve", T2[:, :].rearrange("p (g q) -> p g q", g=32), T2[:, :].rearrange("p (g q) -> p g q", g=32), dtb_b, ALU.mult, ["T2", "dtb"], ["T2"])
    ACT(ta[:], T1[:], AF.Exp, ["T1", "iotap"], ["ta"], scale=iotap[:, 1:2])
    TS("dve", T3[:], T2[:], iotap[:, 0:1], 1.0 / TWO_PI, ALU.mult, ALU.mult, ["T2", "iotap"], ["L1y"])
    frac_sincos(T3[:], T4[:], TI[:], tb[:], T1[:], "L1")
    STT("dve", tb[:], tb[:], -1.0, ta[:], ALU.mult, ALU.mult, ["L1sin", "ta"], ["tb", "L1sin"])
    TT("dve", ta[:], ta[:], T1[:], ALU.mult, ["ta", "L1cos", "tb"], ["ta"])
    A2re, A2im, dt2, m2, th2 = (s2[:, i, :] for i in range(5))
    DMA("sp", A2re, dap(are_d, 0, [[1, 128], [128, 16]]), (), ["A2re"], "A2re", slow=True)
    DMA("sp", A2im, dap(aim_d, 0, [[1, 128], [128, 16]]), (), ["A2im"], "A2im", slow=True)
    for g2 in range(2):
        DMA("sp", s2[64 * g2:64 * g2 + 64, 2, :], dap(ldt_d, g2, [[0, 64], [2, 16]]), (), ["dt2"], "dt2", slow=True)
    ACT(dt2, dt2, AF.Exp, ["dt2"], ["dt2"])
    TT("dve", m2, A2re, dt2, ALU.mult, ["A2re", "dt2"], ["m2"])
    STT("dve", th2, A2im, 1.0 / TWO_PI, dt2, ALU.mult, ALU.mult, ["A2im", "dt2"], ["th2"])
    T3v = T3[:, :].rearrange("p (a t) -> p a t", a=16)
    for pair in range(16):
        ACT(te[:, pair, :], iotat[:], AF.Exp, ["iotat", "m2"], ["te"], scale=s2[:, 3, pair:pair + 1])
        TS("dve", T3v[:, pair, :], iotat[:], s2[:, 4, pair:pair + 1], None, ALU.mult, None, ["iotat", "th2", "L1y", "L1tmp"], ["L2y"])
    frac_sincos(T3[:], T4[:], TI[:], tf[:, :, :].rearrange("p a t -> p (a t)"), T1[:], "L2")
    TT("dve", tf[:, :, :].rearrange("p a t -> p (a t)"), tf[:, :, :].rearrange("p a t -> p (a t)"), te[:, :, :].rearrange("p a t -> p (a t)"),
       ALU.mult, ["L2sin", "te"], ["tf", "L2sin"])
    TT("dve", te[:, :, :].rearrange("p a t -> p (a t)"), te[:, :, :].rearrange("p a t -> p (a t)"), T1[:], ALU.mult, ["te", "L2cos", "tf"], ["te"])
    A3re, A3im, m3, y3, dec3, sin3, cos3, nr3, den3, qre3, qim3, u3a, u3b, tmp3 = (s3[:, i, :] for i in range(14))
    for g8 in range(8):
        DMA("sp", s3[16 * g8:16 * g8 + 16, 0, :].rearrange("p (c q) -> p c q", c=4), dap(are_d, g8 * 64, [[0, 16], [512, 4], [1, 64]]), (), ["A3re"], "A3re")
        DMA("sp", s3[16 * g8:16 * g8 + 16, 1, :].rearrange("p (c q) -> p c q", c=4), dap(aim_d, g8 * 64, [[0, 16], [512, 4], [1, 64]]), (), ["A3im"], "A3im")
        DMA("sp", dt3[16 * g8:16 * g8 + 16, :], dap(ldt_d, g8, [[0, 16], [8, 4]]), (), ["dt3"], "dt3", slow=True)
    ACT(dt3[:], dt3[:], AF.Exp, ["dt3"], ["dt3"])
    dt3_b = dt3[:, :].unsqueeze(2).broadcast_to([128, 4, 64])
    v3 = lambda a: a.rearrange("p (c q) -> p c q", c=4)
    TT("dve", v3(m3), v3(A3re), dt3_b, ALU.mult, ["A3re", "dt3"], ["m3"])
    TT("dve", v3(y3), v3(A3im), dt3_b, ALU.mult, ["A3im", "dt3"], ["L3y"])
    TS("dve", y3, y3, 1.0 / TWO_PI, None, ALU.mult, None, ["L3y"], ["L3y"])
    ACT(dec3, m3, AF.Exp, ["m3"], ["dec3"])
    frac_sincos(y3, tmp3, s3i[:], sin3, cos3, "L3")
    TT("dve", cos3, cos3, dec3, ALU.mult, ["L3cos", "dec3"], ["lbr"])
    TT("dve", sin3, sin3, dec3, ALU.mult, ["L3sin", "dec3"], ["lbi"])
    TS("dve", nr3, cos3, -1.0, None, ALU.add, None, ["lbr"], ["nr3"])
    TT("dve", den3, A3re, A3re, ALU.mult, ["A3re"], ["den3"])
    TT("dve", u3a, A3im, A3im, ALU.mult, ["A3im"], ["u3a"])
    TT("dve", den3, den3, u3a, ALU.add, ["den3", "u3a"], ["den3"])
    RECIP(den3, den3, ["den3"], ["den3"])
    TT("dve", u3a, nr3, A3re, ALU.mult, ["nr3", "A3re", "den3"], ["u3a"])
    TT("dve", u3b, sin3, A3im, ALU.mult, ["lbi", "A3im"], ["u3b"])
    TT("dve", qre3, u3a, u3b, ALU.add, ["u3a", "u3b"], ["qre3"])
    TT("dve", qre3, qre3, den3, ALU.mult, ["qre3", "den3"], ["qre3"])
    TT("dve", u3a, sin3, A3re, ALU.mult, ["lbi", "A3re", "qre3"], ["u3a"])
    TT("dve", u3b, nr3, A3im, ALU.mult, ["nr3", "A3im", "qre3"], ["u3b"])
    TT("dve", qim3, u3a, u3b, ALU.subtract, ["u3a", "u3b"], ["qim3"])
    TT("dve", qim3, qim3, den3, ALU.mult, ["qim3", "den3"], ["qim3"])
    qre_b = v3(qre3).unsqueeze(2).broadcast_to([128, 4, 8, 64])
    qim_b = v3(qim3).unsqueeze(2).broadcast_to([128, 4, 8, 64])
    T1v4 = T1[:, :].rearrange("p (c g q) -> p c g q", c=4, g=8)
    T2v4 = T2[:, :].rearrange("p (c g q) -> p c g q", c=4, g=8)
    TT("dve", T1v4, Bst[:, 0], qre_b, ALU.mult, ["Bst", "qre3", "ta", "te"], ["T1"])
    TT("dve", T2v4, Bst[:, 1], qim_b, ALU.mult, ["Bst", "qim3", "L1y", "L2y"], ["T2"])
    TT("dve", Bbd[:, :, 0, :], T1[:, :].rearrange("p (c x) -> p c x", c=4), T2[:, :].rearrange("p (c x) -> p c x", c=4), ALU.subtract, ["T1", "T2"], ["Bbd"])
    TT("dve", T1v4, Bst[:, 1], qre_b, ALU.mult, ["Bst", "qre3", "Bbd"], ["T1"])
    TT("dve", T2v4, Bst[:, 0], qim_b, ALU.mult, ["Bst", "qim3", "Bbd"], ["T2"])
    TT("dve", Bbd[:, :, 1, :], T1[:, :].rearrange("p (c x) -> p c x", c=4), T2[:, :].rearrange("p (c x) -> p c x", c=4), ALU.add, ["T1", "T2"], ["Bbd"])
    CP("dve", Ccat[:, 0], Cst[:, 0], ["Cst"], ["Ccat"])
    TS("dve", CcatN[:], Cst[:, 0], -1.0, None, ALU.mult, None, ["Cst"], ["CcatN"])
    TS("dve", negtri[:], tri[:], -1.0, None, ALU.mult, None, ["tri"], ["negtri"])
    TT("dve", l128[:, 0, :], te[:, :, 127], te[:, :, 1], ALU.mult, ["te"], ["l128a"])
    TT("dve", l128[:, 1, :], tf[:, :, 127], tf[:, :, 1], ALU.mult, ["tf"], ["l128b"])
    TT("dve", l128[:, 4, :], l128[:, 0, :], l128[:, 1, :], ALU.subtract, ["l128a", "l128b"], ["l128"])
    TT("dve", l128[:, 2, :], te[:, :, 127], tf[:, :, 1], ALU.mult, ["te", "tf"], ["l128c"])
    TT("dve", l128[:, 3, :], tf[:, :, 127], te[:, :, 1], ALU.mult, ["te", "tf"], ["l128d"])
    TT("dve", l128[:, 5, :], l128[:, 2, :], l128[:, 3, :], ALU.add, ["l128c", "l128d"], ["l128"])
    TS("dve", Ccat[:, 1], Cst[:, 1], -1.0, None, ALU.mult, None, ["Cst"], ["Ccat"])
    DMA("sp", dcol[:], dap(dd_d, 0, [[1, 128], [128, 4]]), (), ["dcol"], "dcol", slow=True)
    for ct in range(4):
        TS("dve", Ddiag[:, ct, :], ident[:], dcol[:, ct:ct + 1], None, ALU.mult, None, ["ident", "dcol"], ["Ddiag"])
    MS("dve", cre[0][:], 0.0, ["c0_0", "c0_1", "c0_2", "c0_3"])
    MS("dve", cim[0][:], 0.0, ["c0_0", "c0_1", "c0_2", "c0_3"])
    S.barrier()
    M.release(m_work2)
    hT = M.alloc("hT", [128, 8, 512], BF16)
    xt = [M.alloc("xt", [128, 1024], F32)]
    xn = M.alloc("xn", [128, 1024], BF16)
    uT = M.alloc("uT", [128, 4, 512], BF16)
    gsT = M.alloc("gsT", [128, 4, 512], BF16)
    gT = M.alloc("gT", [128, 4, 512], BF16)
    xs = [M.alloc("xs", [128, 512], F32) for _ in range(2)]
    ys = [M.alloc("ys", [128, 512], F32) for _ in range(2)]
    pa_ = [[M.alloc("pa_", [128, 512], BF16) for _ in range(4)] for _ in range(2)]
    pb_ = [[M.alloc("pb_", [128, 4, 128], BF16) for _ in range(4)] for _ in range(2)]
    sr = [M.alloc("sr", [128, 4, 128], F32) for _ in range(2)]
    si = [M.alloc("si", [128, 4, 128], F32) for _ in range(2)]
    sg = M.alloc("sg", [128, 512], F32)
    xl = M.alloc("xl", [128, 4, 4], F32)

    def ssm_A(st, cc, ct, u):
        tok = slice(cc * 128, (cc + 1) * 128)
        ub = u % 2
        MM(PS[2][:, :], uT[:, ct, tok], Bbd[:, ct, 0, :], True, True, ["uT", "Bbd"], ["ps2"])
        MM(PS[3][:, :], uT[:, ct, tok], Bbd[:, ct, 1, :], True, True, ["uT", "Bbd"], ["ps3"])
        CP("act", xs[ub][:], PS[2][:, :], ["ps2"], [f"xs{ub}"])
        CP("act", ys[ub][:], PS[3][:, :], ["ps3"], [f"ys{ub}"])
        a_ = ta[:, ct * 512:(ct + 1) * 512]
        b_ = tb[:, ct * 512:(ct + 1) * 512]
        p = pa_[ub]
        TT("dve", p[0][:], xs[ub][:], a_, ALU.mult, [f"xs{ub}", "ta"], [f"pa{ub}0"])
        TT("pool", p[1][:], ys[ub][:], b_, ALU.mult, [f"ys{ub}", "tb"], [f"pa{ub}1"])
        TT("dve", p[2][:], ys[ub][:], a_, ALU.mult, [f"ys{ub}", "ta"], [f"pa{ub}2"])
        TT("pool", p[3][:], xs[ub][:], b_, ALU.mult, [f"xs{ub}", "tb"], [f"pa{ub}3"])

    def ssm_A2(st, cc, ct, u):
        ub = u % 2
        p = pa_[ub]
        for pr in range(4):
            cs = slice(pr * 128, (pr + 1) * 128)
            MM(PS[4 + 2 * ub][:, cs], p[0][:, cs], tri[:], True, False, [f"pa{ub}0", "tri"], [f"ps{4 + 2 * ub}"])
            MM(PS[4 + 2 * ub][:, cs], p[1][:, cs], negtri[:], False, True, [f"pa{ub}1", "negtri"], [f"ps{4 + 2 * ub}"])
            MM(PS[5 + 2 * ub][:, cs], p[2][:, cs], tri[:], True, False, [f"pa{ub}2", "tri"], [f"ps{5 + 2 * ub}"])
            MM(PS[5 + 2 * ub][:, cs], p[3][:, cs], tri[:], False, True, [f"pa{ub}3", "tri"], [f"ps{5 + 2 * ub}"])

    def ssm_B(st, cc, ct, u):
        tok = slice(cc * 128, (cc + 1) * 128)
        ci = st * 4 + cc
        ub = u % 2
        cr_in, ci_in = cre[ci % 2], cim[ci % 2]
        cr_out, ci_out = cre[(ci + 1) % 2], cim[(ci + 1) % 2]
        kin, kout = f"c{ci % 2}_{ct}", f"c{(ci + 1) % 2}_{ct}"
        for pr in range(4):
            pair = 4 * ct + pr
            cs = slice(pr * 128, (pr + 1) * 128)
            ACT(sr[ub][:, pr, :], PS[4 + 2 * ub][:, cs], AF.Identity, [f"ps{4 + 2 * ub}", kin], [f"sr{ub}"], bias=cr_in[:, pair:pair + 1])
            ACT(si[ub][:, pr, :], PS[5 + 2 * ub][:, cs], AF.Identity, [f"ps{5 + 2 * ub}", kin], [f"si{ub}"], bias=ci_in[:, pair:pair + 1])
        eh = te[:, 4 * ct:4 * ct + 4, :]
        fh = tf[:, 4 * ct:4 * ct + 4, :]
        q = pb_[ub]
        TT("dve", q[0][:], sr[ub][:], eh, ALU.mult, [f"sr{ub}", "te"], [f"pb{ub}0"])
        TT("pool", q[1][:], si[ub][:], fh, ALU.mult, [f"si{ub}", "tf"], [f"pb{ub}1"])
        TT("dve", q[2][:], si[ub][:], eh, ALU.mult, [f"si{ub}", "te"], [f"pb{ub}2"])
        TT("pool", q[3][:], sr[ub][:], fh, ALU.mult, [f"sr{ub}", "tf"], [f"pb{ub}3"])
        E = l128[:, 4, 4 * ct:4 * ct + 4]
        F_ = l128[:, 5, 4 * ct:4 * ct + 4]
        ps_ = slice(4 * ct, 4 * ct + 4)
        TT("pool", xl[:, 0, :], E, sr[ub][:, :, 127], ALU.mult, ["l128", f"sr{ub}"], ["xl0"])
        TT("pool", xl[:, 1, :], F_, si[ub][:, :, 127], ALU.mult, ["l128", f"si{ub}"], ["xl1"])
        TT("pool", cr_out[:, ps_], xl[:, 0, :], xl[:, 1, :], ALU.subtract, ["xl0", "xl1"], [kout])
        TT("pool", xl[:, 2, :], E, si[ub][:, :, 127], ALU.mult, ["l128", f"si{ub}"], ["xl2"])
        TT("pool", xl[:, 3, :], F_, sr[ub][:, :, 127], ALU.mult, ["l128", f"sr{ub}"], ["xl3"])
        TT("pool", ci_out[:, ps_], xl[:, 2, :], xl[:, 3, :], ALU.add, ["xl2", "xl3"], [kout])

    def ssm_C(st, cc, ct, u):
        tok = slice(cc * 128, (cc + 1) * 128)
        ub = u % 2
        q = pb_[ub]
        ysl = slice(ct * 128, (ct + 1) * 128)
        for pr in range(4):
            pair = 4 * ct + pr
            MM(PS[0][:, ysl], Ccat[:, 0, pair, :], q[0][:, pr, :], pr == 0, False, ["Ccat", f"pb{ub}0"], ["ps0"], fuse=True)
            MM(PS[0][:, ysl], CcatN[:, pair, :], q[1][:, pr, :], False, False, ["CcatN", f"pb{ub}1"], ["ps0"], fuse=True)
            MM(PS[0][:, ysl], Ccat[:, 1, pair, :], q[2][:, pr, :], False, False, ["Ccat", f"pb{ub}2"], ["ps0"], fuse=True)
            MM(PS[0][:, ysl], Ccat[:, 1, pair, :], q[3][:, pr, :], False, False, ["Ccat", f"pb{ub}3"], ["ps0"], fuse=True)
        MM(PS[0][:, ysl], Ddiag[:, ct, :], uT[:, ct, tok], False, True, ["Ddiag", "uT"], ["ps0"], fuse=True)
        if ct == 3:
            ACT(gT[:, :, tok], PS[0][:, :].rearrange("p (c t) -> p c t", c=4), AF.Gelu, ["ps0"], ["gT"])

    for st in range(NST):
        make_hT(st, 1, hT, xt, xn, xn)
        def us_mm(c):
            zb = 1 + (c % 2)
            for kt in range(8):
                MM(PS[zb][:, :], w2[:, kt, c * 128:(c + 1) * 128], hT[:, kt, :], kt == 0, kt == 7, ["w2", "hT"], [f"ps{zb}"], fuse=True)
        us_mm(0)
        for c in range(8):
            if c + 1 < 8:
                us_mm(c + 1)
            zb = 1 + (c % 2)
            if c < 4:
                CP("act", uT[:, c, :], PS[zb][:, :], [f"ps{zb}"], ["uT"])
            else:
                ACT(gsT[:, c - 4, :], PS[zb][:, :], AF.Silu, [f"ps{zb}"], ["gsT"])
        units = [(cc, ct) for cc in range(4) for ct in range(4)]
        NU = len(units)
        for k in range(-2, NU):
            if 0 <= k + 2 < NU:
                ssm_A(st, units[k + 2][0], units[k + 2][1], k + 2)
            if 0 <= k + 1 < NU:
                ssm_A2(st, units[k + 1][0], units[k + 1][1], k + 1)
                ssm_B(st, units[k + 1][0], units[k + 1][1], k + 1)
            if 0 <= k < NU:
                ssm_C(st, units[k][0], units[k][1], k)
        for c in range(4):
            zb = 1 + (c % 2)
            for kt in range(4):
                MM(PS[zb][:, :], gluw[:, kt, c * 128:(c + 1) * 128], gT[:, kt, :], kt == 0, kt == 3, ["gluw", "gT"], [f"ps{zb}"], fuse=True)
            ACT(sg[:], PS[zb][:, :], AF.Sigmoid, [f"ps{zb}", "glub"], ["sg"], bias=glub[:, c:c + 1])
            TT("dve", sg[:], sg[:], gT[:, c, :], ALU.mult, ["sg", "gT"], ["sg"])
            TT("dve", OST[:, c, st * 512:(st + 1) * 512], sg[:], gsT[:, c, :], ALU.mult, ["sg", "gsT"], ["OST", "Bst", "Cst"])
    if dbg == "p2":
        DMA("sp", dbg_d[:, :], OSTm[:, :], ["OST"], ["outd"], "OSTm")
        NOP(["outd"])
        return finish(nc, S)

    S.barrier()
    M.release(m_glob)
    w3 = M.alloc("w3", [128, 8, 2560], BF16)
    pa = M.alloc("pa", [128, 4, 1024], BF16)
    psw = M.alloc("psw", [128, 4, 1024], BF16)
    wo = M.alloc("wo", [128, 8, 1024], BF16)
    mb = M.alloc("mb", [128, 16], F32)
    m_work3 = M.mark()
    stg = [M.alloc("stg", [128, 2048], F32) for _ in range(2)]
    load_weight(w3[:, :, 0:512], w_in_d, 0, 8, 1536, 512, stg, True, "w3")
    load_weight(w3[:, :, 512:2560], w_in_d, 0, 8, 3072, 2048, stg, True, "w3")
    load_weight(pa, pa_d, 0, 4, 0, 1024, stg, False, "pa")
    load_weight(psw, ps_d, 0, 4, 0, 1024, stg, False, "psw")
    load_weight(wo, wo_d, 0, 8, 0, 1024, stg, False, "wo")
    DMA("sp", mb[:], dap(mb_d, 0, [[1, 128], [128, 16]]), (), ["mb"], "mb", slow=True)
    S.barrier()
    M.release(m_work3)
    hT = M.alloc("hT", [128, 8, 512], BF16)
    xt = [M.alloc("xt", [128, 1024], F32)]
    xn = M.alloc("xn", [128, 1024], BF16)
    junk = M.alloc("junk", [128, 1024], BF16)
    mT = M.alloc("mT", [128, 8, 512], BF16)
    g1 = M.alloc("g1", [128, 512], F32)
    g2t = M.alloc("g2t", [128, 512], F32)
    m1 = M.alloc("m1", [128, 512], F32)
    xres = M.alloc("xres", [128, 1024], F32)
    ot = [M.alloc("ot", [128, 512], F32) for _ in range(2)]
    outkeys = []
    for st in range(NST):
        stc = slice(st * 512, (st + 1) * 512)
        make_hT(st, 2, hT, xt, xn, junk)
        for h in range(4):
            for kt in range(8):
                MM(PS[1][:, :], w3[:, kt, h * 128:(h + 1) * 128], hT[:, kt, :], kt == 0, kt == 7, ["w3", "hT"], ["ps1"], fuse=True)
            ACT(g1[:], PS[1][:, :], AF.Silu, ["ps1"], ["g1"])
            TT("dve", OAT[:, h, stc], OAT[:, h, stc], g1[:], ALU.mult, ["OAT", "g1"], ["OAT"])
        for c in range(8):
            for h in range(4):
                MM(PS[1][:, :], pa[:, h, c * 128:(c + 1) * 128], OAT[:, h, stc], h == 0, h == 3, ["pa", "OAT"], ["ps1"], fuse=True)
            for kt in range(8):
                MM(PS[2][:, :], w3[:, kt, 512 + c * 128:512 + (c + 1) * 128], hT[:, kt, :], kt == 0, kt == 7, ["w3", "hT"], ["ps2"], fuse=True)
            ACT(g1[:], PS[2][:, :], AF.Sigmoid, ["ps2", "mb"], ["g1"], bias=mb[:, c:c + 1])
            TT("dve", m1[:], PS[1][:, :], g1[:], ALU.mult, ["ps1", "g1"], ["m1"])
            for k in range(4):
                MM(PS[3][:, :], psw[:, k, c * 128:(c + 1) * 128], OST[:, k, stc], k == 0, k == 3, ["psw", "OST"], ["ps3"], fuse=True)
            for kt in range(8):
                MM(PS[4][:, :], w3[:, kt, 1536 + c * 128:1536 + (c + 1) * 128], hT[:, kt, :], kt == 0, kt == 7, ["w3", "hT"], ["ps4"], fuse=True)
            ACT(g2t[:], PS[4][:, :], AF.Sigmoid, ["ps4", "mb"], ["g2t"], bias=mb[:, 8 + c:9 + c])
            TT("dve", g2t[:], PS[3][:, :], g2t[:], ALU.mult, ["ps3", "g2t"], ["g2t"])
            TT("pool", mT[:, c, :], m1[:], g2t[:], ALU.add, ["m1", "g2t"], ["mT"])
        for tt in range(4):
            row = st * 512 + tt * 128
            DMA("sp", xres[:], x_d[row:row + 128, :], (), ["xres"], "xres")
            for half in range(2):
                for kt in range(8):
                    MM(PS[5 + half][:, :], mT[:, kt, tt * 128:(tt + 1) * 128], wo[:, kt, half * 512:(half + 1) * 512], kt == 0, kt == 7,
                       ["mT", "wo"], [f"ps{5 + half}"])
                TT("dve", ot[half][:], PS[5 + half][:, :], xres[:, half * 512:(half + 1) * 512], ALU.add, [f"ps{5 + half}", "xres"], [f"ot{half}"])
                ok = f"outd{st}_{tt}_{half}"
                DMA("sp", out_d[row:row + 128, half * 512:(half + 1) * 512], ot[half][:], [f"ot{half}"], [ok], f"ot{half}")
                outkeys.append(ok)
    NOP(outkeys)
    return finish(nc, S)


def finish(nc, S):
    import contextlib
    with contextlib.ExitStack() as stk:
        S._sem_ctx = {e: stk.enter_context(nc.semaphore(f"s_{e}")) for e in S.ENGS}
        S._dsem = {k: stk.enter_context(nc.semaphore(f"d_{k}")) for k in S.dma_count}
        block = stk.enter_context(nc.Block())
        S.emit(block)
    return nc


_NC_CACHE = {}


def _core_inputs(inputs, b, consts):
    m = {"x": np.ascontiguousarray(inputs["x"][b], dtype=np.float32)}
    for k, v in inputs.items():
        if k == "x":
            continue
        v = np.asarray(v, dtype=np.float32)
        if k == "rel_bias_table":
            m[k] = np.ascontiguousarray(v)
            continue
        v0 = v[0]
        if k in ("w_in", "ssm_glu_w", "proj_attn", "proj_ssm", "w_out"):
            m[k] = np.ascontiguousarray(v0)
        else:
            m[k] = np.ascontiguousarray(v0).reshape(-1)
    m.update(consts)
    return m


def kernel(**inputs):
    if "nc" not in _NC_CACHE:
        _NC_CACHE["nc"] = build_nc()
    nc = _NC_CACHE["nc"]
    consts = host_consts()
    in_maps = [_core_inputs(inputs, b, consts) for b in range(8)]
    res = run_bass_kernel_spmd(nc, in_maps, core_ids=list(range(8)))
    out = np.stack([np.asarray(res.results[b]["out"], dtype=np.float32) for b in range(8)], axis=0)
    return out
```

```python
import math
import numpy as np
import ml_dtypes
import concourse.bass as bass
import concourse.mybir as mybir
from concourse.bass_utils import run_bass_kernel_spmd

F32 = mybir.dt.float32
BF16 = mybir.dt.bfloat16
AF = mybir.ActivationFunctionType
ALU = mybir.AluOpType
AX = mybir.AxisListType


class _Op:
    __slots__ = ("eng", "fn", "idx", "dma", "dkey", "deps", "signal", "sigval", "sem", "fuse")

    def __init__(self, eng, fn, idx, dma, dkey):
        self.eng = eng
        self.fn = fn
        self.idx = idx
        self.dma = dma
        self.dkey = dkey
        self.deps = {}
        self.signal = False
        self.sigval = 0
        self.sem = None
        self.fuse = False


class Sched:
    ENGS = ("pe", "act", "dve", "pool", "sp")
    FUSE_WAITS = True

    def __init__(self, nc):
        self.nc = nc
        self.ops = {e: [] for e in self.ENGS}
        self.last_w = {}
        self.readers = {}
        self.dma_count = {}

    def add(self, eng, fn, reads=(), writes=(), dma=False, dkey=None, fuse=None):
        op = _Op(eng, fn, len(self.ops[eng]), dma, dkey)
        if fuse is None:
            fuse = (eng in ("act", "dve", "pool")) and not dma
        op.fuse = fuse and self.FUSE_WAITS
        reads = tuple(reads) + ("__B__",)
        for k in reads:
            w = self.last_w.get(k)
            if w is not None:
                op.deps[w] = True
        for k in writes:
            w = self.last_w.get(k)
            if w is not None and w not in op.deps:
                op.deps[w] = False
            rd = self.readers.get(k)
            if rd:
                for r in rd.get("eng", {}).values():
                    if r is not op and r not in op.deps:
                        op.deps[r] = False
                for r in rd.get("dma", []):
                    if r is not op and r not in op.deps:
                        op.deps[r] = False
        for k in reads:
            rd = self.readers.setdefault(k, {"eng": {}, "dma": []})
            if dma:
                rd["dma"].append(op)
            else:
                rd["eng"][eng] = op
        for k in writes:
            self.last_w[k] = op
            self.readers[k] = {"eng": {}, "dma": []}
        if dma:
            assert dkey is not None
            n = self.dma_count.get(dkey, 0) + 1
            self.dma_count[dkey] = n
            op.sigval = 16 * n
        self.ops[eng].append(op)
        return op

    def barrier(self):
        t = self._bar_tile
        self.add("dve", lambda e: e.memset(t[:, 0:1], 0.0), writes=("__B__",))

    def _needs_wait(self, op, dep, is_raw):
        if dep.dma:
            return True
        if dep.eng != op.eng:
            return True
        if op.dma:
            return True
        if op.eng == "pe":
            return False
        return is_raw and (op.idx - dep.idx) <= 3

    def emit(self, block, extra_final=None):
        nc = self.nc
        for e in self.ENGS:
            for op in self.ops[e]:
                for dep, is_raw in op.deps.items():
                    if self._needs_wait(op, dep, is_raw) and not dep.dma:
                        dep.signal = True
        sems = {e: self._sem_ctx[e] for e in self.ENGS}
        dsems = self._dsem
        for e in self.ENGS:
            n = 0
            for op in self.ops[e]:
                if op.dma:
                    op.sem = dsems[op.dkey]
                else:
                    op.sem = sems[e]
                    if op.signal:
                        n += 1
                        op.sigval = n

        def run_engine(ename, eng):
            known = {}
            for op in self.ops[ename]:
                need = {}
                for dep, is_raw in op.deps.items():
                    if not self._needs_wait(op, dep, is_raw):
                        continue
                    s = dep.sem
                    v = dep.sigval
                    if need.get(s, 0) < v:
                        need[s] = v
                pend = [(s, v) for s, v in need.items() if known.get(s, 0) < v]
                fused = None
                if op.fuse and pend:
                    fused = pend.pop()
                for s, v in pend:
                    eng.wait_ge(s, v)
                    known[s] = v
                ins = op.fn(eng)
                if fused is not None:
                    ins._wait_ge(fused[0], fused[1])
                    known[fused[0]] = fused[1]
                if op.dma:
                    ins.then_inc(op.sem, 16)
                elif op.signal:
                    ins.then_inc(op.sem, 1)
            if extra_final is not None:
                extra_final(ename, eng, known)

        @block.tensor
        def _(eng):
            run_engine("pe", eng)

        @block.scalar
        def _(eng):
            run_engine("act", eng)

        @block.vector
        def _(eng):
            run_engine("dve", eng)

        @block.gpsimd
        def _(eng):
            run_engine("pool", eng)

        @block.sync
        def _(eng):
            run_engine("sp", eng)


L = 4096
D = 1024
NST = 8
EPS = 1e-6
LAM_INIT = 0.8 - 0.6 * math.exp(-0.3 * 0)
TWO_PI = 2.0 * math.pi
NEG = -30000.0


def _bucket_np(rel):
    nb = 16
    me = 8
    side = np.where(rel > 0, nb, 0)
    n = np.abs(rel)
    nf = np.maximum(n, 1).astype(np.float32)
    large = me + (np.log(nf / np.float32(me)).astype(np.float32) / np.float32(math.log(128 / 8))
                  * np.float32(nb - me)).astype(np.int32)
    large = np.minimum(large, nb - 1)
    return side + np.where(n < me, n, large)


def host_consts():
    c = {}
    c["c_ident"] = np.eye(128, dtype=np.float32).astype(ml_dtypes.bfloat16)
    bo = np.zeros((128, 128), np.float32)
    bo[:64, :64] = 1.0 / 64
    bo[64:, 64:] = 1.0 / 64
    c["c_bones"] = bo.astype(ml_dtypes.bfloat16)
    c["c_J"] = np.ascontiguousarray(np.eye(128, dtype=np.float32)[::-1])
    rel = np.arange(-255, 128)
    b = _bucket_np(rel)
    oh = np.zeros((32, 384), np.float32)
    oh[b, np.arange(383)] = 1.0
    c["c_oh"] = oh
    k = np.arange(128)[:, None]
    q = np.arange(128)[None, :]
    c["c_maskD"] = np.where((k // 64) <= (q // 64), 0.0, NEG).astype(np.float32)
    c["c_iotap"] = np.stack([np.arange(128), -np.arange(128)], 1).astype(np.float32)
    c["c_iotat"] = np.tile(np.arange(128, dtype=np.float32)[None, :], (128, 1))
    c["c_tri"] = (np.arange(128)[:, None] <= np.arange(128)[None, :]).astype(np.float32).astype(ml_dtypes.bfloat16)
    return c


class Mem:
    def __init__(self, nc, start=16640, end=229376):
        self.nc = nc
        self.p = start
        self.end = end
        self.n = 0

    def alloc(self, name, shape, dtype):
        esz = 4 if dtype in (F32, mybir.dt.int32) else 2
        size = int(np.prod(shape[1:])) * esz
        size = (size + 63) // 64 * 64
        assert self.p + size <= self.end, f"SBUF overflow allocating {name}: {self.p}+{size} > {self.end}"
        self.n += 1
        t = self.nc.alloc_sbuf_tensor_at(f"{name}_{self.n}", list(shape), dtype, offset=self.p)
        self.p += size
        return t

    def mark(self):
        return self.p

    def release(self, m):
        self.p = m


def build_nc(dbg=None):
    nc = bass.Bass("TRN2", target_bir_lowering=False)
    S = Sched(nc)
    M = Mem(nc)

    def din(name, shape, dt=F32):
        return nc.dram_tensor(name, list(shape), dt, kind="ExternalInput").ap()

    x_d = din("x", [L, D])
    w_in_d = din("w_in", [D, 5120])
    ng_d = din("norm_gain", [D])
    mb_d = din("merge_gate_b", [2048])
    qg_d = din("q_norm_gain", [64])
    kg_d = din("k_norm_gain", [64])
    lq1_d = din("lambda_q1", [64]); lk1_d = din("lambda_k1", [64])
    lq2_d = din("lambda_q2", [64]); lk2_d = din("lambda_k2", [64])
    sg_d = din("diff_subln_gain", [128])
    rb_d = din("rel_bias_table", [32, 4])
    are_d = din("ssm_A_re", [2048]); aim_d = din("ssm_A_im", [2048])
    ldt_d = din("ssm_log_dt", [32])
    bre_d = din("ssm_B_re", [32 * 64 * 16]); bim_d = din("ssm_B_im", [32 * 64 * 16])
    cre_d = din("ssm_C_re", [32 * 16 * 64]); cim_d = din("ssm_C_im", [32 * 16 * 64])
    dd_d = din("ssm_D", [512])
    gw_d = din("ssm_glu_w", [512, 512])
    gb_d = din("ssm_glu_b", [512])
    pa_d = din("proj_attn", [512, 1024])
    ps_d = din("proj_ssm", [512, 1024])
    wo_d = din("w_out", [1024, 1024])
    c_ident_d = din("c_ident", [128, 128], BF16)
    c_bones_d = din("c_bones", [128, 128], BF16)
    c_J_d = din("c_J", [128, 128])
    c_oh_d = din("c_oh", [32, 384])
    c_maskD_d = din("c_maskD", [128, 128])
    c_iotap_d = din("c_iotap", [128, 2])
    c_iotat_d = din("c_iotat", [128, 128])
    c_tri_d = din("c_tri", [128, 128], BF16)
    out_d = nc.dram_tensor("out", [L, D], F32, kind="ExternalOutput").ap()
    fsc_d = nc.dram_tensor("fscratch", [4, 384], F32).ap()
    dbg_d = None
    if dbg:
        dbg_d = nc.dram_tensor("dbg", [128, 4 * L], BF16, kind="ExternalOutput").ap()

    def dap(t, off, pat):
        return bass.AP(t.tensor, off, [list(p) for p in pat])

    PS = [nc.alloc_psum_tensor(f"psb{i}", [128, 512], F32) for i in range(8)]

    def MM(out, lhsT, rhs, start, stop, r, w, fuse=False):
        S.add("pe", lambda e: e.matmul(out, lhsT=lhsT, rhs=rhs, start=start, stop=stop), r, w, fuse=fuse)

    def TR(out, in_, r, w):
        S.add("pe", lambda e: e.transpose(out=out, in_=in_, identity=ident[:]), tuple(r) + ("ident",), w)

    def ACT(out, in_, func, r, w, bias=0.0, scale=1.0, accum=None):
        if accum is None:
            S.add("act", lambda e: e.activation(out=out, in_=in_, func=func, bias=bias, scale=scale), r, w)
        else:
            S.add("act", lambda e: e.activation(out=out, in_=in_, func=func, bias=bias, scale=scale, accum_out=accum), r, w)

    def TS(eng, out, in0, s1, s2, op0, op1, r, w):
        if s2 is None:
            S.add(eng, lambda e: e.tensor_scalar(out=out, in0=in0, scalar1=s1, scalar2=None, op0=op0), r, w)
        else:
            S.add(eng, lambda e: e.tensor_scalar(out=out, in0=in0, scalar1=s1, scalar2=s2, op0=op0, op1=op1), r, w)

    def TT(eng, out, in0, in1, op, r, w):
        S.add(eng, lambda e: e.tensor_tensor(out=out, in0=in0, in1=in1, op=op), r, w)

    def STT(eng, out, in0, scalar, in1, op0, op1, r, w):
        S.add(eng, lambda e: e.scalar_tensor_tensor(out=out, in0=in0, scalar=scalar, in1=in1, op0=op0, op1=op1), r, w)

    def CP(eng, out, in_, r, w):
        if eng == "act":
            S.add("act", lambda e: e.copy(out=out, in_=in_), r, w)
        else:
            S.add(eng, lambda e: e.tensor_copy(out=out, in_=in_), r, w)

    def RSUM(out, in_, r, w):
        S.add("dve", lambda e: e.reduce_sum(out=out, in_=in_, axis=AX.X), r, w)

    def RECIP(out, in_, r, w):
        S.add("dve", lambda e: e.reciprocal(out=out, in_=in_), r, w)

    def NOP(r):
        S.add("sp", lambda e: e.nop(), r, [])

    def MS(eng, ap, val, w):
        S.add(eng, lambda e: e.memset(ap, val), (), w)

    def DMA(eng, out, in_, r, w, dkey, slow=False):
        if slow:
            S.add(eng, lambda e: e.dma_start(out=out, in_=in_, allow_slow_non_contiguous=True), r, w, dma=True, dkey=dkey)
        else:
            S.add(eng, lambda e: e.dma_start(out=out, in_=in_), r, w, dma=True, dkey=dkey)

    bar = M.alloc("bar", [128, 16], F32)
    S._bar_tile = bar
    ident = M.alloc("ident", [128, 128], BF16)
    bones = M.alloc("bones", [128, 128], BF16)
    tri = M.alloc("tri", [128, 128], BF16)
    iotap = M.alloc("iotap", [128, 2], F32)
    iotat = M.alloc("iotat", [128, 128], F32)
    gcol = M.alloc("gcol", [128, 8], F32)
    rs_all = M.alloc("rs_all", [128, 3 * 32 * 2], F32)
    OAT = M.alloc("OAT", [128, 4, L], BF16)
    OSTm = M.alloc("OSTm", [128, 4 * L], BF16)
    OST = OSTm[:, :].rearrange("p (c t) -> p c t", c=4)
    stag32 = OSTm[:, :].bitcast(F32)
    Bst = stag32[:, 0:4096].rearrange("p (r c g q) -> p r c g q", r=2, c=4, g=8)
    Cst = stag32[:, 4096:8192].rearrange("p (r a c) -> p r a c", r=2, a=16)

    DMA("sp", ident[:], c_ident_d[:, :], (), ["ident"], "ident")
    DMA("sp", bones[:], c_bones_d[:, :], (), ["bones"], "bones")
    DMA("sp", tri[:], c_tri_d[:, :], (), ["tri"], "tri")
    DMA("sp", iotap[:], c_iotap_d[:, :], (), ["iotap"], "iotap")
    DMA("sp", iotat[:], c_iotat_d[:, :], (), ["iotat"], "iotat")
    DMA("sp", gcol[:], dap(ng_d, 0, [[1, 128], [128, 8]]), (), ["gcol"], "gcol", slow=True)

    MS("dve", stag32[:, 0:4096], 0.0, ["Bst"])
    MS("pool", stag32[:, 4096:8192], 0.0, ["Cst"])
    bc_dmas = []
    for g in range(32):
        for ri in range(2):
            bc_dmas.append((g, ri))

    def emit_bc(n):
        for _ in range(n):
            if not bc_dmas:
                return
            g, ri = bc_dmas.pop(0)
            ct, g8 = g // 8, g % 8
            pair, g2 = g // 2, g % 2
            bd = (bre_d, bim_d)[ri]
            cd = (cre_d, cim_d)[ri]
            DMA("sp", Bst[16 * g8:16 * g8 + 16, ri, ct, g8, :], dap(bd, g * 1024, [[1, 16], [16, 64]]),
                (), ["Bst"], "Bst", slow=True)
            DMA("sp", Cst[64 * g2:64 * g2 + 64, ri, pair, (g % 8) * 16:(g % 8) * 16 + 16],
                dap(cd, g * 1024, [[1, 64], [64, 16]]), (), ["Cst"], "Cst", slow=True)

    def load_weight(dst, src_d, row0, nrows_tiles, col0, ncols, stg, fold_gain, keyw):
        i = 0
        for kt in range(nrows_tiles):
            for c0 in range(0, ncols, 2048):
                cw = min(2048, ncols - c0)
                sb = stg[i % 2]
                sk = f"stg{i % 2}"
                DMA("sp", sb[:, 0:cw], src_d[row0 + kt * 128: row0 + (kt + 1) * 128, col0 + c0: col0 + c0 + cw],
                    (), [sk], sk)
                if fold_gain:
                    if i % 2 == 0:
                        ACT(dst[:, kt, c0:c0 + cw], sb[:, 0:cw], AF.Copy, [sk, "gcol"], [keyw], scale=gcol[:, kt:kt + 1])
                    else:
                        TS("dve", dst[:, kt, c0:c0 + cw], sb[:, 0:cw], gcol[:, kt:kt + 1], None, ALU.mult, None, [sk, "gcol"], [keyw])
                else:
                    if i % 2 == 0:
                        CP("act", dst[:, kt, c0:c0 + cw], sb[:, 0:cw], [sk], [keyw])
                    else:
                        CP("dve", dst[:, kt, c0:c0 + cw], sb[:, 0:cw], [sk], [keyw])
                i += 1

    def make_hT(st, phase, hT, xt, xn, tbanks):
        for tt in range(4):
            i = st * 4 + tt
            b = i % len(xt)
            nb_ = i % len(xn)
            tbk = tbanks[i % len(tbanks)]
            psT = PS[tbk][:, 0:512].bitcast(BF16)
            row = st * 512 + tt * 128
            col = (phase * 32 + i) * 2
            ss = rs_all[:, col:col + 1]
            rs = rs_all[:, col + 1:col + 2]
            xk, nk = f"xt{b}", f"xn{nb_}"
            DMA("sp", xt[b][:], x_d[row:row + 128, :], (), [xk], xk)
            ACT(xn[nb_][:], xt[b][:], AF.Square, [xk], [nk, f"ss{col}"], accum=ss)
            ACT(rs, ss, AF.Ln, [f"ss{col}"], [f"rs{col}"], bias=EPS, scale=1.0 / D)
            ACT(rs, rs, AF.Exp, [f"rs{col}"], [f"rs{col}"], scale=-0.5)
            TS("dve", xn[nb_][:], xt[b][:], rs, None, ALU.mult, None, [xk, f"rs{col}"], [nk])
            for kt in range(8):
                TR(psT[:, kt * 128:(kt + 1) * 128], xn[nb_][:, kt * 128:(kt + 1) * 128], [nk], [f"ps{tbk}"])
            CP("dve", hT[:, :, tt * 128:(tt + 1) * 128], psT[:, :].rearrange("p (k t) -> p k t", k=8), [f"ps{tbk}"], ["hT"])

    m_glob = M.mark()
    wq = M.alloc("wq", [128, 8, 1536], BF16)
    KT = M.alloc("KT", [128, 4, L], BF16)
    V = M.alloc("V", [128, 32, 4, 128], BF16)
    ones128 = M.alloc("ones128", [128, 128], BF16)
    sgcol = M.alloc("sgcol", [128, 1], F32)
    c15 = M.alloc("c15", [128, 4], F32)
    biasP = M.alloc("biasP", [128, 4, 128], F32)
    biasD = M.alloc("biasD", [128, 4, 128], F32)
    gq = M.alloc("gq", [128, 2], F32)
    lamt = M.alloc("lamt", [128, 8], F32)
    m_work = M.mark()
    stg = [M.alloc("stg", [128, 2048], F32) for _ in range(2)]
    lamv = M.alloc("lamv", [128, 4, 64], F32)
    tbl = M.alloc("tbl", [32, 4], F32)
    oh = M.alloc("oh", [32, 384], F32)
    fsb = M.alloc("fsb", [4, 384], F32)
    Jt = M.alloc("Jt", [128, 128], F32)
    hank = M.alloc("hank", [128, 4, 256], F32)
    maskD = M.alloc("maskD", [128, 128], F32)

    load_weight(wq, w_in_d, 0, 8, 0, 1536, stg, True, "wq")
    MS("dve", ones128[:], 1.0, ["ones128"])
    DMA("sp", sgcol[:], dap(sg_d, 0, [[1, 128], [1, 1]]), (), ["sgcol"], "sgcol")
    TS("dve", sgcol[:], sgcol[:], 1.0 - LAM_INIT, None, ALU.mult, None, ["sgcol"], ["sgcol"])
    for hlf in range(2):
        DMA("sp", gq[64 * hlf:64 * hlf + 64, 0:1], dap(qg_d, 0, [[1, 64], [1, 1]]), (), ["gq"], "gq")
        DMA("sp", gq[64 * hlf:64 * hlf + 64, 1:2], dap(kg_d, 0, [[1, 64], [1, 1]]), (), ["gq"], "gq")
    TS("dve", gq[:, 0:1], gq[:, 0:1], 0.125, None, ALU.mult, None, ["gq"], ["gq"])
    for i, dd in enumerate((lq1_d, lk1_d, lq2_d, lk2_d)):
        DMA("sp", lamv[:, i, :], dap(dd, 0, [[0, 128], [1, 64]]), (), ["lamv"], "lamv")
    TT("dve", lamv[:, 0, :], lamv[:, 0, :], lamv[:, 1, :], ALU.mult, ["lamv"], ["lamv"])
    TT("dve", lamv[:, 2, :], lamv[:, 2, :], lamv[:, 3, :], ALU.mult, ["lamv"], ["lamv"])
    RSUM(lamt[:, 0:1], lamv[:, 0, :], ["lamv"], ["lamt"])
    RSUM(lamt[:, 1:2], lamv[:, 2, :], ["lamv"], ["lamt"])
    ACT(lamt[:, 2:4], lamt[:, 0:2], AF.Exp, ["lamt"], ["lamt"])
    TT("dve", lamt[:, 4:5], lamt[:, 3:4], lamt[:, 2:3], ALU.subtract, ["lamt"], ["lamt"])
    TS("dve", lamt[:, 5:6], lamt[:, 4:5], -LAM_INIT, None, ALU.add, None, ["lamt"], ["lamt"])
    neglam = lamt[:, 5:6]
    DMA("sp", tbl[:], rb_d[:, :], (), ["tbl"], "tbl")
    DMA("sp", oh[:], c_oh_d[:, :], (), ["oh"], "oh")
    DMA("sp", Jt[:], c_J_d[:, :], (), ["Jt"], "Jt")
    DMA("sp", maskD[:], c_maskD_d[:, :], (), ["maskD"], "maskD")
    DMA("sp", c15[:], dap(rb_d, 15 * 4, [[0, 128], [1, 4]]), (), ["c15"], "c15")
    MM(PS[1][0:4, 0:384], tbl[:], oh[:], True, True, ["tbl", "oh"], ["ps1"])
    CP("dve", fsb[:], PS[1][0:4, 0:384], ["ps1"], ["fsb"])
    DMA("sp", fsc_d[:, :], fsb[:], ["fsb"], ["fsc"], "fsb")
    for h in range(4):
        DMA("sp", hank[:, h, :], dap(fsc_d, h * 384, [[1, 128], [1, 256]]), ["fsc"], ["hank"], "hank")
    for h in range(4):
        MM(PS[2][:, h * 128:(h + 1) * 128], hank[:, h, 0:128], Jt[:], True, True, ["hank", "Jt"], ["ps2"])
        MM(PS[3][:, h * 128:(h + 1) * 128], hank[:, h, 128:256], Jt[:], True, True, ["hank", "Jt"], ["ps3"])
    CP("dve", biasP[:], PS[2][:, :].rearrange("p (h q) -> p h q", h=4), ["ps2"], ["biasP"])
    TT("dve", biasD[:], PS[3][:, :].rearrange("p (h q) -> p h q", h=4),
       maskD[:].unsqueeze(1).broadcast_to([128, 4, 128]), ALU.add, ["ps3", "maskD"], ["biasD"])
    S.barrier()
    M.release(m_work)
    hT = M.alloc("hT", [128, 8, 512], BF16)
    xt = [M.alloc("xt", [128, 1024], F32) for _ in range(2)]
    xn = [M.alloc("xn", [128, 1024], BF16) for _ in range(2)]
    qT = M.alloc("qT", [128, 4, 512], BF16)
    sq = M.alloc("sq", [128, 512], BF16)
    rstd = M.alloc("rstd", [128, 512], F32)
    PT = [M.alloc("PT", [128, 512], BF16) for _ in range(4)]
    tmpb = [M.alloc("tmpb", [128, 128], F32) for _ in range(2)]
    rcp = rstd
    oraw = [M.alloc("oraw", [128, 512], F32) for _ in range(2)]
    racc = [[M.alloc("racc", [128, 512], F32) for _ in range(2)] for _ in range(2)]
    tob = oraw[1]
    rsb = [M.alloc("rsb", [128, 512], BF16) for _ in range(2)]
    sqb = sq
    rs2 = M.alloc("rs2", [128, 512], F32)
    sq2 = [sq, M.alloc("sq2", [128, 512], BF16)]
    rstd2 = [rstd, rs2]
    rkeys = ["rstd0", "rs2"]

    PSB = [3, 4, 7, 6]
    NB = 4
    SKEW = 2
    PAIRED = True
    FILL = 0
    deferred = []

    def qk_mm(c, st):
        zb = 1 + (c % 2)
        for kt in range(8):
            MM(PS[zb][:, :], wq[:, kt, c * 128:(c + 1) * 128], hT[:, kt, :], kt == 0, kt == 7, ["wq", "hT"], [f"ps{zb}"], fuse=True)

    MB = [0, 3]

    def qk_c1(c, st):
        zb = 1 + (c % 2)
        sqc = sq2[c % 2]
        mb_ = MB[c % 2]
        ACT(sqc[:], PS[zb][:, :], AF.Square, [f"ps{zb}"], [f"sq{c % 2}"])
        MM(PS[mb_][:, :], bones[:], sqc[:], True, True, ["bones", f"sq{c % 2}"], [f"ps{mb_}"])

    def qk_c2(c, st):
        zb = 1 + (c % 2)
        rsc = rstd2[c % 2]
        mb_ = MB[c % 2]
        ACT(rsc[:], PS[mb_][:, :], AF.Ln, [f"ps{mb_}"], [rkeys[c % 2]], bias=EPS)
        ACT(rsc[:], rsc[:], AF.Exp, [rkeys[c % 2]], [rkeys[c % 2]], scale=-0.5)
        if c < 4:
            STT("dve", qT[:, c, :], PS[zb][:, :], gq[:, 0:1], rsc[:], ALU.mult, ALU.mult, [f"ps{zb}", "gq", rkeys[c % 2]], ["qT"])
        else:
            STT("dve", KT[:, c - 4, st * 512:(st + 1) * 512], PS[zb][:, :], gq[:, 1:2], rsc[:], ALU.mult, ALU.mult,
                [f"ps{zb}", "gq", rkeys[c % 2]], [f"KT{st}"])

    for st in range(NST):
        make_hT(st, 0, hT, xt, xn, [0, 3])
        emit_bc(8)
        qk_mm(0, st)
        qk_c1(0, st)
        for c in range(8):
            if c + 1 < 8:
                qk_mm(c + 1, st)
                qk_c1(c + 1, st)
            qk_c2(c, st)
        for tt in range(4):
            blk = st * 4 + tt
            zb = 1 + (tt % 2)
            for kt in range(8):
                MM(PS[zb][:, :], hT[:, kt, tt * 128:(tt + 1) * 128], wq[:, kt, 1024:1536], kt == 0, kt == 7, ["wq", "hT"], [f"ps{zb}"])
            CP("dve", V[:, blk, :, :], PS[zb][:, :].rearrange("p (h d) -> p h d", h=4), [f"ps{zb}"], [f"V{st}"])
        kv_keys = [f"KT{i}" for i in range(st + 1)] + [f"V{i}" for i in range(st + 1)]
        iters = [(h, s, j) for h in range(4) for j in range(4 * st + 4) for s in range(2)]
        NI = len(iters)

        def emit_S(i, st=st, kv_keys=kv_keys, iters=iters):
            h, s, j = iters[i]
            lo = max(0, j - 4 * st)
            cols = slice(lo * 128, 512)
            pb = i % NB
            psS = PS[PSB[pb]]
            MM(psS[:, cols], KT[64 * s:64 * s + 64, h, j * 128:(j + 1) * 128], qT[64 * s:64 * s + 64, h, cols],
               True, True, kv_keys + ["qT"], [f"ps{PSB[pb]}"], fuse=(j < 4 * st))

        def fin_A(h, s, st):
            MM(PS[0][:, :], ones128[:], rsb[s][:], True, True, ["ones128", f"rsb{s}"], ["ps0"])
            RECIP(rcp[:], PS[0][:, :], ["ps0"], ["rstd0"])
            TT("dve", oraw[s][:], oraw[s][:], rcp[:], ALU.mult, [f"oraw{s}", "rstd0"], [f"oraw{s}"])
            if s == 1:
                STT("dve", tob[:], oraw[1][:], neglam, oraw[0][:], ALU.mult, ALU.add, ["oraw0", "oraw1", "lamt"], ["oraw1"])
                ACT(sqb[:], tob[:], AF.Square, ["oraw1"], ["sq0"])

        def fin_B(h, st):
            stc = slice(st * 512, (st + 1) * 512)
            MM(PS[0][:, :], ones128[:], sqb[:], True, True, ["ones128", "sq0"], ["ps0"])
            ACT(rs2[:], PS[0][:, :], AF.Ln, ["ps0"], ["rs2"], bias=EPS, scale=1.0 / 128)
            ACT(rs2[:], rs2[:], AF.Exp, ["rs2"], ["rs2"], scale=-0.5)
            STT("dve", OAT[:, h, stc], tob[:], sgcol[:, 0:1], rs2[:], ALU.mult, ALU.mult, ["oraw1", "sgcol", "rs2"], ["OAT"])

        def emit_rest(i, st=st, kv_keys=kv_keys, iters=iters):
            h, s, j = iters[i]
            lo = max(0, j - 4 * st)
            pb = i % NB
            pk = f"ps{PSB[pb]}"
            psS = PS[PSB[pb]]
            par = s
            accb = 5 if s == 0 else 2
            if j == 0:
                MS("dve", racc[par][0][:], 0.0, [f"racc{par}0"])
                MS("pool", racc[par][1][:], 0.0, [f"racc{par}1"])
            far_lo = lo
            for qb in range(lo, 4):
                d = 4 * st + qb - j
                if d >= 2:
                    break
                bt = biasD if d == 0 else biasP
                tb_ = tmpb[d]
                TT("dve", tb_[:], psS[:, qb * 128:(qb + 1) * 128], bt[:, h, :], ALU.add, [pk, "biasD", "biasP"], [f"tmpb{d}"])
                ACT(PT[pb][:, qb * 128:(qb + 1) * 128], tb_[:], AF.Exp, [f"tmpb{d}"], [f"PT{pb}"])
                far_lo = qb + 1
            if far_lo < 4:
                ACT(PT[pb][:, far_lo * 128:512], psS[:, far_lo * 128:512], AF.Exp, [pk, "c15"], [f"PT{pb}"], bias=c15[:, h:h + 1])
            qs = slice(lo * 128, 512)
            last = (j == 4 * st + 3)
            MM(PS[accb][:, qs], V[:, j, h, :], PT[pb][:, qs], j == 0, last, [f"PT{pb}"] + kv_keys, [f"ps{accb}"], fuse=(j < 4 * st))
            for _f in range(FILL):
                MM(PS[6][:, :], ones128[:], wq[:, 0, 0:512], True, True, ["ones128", "wq"], ["ps6"])
            e_ = 0 if (j % 3 == 0) else 1
            TT(("dve", "pool")[e_], racc[par][e_][:, qs], racc[par][e_][:, qs], PT[pb][:, qs], ALU.add,
               [f"racc{par}{e_}", f"PT{pb}"], [f"racc{par}{e_}"])
            if not last:
                return
            CP("act", oraw[s][:], PS[accb][:, :], [f"ps{accb}"], [f"oraw{s}"])
            TT("dve", rsb[s][:], racc[par][0][:], racc[par][1][:], ALU.add, [f"racc{par}0", f"racc{par}1"], [f"rsb{s}"])
            deferred.append((i + 4, "A", h, s, st))
            if s == 1:
                deferred.append((i + 7, "B", h, s, st))

        def run_deferred(k):
            while deferred and deferred[0][0] <= k:
                _, kind, hh, ss, sst = deferred.pop(0)
                if kind == "A":
                    fin_A(hh, ss, sst)
                else:
                    fin_B(hh, sst)

        for i in range(0, NI + SKEW, 2):
            if PAIRED:
                for ii in (i, i + 1):
                    if ii < NI:
                        emit_S(ii)
                for ii in (i, i + 1):
                    k = ii - SKEW
                    if 0 <= k < NI:
                        emit_rest(k)
                        run_deferred(k)
            else:
                for ii in (i, i + 1):
                    if ii < NI:
                        emit_S(ii)
                    k = ii - SKEW
                    if 0 <= k < NI:
                        emit_rest(k)
                        run_deferred(k)
        run_deferred(10 ** 9)

    if dbg == "p1":
        DMA("sp", dbg_d[:, :], OAT[:, :, :].rearrange("p h t -> p (h t)"), ["OAT"], ["outd"], "OAT")
        NOP(["outd"])
        return finish(nc, S)
    I32 = mybir.dt.int32
    S.barrier()
    M.release(m_glob)
    w2 = M.alloc("w2", [128, 8, 1024], BF16)
    gluw = M.alloc("gluw", [128, 4, 512], BF16)
    glub = M.alloc("glub", [128, 4], F32)
    ta = M.alloc("ta", [128, 2048], F32)
    tb = M.alloc("tb", [128, 2048], F32)
    te = M.alloc("te", [128, 16, 128], F32)
    tf = M.alloc("tf", [128, 16, 128], F32)
    Bbd = M.alloc("Bbd", [128, 4, 2, 512], BF16)
    Ccat = M.alloc("Ccat", [128, 2, 16, 128], BF16)
    Ddiag = M.alloc("Ddiag", [128, 4, 128], BF16)
    dcol = M.alloc("dcol", [128, 4], F32)
    negtri = M.alloc("negtri", [128, 128], BF16)
    CcatN = M.alloc("CcatN", [128, 16, 128], BF16)
    l128 = M.alloc("l128", [128, 6, 16], F32)
    cre = [M.alloc("cre", [128, 16], F32) for _ in range(2)]
    cim = [M.alloc("cim", [128, 16], F32) for _ in range(2)]
    m_work2 = M.mark()
    stg = [M.alloc("stg", [128, 2048], F32) for _ in range(2)]
    load_weight(w2, w_in_d, 0, 8, 2048, 1024, stg, True, "w2")
    load_weight(gluw, gw_d, 0, 4, 0, 512, stg, False, "gluw")
    S.barrier()
    M.release(m_work2)
    T1 = M.alloc("T1", [128, 2048], F32)
    T2 = M.alloc("T2", [128, 2048], F32)
    T3 = M.alloc("T3", [128, 2048], F32)
    T4 = M.alloc("T4", [128, 2048], F32)
    TI = M.alloc("TI", [128, 2048], I32)
    dtb = M.alloc("dtb", [128, 32], F32)
    s2 = M.alloc("s2", [128, 8, 16], F32)
    s3 = M.alloc("s3", [128, 14, 256], F32)
    s3i = M.alloc("s3i", [128, 256], I32)
    dt3 = M.alloc("dt3", [128, 4], F32)

    DMA("sp", glub[:], dap(gb_d, 0, [[1, 128], [128, 4]]), (), ["glub"], "glub", slow=True)

    def frac_sincos(y, tmp, ti, sin_out, cos_out, key):
        CP("dve", ti, y, [key + "y"], [key + "ti"])
        CP("dve", tmp, ti, [key + "ti"], [key + "tmp"])
        TT("dve", tmp, y, tmp, ALU.subtract, [key + "y", key + "tmp"], [key + "tmp"])
        ACT(sin_out, tmp, AF.Sin, [key + "tmp"], [key + "sin"], scale=TWO_PI)
        TS("dve", y, y, 0.25, None, ALU.add, None, [key + "y"], [key + "y"])
        CP("dve", ti, y, [key + "y"], [key + "ti"])
        CP("dve", tmp, ti, [key + "ti"], [key + "tmp"])
        TT("dve", tmp, y, tmp, ALU.subtract, [key + "y", key + "tmp"], [key + "tmp"])
        ACT(cos_out, tmp, AF.Sin, [key + "tmp"], [key + "cos"], scale=TWO_PI)

    DMA("sp", T1[:], dap(are_d, 0, [[0, 128], [1, 2048]]), (), ["T1"], "T1")
    DMA("sp", T2[:], dap(aim_d, 0, [[0, 128], [1, 2048]]), (), ["T2"], "T2")
    DMA("sp", dtb[:], dap(ldt_d, 0, [[0, 128], [1, 32]]), (), ["dtb"], "dtb")
    ACT(dtb[:], dtb[:], AF.Exp, ["dtb"], ["dtb"])
    dtb_b = dtb[:, :].unsqueeze(2).broadcast_to([128, 32, 64])
    TT("dve", T1[:, :].rearrange("p (g q) -> p g q", g=32), T1[:, :].rearrange("p (g q) -> p g q", g=32), dtb_b, ALU.mult, ["T1", "dtb"], ["T1"])
    TT("dve", T2[:, :].rearrange("p (g q) -> p g q", g=32), T2[:, :].rearrange("p (g q) -> p g q", g=32), dtb_b, ALU.mult, ["T2", "dtb"], ["T2"])
    ACT(ta[:], T1[:], AF.Exp, ["T1", "iotap"], ["ta"], scale=iotap[:, 1:2])
    TS("dve", T3[:], T2[:], iotap[:, 0:1], 1.0 / TWO_PI, ALU.mult, ALU.mult, ["T2", "iotap"], ["L1y"])
    frac_sincos(T3[:], T4[:], TI[:], tb[:], T1[:], "L1")
    STT("dve", tb[:], tb[:], -1.0, ta[:], ALU.mult, ALU.mult, ["L1sin", "ta"], ["tb", "L1sin"])
    TT("dve", ta[:], ta[:], T1[:], ALU.mult, ["ta", "L1cos", "tb"], ["ta"])
    A2re, A2im, dt2, m2, th2 = (s2[:, i, :] for i in range(5))
    DMA("sp", A2re, dap(are_d, 0, [[1, 128], [128, 16]]), (), ["A2re"], "A2re", slow=True)
    DMA("sp", A2im, dap(aim_d, 0, [[1, 128], [128, 16]]), (), ["A2im"], "A2im", slow=True)
    for g2 in range(2):
        DMA("sp", s2[64 * g2:64 * g2 + 64, 2, :], dap(ldt_d, g2, [[0, 64], [2, 16]]), (), ["dt2"], "dt2", slow=True)
    ACT(dt2, dt2, AF.Exp, ["dt2"], ["dt2"])
    TT("dve", m2, A2re, dt2, ALU.mult, ["A2re", "dt2"], ["m2"])
    STT("dve", th2, A2im, 1.0 / TWO_PI, dt2, ALU.mult, ALU.mult, ["A2im", "dt2"], ["th2"])
    T3v = T3[:, :].rearrange("p (a t) -> p a t", a=16)
    for pair in range(16):
        ACT(te[:, pair, :], iotat[:], AF.Exp, ["iotat", "m2"], ["te"], scale=s2[:, 3, pair:pair + 1])
        TS("dve", T3v[:, pair, :], iotat[:], s2[:, 4, pair:pair + 1], None, ALU.mult, None, ["iotat", "th2", "L1y", "L1tmp"], ["L2y"])
    frac_sincos(T3[:], T4[:], TI[:], tf[:, :, :].rearrange("p a t -> p (a t)"), T1[:], "L2")
    TT("dve", tf[:, :, :].rearrange("p a t -> p (a t)"), tf[:, :, :].rearrange("p a t -> p (a t)"), te[:, :, :].rearrange("p a t -> p (a t)"),
       ALU.mult, ["L2sin", "te"], ["tf", "L2sin"])
    TT("dve", te[:, :, :].rearrange("p a t -> p (a t)"), te[:, :, :].rearrange("p a t -> p (a t)"), T1[:], ALU.mult, ["te", "L2cos", "tf"], ["te"])
    A3re, A3im, m3, y3, dec3, sin3, cos3, nr3, den3, qre3, qim3, u3a, u3b, tmp3 = (s3[:, i, :] for i in range(14))
    for g8 in range(8):
        DMA("sp", s3[16 * g8:16 * g8 + 16, 0, :].rearrange("p (c q) -> p c q", c=4), dap(are_d, g8 * 64, [[0, 16], [512, 4], [1, 64]]), (), ["A3re"], "A3re")
        DMA("sp", s3[16 * g8:16 * g8 + 16, 1, :].rearrange("p (c q) -> p c q", c=4), dap(aim_d, g8 * 64, [[0, 16], [512, 4], [1, 64]]), (), ["A3im"], "A3im")
        DMA("sp", dt3[16 * g8:16 * g8 + 16, :], dap(ldt_d, g8, [[0, 16], [8, 4]]), (), ["dt3"], "dt3", slow=True)
    ACT(dt3[:], dt3[:], AF.Exp, ["dt3"], ["dt3"])
    dt3_b = dt3[:, :].unsqueeze(2).broadcast_to([128, 4, 64])
    v3 = lambda a: a.rearrange("p (c q) -> p c q", c=4)
    TT("dve", v3(m3), v3(A3re), dt3_b, ALU.mult, ["A3re", "dt3"], ["m3"])
    TT("dve", v3(y3), v3(A3im), dt3_b, ALU.mult, ["A3im", "dt3"], ["L3y"])
    TS("dve", y3, y3, 1.0 / TWO_PI, None, ALU.mult, None, ["L3y"], ["L3y"])
    ACT(dec3, m3, AF.Exp, ["m3"], ["dec3"])
    frac_sincos(y3, tmp3, s3i[:], sin3, cos3, "L3")
    TT("dve", cos3, cos3, dec3, ALU.mult, ["L3cos", "dec3"], ["lbr"])
    TT("dve", sin3, sin3, dec3, ALU.mult, ["L3sin", "dec3"], ["lbi"])
    TS("dve", nr3, cos3, -1.0, None, ALU.add, None, ["lbr"], ["nr3"])
    TT("dve", den3, A3re, A3re, ALU.mult, ["A3re"], ["den3"])
    TT("dve", u3a, A3im, A3im, ALU.mult, ["A3im"], ["u3a"])
    TT("dve", den3, den3, u3a, ALU.add, ["den3", "u3a"], ["den3"])
    RECIP(den3, den3, ["den3"], ["den3"])
    TT("dve", u3a, nr3, A3re, ALU.mult, ["nr3", "A3re", "den3"], ["u3a"])
    TT("dve", u3b, sin3, A3im, ALU.mult, ["lbi", "A3im"], ["u3b"])
    TT("dve", qre3, u3a, u3b, ALU.add, ["u3a", "u3b"], ["qre3"])
    TT("dve", qre3, qre3, den3, ALU.mult, ["qre3", "den3"], ["qre3"])
    TT("dve", u3a, sin3, A3re, ALU.mult, ["lbi", "A3re", "qre3"], ["u3a"])
    TT("dve", u3b, nr3, A3im, ALU.mult, ["nr3", "A3im", "qre3"], ["u3b"])
    TT("dve", qim3, u3a, u3b, ALU.subtract, ["u3a", "u3b"], ["qim3"])
    TT("dve", qim3, qim3, den3, ALU.mult, ["qim3", "den3"], ["qim3"])
    qre_b = v3(qre3).unsqueeze(2).broadcast_to([128, 4, 8, 64])
    qim_b = v3(qim3).unsqueeze(2).broadcast_to([128, 4, 8, 64])
    T1v4 = T1[:, :].rearrange("p (c g q) -> p c g q", c=4, g=8)
    T2v4 = T2[:, :].rearrange("p (c g q) -> p c g q", c=4, g=8)
    TT("dve", T1v4, Bst[:, 0], qre_b, ALU.mult, ["Bst", "qre3", "ta", "te"], ["T1"])
    TT("dve", T2v4, Bst[:, 1], qim_b, ALU.mult, ["Bst", "qim3", "L1y", "L2y"], ["T2"])
    TT("dve", Bbd[:, :, 0, :], T1[:, :].rearrange("p (c x) -> p c x", c=4), T2[:, :].rearrange("p (c x) -> p c x", c=4), ALU.subtract, ["T1", "T2"], ["Bbd"])
    TT("dve", T1v4, Bst[:, 1], qre_b, ALU.mult, ["Bst", "qre3", "Bbd"], ["T1"])
    TT("dve", T2v4, Bst[:, 0], qim_b, ALU.mult, ["Bst", "qim3", "Bbd"], ["T2"])
    TT("dve", Bbd[:, :, 1, :], T1[:, :].rearrange("p (c x) -> p c x", c=4), T2[:, :].rearrange("p (c x) -> p c x", c=4), ALU.add, ["T1", "T2"], ["Bbd"])
    CP("dve", Ccat[:, 0], Cst[:, 0], ["Cst"], ["Ccat"])
    TS("dve", CcatN[:], Cst[:, 0], -1.0, None, ALU.mult, None, ["Cst"], ["CcatN"])
    TS("dve", negtri[:], tri[:], -1.0, None, ALU.mult, None, ["tri"], ["negtri"])
    TT("dve", l128[:, 0, :], te[:, :, 127], te[:, :, 1], ALU.mult, ["te"], ["l128a"])
    TT("dve", l128[:, 1, :], tf[:, :, 127], tf[:, :, 1], ALU.mult, ["tf"], ["l128b"])
    TT("dve", l128[:, 4, :], l128[:, 0, :], l128[:, 1, :], ALU.subtract, ["l128a", "l128b"], ["l128"])
    TT("dve", l128[:, 2, :], te[:, :, 127], tf[:, :, 1], ALU.mult, ["te", "tf"], ["l128c"])
    TT("dve", l128[:, 3, :], tf[:, :, 127], te[:, :, 1], ALU.mult, ["te", "tf"], ["l128d"])
    TT("dve", l128[:, 5, :], l128[:, 2, :], l128[:, 3, :], ALU.add, ["l128c", "l128d"], ["l128"])
    TS("dve", Ccat[:, 1], Cst[:, 1], -1.0, None, ALU.mult, None, ["Cst"], ["Ccat"])
    DMA("sp", dcol[:], dap(dd_d, 0, [[1, 128], [128, 4]]), (), ["dcol"], "dcol", slow=True)
    for ct in range(4):
        TS("dve", Ddiag[:, ct, :], ident[:], dcol[:, ct:ct + 1], None, ALU.mult, None, ["ident", "dcol"], ["Ddiag"])
    MS("dve", cre[0][:], 0.0, ["c0_0", "c0_1", "c0_2", "c0_3"])
    MS("dve", cim[0][:], 0.0, ["c0_0", "c0_1", "c0_2", "c0_3"])
    S.barrier()
    M.release(m_work2)
    hT = M.alloc("hT", [128, 8, 512], BF16)
    xt = [M.alloc("xt", [128, 1024], F32) for _ in range(2)]
    xn = [M.alloc("xn", [128, 1024], BF16) for _ in range(2)]
    uT = M.alloc("uT", [128, 4, 512], BF16)
    gsT = M.alloc("gsT", [128, 4, 512], BF16)
    gT = M.alloc("gT", [128, 4, 512], BF16)
    xs = [M.alloc("xs", [128, 512], F32) for _ in range(2)]
    ys = [M.alloc("ys", [128, 512], F32) for _ in range(2)]
    pa_ = [[M.alloc("pa_", [128, 512], BF16) for _ in range(4)] for _ in range(2)]
    pb_ = [[M.alloc("pb_", [128, 4, 128], BF16) for _ in range(4)] for _ in range(2)]
    sr = [M.alloc("sr", [128, 4, 128], F32) for _ in range(2)]
    si = [M.alloc("si", [128, 4, 128], F32) for _ in range(2)]
    sg = M.alloc("sg", [128, 512], F32)
    xl = M.alloc("xl", [128, 4, 4], F32)

    def ssm_A(st, cc, ct, u):
        tok = slice(cc * 128, (cc + 1) * 128)
        ub = u % 2
        MM(PS[2][:, :], uT[:, ct, tok], Bbd[:, ct, 0, :], True, True, ["uT", "Bbd"], ["ps2"])
        MM(PS[3][:, :], uT[:, ct, tok], Bbd[:, ct, 1, :], True, True, ["uT", "Bbd"], ["ps3"])
        CP("act", xs[ub][:], PS[2][:, :], ["ps2"], [f"xs{ub}"])
        CP("act", ys[ub][:], PS[3][:, :], ["ps3"], [f"ys{ub}"])
        a_ = ta[:, ct * 512:(ct + 1) * 512]
        b_ = tb[:, ct * 512:(ct + 1) * 512]
        p = pa_[ub]
        TT("dve", p[0][:], xs[ub][:], a_, ALU.mult, [f"xs{ub}", "ta"], [f"pa{ub}0"])
        TT("pool", p[1][:], ys[ub][:], b_, ALU.mult, [f"ys{ub}", "tb"], [f"pa{ub}1"])
        TT("dve", p[2][:], ys[ub][:], a_, ALU.mult, [f"ys{ub}", "ta"], [f"pa{ub}2"])
        TT("pool", p[3][:], xs[ub][:], b_, ALU.mult, [f"xs{ub}", "tb"], [f"pa{ub}3"])

    def ssm_A2(st, cc, ct, u):
        ub = u % 2
        p = pa_[ub]
        for pr in range(4):
            cs = slice(pr * 128, (pr + 1) * 128)
            MM(PS[4 + 2 * ub][:, cs], p[0][:, cs], tri[:], True, False, [f"pa{ub}0", "tri"], [f"ps{4 + 2 * ub}"])
            MM(PS[4 + 2 * ub][:, cs], p[1][:, cs], negtri[:], False, True, [f"pa{ub}1", "negtri"], [f"ps{4 + 2 * ub}"])
            MM(PS[5 + 2 * ub][:, cs], p[2][:, cs], tri[:], True, False, [f"pa{ub}2", "tri"], [f"ps{5 + 2 * ub}"])
            MM(PS[5 + 2 * ub][:, cs], p[3][:, cs], tri[:], False, True, [f"pa{ub}3", "tri"], [f"ps{5 + 2 * ub}"])

    def ssm_B(st, cc, ct, u):
        tok = slice(cc * 128, (cc + 1) * 128)
        ci = st * 4 + cc
        ub = u % 2
        cr_in, ci_in = cre[ci % 2], cim[ci % 2]
        cr_out, ci_out = cre[(ci + 1) % 2], cim[(ci + 1) % 2]
        kin, kout = f"c{ci % 2}_{ct}", f"c{(ci + 1) % 2}_{ct}"
        for pr in range(4):
            pair = 4 * ct + pr
            cs = slice(pr * 128, (pr + 1) * 128)
            ACT(sr[ub][:, pr, :], PS[4 + 2 * ub][:, cs], AF.Identity, [f"ps{4 + 2 * ub}", kin], [f"sr{ub}"], bias=cr_in[:, pair:pair + 1])
            ACT(si[ub][:, pr, :], PS[5 + 2 * ub][:, cs], AF.Identity, [f"ps{5 + 2 * ub}", kin], [f"si{ub}"], bias=ci_in[:, pair:pair + 1])
        eh = te[:, 4 * ct:4 * ct + 4, :]
        fh = tf[:, 4 * ct:4 * ct + 4, :]
        q = pb_[ub]
        TT("dve", q[0][:], sr[ub][:], eh, ALU.mult, [f"sr{ub}", "te"], [f"pb{ub}0"])
        TT("pool", q[1][:], si[ub][:], fh, ALU.mult, [f"si{ub}", "tf"], [f"pb{ub}1"])
        TT("dve", q[2][:], si[ub][:], eh, ALU.mult, [f"si{ub}", "te"], [f"pb{ub}2"])
        TT("pool", q[3][:], sr[ub][:], fh, ALU.mult, [f"sr{ub}", "tf"], [f"pb{ub}3"])
        E = l128[:, 4, 4 * ct:4 * ct + 4]
        F_ = l128[:, 5, 4 * ct:4 * ct + 4]
        ps_ = slice(4 * ct, 4 * ct + 4)
        TT("pool", xl[:, 0, :], E, sr[ub][:, :, 127], ALU.mult, ["l128", f"sr{ub}"], ["xl0"])
        TT("pool", xl[:, 1, :], F_, si[ub][:, :, 127], ALU.mult, ["l128", f"si{ub}"], ["xl1"])
        TT("pool", cr_out[:, ps_], xl[:, 0, :], xl[:, 1, :], ALU.subtract, ["xl0", "xl1"], [kout])
        TT("pool", xl[:, 2, :], E, si[ub][:, :, 127], ALU.mult, ["l128", f"si{ub}"], ["xl2"])
        TT("pool", xl[:, 3, :], F_, sr[ub][:, :, 127], ALU.mult, ["l128", f"sr{ub}"], ["xl3"])
        TT("pool", ci_out[:, ps_], xl[:, 2, :], xl[:, 3, :], ALU.add, ["xl2", "xl3"], [kout])

    def ssm_C(st, cc, ct, u):
        tok = slice(cc * 128, (cc + 1) * 128)
        ub = u % 2
        q = pb_[ub]
        ysl = slice(ct * 128, (ct + 1) * 128)
        for pr in range(4):
            pair = 4 * ct + pr
            MM(PS[0][:, ysl], Ccat[:, 0, pair, :], q[0][:, pr, :], pr == 0, False, ["Ccat", f"pb{ub}0"], ["ps0"], fuse=True)
            MM(PS[0][:, ysl], CcatN[:, pair, :], q[1][:, pr, :], False, False, ["CcatN", f"pb{ub}1"], ["ps0"], fuse=True)
            MM(PS[0][:, ysl], Ccat[:, 1, pair, :], q[2][:, pr, :], False, False, ["Ccat", f"pb{ub}2"], ["ps0"], fuse=True)
            MM(PS[0][:, ysl], Ccat[:, 1, pair, :], q[3][:, pr, :], False, False, ["Ccat", f"pb{ub}3"], ["ps0"], fuse=True)
        MM(PS[0][:, ysl], Ddiag[:, ct, :], uT[:, ct, tok], False, True, ["Ddiag", "uT"], ["ps0"], fuse=True)
        if ct == 3:
            ACT(gT[:, :, tok], PS[0][:, :].rearrange("p (c t) -> p c t", c=4), AF.Gelu, ["ps0"], ["gT"])

    for st in range(NST):
        make_hT(st, 1, hT, xt, xn, [0, 1])
        def us_mm(c):
            zb = 1 + (c % 2)
            for kt in range(8):
                MM(PS[zb][:, :], w2[:, kt, c * 128:(c + 1) * 128], hT[:, kt, :], kt == 0, kt == 7, ["w2", "hT"], [f"ps{zb}"], fuse=True)
        us_mm(0)
        for c in range(8):
            if c + 1 < 8:
                us_mm(c + 1)
            zb = 1 + (c % 2)
            if c < 4:
                CP("act", uT[:, c, :], PS[zb][:, :], [f"ps{zb}"], ["uT"])
            else:
                ACT(gsT[:, c - 4, :], PS[zb][:, :], AF.Silu, [f"ps{zb}"], ["gsT"])
        units = [(cc, ct) for cc in range(4) for ct in range(4)]
        NU = len(units)
        for k in range(-2, NU):
            if 0 <= k + 2 < NU:
                ssm_A(st, units[k + 2][0], units[k + 2][1], k + 2)
            if 0 <= k + 1 < NU:
                ssm_A2(st, units[k + 1][0], units[k + 1][1], k + 1)
                ssm_B(st, units[k + 1][0], units[k + 1][1], k + 1)
            if 0 <= k < NU:
                ssm_C(st, units[k][0], units[k][1], k)
        for c in range(4):
            zb = 1 + (c % 2)
            for kt in range(4):
                MM(PS[zb][:, :], gluw[:, kt, c * 128:(c + 1) * 128], gT[:, kt, :], kt == 0, kt == 3, ["gluw", "gT"], [f"ps{zb}"], fuse=True)
            ACT(sg[:], PS[zb][:, :], AF.Sigmoid, [f"ps{zb}", "glub"], ["sg"], bias=glub[:, c:c + 1])
            TT("dve", sg[:], sg[:], gT[:, c, :], ALU.mult, ["sg", "gT"], ["sg"])
            TT("dve", OST[:, c, st * 512:(st + 1) * 512], sg[:], gsT[:, c, :], ALU.mult, ["sg", "gsT"], ["OST", "Bst", "Cst"])
    if dbg == "p2":
        DMA("sp", dbg_d[:, :], OSTm[:, :], ["OST"], ["outd"], "OSTm")
        NOP(["outd"])
        return finish(nc, S)

    S.barrier()
    M.release(m_glob)
    w3 = M.alloc("w3", [128, 8, 2560], BF16)
    pa = M.alloc("pa", [128, 4, 1024], BF16)
    psw = M.alloc("psw", [128, 4, 1024], BF16)
    wo = M.alloc("wo", [128, 8, 1024], BF16)
    mb = M.alloc("mb", [128, 16], F32)
    m_work3 = M.mark()
    stg = [M.alloc("stg", [128, 2048], F32) for _ in range(2)]
    load_weight(w3[:, :, 0:512], w_in_d, 0, 8, 1536, 512, stg, True, "w3")
    load_weight(w3[:, :, 512:2560], w_in_d, 0, 8, 3072, 2048, stg, True, "w3")
    load_weight(pa, pa_d, 0, 4, 0, 1024, stg, False, "pa")
    load_weight(psw, ps_d, 0, 4, 0, 1024, stg, False, "psw")
    load_weight(wo, wo_d, 0, 8, 0, 1024, stg, False, "wo")
    DMA("sp", mb[:], dap(mb_d, 0, [[1, 128], [128, 16]]), (), ["mb"], "mb", slow=True)
    S.barrier()
    M.release(m_work3)
    hT = M.alloc("hT", [128, 8, 512], BF16)
    xt = [M.alloc("xt", [128, 1024], F32) for _ in range(2)]
    xn = [M.alloc("xn", [128, 1024], BF16) for _ in range(2)]
    mT = M.alloc("mT", [128, 8, 512], BF16)
    g1 = M.alloc("g1", [128, 512], F32)
    g2t = M.alloc("g2t", [128, 512], F32)
    m1 = M.alloc("m1", [128, 512], F32)
    xres = M.alloc("xres", [128, 1024], F32)
    ot = [M.alloc("ot", [128, 512], F32) for _ in range(2)]
    outkeys = []
    for st in range(NST):
        stc = slice(st * 512, (st + 1) * 512)
        make_hT(st, 2, hT, xt, xn, [0, 7])
        for h in range(4):
            for kt in range(8):
                MM(PS[1][:, :], w3[:, kt, h * 128:(h + 1) * 128], hT[:, kt, :], kt == 0, kt == 7, ["w3", "hT"], ["ps1"], fuse=True)
            ACT(g1[:], PS[1][:, :], AF.Silu, ["ps1"], ["g1"])
            TT("dve", OAT[:, h, stc], OAT[:, h, stc], g1[:], ALU.mult, ["OAT", "g1"], ["OAT"])
        for c in range(8):
            for h in range(4):
                MM(PS[1][:, :], pa[:, h, c * 128:(c + 1) * 128], OAT[:, h, stc], h == 0, h == 3, ["pa", "OAT"], ["ps1"], fuse=True)
            for kt in range(8):
                MM(PS[2][:, :], w3[:, kt, 512 + c * 128:512 + (c + 1) * 128], hT[:, kt, :], kt == 0, kt == 7, ["w3", "hT"], ["ps2"], fuse=True)
            ACT(g1[:], PS[2][:, :], AF.Sigmoid, ["ps2", "mb"], ["g1"], bias=mb[:, c:c + 1])
            TT("dve", m1[:], PS[1][:, :], g1[:], ALU.mult, ["ps1", "g1"], ["m1"])
            for k in range(4):
                MM(PS[3][:, :], psw[:, k, c * 128:(c + 1) * 128], OST[:, k, stc], k == 0, k == 3, ["psw", "OST"], ["ps3"], fuse=True)
            for kt in range(8):
                MM(PS[4][:, :], w3[:, kt, 1536 + c * 128:1536 + (c + 1) * 128], hT[:, kt, :], kt == 0, kt == 7, ["w3", "hT"], ["ps4"], fuse=True)
            ACT(g2t[:], PS[4][:, :], AF.Sigmoid, ["ps4", "mb"], ["g2t"], bias=mb[:, 8 + c:9 + c])
            TT("dve", g2t[:], PS[3][:, :], g2t[:], ALU.mult, ["ps3", "g2t"], ["g2t"])
            TT("pool", mT[:, c, :], m1[:], g2t[:], ALU.add, ["m1", "g2t"], ["mT"])
        for tt in range(4):
            row = st * 512 + tt * 128
            DMA("sp", xres[:], x_d[row:row + 128, :], (), ["xres"], "xres")
            for half in range(2):
                for kt in range(8):
                    MM(PS[5 + half][:, :], mT[:, kt, tt * 128:(tt + 1) * 128], wo[:, kt, half * 512:(half + 1) * 512], kt == 0, kt == 7,
                       ["mT", "wo"], [f"ps{5 + half}"])
                TT("dve", ot[half][:], PS[5 + half][:, :], xres[:, half * 512:(half + 1) * 512], ALU.add, [f"ps{5 + half}", "xres"], [f"ot{half}"])
                ok = f"outd{st}_{tt}_{half}"
                DMA("sp", out_d[row:row + 128, half * 512:(half + 1) * 512], ot[half][:], [f"ot{half}"], [ok], f"ot{half}")
                outkeys.append(ok)
    NOP(outkeys)
    return finish(nc, S)


def finish(nc, S):
    import contextlib
    with contextlib.ExitStack() as stk:
        S._sem_ctx = {e: stk.enter_context(nc.semaphore(f"s_{e}")) for e in S.ENGS}
        S._dsem = {k: stk.enter_context(nc.semaphore(f"d_{k}")) for k in S.dma_count}
        block = stk.enter_context(nc.Block())
        S.emit(block)
    return nc


_NC_CACHE = {}


def _core_inputs(inputs, b, consts):
    m = {"x": np.ascontiguousarray(inputs["x"][b], dtype=np.float32)}
    for k, v in inputs.items():
        if k == "x":
            continue
        v = np.asarray(v, dtype=np.float32)
        if k == "rel_bias_table":
            m[k] = np.ascontiguousarray(v)
            continue
        v0 = v[0]
        if k in ("w_in", "ssm_glu_w", "proj_attn", "proj_ssm", "w_out"):
            m[k] = np.ascontiguousarray(v0)
        else:
            m[k] = np.ascontiguousarray(v0).reshape(-1)
    m.update(consts)
    return m


def kernel(**inputs):
    if "nc" not in _NC_CACHE:
        _NC_CACHE["nc"] = build_nc()
    nc = _NC_CACHE["nc"]
    consts = host_consts()
    in_maps = [_core_inputs(inputs, b, consts) for b in range(8)]
    res = run_bass_kernel_spmd(nc, in_maps, core_ids=list(range(8)))
    out = np.stack([np.asarray(res.results[b]["out"], dtype=np.float32) for b in range(8)], axis=0)
    return out
```

```python
import math
import numpy as np
import ml_dtypes
import concourse.bass as bass
import concourse.mybir as mybir
from concourse.bass_utils import run_bass_kernel_spmd

F32 = mybir.dt.float32
BF16 = mybir.dt.bfloat16
AF = mybir.ActivationFunctionType
ALU = mybir.AluOpType
AX = mybir.AxisListType


class _Op:
    __slots__ = ("eng", "fn", "idx", "dma", "dkey", "deps", "signal", "sigval", "sem", "fuse")

    def __init__(self, eng, fn, idx, dma, dkey):
        self.eng = eng
        self.fn = fn
        self.idx = idx
        self.dma = dma
        self.dkey = dkey
        self.deps = {}
        self.signal = False
        self.sigval = 0
        self.sem = None
        self.fuse = False


class Sched:
    ENGS = ("pe", "act", "dve", "pool", "sp")
    FUSE_WAITS = True

    def __init__(self, nc):
        self.nc = nc
        self.ops = {e: [] for e in self.ENGS}
        self.last_w = {}
        self.readers = {}
        self.dma_count = {}

    def add(self, eng, fn, reads=(), writes=(), dma=False, dkey=None, fuse=None):
        op = _Op(eng, fn, len(self.ops[eng]), dma, dkey)
        if fuse is None:
            fuse = (eng in ("act", "dve", "pool")) and not dma
        op.fuse = fuse and self.FUSE_WAITS
        reads = tuple(reads) + ("__B__",)
        for k in reads:
            w = self.last_w.get(k)
            if w is not None:
                op.deps[w] = True
        for k in writes:
            w = self.last_w.get(k)
            if w is not None and w not in op.deps:
                op.deps[w] = False
            rd = self.readers.get(k)
            if rd:
                for r in rd.get("eng", {}).values():
                    if r is not op and r not in op.deps:
                        op.deps[r] = False
                for r in rd.get("dma", []):
                    if r is not op and r not in op.deps:
                        op.deps[r] = False
        for k in reads:
            rd = self.readers.setdefault(k, {"eng": {}, "dma": []})
            if dma:
                rd["dma"].append(op)
            else:
                rd["eng"][eng] = op
        for k in writes:
            self.last_w[k] = op
            self.readers[k] = {"eng": {}, "dma": []}
        if dma:
            assert dkey is not None
            n = self.dma_count.get(dkey, 0) + 1
            self.dma_count[dkey] = n
            op.sigval = 16 * n
        self.ops[eng].append(op)
        return op

    def barrier(self):
        t = self._bar_tile
        self.add("dve", lambda e: e.memset(t[:, 0:1], 0.0), writes=("__B__",))

    def _needs_wait(self, op, dep, is_raw):
        if dep.dma:
            return True
        if dep.eng != op.eng:
            return True
        if op.dma:
            return True
        if op.eng == "pe":
            return False
        return is_raw and (op.idx - dep.idx) <= 3

    def emit(self, block, extra_final=None):
        nc = self.nc
        for e in self.ENGS:
            for op in self.ops[e]:
                for dep, is_raw in op.deps.items():
                    if self._needs_wait(op, dep, is_raw) and not dep.dma:
                        dep.signal = True
        sems = {e: self._sem_ctx[e] for e in self.ENGS}
        dsems = self._dsem
        for e in self.ENGS:
            n = 0
            for op in self.ops[e]:
                if op.dma:
                    op.sem = dsems[op.dkey]
                else:
                    op.sem = sems[e]
                    if op.signal:
                        n += 1
                        op.sigval = n

        def run_engine(ename, eng):
            known = {}
            for op in self.ops[ename]:
                need = {}
                for dep, is_raw in op.deps.items():
                    if not self._needs_wait(op, dep, is_raw):
                        continue
                    s = dep.sem
                    v = dep.sigval
                    if need.get(s, 0) < v:
                        need[s] = v
                pend = [(s, v) for s, v in need.items() if known.get(s, 0) < v]
                fused = None
                if op.fuse and pend:
                    fused = pend.pop()
                for s, v in pend:
                    eng.wait_ge(s, v)
                    known[s] = v
                ins = op.fn(eng)
                if fused is not None:
                    ins._wait_ge(fused[0], fused[1])
                    known[fused[0]] = fused[1]
                if op.dma:
                    ins.then_inc(op.sem, 16)
                elif op.signal:
                    ins.then_inc(op.sem, 1)
            if extra_final is not None:
                extra_final(ename, eng, known)

        @block.tensor
        def _(eng):
            run_engine("pe", eng)

        @block.scalar
        def _(eng):
            run_engine("act", eng)

        @block.vector
        def _(eng):
            run_engine("dve", eng)

        @block.gpsimd
        def _(eng):
            run_engine("pool", eng)

        @block.sync
        def _(eng):
            run_engine("sp", eng)


L = 4096
D = 1024
NST = 8
EPS = 1e-6
LAM_INIT = 0.8 - 0.6 * math.exp(-0.3 * 0)
TWO_PI = 2.0 * math.pi
NEG = -30000.0


def _bucket_np(rel):
    nb = 16
    me = 8
    side = np.where(rel > 0, nb, 0)
    n = np.abs(rel)
    nf = np.maximum(n, 1).astype(np.float32)
    large = me + (np.log(nf / np.float32(me)).astype(np.float32) / np.float32(math.log(128 / 8))
                  * np.float32(nb - me)).astype(np.int32)
    large = np.minimum(large, nb - 1)
    return side + np.where(n < me, n, large)


def host_consts():
    c = {}
    c["c_ident"] = np.eye(128, dtype=np.float32).astype(ml_dtypes.bfloat16)
    bo = np.zeros((128, 128), np.float32)
    bo[:64, :64] = 1.0 / 64
    bo[64:, 64:] = 1.0 / 64
    c["c_bones"] = bo.astype(ml_dtypes.bfloat16)
    c["c_J"] = np.ascontiguousarray(np.eye(128, dtype=np.float32)[::-1])
    rel = np.arange(-255, 128)
    b = _bucket_np(rel)
    oh = np.zeros((32, 384), np.float32)
    oh[b, np.arange(383)] = 1.0
    c["c_oh"] = oh
    k = np.arange(128)[:, None]
    q = np.arange(128)[None, :]
    c["c_maskD"] = np.where((k // 64) <= (q // 64), 0.0, NEG).astype(np.float32)
    c["c_iotap"] = np.stack([np.arange(128), -np.arange(128)], 1).astype(np.float32)
    c["c_iotat"] = np.tile(np.arange(128, dtype=np.float32)[None, :], (128, 1))
    c["c_tri"] = (np.arange(128)[:, None] <= np.arange(128)[None, :]).astype(np.float32).astype(ml_dtypes.bfloat16)
    return c


class Mem:
    def __init__(self, nc, start=16640, end=229376):
        self.nc = nc
        self.p = start
        self.end = end
        self.n = 0

    def alloc(self, name, shape, dtype):
        esz = 4 if dtype in (F32, mybir.dt.int32) else 2
        size = int(np.prod(shape[1:])) * esz
        size = (size + 63) // 64 * 64
        assert self.p + size <= self.end, f"SBUF overflow allocating {name}: {self.p}+{size} > {self.end}"
        self.n += 1
        t = self.nc.alloc_sbuf_tensor_at(f"{name}_{self.n}", list(shape), dtype, offset=self.p)
        self.p += size
        return t

    def mark(self):
        return self.p

    def release(self, m):
        self.p = m


def build_nc(dbg=None):
    nc = bass.Bass("TRN2", target_bir_lowering=False)
    S = Sched(nc)
    M = Mem(nc)

    def din(name, shape, dt=F32):
        return nc.dram_tensor(name, list(shape), dt, kind="ExternalInput").ap()

    x_d = din("x", [L, D])
    w_in_d = din("w_in", [D, 5120])
    ng_d = din("norm_gain", [D])
    mb_d = din("merge_gate_b", [2048])
    qg_d = din("q_norm_gain", [64])
    kg_d = din("k_norm_gain", [64])
    lq1_d = din("lambda_q1", [64]); lk1_d = din("lambda_k1", [64])
    lq2_d = din("lambda_q2", [64]); lk2_d = din("lambda_k2", [64])
    sg_d = din("diff_subln_gain", [128])
    rb_d = din("rel_bias_table", [32, 4])
    are_d = din("ssm_A_re", [2048]); aim_d = din("ssm_A_im", [2048])
    ldt_d = din("ssm_log_dt", [32])
    bre_d = din("ssm_B_re", [32 * 64 * 16]); bim_d = din("ssm_B_im", [32 * 64 * 16])
    cre_d = din("ssm_C_re", [32 * 16 * 64]); cim_d = din("ssm_C_im", [32 * 16 * 64])
    dd_d = din("ssm_D", [512])
    gw_d = din("ssm_glu_w", [512, 512])
    gb_d = din("ssm_glu_b", [512])
    pa_d = din("proj_attn", [512, 1024])
    ps_d = din("proj_ssm", [512, 1024])
    wo_d = din("w_out", [1024, 1024])
    c_ident_d = din("c_ident", [128, 128], BF16)
    c_bones_d = din("c_bones", [128, 128], BF16)
    c_J_d = din("c_J", [128, 128])
    c_oh_d = din("c_oh", [32, 384])
    c_maskD_d = din("c_maskD", [128, 128])
    c_iotap_d = din("c_iotap", [128, 2])
    c_iotat_d = din("c_iotat", [128, 128])
    c_tri_d = din("c_tri", [128, 128], BF16)
    out_d = nc.dram_tensor("out", [L, D], F32, kind="ExternalOutput").ap()
    fsc_d = nc.dram_tensor("fscratch", [4, 384], F32).ap()
    dbg_d = None
    if dbg:
        dbg_d = nc.dram_tensor("dbg", [128, 4 * L], BF16, kind="ExternalOutput").ap()

    def dap(t, off, pat):
        return bass.AP(t.tensor, off, [list(p) for p in pat])

    PS = [nc.alloc_psum_tensor(f"psb{i}", [128, 512], F32) for i in range(8)]

    def MM(out, lhsT, rhs, start, stop, r, w, fuse=False):
        S.add("pe", lambda e: e.matmul(out, lhsT=lhsT, rhs=rhs, start=start, stop=stop), r, w, fuse=fuse)

    def TR(out, in_, r, w):
        S.add("pe", lambda e: e.transpose(out=out, in_=in_, identity=ident[:]), tuple(r) + ("ident",), w)

    def ACT(out, in_, func, r, w, bias=0.0, scale=1.0, accum=None):
        if accum is None:
            S.add("act", lambda e: e.activation(out=out, in_=in_, func=func, bias=bias, scale=scale), r, w)
        else:
            S.add("act", lambda e: e.activation(out=out, in_=in_, func=func, bias=bias, scale=scale, accum_out=accum), r, w)

    def TS(eng, out, in0, s1, s2, op0, op1, r, w):
        if s2 is None:
            S.add(eng, lambda e: e.tensor_scalar(out=out, in0=in0, scalar1=s1, scalar2=None, op0=op0), r, w)
        else:
            S.add(eng, lambda e: e.tensor_scalar(out=out, in0=in0, scalar1=s1, scalar2=s2, op0=op0, op1=op1), r, w)

    def TT(eng, out, in0, in1, op, r, w):
        S.add(eng, lambda e: e.tensor_tensor(out=out, in0=in0, in1=in1, op=op), r, w)

    def STT(eng, out, in0, scalar, in1, op0, op1, r, w):
        S.add(eng, lambda e: e.scalar_tensor_tensor(out=out, in0=in0, scalar=scalar, in1=in1, op0=op0, op1=op1), r, w)

    def CP(eng, out, in_, r, w):
        if eng == "act":
            S.add("act", lambda e: e.copy(out=out, in_=in_), r, w)
        else:
            S.add(eng, lambda e: e.tensor_copy(out=out, in_=in_), r, w)

    def RSUM(out, in_, r, w):
        S.add("dve", lambda e: e.reduce_sum(out=out, in_=in_, axis=AX.X), r, w)

    def RECIP(out, in_, r, w):
        S.add("dve", lambda e: e.reciprocal(out=out, in_=in_), r, w)

    def NOP(r):
        S.add("sp", lambda e: e.nop(), r, [])

    def MS(eng, ap, val, w):
        S.add(eng, lambda e: e.memset(ap, val), (), w)

    def DMA(eng, out, in_, r, w, dkey, slow=False):
        if slow:
            S.add(eng, lambda e: e.dma_start(out=out, in_=in_, allow_slow_non_contiguous=True), r, w, dma=True, dkey=dkey)
        else:
            S.add(eng, lambda e: e.dma_start(out=out, in_=in_), r, w, dma=True, dkey=dkey)

    bar = M.alloc("bar", [128, 16], F32)
    S._bar_tile = bar
    ident = M.alloc("ident", [128, 128], BF16)
    bones = M.alloc("bones", [128, 128], BF16)
    tri = M.alloc("tri", [128, 128], BF16)
    iotap = M.alloc("iotap", [128, 2], F32)
    iotat = M.alloc("iotat", [128, 128], F32)
    gcol = M.alloc("gcol", [128, 8], F32)
    rs_all = M.alloc("rs_all", [128, 3 * 32 * 2], F32)
    OAT = M.alloc("OAT", [128, 4, L], BF16)
    OSTm = M.alloc("OSTm", [128, 4 * L], BF16)
    OST = OSTm[:, :].rearrange("p (c t) -> p c t", c=4)
    stag32 = OSTm[:, :].bitcast(F32)
    Bst = stag32[:, 0:4096].rearrange("p (r c g q) -> p r c g q", r=2, c=4, g=8)
    Cst = stag32[:, 4096:8192].rearrange("p (r a c) -> p r a c", r=2, a=16)

    DMA("sp", ident[:], c_ident_d[:, :], (), ["ident"], "ident")
    DMA("sp", bones[:], c_bones_d[:, :], (), ["bones"], "bones")
    DMA("sp", tri[:], c_tri_d[:, :], (), ["tri"], "tri")
    DMA("sp", iotap[:], c_iotap_d[:, :], (), ["iotap"], "iotap")
    DMA("sp", iotat[:], c_iotat_d[:, :], (), ["iotat"], "iotat")
    DMA("sp", gcol[:], dap(ng_d, 0, [[1, 128], [128, 8]]), (), ["gcol"], "gcol", slow=True)

    MS("dve", stag32[:, 0:4096], 0.0, ["Bst"])
    MS("pool", stag32[:, 4096:8192], 0.0, ["Cst"])
    bc_dmas = []
    for g in range(32):
        for ri in range(2):
            bc_dmas.append((g, ri))

    def emit_bc(n):
        for _ in range(n):
            if not bc_dmas:
                return
            g, ri = bc_dmas.pop(0)
            ct, g8 = g // 8, g % 8
            pair, g2 = g // 2, g % 2
            bd = (bre_d, bim_d)[ri]
            cd = (cre_d, cim_d)[ri]
            DMA("sp", Bst[16 * g8:16 * g8 + 16, ri, ct, g8, :], dap(bd, g * 1024, [[1, 16], [16, 64]]),
                (), ["Bst"], "Bst", slow=True)
            DMA("sp", Cst[64 * g2:64 * g2 + 64, ri, pair, (g % 8) * 16:(g % 8) * 16 + 16],
                dap(cd, g * 1024, [[1, 64], [64, 16]]), (), ["Cst"], "Cst", slow=True)

    def load_weight(dst, src_d, row0, nrows_tiles, col0, ncols, stg, fold_gain, keyw):
        i = 0
        for kt in range(nrows_tiles):
            for c0 in range(0, ncols, 2048):
                cw = min(2048, ncols - c0)
                sb = stg[i % 2]
                sk = f"stg{i % 2}"
                DMA("sp", sb[:, 0:cw], src_d[row0 + kt * 128: row0 + (kt + 1) * 128, col0 + c0: col0 + c0 + cw],
                    (), [sk], sk)
                if fold_gain:
                    if i % 2 == 0:
                        ACT(dst[:, kt, c0:c0 + cw], sb[:, 0:cw], AF.Copy, [sk, "gcol"], [keyw], scale=gcol[:, kt:kt + 1])
                    else:
                        TS("dve", dst[:, kt, c0:c0 + cw], sb[:, 0:cw], gcol[:, kt:kt + 1], None, ALU.mult, None, [sk, "gcol"], [keyw])
                else:
                    if i % 2 == 0:
                        CP("act", dst[:, kt, c0:c0 + cw], sb[:, 0:cw], [sk], [keyw])
                    else:
                        CP("dve", dst[:, kt, c0:c0 + cw], sb[:, 0:cw], [sk], [keyw])
                i += 1

    def make_hT(st, phase, hT, xt, xn, tbanks):
        for tt in range(4):
            i = st * 4 + tt
            b = i % len(xt)
            nb_ = i % len(xn)
            tbk = tbanks[i % len(tbanks)]
            psT = PS[tbk][:, 0:512].bitcast(BF16)
            row = st * 512 + tt * 128
            col = (phase * 32 + i) * 2
            ss = rs_all[:, col:col + 1]
            rs = rs_all[:, col + 1:col + 2]
            xk, nk = f"xt{b}", f"xn{nb_}"
            DMA("sp", xt[b][:], x_d[row:row + 128, :], (), [xk], xk)
            ACT(xn[nb_][:], xt[b][:], AF.Square, [xk], [nk, f"ss{col}"], accum=ss)
            ACT(rs, ss, AF.Ln, [f"ss{col}"], [f"rs{col}"], bias=EPS, scale=1.0 / D)
            ACT(rs, rs, AF.Exp, [f"rs{col}"], [f"rs{col}"], scale=-0.5)
            TS("dve", xn[nb_][:], xt[b][:], rs, None, ALU.mult, None, [xk, f"rs{col}"], [nk])
            for kt in range(8):
                TR(psT[:, kt * 128:(kt + 1) * 128], xn[nb_][:, kt * 128:(kt + 1) * 128], [nk], [f"ps{tbk}"])
            CP("dve", hT[:, :, tt * 128:(tt + 1) * 128], psT[:, :].rearrange("p (k t) -> p k t", k=8), [f"ps{tbk}"], ["hT"])

    m_glob = M.mark()
    wq = M.alloc("wq", [128, 8, 1536], BF16)
    KT = M.alloc("KT", [128, 4, L], BF16)
    V = M.alloc("V", [128, 32, 4, 128], BF16)
    ones128 = M.alloc("ones128", [128, 128], BF16)
    sgcol = M.alloc("sgcol", [128, 1], F32)
    c15 = M.alloc("c15", [128, 4], F32)
    biasP = M.alloc("biasP", [128, 4, 128], F32)
    biasD = M.alloc("biasD", [128, 4, 128], F32)
    gq = M.alloc("gq", [128, 2], F32)
    lamt = M.alloc("lamt", [128, 8], F32)
    m_work = M.mark()
    stg = [M.alloc("stg", [128, 2048], F32) for _ in range(2)]
    lamv = M.alloc("lamv", [128, 4, 64], F32)
    tbl = M.alloc("tbl", [32, 4], F32)
    oh = M.alloc("oh", [32, 384], F32)
    fsb = M.alloc("fsb", [4, 384], F32)
    Jt = M.alloc("Jt", [128, 128], F32)
    hank = M.alloc("hank", [128, 4, 256], F32)
    maskD = M.alloc("maskD", [128, 128], F32)

    load_weight(wq, w_in_d, 0, 8, 0, 1536, stg, True, "wq")
    MS("dve", ones128[:], 1.0, ["ones128"])
    DMA("sp", sgcol[:], dap(sg_d, 0, [[1, 128], [1, 1]]), (), ["sgcol"], "sgcol")
    TS("dve", sgcol[:], sgcol[:], 1.0 - LAM_INIT, None, ALU.mult, None, ["sgcol"], ["sgcol"])
    for hlf in range(2):
        DMA("sp", gq[64 * hlf:64 * hlf + 64, 0:1], dap(qg_d, 0, [[1, 64], [1, 1]]), (), ["gq"], "gq")
        DMA("sp", gq[64 * hlf:64 * hlf + 64, 1:2], dap(kg_d, 0, [[1, 64], [1, 1]]), (), ["gq"], "gq")
    TS("dve", gq[:, 0:1], gq[:, 0:1], 0.125, None, ALU.mult, None, ["gq"], ["gq"])
    for i, dd in enumerate((lq1_d, lk1_d, lq2_d, lk2_d)):
        DMA("sp", lamv[:, i, :], dap(dd, 0, [[0, 128], [1, 64]]), (), ["lamv"], "lamv")
    TT("dve", lamv[:, 0, :], lamv[:, 0, :], lamv[:, 1, :], ALU.mult, ["lamv"], ["lamv"])
    TT("dve", lamv[:, 2, :], lamv[:, 2, :], lamv[:, 3, :], ALU.mult, ["lamv"], ["lamv"])
    RSUM(lamt[:, 0:1], lamv[:, 0, :], ["lamv"], ["lamt"])
    RSUM(lamt[:, 1:2], lamv[:, 2, :], ["lamv"], ["lamt"])
    ACT(lamt[:, 2:4], lamt[:, 0:2], AF.Exp, ["lamt"], ["lamt"])
    TT("dve", lamt[:, 4:5], lamt[:, 3:4], lamt[:, 2:3], ALU.subtract, ["lamt"], ["lamt"])
    TS("dve", lamt[:, 5:6], lamt[:, 4:5], -LAM_INIT, None, ALU.add, None, ["lamt"], ["lamt"])
    neglam = lamt[:, 5:6]
    DMA("sp", tbl[:], rb_d[:, :], (), ["tbl"], "tbl")
    DMA("sp", oh[:], c_oh_d[:, :], (), ["oh"], "oh")
    DMA("sp", Jt[:], c_J_d[:, :], (), ["Jt"], "Jt")
    DMA("sp", maskD[:], c_maskD_d[:, :], (), ["maskD"], "maskD")
    DMA("sp", c15[:], dap(rb_d, 15 * 4, [[0, 128], [1, 4]]), (), ["c15"], "c15")
    MM(PS[1][0:4, 0:384], tbl[:], oh[:], True, True, ["tbl", "oh"], ["ps1"])
    CP("dve", fsb[:], PS[1][0:4, 0:384], ["ps1"], ["fsb"])
    DMA("sp", fsc_d[:, :], fsb[:], ["fsb"], ["fsc"], "fsb")
    for h in range(4):
        DMA("sp", hank[:, h, :], dap(fsc_d, h * 384, [[1, 128], [1, 256]]), ["fsc"], ["hank"], "hank")
    for h in range(4):
        MM(PS[2][:, h * 128:(h + 1) * 128], hank[:, h, 0:128], Jt[:], True, True, ["hank", "Jt"], ["ps2"])
        MM(PS[3][:, h * 128:(h + 1) * 128], hank[:, h, 128:256], Jt[:], True, True, ["hank", "Jt"], ["ps3"])
    CP("dve", biasP[:], PS[2][:, :].rearrange("p (h q) -> p h q", h=4), ["ps2"], ["biasP"])
    TT("dve", biasD[:], PS[3][:, :].rearrange("p (h q) -> p h q", h=4),
       maskD[:].unsqueeze(1).broadcast_to([128, 4, 128]), ALU.add, ["ps3", "maskD"], ["biasD"])
    S.barrier()
    M.release(m_work)
    hT = M.alloc("hT", [128, 8, 512], BF16)
    xt = [M.alloc("xt", [128, 1024], F32) for _ in range(2)]
    xn = [M.alloc("xn", [128, 1024], BF16) for _ in range(2)]
    qT = M.alloc("qT", [128, 4, 512], BF16)
    sq = M.alloc("sq", [128, 512], BF16)
    rstd = M.alloc("rstd", [128, 512], F32)
    PTP = [M.alloc("PTP", [128, 2, 512], BF16) for _ in range(2)]
    PT = [PTP[0][:, 0, :], PTP[0][:, 1, :], PTP[1][:, 0, :], PTP[1][:, 1, :]]
    tmpb = [M.alloc("tmpb", [128, 128], F32) for _ in range(2)]
    rcp = rstd
    oraw = [M.alloc("oraw", [128, 512], F32) for _ in range(2)]
    racc = [M.alloc("racc", [128, 2, 512], F32) for _ in range(2)]
    tob = oraw[1]
    rsb = [M.alloc("rsb", [128, 512], BF16) for _ in range(2)]
    sqb = sq
    rs2 = M.alloc("rs2", [128, 512], F32)
    sq2 = [sq, M.alloc("sq2", [128, 512], BF16)]
    rstd2 = [rstd, rs2]
    rkeys = ["rstd0", "rs2"]

    PSB = [3, 4, 7, 6]
    NB = 4
    SKEW = 2
    PAIRED = True
    QPAD = False
    FILL = 0
    deferred = []

    ZB3 = [1, 2, 4]

    def qk_mm(c, st):
        zb = ZB3[c % 3]
        for kt in range(8):
            MM(PS[zb][:, :], wq[:, kt, c * 128:(c + 1) * 128], hT[:, kt, :], kt == 0, kt == 7, ["wq", "hT"], [f"ps{zb}"], fuse=True)

    MB = [0, 3]

    def qk_c1(c, st):
        zb = ZB3[c % 3]
        sqc = sq2[c % 2]
        mb_ = MB[c % 2]
        ACT(sqc[:], PS[zb][:, :], AF.Square, [f"ps{zb}"], [f"sq{c % 2}"])
        MM(PS[mb_][:, :], bones[:], sqc[:], True, True, ["bones", f"sq{c % 2}"], [f"ps{mb_}"])

    def qk_c2(c, st):
        zb = ZB3[c % 3]
        rsc = rstd2[c % 2]
        mb_ = MB[c % 2]
        ACT(rsc[:], PS[mb_][:, :], AF.Ln, [f"ps{mb_}"], [rkeys[c % 2]], bias=EPS)
        ACT(rsc[:], rsc[:], AF.Exp, [rkeys[c % 2]], [rkeys[c % 2]], scale=-0.5)
        if c < 4:
            STT("dve", qT[:, c, :], PS[zb][:, :], gq[:, 0:1], rsc[:], ALU.mult, ALU.mult, [f"ps{zb}", "gq", rkeys[c % 2]], ["qT"])
        else:
            STT("dve", KT[:, c - 4, st * 512:(st + 1) * 512], PS[zb][:, :], gq[:, 1:2], rsc[:], ALU.mult, ALU.mult,
                [f"ps{zb}", "gq", rkeys[c % 2]], [f"KT{st}"])

    for st in range(NST):
        make_hT(st, 0, hT, xt, xn, [0, 3])
        emit_bc(8)
        qk_mm(0, st)
        qk_mm(1, st)
        qk_c1(0, st)
        for c in range(8):
            if c + 2 < 8:
                qk_mm(c + 2, st)
            if c + 1 < 8:
                qk_c1(c + 1, st)
            qk_c2(c, st)
        for tt in range(4):
            blk = st * 4 + tt
            zb = 1 + (tt % 2)
            for kt in range(8):
                MM(PS[zb][:, :], hT[:, kt, tt * 128:(tt + 1) * 128], wq[:, kt, 1024:1536], kt == 0, kt == 7, ["wq", "hT"], [f"ps{zb}"])
            CP("dve", V[:, blk, :, :], PS[zb][:, :].rearrange("p (h d) -> p h d", h=4), [f"ps{zb}"], [f"V{st}"])
        if QPAD:
            qp = hT[:, :, :].rearrange("p (s h) t -> p s h t", s=2)
            MS("pool", qp[64:128, 0, :, :], 0.0, ["hT"])
            MS("pool", qp[0:64, 1, :, :], 0.0, ["hT"])
            CP("dve", qp[0:64, 0, :, :], qT[0:64, :, :], ["qT"], ["hT"])
            CP("act", qp[64:128, 1, :, :], qT[64:128, :, :], ["qT"], ["hT"])
        kv_keys = [f"KT{i}" for i in range(st + 1)] + [f"V{i}" for i in range(st + 1)]
        iters = [(h, s, j) for h in range(4) for j in range(4 * st + 4) for s in range(2)]
        NI = len(iters)

        def emit_S(i, st=st, kv_keys=kv_keys, iters=iters):
            h, s, j = iters[i]
            lo = max(0, j - 4 * st)
            cols = slice(lo * 128, 512)
            pb = i % NB
            psS = PS[PSB[pb]]
            if QPAD:
                MM(psS[:, cols], KT[:, h, j * 128:(j + 1) * 128], qp[:, s, h, cols],
                   True, True, kv_keys + ["hT"], [f"ps{PSB[pb]}"], fuse=(j < 4 * st))
            else:
                MM(psS[:, cols], KT[64 * s:64 * s + 64, h, j * 128:(j + 1) * 128], qT[64 * s:64 * s + 64, h, cols],
                   True, True, kv_keys + ["qT"], [f"ps{PSB[pb]}"], fuse=(j < 4 * st))

        def fin_A(h, s, st):
            MM(PS[0][:, :], ones128[:], rsb[s][:], True, True, ["ones128", f"rsb{s}"], ["ps0"])
            ACT(rcp[:], PS[0][:, :], AF.Ln, ["ps0"], ["rstd0"])
            ACT(rcp[:], rcp[:], AF.Exp, ["rstd0"], ["rstd0"], scale=-1.0)
            TT("dve", oraw[s][:], oraw[s][:], rcp[:], ALU.mult, [f"oraw{s}", "rstd0"], [f"oraw{s}"])
            if s == 1:
                STT("dve", tob[:], oraw[1][:], neglam, oraw[0][:], ALU.mult, ALU.add, ["oraw0", "oraw1", "lamt"], ["oraw1"])
                ACT(sqb[:], tob[:], AF.Square, ["oraw1"], ["sq0"])

        def fin_B(h, st):
            stc = slice(st * 512, (st + 1) * 512)
            MM(PS[0][:, :], ones128[:], sqb[:], True, True, ["ones128", "sq0"], ["ps0"])
            ACT(rs2[:], PS[0][:, :], AF.Ln, ["ps0"], ["rs2"], bias=EPS, scale=1.0 / 128)
            ACT(rs2[:], rs2[:], AF.Exp, ["rs2"], ["rs2"], scale=-0.5)
            STT("dve", OAT[:, h, stc], tob[:], sgcol[:, 0:1], rs2[:], ALU.mult, ALU.mult, ["oraw1", "sgcol", "rs2"], ["OAT"])

        def emit_rest(i, st=st, kv_keys=kv_keys, iters=iters):
            h, s, j = iters[i]
            lo = max(0, j - 4 * st)
            pb = i % NB
            pk = f"ps{PSB[pb]}"
            psS = PS[PSB[pb]]
            accb = 5 if s == 0 else 2
            if j == 0 and s == 0:
                MS("dve", racc[0][:], 0.0, ["racc0"])
                MS("pool", racc[1][:], 0.0, ["racc1"])
            far_lo = lo
            for qb in range(lo, 4):
                d = 4 * st + qb - j
                if d >= 2:
                    break
                bt = biasD if d == 0 else biasP
                tb_ = tmpb[d]
                TT("dve", tb_[:], psS[:, qb * 128:(qb + 1) * 128], bt[:, h, :], ALU.add, [pk, "biasD", "biasP"], [f"tmpb{d}"])
                ACT(PT[pb][:, qb * 128:(qb + 1) * 128], tb_[:], AF.Exp, [f"tmpb{d}"], [f"PT{pb}"])
                far_lo = qb + 1
            if far_lo < 4:
                ACT(PT[pb][:, far_lo * 128:512], psS[:, far_lo * 128:512], AF.Exp, [pk, "c15"], [f"PT{pb}"], bias=c15[:, h:h + 1])
            qs = slice(lo * 128, 512)
            last = (j == 4 * st + 3)
            MM(PS[accb][:, qs], V[:, j, h, :], PT[pb][:, qs], j == 0, last, [f"PT{pb}"] + kv_keys, [f"ps{accb}"], fuse=(j < 4 * st))
            for _f in range(FILL):
                MM(PS[6][:, :], ones128[:], wq[:, 0, 0:512], True, True, ["ones128", "wq"], ["ps6"])
            if s == 1:
                e_ = j % 2
                pbuf = (i // 2) % 2
                TT(("dve", "pool")[e_], racc[e_][:, :, qs], racc[e_][:, :, qs], PTP[pbuf][:, :, qs], ALU.add,
                   [f"racc{e_}", f"PT{pb}", f"PT{pb - 1}"], [f"racc{e_}"])
            if not last:
                return
            CP("act", oraw[s][:], PS[accb][:, :], [f"ps{accb}"], [f"oraw{s}"])
            if s == 1:
                for ss_ in range(2):
                    TT("dve", rsb[ss_][:], racc[0][:, ss_, :], racc[1][:, ss_, :], ALU.add, ["racc0", "racc1"], [f"rsb{ss_}"])
            deferred.append((i + 4, "A", h, s, st))
            if s == 1:
                deferred.append((i + 7, "B", h, s, st))

        def run_deferred(k):
            while deferred and deferred[0][0] <= k:
                _, kind, hh, ss, sst = deferred.pop(0)
                if kind == "A":
                    fin_A(hh, ss, sst)
                else:
                    fin_B(hh, sst)

        for i in range(0, NI + SKEW, 2):
            if PAIRED:
                for ii in (i, i + 1):
                    if ii < NI:
                        emit_S(ii)
                for ii in (i, i + 1):
                    k = ii - SKEW
                    if 0 <= k < NI:
                        emit_rest(k)
                        run_deferred(k)
            else:
                for ii in (i, i + 1):
                    if ii < NI:
                        emit_S(ii)
                    k = ii - SKEW
                    if 0 <= k < NI:
                        emit_rest(k)
                        run_deferred(k)
        run_deferred(10 ** 9)

    if dbg == "p1":
        DMA("sp", dbg_d[:, :], OAT[:, :, :].rearrange("p h t -> p (h t)"), ["OAT"], ["outd"], "OAT")
        NOP(["outd"])
        return finish(nc, S)
    I32 = mybir.dt.int32
    S.barrier()
    M.release(m_glob)
    w2 = M.alloc("w2", [128, 8, 1024], BF16)
    gluw = M.alloc("gluw", [128, 4, 512], BF16)
    glub = M.alloc("glub", [128, 4], F32)
    ta = M.alloc("ta", [128, 2048], F32)
    tb = M.alloc("tb", [128, 2048], F32)
    te = M.alloc("te", [128, 16, 128], F32)
    tf = M.alloc("tf", [128, 16, 128], F32)
    Bbd = M.alloc("Bbd", [128, 4, 2, 512], BF16)
    Ccat = M.alloc("Ccat", [128, 2, 16, 128], BF16)
    Ddiag = M.alloc("Ddiag", [128, 4, 128], BF16)
    dcol = M.alloc("dcol", [128, 4], F32)
    negtri = M.alloc("negtri", [128, 128], BF16)
    CcatN = M.alloc("CcatN", [128, 16, 128], BF16)
    l128 = M.alloc("l128", [128, 6, 16], F32)
    cre = [M.alloc("cre", [128, 16], F32) for _ in range(2)]
    cim = [M.alloc("cim", [128, 16], F32) for _ in range(2)]
    m_work2 = M.mark()
    stg = [M.alloc("stg", [128, 2048], F32) for _ in range(2)]
    load_weight(w2, w_in_d, 0, 8, 2048, 1024, stg, True, "w2")
    load_weight(gluw, gw_d, 0, 4, 0, 512, stg, False, "gluw")
    S.barrier()
    M.release(m_work2)
    T1 = M.alloc("T1", [128, 2048], F32)
    T2 = M.alloc("T2", [128, 2048], F32)
    T3 = M.alloc("T3", [128, 2048], F32)
    T4 = M.alloc("T4", [128, 2048], F32)
    TI = M.alloc("TI", [128, 2048], I32)
    dtb = M.alloc("dtb", [128, 32], F32)
    s2 = M.alloc("s2", [128, 8, 16], F32)
    s3 = M.alloc("s3", [128, 14, 256], F32)
    s3i = M.alloc("s3i", [128, 256], I32)
    dt3 = M.alloc("dt3", [128, 4], F32)

    DMA("sp", glub[:], dap(gb_d, 0, [[1, 128], [128, 4]]), (), ["glub"], "glub", slow=True)

    def frac_sincos(y, tmp, ti, sin_out, cos_out, key):
        CP("dve", ti, y, [key + "y"], [key + "ti"])
        CP("dve", tmp, ti, [key + "ti"], [key + "tmp"])
        TT("dve", tmp, y, tmp, ALU.subtract, [key + "y", key + "tmp"], [key + "tmp"])
        ACT(sin_out, tmp, AF.Sin, [key + "tmp"], [key + "sin"], scale=TWO_PI)
        TS("dve", y, y, 0.25, None, ALU.add, None, [key + "y"], [key + "y"])
        CP("dve", ti, y, [key + "y"], [key + "ti"])
        CP("dve", tmp, ti, [key + "ti"], [key + "tmp"])
        TT("dve", tmp, y, tmp, ALU.subtract, [key + "y", key + "tmp"], [key + "tmp"])
        ACT(cos_out, tmp, AF.Sin, [key + "tmp"], [key + "cos"], scale=TWO_PI)

    DMA("sp", T1[:], dap(are_d, 0, [[0, 128], [1, 2048]]), (), ["T1"], "T1")
    DMA("sp", T2[:], dap(aim_d, 0, [[0, 128], [1, 2048]]), (), ["T2"], "T2")
    DMA("sp", dtb[:], dap(ldt_d, 0, [[0, 128], [1, 32]]), (), ["dtb"], "dtb")
    ACT(dtb[:], dtb[:], AF.Exp, ["dtb"], ["dtb"])
    dtb_b = dtb[:, :].unsqueeze(2).broadcast_to([128, 32, 64])
    TT("dve", T1[:, :].rearrange("p (g q) -> p g q", g=32), T1[:, :].rearrange("p (g q) -> p g q", g=32), dtb_b, ALU.mult, ["T1", "dtb"], ["T1"])
    TT("dve", T2[:, :].rearrange("p (g q) -> p g q", g=32), T2[:, :].rearrange("p (g q) -> p g q", g=32), dtb_b, ALU.mult, ["T2", "dtb"], ["T2"])
    ACT(ta[:], T1[:], AF.Exp, ["T1", "iotap"], ["ta"], scale=iotap[:, 1:2])
    TS("dve", T3[:], T2[:], iotap[:, 0:1], 1.0 / TWO_PI, ALU.mult, ALU.mult, ["T2", "iotap"], ["L1y"])
    frac_sincos(T3[:], T4[:], TI[:], tb[:], T1[:], "L1")
    STT("dve", tb[:], tb[:], -1.0, ta[:], ALU.mult, ALU.mult, ["L1sin", "ta"], ["tb", "L1sin"])
    TT("dve", ta[:], ta[:], T1[:], ALU.mult, ["ta", "L1cos", "tb"], ["ta"])
    A2re, A2im, dt2, m2, th2 = (s2[:, i, :] for i in range(5))
    DMA("sp", A2re, dap(are_d, 0, [[1, 128], [128, 16]]), (), ["A2re"], "A2re", slow=True)
    DMA("sp", A2im, dap(aim_d, 0, [[1, 128], [128, 16]]), (), ["A2im"], "A2im", slow=True)
    for g2 in range(2):
        DMA("sp", s2[64 * g2:64 * g2 + 64, 2, :], dap(ldt_d, g2, [[0, 64], [2, 16]]), (), ["dt2"], "dt2", slow=True)
    ACT(dt2, dt2, AF.Exp, ["dt2"], ["dt2"])
    TT("dve", m2, A2re, dt2, ALU.mult, ["A2re", "dt2"], ["m2"])
    STT("dve", th2, A2im, 1.0 / TWO_PI, dt2, ALU.mult, ALU.mult, ["A2im", "dt2"], ["th2"])
    T3v = T3[:, :].rearrange("p (a t) -> p a t", a=16)
    for pair in range(16):
        ACT(te[:, pair, :], iotat[:], AF.Exp, ["iotat", "m2"], ["te"], scale=s2[:, 3, pair:pair + 1])
        TS("dve", T3v[:, pair, :], iotat[:], s2[:, 4, pair:pair + 1], None, ALU.mult, None, ["iotat", "th2", "L1y", "L1tmp"], ["L2y"])
    frac_sincos(T3[:], T4[:], TI[:], tf[:, :, :].rearrange("p a t -> p (a t)"), T1[:], "L2")
    TT("dve", tf[:, :, :].rearrange("p a t -> p (a t)"), tf[:, :, :].rearrange("p a t -> p (a t)"), te[:, :, :].rearrange("p a t -> p (a t)"),
       ALU.mult, ["L2sin", "te"], ["tf", "L2sin"])
    TT("dve", te[:, :, :].rearrange("p a t -> p (a t)"), te[:, :, :].rearrange("p a t -> p (a t)"), T1[:], ALU.mult, ["te", "L2cos", "tf"], ["te"])
    A3re, A3im, m3, y3, dec3, sin3, cos3, nr3, den3, qre3, qim3, u3a, u3b, tmp3 = (s3[:, i, :] for i in range(14))
    for g8 in range(8):
        DMA("sp", s3[16 * g8:16 * g8 + 16, 0, :].rearrange("p (c q) -> p c q", c=4), dap(are_d, g8 * 64, [[0, 16], [512, 4], [1, 64]]), (), ["A3re"], "A3re")
        DMA("sp", s3[16 * g8:16 * g8 + 16, 1, :].rearrange("p (c q) -> p c q", c=4), dap(aim_d, g8 * 64, [[0, 16], [512, 4], [1, 64]]), (), ["A3im"], "A3im")
        DMA("sp", dt3[16 * g8:16 * g8 + 16, :], dap(ldt_d, g8, [[0, 16], [8, 4]]), (), ["dt3"], "dt3", slow=True)
    ACT(dt3[:], dt3[:], AF.Exp, ["dt3"], ["dt3"])
    dt3_b = dt3[:, :].unsqueeze(2).broadcast_to([128, 4, 64])
    v3 = lambda a: a.rearrange("p (c q) -> p c q", c=4)
    TT("dve", v3(m3), v3(A3re), dt3_b, ALU.mult, ["A3re", "dt3"], ["m3"])
    TT("dve", v3(y3), v3(A3im), dt3_b, ALU.mult, ["A3im", "dt3"], ["L3y"])
    TS("dve", y3, y3, 1.0 / TWO_PI, None, ALU.mult, None, ["L3y"], ["L3y"])
    ACT(dec3, m3, AF.Exp, ["m3"], ["dec3"])
    frac_sincos(y3, tmp3, s3i[:], sin3, cos3, "L3")
    TT("dve", cos3, cos3, dec3, ALU.mult, ["L3cos", "dec3"], ["lbr"])
    TT("dve", sin3, sin3, dec3, ALU.mult, ["L3sin", "dec3"], ["lbi"])
    TS("dve", nr3, cos3, -1.0, None, ALU.add, None, ["lbr"], ["nr3"])
    TT("dve", den3, A3re, A3re, ALU.mult, ["A3re"], ["den3"])
    TT("dve", u3a, A3im, A3im, ALU.mult, ["A3im"], ["u3a"])
    TT("dve", den3, den3, u3a, ALU.add, ["den3", "u3a"], ["den3"])
    RECIP(den3, den3, ["den3"], ["den3"])
    TT("dve", u3a, nr3, A3re, ALU.mult, ["nr3", "A3re", "den3"], ["u3a"])
    TT("dve", u3b, sin3, A3im, ALU.mult, ["lbi", "A3im"], ["u3b"])
    TT("dve", qre3, u3a, u3b, ALU.add, ["u3a", "u3b"], ["qre3"])
    TT("dve", qre3, qre3, den3, ALU.mult, ["qre3", "den3"], ["qre3"])
    TT("dve", u3a, sin3, A3re, ALU.mult, ["lbi", "A3re", "qre3"], ["u3a"])
    TT("dve", u3b, nr3, A3im, ALU.mult, ["nr3", "A3im", "qre3"], ["u3b"])
    TT("dve", qim3, u3a, u3b, ALU.subtract, ["u3a", "u3b"], ["qim3"])
    TT("dve", qim3, qim3, den3, ALU.mult, ["qim3", "den3"], ["qim3"])
    qre_b = v3(qre3).unsqueeze(2).broadcast_to([128, 4, 8, 64])
    qim_b = v3(qim3).unsqueeze(2).broadcast_to([128, 4, 8, 64])
    T1v4 = T1[:, :].rearrange("p (c g q) -> p c g q", c=4, g=8)
    T2v4 = T2[:, :].rearrange("p (c g q) -> p c g q", c=4, g=8)
    TT("dve", T1v4, Bst[:, 0], qre_b, ALU.mult, ["Bst", "qre3", "ta", "te"], ["T1"])
    TT("dve", T2v4, Bst[:, 1], qim_b, ALU.mult, ["Bst", "qim3", "L1y", "L2y"], ["T2"])
    TT("dve", Bbd[:, :, 0, :], T1[:, :].rearrange("p (c x) -> p c x", c=4), T2[:, :].rearrange("p (c x) -> p c x", c=4), ALU.subtract, ["T1", "T2"], ["Bbd"])
    TT("dve", T1v4, Bst[:, 1], qre_b, ALU.mult, ["Bst", "qre3", "Bbd"], ["T1"])
    TT("dve", T2v4, Bst[:, 0], qim_b, ALU.mult, ["Bst", "qim3", "Bbd"], ["T2"])
    TT("dve", Bbd[:, :, 1, :], T1[:, :].rearrange("p (c x) -> p c x", c=4), T2[:, :].rearrange("p (c x) -> p c x", c=4), ALU.add, ["T1", "T2"], ["Bbd"])
    CP("dve", Ccat[:, 0], Cst[:, 0], ["Cst"], ["Ccat"])
    TS("dve", CcatN[:], Cst[:, 0], -1.0, None, ALU.mult, None, ["Cst"], ["CcatN"])
    TS("dve", negtri[:], tri[:], -1.0, None, ALU.mult, None, ["tri"], ["negtri"])
    TT("dve", l128[:, 0, :], te[:, :, 127], te[:, :, 1], ALU.mult, ["te"], ["l128a"])
    TT("dve", l128[:, 1, :], tf[:, :, 127], tf[:, :, 1], ALU.mult, ["tf"], ["l128b"])
    TT("dve", l128[:, 4, :], l128[:, 0, :], l128[:, 1, :], ALU.subtract, ["l128a", "l128b"], ["l128"])
    TT("dve", l128[:, 2, :], te[:, :, 127], tf[:, :, 1], ALU.mult, ["te", "tf"], ["l128c"])
    TT("dve", l128[:, 3, :], tf[:, :, 127], te[:, :, 1], ALU.mult, ["te", "tf"], ["l128d"])
    TT("dve", l128[:, 5, :], l128[:, 2, :], l128[:, 3, :], ALU.add, ["l128c", "l128d"], ["l128"])
    TS("dve", Ccat[:, 1], Cst[:, 1], -1.0, None, ALU.mult, None, ["Cst"], ["Ccat"])
    DMA("sp", dcol[:], dap(dd_d, 0, [[1, 128], [128, 4]]), (), ["dcol"], "dcol", slow=True)
    for ct in range(4):
        TS("dve", Ddiag[:, ct, :], ident[:], dcol[:, ct:ct + 1], None, ALU.mult, None, ["ident", "dcol"], ["Ddiag"])
    MS("dve", cre[0][:], 0.0, ["c0_0", "c0_1", "c0_2", "c0_3"])
    MS("dve", cim[0][:], 0.0, ["c0_0", "c0_1", "c0_2", "c0_3"])
    S.barrier()
    M.release(m_work2)
    hT = M.alloc("hT", [128, 8, 512], BF16)
    xt = [M.alloc("xt", [128, 1024], F32) for _ in range(2)]
    xn = [M.alloc("xn", [128, 1024], BF16) for _ in range(2)]
    uT = M.alloc("uT", [128, 4, 512], BF16)
    gsT = M.alloc("gsT", [128, 4, 512], BF16)
    gT = M.alloc("gT", [128, 4, 512], BF16)
    xs = [M.alloc("xs", [128, 512], F32) for _ in range(2)]
    ys = [M.alloc("ys", [128, 512], F32) for _ in range(2)]
    pa_ = [[M.alloc("pa_", [128, 512], BF16) for _ in range(4)] for _ in range(2)]
    pb_ = [[M.alloc("pb_", [128, 4, 128], BF16) for _ in range(4)] for _ in range(2)]
    sr = [M.alloc("sr", [128, 4, 128], F32) for _ in range(2)]
    si = [M.alloc("si", [128, 4, 128], F32) for _ in range(2)]
    sg = M.alloc("sg", [128, 512], F32)
    xl = M.alloc("xl", [128, 4, 4], F32)

    def ssm_A(st, cc, ct, u):
        tok = slice(cc * 128, (cc + 1) * 128)
        ub = u % 2
        MM(PS[2][:, :], uT[:, ct, tok], Bbd[:, ct, 0, :], True, True, ["uT", "Bbd"], ["ps2"])
        MM(PS[3][:, :], uT[:, ct, tok], Bbd[:, ct, 1, :], True, True, ["uT", "Bbd"], ["ps3"])
        CP("act", xs[ub][:], PS[2][:, :], ["ps2"], [f"xs{ub}"])
        CP("act", ys[ub][:], PS[3][:, :], ["ps3"], [f"ys{ub}"])
        a_ = ta[:, ct * 512:(ct + 1) * 512]
        b_ = tb[:, ct * 512:(ct + 1) * 512]
        p = pa_[ub]
        TT("dve", p[0][:], xs[ub][:], a_, ALU.mult, [f"xs{ub}", "ta"], [f"pa{ub}0"])
        TT("dve", p[1][:], ys[ub][:], b_, ALU.mult, [f"ys{ub}", "tb"], [f"pa{ub}1"])
        TT("dve", p[2][:], ys[ub][:], a_, ALU.mult, [f"ys{ub}", "ta"], [f"pa{ub}2"])
        TT("pool", p[3][:], xs[ub][:], b_, ALU.mult, [f"xs{ub}", "tb"], [f"pa{ub}3"])

    def ssm_A2(st, cc, ct, u):
        ub = u % 2
        p = pa_[ub]
        for pr in range(4):
            cs = slice(pr * 128, (pr + 1) * 128)
            MM(PS[4 + 2 * ub][:, cs], p[0][:, cs], tri[:], True, False, [f"pa{ub}0", "tri"], [f"ps{4 + 2 * ub}"])
            MM(PS[4 + 2 * ub][:, cs], p[1][:, cs], negtri[:], False, True, [f"pa{ub}1", "negtri"], [f"ps{4 + 2 * ub}"])
            MM(PS[5 + 2 * ub][:, cs], p[2][:, cs], tri[:], True, False, [f"pa{ub}2", "tri"], [f"ps{5 + 2 * ub}"])
            MM(PS[5 + 2 * ub][:, cs], p[3][:, cs], tri[:], False, True, [f"pa{ub}3", "tri"], [f"ps{5 + 2 * ub}"])

    def ssm_B(st, cc, ct, u):
        tok = slice(cc * 128, (cc + 1) * 128)
        ci = st * 4 + cc
        ub = u % 2
        cr_in, ci_in = cre[ci % 2], cim[ci % 2]
        cr_out, ci_out = cre[(ci + 1) % 2], cim[(ci + 1) % 2]
        kin, kout = f"c{ci % 2}_{ct}", f"c{(ci + 1) % 2}_{ct}"
        for pr in range(4):
            pair = 4 * ct + pr
            cs = slice(pr * 128, (pr + 1) * 128)
            ACT(sr[ub][:, pr, :], PS[4 + 2 * ub][:, cs], AF.Identity, [f"ps{4 + 2 * ub}", kin], [f"sr{ub}"], bias=cr_in[:, pair:pair + 1])
            ACT(si[ub][:, pr, :], PS[5 + 2 * ub][:, cs], AF.Identity, [f"ps{5 + 2 * ub}", kin], [f"si{ub}"], bias=ci_in[:, pair:pair + 1])
        eh = te[:, 4 * ct:4 * ct + 4, :]
        fh = tf[:, 4 * ct:4 * ct + 4, :]
        q = pb_[ub]
        TT("dve", q[0][:], sr[ub][:], eh, ALU.mult, [f"sr{ub}", "te"], [f"pb{ub}0"])
        TT("pool", q[1][:], si[ub][:], fh, ALU.mult, [f"si{ub}", "tf"], [f"pb{ub}1"])
        TT("dve", q[2][:], si[ub][:], eh, ALU.mult, [f"si{ub}", "te"], [f"pb{ub}2"])
        TT("pool", q[3][:], sr[ub][:], fh, ALU.mult, [f"sr{ub}", "tf"], [f"pb{ub}3"])
        E = l128[:, 4, 4 * ct:4 * ct + 4]
        F_ = l128[:, 5, 4 * ct:4 * ct + 4]
        ps_ = slice(4 * ct, 4 * ct + 4)
        TT("pool", xl[:, 0, :], E, sr[ub][:, :, 127], ALU.mult, ["l128", f"sr{ub}"], ["xl0"])
        TT("pool", xl[:, 1, :], F_, si[ub][:, :, 127], ALU.mult, ["l128", f"si{ub}"], ["xl1"])
        TT("pool", cr_out[:, ps_], xl[:, 0, :], xl[:, 1, :], ALU.subtract, ["xl0", "xl1"], [kout])
        TT("pool", xl[:, 2, :], E, si[ub][:, :, 127], ALU.mult, ["l128", f"si{ub}"], ["xl2"])
        TT("pool", xl[:, 3, :], F_, sr[ub][:, :, 127], ALU.mult, ["l128", f"sr{ub}"], ["xl3"])
        TT("pool", ci_out[:, ps_], xl[:, 2, :], xl[:, 3, :], ALU.add, ["xl2", "xl3"], [kout])

    def ssm_C(st, cc, ct, u):
        tok = slice(cc * 128, (cc + 1) * 128)
        ub = u % 2
        q = pb_[ub]
        ysl = slice(ct * 128, (ct + 1) * 128)
        for pr in range(4):
            pair = 4 * ct + pr
            MM(PS[0][:, ysl], Ccat[:, 0, pair, :], q[0][:, pr, :], pr == 0, False, ["Ccat", f"pb{ub}0"], ["ps0"], fuse=True)
            MM(PS[0][:, ysl], CcatN[:, pair, :], q[1][:, pr, :], False, False, ["CcatN", f"pb{ub}1"], ["ps0"], fuse=True)
            MM(PS[0][:, ysl], Ccat[:, 1, pair, :], q[2][:, pr, :], False, False, ["Ccat", f"pb{ub}2"], ["ps0"], fuse=True)
            MM(PS[0][:, ysl], Ccat[:, 1, pair, :], q[3][:, pr, :], False, False, ["Ccat", f"pb{ub}3"], ["ps0"], fuse=True)
        MM(PS[0][:, ysl], Ddiag[:, ct, :], uT[:, ct, tok], False, True, ["Ddiag", "uT"], ["ps0"], fuse=True)
        if ct == 3:
            ACT(gT[:, :, tok], PS[0][:, :].rearrange("p (c t) -> p c t", c=4), AF.Gelu, ["ps0"], ["gT"])

    for st in range(NST):
        make_hT(st, 1, hT, xt, xn, [0, 1])
        def us_mm(c):
            zb = 1 + (c % 2)
            for kt in range(8):
                MM(PS[zb][:, :], w2[:, kt, c * 128:(c + 1) * 128], hT[:, kt, :], kt == 0, kt == 7, ["w2", "hT"], [f"ps{zb}"], fuse=True)
        us_mm(0)
        for c in range(8):
            if c + 1 < 8:
                us_mm(c + 1)
            zb = 1 + (c % 2)
            if c < 4:
                CP("act", uT[:, c, :], PS[zb][:, :], [f"ps{zb}"], ["uT"])
            else:
                ACT(gsT[:, c - 4, :], PS[zb][:, :], AF.Silu, [f"ps{zb}"], ["gsT"])
        units = [(cc, ct) for cc in range(4) for ct in range(4)]
        NU = len(units)
        for k in range(-2, NU):
            if 0 <= k + 2 < NU:
                ssm_A(st, units[k + 2][0], units[k + 2][1], k + 2)
            if 0 <= k + 1 < NU:
                ssm_A2(st, units[k + 1][0], units[k + 1][1], k + 1)
                ssm_B(st, units[k + 1][0], units[k + 1][1], k + 1)
            if 0 <= k < NU:
                ssm_C(st, units[k][0], units[k][1], k)
        for c in range(4):
            zb = 1 + (c % 2)
            for kt in range(4):
                MM(PS[zb][:, :], gluw[:, kt, c * 128:(c + 1) * 128], gT[:, kt, :], kt == 0, kt == 3, ["gluw", "gT"], [f"ps{zb}"], fuse=True)
            ACT(sg[:], PS[zb][:, :], AF.Sigmoid, [f"ps{zb}", "glub"], ["sg"], bias=glub[:, c:c + 1])
            TT("dve", sg[:], sg[:], gT[:, c, :], ALU.mult, ["sg", "gT"], ["sg"])
            TT("dve", OST[:, c, st * 512:(st + 1) * 512], sg[:], gsT[:, c, :], ALU.mult, ["sg", "gsT"], ["OST", "Bst", "Cst"])
    if dbg == "p2":
        DMA("sp", dbg_d[:, :], OSTm[:, :], ["OST"], ["outd"], "OSTm")
        NOP(["outd"])
        return finish(nc, S)

    S.barrier()
    M.release(m_glob)
    w3 = M.alloc("w3", [128, 8, 2560], BF16)
    pa = M.alloc("pa", [128, 4, 1024], BF16)
    psw = M.alloc("psw", [128, 4, 1024], BF16)
    wo = M.alloc("wo", [128, 8, 1024], BF16)
    mb = M.alloc("mb", [128, 16], F32)
    m_work3 = M.mark()
    stg = [M.alloc("stg", [128, 2048], F32) for _ in range(2)]
    load_weight(w3[:, :, 0:512], w_in_d, 0, 8, 1536, 512, stg, True, "w3")
    load_weight(w3[:, :, 512:2560], w_in_d, 0, 8, 3072, 2048, stg, True, "w3")
    load_weight(pa, pa_d, 0, 4, 0, 1024, stg, False, "pa")
    load_weight(psw, ps_d, 0, 4, 0, 1024, stg, False, "psw")
    load_weight(wo, wo_d, 0, 8, 0, 1024, stg, False, "wo")
    DMA("sp", mb[:], dap(mb_d, 0, [[1, 128], [128, 16]]), (), ["mb"], "mb", slow=True)
    S.barrier()
    M.release(m_work3)
    hT = M.alloc("hT", [128, 8, 512], BF16)
    xt = [M.alloc("xt", [128, 1024], F32) for _ in range(2)]
    xn = [M.alloc("xn", [128, 1024], BF16) for _ in range(2)]
    mT = M.alloc("mT", [128, 8, 512], BF16)
    g1 = M.alloc("g1", [128, 512], F32)
    g2t = M.alloc("g2t", [128, 512], F32)
    m1 = M.alloc("m1", [128, 512], F32)
    xres = M.alloc("xres", [128, 1024], F32)
    ot = [M.alloc("ot", [128, 512], F32) for _ in range(2)]
    outkeys = []
    for st in range(NST):
        stc = slice(st * 512, (st + 1) * 512)
        make_hT(st, 2, hT, xt, xn, [0, 7])
        for h in range(4):
            for kt in range(8):
                MM(PS[1][:, :], w3[:, kt, h * 128:(h + 1) * 128], hT[:, kt, :], kt == 0, kt == 7, ["w3", "hT"], ["ps1"], fuse=True)
            ACT(g1[:], PS[1][:, :], AF.Silu, ["ps1"], ["g1"])
            TT("dve", OAT[:, h, stc], OAT[:, h, stc], g1[:], ALU.mult, ["OAT", "g1"], ["OAT"])
        for c in range(8):
            for h in range(4):
                MM(PS[1][:, :], pa[:, h, c * 128:(c + 1) * 128], OAT[:, h, stc], h == 0, h == 3, ["pa", "OAT"], ["ps1"], fuse=True)
            for kt in range(8):
                MM(PS[2][:, :], w3[:, kt, 512 + c * 128:512 + (c + 1) * 128], hT[:, kt, :], kt == 0, kt == 7, ["w3", "hT"], ["ps2"], fuse=True)
            ACT(g1[:], PS[2][:, :], AF.Sigmoid, ["ps2", "mb"], ["g1"], bias=mb[:, c:c + 1])
            TT("dve", m1[:], PS[1][:, :], g1[:], ALU.mult, ["ps1", "g1"], ["m1"])
            for k in range(4):
                MM(PS[3][:, :], psw[:, k, c * 128:(c + 1) * 128], OST[:, k, stc], k == 0, k == 3, ["psw", "OST"], ["ps3"], fuse=True)
            for kt in range(8):
                MM(PS[4][:, :], w3[:, kt, 1536 + c * 128:1536 + (c + 1) * 128], hT[:, kt, :], kt == 0, kt == 7, ["w3", "hT"], ["ps4"], fuse=True)
            ACT(g2t[:], PS[4][:, :], AF.Sigmoid, ["ps4", "mb"], ["g2t"], bias=mb[:, 8 + c:9 + c])
            TT("dve", g2t[:], PS[3][:, :], g2t[:], ALU.mult, ["ps3", "g2t"], ["g2t"])
            TT("pool", mT[:, c, :], m1[:], g2t[:], ALU.add, ["m1", "g2t"], ["mT"])
        for tt in range(4):
            row = st * 512 + tt * 128
            DMA("sp", xres[:], x_d[row:row + 128, :], (), ["xres"], "xres")
            for half in range(2):
                for kt in range(8):
                    MM(PS[5 + half][:, :], mT[:, kt, tt * 128:(tt + 1) * 128], wo[:, kt, half * 512:(half + 1) * 512], kt == 0, kt == 7,
                       ["mT", "wo"], [f"ps{5 + half}"])
                TT("dve", ot[half][:], PS[5 + half][:, :], xres[:, half * 512:(half + 1) * 512], ALU.add, [f"ps{5 + half}", "xres"], [f"ot{half}"])
                ok = f"outd{st}_{tt}_{half}"
                DMA("sp", out_d[row:row + 128, half * 512:(half + 1) * 512], ot[half][:], [f"ot{half}"], [ok], f"ot{half}")
                outkeys.append(ok)
    NOP(outkeys)
    return finish(nc, S)


def finish(nc, S):
    import contextlib
    with contextlib.ExitStack() as stk:
        S._sem_ctx = {e: stk.enter_context(nc.semaphore(f"s_{e}")) for e in S.ENGS}
        S._dsem = {k: stk.enter_context(nc.semaphore(f"d_{k}")) for k in S.dma_count}
        block = stk.enter_context(nc.Block())
        S.emit(block)
    return nc


_NC_CACHE = {}


def _core_inputs(inputs, b, consts):
    m = {"x": np.ascontiguousarray(inputs["x"][b], dtype=np.float32)}
    for k, v in inputs.items():
        if k == "x":
            continue
        v = np.asarray(v, dtype=np.float32)
        if k == "rel_bias_table":
            m[k] = np.ascontiguousarray(v)
            continue
        v0 = v[0]
        if k in ("w_in", "ssm_glu_w", "proj_attn", "proj_ssm", "w_out"):
            m[k] = np.ascontiguousarray(v0)
        else:
            m[k] = np.ascontiguousarray(v0).reshape(-1)
    m.update(consts)
    return m


def kernel(**inputs):
    if "nc" not in _NC_CACHE:
        _NC_CACHE["nc"] = build_nc()
    nc = _NC_CACHE["nc"]
    consts = host_consts()
    in_maps = [_core_inputs(inputs, b, consts) for b in range(8)]
    res = run_bass_kernel_spmd(nc, in_maps, core_ids=list(range(8)))
    out = np.stack([np.asarray(res.results[b]["out"], dtype=np.float32) for b in range(8)], axis=0)
    return out
```

```python
import math
import numpy as np
import ml_dtypes
import concourse.bass as bass
import concourse.mybir as mybir
from concourse.bass_utils import run_bass_kernel_spmd

F32 = mybir.dt.float32
BF16 = mybir.dt.bfloat16
AF = mybir.ActivationFunctionType
ALU = mybir.AluOpType
AX = mybir.AxisListType


class _Op:
    __slots__ = ("eng", "fn", "idx", "dma", "dkey", "deps", "signal", "sigval", "sem", "fuse")

    def __init__(self, eng, fn, idx, dma, dkey):
        self.eng = eng
        self.fn = fn
        self.idx = idx
        self.dma = dma
        self.dkey = dkey
        self.deps = {}
        self.signal = False
        self.sigval = 0
        self.sem = None
        self.fuse = False


class Sched:
    ENGS = ("pe", "act", "dve", "pool", "sp")
    FUSE_WAITS = True

    def __init__(self, nc):
        self.nc = nc
        self.ops = {e: [] for e in self.ENGS}
        self.last_w = {}
        self.readers = {}
        self.dma_count = {}

    def add(self, eng, fn, reads=(), writes=(), dma=False, dkey=None, fuse=None):
        op = _Op(eng, fn, len(self.ops[eng]), dma, dkey)
        if fuse is None:
            fuse = (eng in ("act", "dve", "pool")) and not dma
        op.fuse = fuse and self.FUSE_WAITS
        reads = tuple(reads) + ("__B__",)
        for k in reads:
            w = self.last_w.get(k)
            if w is not None:
                op.deps[w] = True
        for k in writes:
            w = self.last_w.get(k)
            if w is not None and w not in op.deps:
                op.deps[w] = False
            rd = self.readers.get(k)
            if rd:
                for r in rd.get("eng", {}).values():
                    if r is not op and r not in op.deps:
                        op.deps[r] = False
                for r in rd.get("dma", []):
                    if r is not op and r not in op.deps:
                        op.deps[r] = False
        for k in reads:
            rd = self.readers.setdefault(k, {"eng": {}, "dma": []})
            if dma:
                rd["dma"].append(op)
            else:
                rd["eng"][eng] = op
        for k in writes:
            self.last_w[k] = op
            self.readers[k] = {"eng": {}, "dma": []}
        if dma:
            assert dkey is not None
            n = self.dma_count.get(dkey, 0) + 1
            self.dma_count[dkey] = n
            op.sigval = 16 * n
        self.ops[eng].append(op)
        return op

    def barrier(self):
        t = self._bar_tile
        self.add("dve", lambda e: e.memset(t[:, 0:1], 0.0), writes=("__B__",))

    def _needs_wait(self, op, dep, is_raw):
        if dep.dma:
            return True
        if dep.eng != op.eng:
            return True
        if op.dma:
            return True
        if op.eng == "pe":
            return False
        return is_raw and (op.idx - dep.idx) <= 3

    def emit(self, block, extra_final=None):
        nc = self.nc
        for e in self.ENGS:
            for op in self.ops[e]:
                for dep, is_raw in op.deps.items():
                    if self._needs_wait(op, dep, is_raw) and not dep.dma:
                        dep.signal = True
        sems = {e: self._sem_ctx[e] for e in self.ENGS}
        dsems = self._dsem
        for e in self.ENGS:
            n = 0
            for op in self.ops[e]:
                if op.dma:
                    op.sem = dsems[op.dkey]
                else:
                    op.sem = sems[e]
                    if op.signal:
                        n += 1
                        op.sigval = n

        def run_engine(ename, eng):
            known = {}
            for op in self.ops[ename]:
                need = {}
                for dep, is_raw in op.deps.items():
                    if not self._needs_wait(op, dep, is_raw):
                        continue
                    s = dep.sem
                    v = dep.sigval
                    if need.get(s, 0) < v:
                        need[s] = v
                pend = [(s, v) for s, v in need.items() if known.get(s, 0) < v]
                fused = None
                if op.fuse and pend:
                    fused = pend.pop()
                for s, v in pend:
                    eng.wait_ge(s, v)
                    known[s] = v
                ins = op.fn(eng)
                if fused is not None:
                    ins._wait_ge(fused[0], fused[1])
                    known[fused[0]] = fused[1]
                if op.dma:
                    ins.then_inc(op.sem, 16)
                elif op.signal:
                    ins.then_inc(op.sem, 1)
            if extra_final is not None:
                extra_final(ename, eng, known)

        @block.tensor
        def _(eng):
            run_engine("pe", eng)

        @block.scalar
        def _(eng):
            run_engine("act", eng)

        @block.vector
        def _(eng):
            run_engine("dve", eng)

        @block.gpsimd
        def _(eng):
            run_engine("pool", eng)

        @block.sync
        def _(eng):
            run_engine("sp", eng)


L = 4096
D = 1024
NST = 8
EPS = 1e-6
LAM_INIT = 0.8 - 0.6 * math.exp(-0.3 * 0)
TWO_PI = 2.0 * math.pi
NEG = -30000.0


def _bucket_np(rel):
    nb = 16
    me = 8
    side = np.where(rel > 0, nb, 0)
    n = np.abs(rel)
    nf = np.maximum(n, 1).astype(np.float32)
    large = me + (np.log(nf / np.float32(me)).astype(np.float32) / np.float32(math.log(128 / 8))
                  * np.float32(nb - me)).astype(np.int32)
    large = np.minimum(large, nb - 1)
    return side + np.where(n < me, n, large)


def host_consts():
    c = {}
    c["c_ident"] = np.eye(128, dtype=np.float32).astype(ml_dtypes.bfloat16)
    bo = np.zeros((128, 128), np.float32)
    bo[:64, :64] = 1.0 / 64
    bo[64:, 64:] = 1.0 / 64
    c["c_bones"] = bo.astype(ml_dtypes.bfloat16)
    c["c_J"] = np.ascontiguousarray(np.eye(128, dtype=np.float32)[::-1])
    rel = np.arange(-255, 128)
    b = _bucket_np(rel)
    oh = np.zeros((32, 384), np.float32)
    oh[b, np.arange(383)] = 1.0
    c["c_oh"] = oh
    k = np.arange(128)[:, None]
    q = np.arange(128)[None, :]
    c["c_maskD"] = np.where((k // 64) <= (q // 64), 0.0, NEG).astype(np.float32)
    c["c_iotap"] = np.stack([np.arange(128), -np.arange(128)], 1).astype(np.float32)
    c["c_iotat"] = np.tile(np.arange(128, dtype=np.float32)[None, :], (128, 1))
    c["c_tri"] = (np.arange(128)[:, None] <= np.arange(128)[None, :]).astype(np.float32).astype(ml_dtypes.bfloat16)
    return c


class Mem:
    def __init__(self, nc, start=16640, end=229376):
        self.nc = nc
        self.p = start
        self.end = end
        self.n = 0

    def alloc(self, name, shape, dtype):
        esz = 4 if dtype in (F32, mybir.dt.int32) else 2
        size = int(np.prod(shape[1:])) * esz
        size = (size + 63) // 64 * 64
        assert self.p + size <= self.end, f"SBUF overflow allocating {name}: {self.p}+{size} > {self.end}"
        self.n += 1
        t = self.nc.alloc_sbuf_tensor_at(f"{name}_{self.n}", list(shape), dtype, offset=self.p)
        self.p += size
        return t

    def mark(self):
        return self.p

    def release(self, m):
        self.p = m


def build_nc(dbg=None):
    nc = bass.Bass("TRN2", target_bir_lowering=False)
    S = Sched(nc)
    M = Mem(nc)

    def din(name, shape, dt=F32):
        return nc.dram_tensor(name, list(shape), dt, kind="ExternalInput").ap()

    x_d = din("x", [L, D])
    w_in_d = din("w_in", [D, 5120])
    ng_d = din("norm_gain", [D])
    mb_d = din("merge_gate_b", [2048])
    qg_d = din("q_norm_gain", [64])
    kg_d = din("k_norm_gain", [64])
    lq1_d = din("lambda_q1", [64]); lk1_d = din("lambda_k1", [64])
    lq2_d = din("lambda_q2", [64]); lk2_d = din("lambda_k2", [64])
    sg_d = din("diff_subln_gain", [128])
    rb_d = din("rel_bias_table", [32, 4])
    are_d = din("ssm_A_re", [2048]); aim_d = din("ssm_A_im", [2048])
    ldt_d = din("ssm_log_dt", [32])
    bre_d = din("ssm_B_re", [32 * 64 * 16]); bim_d = din("ssm_B_im", [32 * 64 * 16])
    cre_d = din("ssm_C_re", [32 * 16 * 64]); cim_d = din("ssm_C_im", [32 * 16 * 64])
    dd_d = din("ssm_D", [512])
    gw_d = din("ssm_glu_w", [512, 512])
    gb_d = din("ssm_glu_b", [512])
    pa_d = din("proj_attn", [512, 1024])
    ps_d = din("proj_ssm", [512, 1024])
    wo_d = din("w_out", [1024, 1024])
    c_ident_d = din("c_ident", [128, 128], BF16)
    c_bones_d = din("c_bones", [128, 128], BF16)
    c_J_d = din("c_J", [128, 128])
    c_oh_d = din("c_oh", [32, 384])
    c_maskD_d = din("c_maskD", [128, 128])
    c_iotap_d = din("c_iotap", [128, 2])
    c_iotat_d = din("c_iotat", [128, 128])
    c_tri_d = din("c_tri", [128, 128], BF16)
    out_d = nc.dram_tensor("out", [L, D], F32, kind="ExternalOutput").ap()
    fsc_d = nc.dram_tensor("fscratch", [4, 384], F32).ap()
    dbg_d = None
    if dbg:
        dbg_d = nc.dram_tensor("dbg", [128, 4 * L], BF16, kind="ExternalOutput").ap()

    def dap(t, off, pat):
        return bass.AP(t.tensor, off, [list(p) for p in pat])

    PS = [nc.alloc_psum_tensor(f"psb{i}", [128, 512], F32) for i in range(8)]

    def MM(out, lhsT, rhs, start, stop, r, w, fuse=False):
        S.add("pe", lambda e: e.matmul(out, lhsT=lhsT, rhs=rhs, start=start, stop=stop), r, w, fuse=fuse)

    def TR(out, in_, r, w):
        S.add("pe", lambda e: e.transpose(out=out, in_=in_, identity=ident[:]), tuple(r) + ("ident",), w)

    def ACT(out, in_, func, r, w, bias=0.0, scale=1.0, accum=None):
        if accum is None:
            S.add("act", lambda e: e.activation(out=out, in_=in_, func=func, bias=bias, scale=scale), r, w)
        else:
            S.add("act", lambda e: e.activation(out=out, in_=in_, func=func, bias=bias, scale=scale, accum_out=accum), r, w)

    def TS(eng, out, in0, s1, s2, op0, op1, r, w):
        if s2 is None:
            S.add(eng, lambda e: e.tensor_scalar(out=out, in0=in0, scalar1=s1, scalar2=None, op0=op0), r, w)
        else:
            S.add(eng, lambda e: e.tensor_scalar(out=out, in0=in0, scalar1=s1, scalar2=s2, op0=op0, op1=op1), r, w)

    def TT(eng, out, in0, in1, op, r, w):
        S.add(eng, lambda e: e.tensor_tensor(out=out, in0=in0, in1=in1, op=op), r, w)

    def STT(eng, out, in0, scalar, in1, op0, op1, r, w):
        S.add(eng, lambda e: e.scalar_tensor_tensor(out=out, in0=in0, scalar=scalar, in1=in1, op0=op0, op1=op1), r, w)

    def CP(eng, out, in_, r, w):
        if eng == "act":
            S.add("act", lambda e: e.copy(out=out, in_=in_), r, w)
        else:
            S.add(eng, lambda e: e.tensor_copy(out=out, in_=in_), r, w)

    def RSUM(out, in_, r, w):
        S.add("dve", lambda e: e.reduce_sum(out=out, in_=in_, axis=AX.X), r, w)

    def RECIP(out, in_, r, w):
        S.add("dve", lambda e: e.reciprocal(out=out, in_=in_), r, w)

    def NOP(r):
        S.add("sp", lambda e: e.nop(), r, [])

    def MS(eng, ap, val, w):
        S.add(eng, lambda e: e.memset(ap, val), (), w)

    def DMA(eng, out, in_, r, w, dkey, slow=False):
        if slow:
            S.add(eng, lambda e: e.dma_start(out=out, in_=in_, allow_slow_non_contiguous=True), r, w, dma=True, dkey=dkey)
        else:
            S.add(eng, lambda e: e.dma_start(out=out, in_=in_), r, w, dma=True, dkey=dkey)

    bar = M.alloc("bar", [128, 16], F32)
    S._bar_tile = bar
    ident = M.alloc("ident", [128, 128], BF16)
    bones = M.alloc("bones", [128, 128], BF16)
    tri = M.alloc("tri", [128, 128], BF16)
    iotap = M.alloc("iotap", [128, 2], F32)
    iotat = M.alloc("iotat", [128, 128], F32)
    gcol = M.alloc("gcol", [128, 8], F32)
    rs_all = M.alloc("rs_all", [128, 3 * 32 * 2], F32)
    OAT = M.alloc("OAT", [128, 4, L], BF16)
    OSTm = M.alloc("OSTm", [128, 4 * L], BF16)
    OST = OSTm[:, :].rearrange("p (c t) -> p c t", c=4)
    stag32 = OSTm[:, :].bitcast(F32)
    Bst = stag32[:, 0:4096].rearrange("p (r c g q) -> p r c g q", r=2, c=4, g=8)
    Cst = stag32[:, 4096:8192].rearrange("p (r a c) -> p r a c", r=2, a=16)

    DMA("sp", ident[:], c_ident_d[:, :], (), ["ident"], "ident")
    DMA("sp", bones[:], c_bones_d[:, :], (), ["bones"], "bones")
    DMA("sp", tri[:], c_tri_d[:, :], (), ["tri"], "tri")
    DMA("sp", iotap[:], c_iotap_d[:, :], (), ["iotap"], "iotap")
    DMA("sp", iotat[:], c_iotat_d[:, :], (), ["iotat"], "iotat")
    DMA("sp", gcol[:], dap(ng_d, 0, [[1, 128], [128, 8]]), (), ["gcol"], "gcol", slow=True)

    MS("dve", stag32[:, 0:4096], 0.0, ["Bst"])
    MS("pool", stag32[:, 4096:8192], 0.0, ["Cst"])
    bc_dmas = []
    for g in range(32):
        for ri in range(2):
            bc_dmas.append((g, ri))

    def emit_bc(n):
        for _ in range(n):
            if not bc_dmas:
                return
            g, ri = bc_dmas.pop(0)
            ct, g8 = g // 8, g % 8
            pair, g2 = g // 2, g % 2
            bd = (bre_d, bim_d)[ri]
            cd = (cre_d, cim_d)[ri]
            DMA("sp", Bst[16 * g8:16 * g8 + 16, ri, ct, g8, :], dap(bd, g * 1024, [[1, 16], [16, 64]]),
                (), ["Bst"], "Bst", slow=True)
            DMA("sp", Cst[64 * g2:64 * g2 + 64, ri, pair, (g % 8) * 16:(g % 8) * 16 + 16],
                dap(cd, g * 1024, [[1, 64], [64, 16]]), (), ["Cst"], "Cst", slow=True)

    def load_weight(dst, src_d, row0, nrows_tiles, col0, ncols, stg, fold_gain, keyw):
        i = 0
        for kt in range(nrows_tiles):
            for c0 in range(0, ncols, 2048):
                cw = min(2048, ncols - c0)
                sb = stg[i % 2]
                sk = f"stg{i % 2}"
                DMA("sp", sb[:, 0:cw], src_d[row0 + kt * 128: row0 + (kt + 1) * 128, col0 + c0: col0 + c0 + cw],
                    (), [sk], sk)
                if fold_gain:
                    if i % 2 == 0:
                        ACT(dst[:, kt, c0:c0 + cw], sb[:, 0:cw], AF.Copy, [sk, "gcol"], [keyw], scale=gcol[:, kt:kt + 1])
                    else:
                        TS("dve", dst[:, kt, c0:c0 + cw], sb[:, 0:cw], gcol[:, kt:kt + 1], None, ALU.mult, None, [sk, "gcol"], [keyw])
                else:
                    if i % 2 == 0:
                        CP("act", dst[:, kt, c0:c0 + cw], sb[:, 0:cw], [sk], [keyw])
                    else:
                        CP("dve", dst[:, kt, c0:c0 + cw], sb[:, 0:cw], [sk], [keyw])
                i += 1

    def make_hT(st, phase, hT, xt, xn, tbanks):
        for tt in range(4):
            i = st * 4 + tt
            b = i % len(xt)
            nb_ = i % len(xn)
            tbk = tbanks[i % len(tbanks)]
            psT = PS[tbk][:, 0:512].bitcast(BF16)
            row = st * 512 + tt * 128
            col = (phase * 32 + i) * 2
            ss = rs_all[:, col:col + 1]
            rs = rs_all[:, col + 1:col + 2]
            xk, nk = f"xt{b}", f"xn{nb_}"
            DMA("sp", xt[b][:], x_d[row:row + 128, :], (), [xk], xk)
            ACT(xn[nb_][:], xt[b][:], AF.Square, [xk], [nk, f"ss{col}"], accum=ss)
            ACT(rs, ss, AF.Ln, [f"ss{col}"], [f"rs{col}"], bias=EPS, scale=1.0 / D)
            ACT(rs, rs, AF.Exp, [f"rs{col}"], [f"rs{col}"], scale=-0.5)
            TS("dve", xn[nb_][:], xt[b][:], rs, None, ALU.mult, None, [xk, f"rs{col}"], [nk])
            for kt in range(8):
                TR(psT[:, kt * 128:(kt + 1) * 128], xn[nb_][:, kt * 128:(kt + 1) * 128], [nk], [f"ps{tbk}"])
            CP("dve", hT[:, :, tt * 128:(tt + 1) * 128], psT[:, :].rearrange("p (k t) -> p k t", k=8), [f"ps{tbk}"], ["hT"])

    m_glob = M.mark()
    wq = M.alloc("wq", [128, 8, 1536], BF16)
    KT = M.alloc("KT", [128, 4, L], BF16)
    V = M.alloc("V", [128, 32, 4, 128], BF16)
    ones128 = M.alloc("ones128", [128, 128], BF16)
    sgcol = M.alloc("sgcol", [128, 1], F32)
    c15 = M.alloc("c15", [128, 4], F32)
    bnear = M.alloc("bnear", [128, 2, 2, 4, 128], BF16)
    gq = M.alloc("gq", [128, 2], F32)
    lamt = M.alloc("lamt", [128, 8], F32)
    m_work = M.mark()
    stg = [M.alloc("stg", [128, 2048], F32) for _ in range(2)]
    lamv = M.alloc("lamv", [128, 4, 64], F32)
    tbl = M.alloc("tbl", [32, 4], F32)
    oh = M.alloc("oh", [32, 384], F32)
    fsb = M.alloc("fsb", [4, 384], F32)
    Jt = M.alloc("Jt", [128, 128], F32)
    hank = M.alloc("hank", [128, 4, 256], F32)
    maskD = M.alloc("maskD", [128, 128], F32)
    biasP = M.alloc("biasP", [128, 4, 128], F32)
    biasD = M.alloc("biasD", [128, 4, 128], F32)
    bhi32 = M.alloc("bhi32", [128, 4, 128], F32)

    load_weight(wq, w_in_d, 0, 8, 0, 1536, stg, True, "wq")
    MS("dve", ones128[:], 1.0, ["ones128"])
    DMA("sp", sgcol[:], dap(sg_d, 0, [[1, 128], [1, 1]]), (), ["sgcol"], "sgcol")
    TS("dve", sgcol[:], sgcol[:], 1.0 - LAM_INIT, None, ALU.mult, None, ["sgcol"], ["sgcol"])
    for hlf in range(2):
        DMA("sp", gq[64 * hlf:64 * hlf + 64, 0:1], dap(qg_d, 0, [[1, 64], [1, 1]]), (), ["gq"], "gq")
        DMA("sp", gq[64 * hlf:64 * hlf + 64, 1:2], dap(kg_d, 0, [[1, 64], [1, 1]]), (), ["gq"], "gq")
    TS("dve", gq[:, 0:1], gq[:, 0:1], 0.125, None, ALU.mult, None, ["gq"], ["gq"])
    for i, dd in enumerate((lq1_d, lk1_d, lq2_d, lk2_d)):
        DMA("sp", lamv[:, i, :], dap(dd, 0, [[0, 128], [1, 64]]), (), ["lamv"], "lamv")
    TT("dve", lamv[:, 0, :], lamv[:, 0, :], lamv[:, 1, :], ALU.mult, ["lamv"], ["lamv"])
    TT("dve", lamv[:, 2, :], lamv[:, 2, :], lamv[:, 3, :], ALU.mult, ["lamv"], ["lamv"])
    RSUM(lamt[:, 0:1], lamv[:, 0, :], ["lamv"], ["lamt"])
    RSUM(lamt[:, 1:2], lamv[:, 2, :], ["lamv"], ["lamt"])
    ACT(lamt[:, 2:4], lamt[:, 0:2], AF.Exp, ["lamt"], ["lamt"])
    TT("dve", lamt[:, 4:5], lamt[:, 3:4], lamt[:, 2:3], ALU.subtract, ["lamt"], ["lamt"])
    TS("dve", lamt[:, 5:6], lamt[:, 4:5], -LAM_INIT, None, ALU.add, None, ["lamt"], ["lamt"])
    neglam = lamt[:, 5:6]
    DMA("sp", tbl[:], rb_d[:, :], (), ["tbl"], "tbl")
    DMA("sp", oh[:], c_oh_d[:, :], (), ["oh"], "oh")
    DMA("sp", Jt[:], c_J_d[:, :], (), ["Jt"], "Jt")
    DMA("sp", maskD[:], c_maskD_d[:, :], (), ["maskD"], "maskD")
    DMA("sp", c15[:], dap(rb_d, 15 * 4, [[0, 128], [1, 4]]), (), ["c15"], "c15")
    MM(PS[1][0:4, 0:384], tbl[:], oh[:], True, True, ["tbl", "oh"], ["ps1"])
    CP("dve", fsb[:], PS[1][0:4, 0:384], ["ps1"], ["fsb"])
    DMA("sp", fsc_d[:, :], fsb[:], ["fsb"], ["fsc"], "fsb")
    for h in range(4):
        DMA("sp", hank[:, h, :], dap(fsc_d, h * 384, [[1, 128], [1, 256]]), ["fsc"], ["hank"], "hank")
    for h in range(4):
        MM(PS[2][:, h * 128:(h + 1) * 128], hank[:, h, 0:128], Jt[:], True, True, ["hank", "Jt"], ["ps2"])
        MM(PS[3][:, h * 128:(h + 1) * 128], hank[:, h, 128:256], Jt[:], True, True, ["hank", "Jt"], ["ps3"])
    CP("dve", biasP[:], PS[2][:, :].rearrange("p (h q) -> p h q", h=4), ["ps2"], ["biasP"])
    TT("dve", biasD[:], PS[3][:, :].rearrange("p (h q) -> p h q", h=4),
       maskD[:].unsqueeze(1).broadcast_to([128, 4, 128]), ALU.add, ["ps3", "maskD"], ["biasD"])
    for d_, bt_ in ((0, biasD), (1, biasP)):
        TT("dve", bt_[:], bt_[:], c15[:, :].unsqueeze(2).broadcast_to([128, 4, 128]), ALU.subtract, ["biasD", "biasP", "c15"], ["biasD", "biasP"])
        CP("dve", bnear[:, d_, 0], bt_[:], ["biasD", "biasP"], ["bnear"])
        CP("dve", bhi32[:], bnear[:, d_, 0], ["bnear"], ["bhi32"])
        TT("dve", bnear[:, d_, 1], bt_[:], bhi32[:], ALU.subtract, ["biasD", "biasP", "bhi32"], ["bnear"])
    S.barrier()
    M.release(m_work)
    hT = M.alloc("hT", [128, 8, 512], BF16)
    xt = [M.alloc("xt", [128, 1024], F32) for _ in range(2)]
    xn = [M.alloc("xn", [128, 1024], BF16) for _ in range(2)]
    qT = M.alloc("qT", [128, 4, 512], BF16)
    sq = M.alloc("sq", [128, 512], BF16)
    rstd = M.alloc("rstd", [128, 512], F32)
    PTP = [M.alloc("PTP", [128, 2, 512], BF16) for _ in range(2)]
    PT = [PTP[0][:, 0, :], PTP[0][:, 1, :], PTP[1][:, 0, :], PTP[1][:, 1, :]]
    rcp = rstd
    oraw = [M.alloc("oraw", [128, 512], F32) for _ in range(2)]
    racc = [M.alloc("racc", [128, 2, 512], F32) for _ in range(2)]
    tob = oraw[1]
    rsb = [M.alloc("rsb", [128, 512], BF16) for _ in range(2)]
    sqb = sq
    rs2 = M.alloc("rs2", [128, 512], F32)
    sq2 = [sq, M.alloc("sq2", [128, 512], BF16)]
    rstd2 = [rstd, rs2]
    rkeys = ["rstd0", "rs2"]

    PSB = [3, 4, 7, 6]
    NB = 4
    SKEW = 2
    PAIRED = True
    QPAD = False
    FILL = 0
    deferred = []

    ZB3 = [1, 2, 4]

    def qk_mm(c, st):
        zb = ZB3[c % 3]
        for kt in range(8):
            MM(PS[zb][:, :], wq[:, kt, c * 128:(c + 1) * 128], hT[:, kt, :], kt == 0, kt == 7, ["wq", "hT"], [f"ps{zb}"], fuse=True)

    MB = [0, 3]

    def qk_c1(c, st):
        zb = ZB3[c % 3]
        sqc = sq2[c % 2]
        mb_ = MB[c % 2]
        ACT(sqc[:], PS[zb][:, :], AF.Square, [f"ps{zb}"], [f"sq{c % 2}"])
        MM(PS[mb_][:, :], bones[:], sqc[:], True, True, ["bones", f"sq{c % 2}"], [f"ps{mb_}"])

    def qk_c2(c, st):
        zb = ZB3[c % 3]
        rsc = rstd2[c % 2]
        mb_ = MB[c % 2]
        ACT(rsc[:], PS[mb_][:, :], AF.Ln, [f"ps{mb_}"], [rkeys[c % 2]], bias=EPS)
        ACT(rsc[:], rsc[:], AF.Exp, [rkeys[c % 2]], [rkeys[c % 2]], scale=-0.5)
        if c < 4:
            STT("dve", qT[:, c, :], PS[zb][:, :], gq[:, 0:1], rsc[:], ALU.mult, ALU.mult, [f"ps{zb}", "gq", rkeys[c % 2]], ["qT"])
        else:
            STT("dve", KT[:, c - 4, st * 512:(st + 1) * 512], PS[zb][:, :], gq[:, 1:2], rsc[:], ALU.mult, ALU.mult,
                [f"ps{zb}", "gq", rkeys[c % 2]], [f"KT{st}"])

    for st in range(NST):
        make_hT(st, 0, hT, xt, xn, [0, 3])
        emit_bc(8)
        qk_mm(0, st)
        qk_mm(1, st)
        qk_c1(0, st)
        for c in range(8):
            if c + 2 < 8:
                qk_mm(c + 2, st)
            if c + 1 < 8:
                qk_c1(c + 1, st)
            qk_c2(c, st)
        for tt in range(4):
            blk = st * 4 + tt
            zb = 1 + (tt % 2)
            for kt in range(8):
                MM(PS[zb][:, :], hT[:, kt, tt * 128:(tt + 1) * 128], wq[:, kt, 1024:1536], kt == 0, kt == 7, ["wq", "hT"], [f"ps{zb}"])
            CP("dve", V[:, blk, :, :], PS[zb][:, :].rearrange("p (h d) -> p h d", h=4), [f"ps{zb}"], [f"V{st}"])
        kv_keys = [f"KT{i}" for i in range(st + 1)] + [f"V{i}" for i in range(st + 1)]
        iters = [(h, s, j) for h in range(4) for j in range(4 * st + 4) for s in range(2)]
        NI = len(iters)

        def emit_S(i, st=st, kv_keys=kv_keys, iters=iters):
            h, s, j = iters[i]
            lo = max(0, j - 4 * st)
            cols = slice(lo * 128, 512)
            pb = i % NB
            psS = PS[PSB[pb]]
            near = [(qb, 4 * st + qb - j) for qb in range(lo, 4) if 4 * st + qb - j <= 1]
            MM(psS[:, cols], KT[64 * s:64 * s + 64, h, j * 128:(j + 1) * 128], qT[64 * s:64 * s + 64, h, cols],
               True, not near, kv_keys + ["qT"], [f"ps{PSB[pb]}"], fuse=(j < 4 * st))
            for n_, (qb, d) in enumerate(near):
                qs_ = slice(qb * 128, (qb + 1) * 128)
                MM(psS[:, qs_], ident[:], bnear[:, d, 0, h, :], False, False, ["ident", "bnear"], [f"ps{PSB[pb]}"], fuse=True)
                MM(psS[:, qs_], ident[:], bnear[:, d, 1, h, :], False, n_ == len(near) - 1, ["ident", "bnear"], [f"ps{PSB[pb]}"], fuse=True)

        def fin_A(h, s, st):
            MM(PS[0][:, :], ones128[:], rsb[s][:], True, True, ["ones128", f"rsb{s}"], ["ps0"])
            ACT(rcp[:], PS[0][:, :], AF.Ln, ["ps0"], ["rstd0"])
            ACT(rcp[:], rcp[:], AF.Exp, ["rstd0"], ["rstd0"], scale=-1.0)
            TT("dve", oraw[s][:], oraw[s][:], rcp[:], ALU.mult, [f"oraw{s}", "rstd0"], [f"oraw{s}"])
            if s == 1:
                STT("dve", tob[:], oraw[1][:], neglam, oraw[0][:], ALU.mult, ALU.add, ["oraw0", "oraw1", "lamt"], ["oraw1"])
                ACT(sqb[:], tob[:], AF.Square, ["oraw1"], ["sq0"])

        def fin_B(h, st):
            stc = slice(st * 512, (st + 1) * 512)
            MM(PS[0][:, :], ones128[:], sqb[:], True, True, ["ones128", "sq0"], ["ps0"])
            ACT(rs2[:], PS[0][:, :], AF.Ln, ["ps0"], ["rs2"], bias=EPS, scale=1.0 / 128)
            ACT(rs2[:], rs2[:], AF.Exp, ["rs2"], ["rs2"], scale=-0.5)
            STT("dve", OAT[:, h, stc], tob[:], sgcol[:, 0:1], rs2[:], ALU.mult, ALU.mult, ["oraw1", "sgcol", "rs2"], ["OAT"])

        def emit_rest(i, st=st, kv_keys=kv_keys, iters=iters):
            h, s, j = iters[i]
            lo = max(0, j - 4 * st)
            pb = i % NB
            pk = f"ps{PSB[pb]}"
            psS = PS[PSB[pb]]
            accb = 5 if s == 0 else 2
            if j == 0 and s == 0:
                MS("dve", racc[0][:], 0.0, ["racc0"])
                MS("pool", racc[1][:], 0.0, ["racc1"])
            ACT(PT[pb][:, lo * 128:512], psS[:, lo * 128:512], AF.Exp, [pk, "c15"], [f"PT{pb}"], bias=c15[:, h:h + 1])
            qs = slice(lo * 128, 512)
            last = (j == 4 * st + 3)
            MM(PS[accb][:, qs], V[:, j, h, :], PT[pb][:, qs], j == 0, last, [f"PT{pb}"] + kv_keys, [f"ps{accb}"], fuse=(j < 4 * st))
            for _f in range(FILL):
                MM(PS[6][:, :], ones128[:], wq[:, 0, 0:512], True, True, ["ones128", "wq"], ["ps6"])
            if s == 1:
                e_ = j % 2
                pbuf = (i // 2) % 2
                TT(("dve", "pool")[e_], racc[e_][:, :, qs], racc[e_][:, :, qs], PTP[pbuf][:, :, qs], ALU.add,
                   [f"racc{e_}", f"PT{pb}", f"PT{pb - 1}"], [f"racc{e_}"])
            if not last:
                return
            CP("act", oraw[s][:], PS[accb][:, :], [f"ps{accb}"], [f"oraw{s}"])
            if s == 1:
                for ss_ in range(2):
                    TT("dve", rsb[ss_][:], racc[0][:, ss_, :], racc[1][:, ss_, :], ALU.add, ["racc0", "racc1"], [f"rsb{ss_}"])
            deferred.append((i + 4, "A", h, s, st))
            if s == 1:
                deferred.append((i + 7, "B", h, s, st))

        def run_deferred(k):
            while deferred and deferred[0][0] <= k:
                _, kind, hh, ss, sst = deferred.pop(0)
                if kind == "A":
                    fin_A(hh, ss, sst)
                else:
                    fin_B(hh, sst)

        for i in range(0, NI + SKEW, 2):
            if PAIRED:
                for ii in (i, i + 1):
                    if ii < NI:
                        emit_S(ii)
                for ii in (i, i + 1):
                    k = ii - SKEW
                    if 0 <= k < NI:
                        emit_rest(k)
                        run_deferred(k)
            else:
                for ii in (i, i + 1):
                    if ii < NI:
                        emit_S(ii)
                    k = ii - SKEW
                    if 0 <= k < NI:
                        emit_rest(k)
                        run_deferred(k)
        run_deferred(10 ** 9)

    if dbg == "p1":
        DMA("sp", dbg_d[:, :], OAT[:, :, :].rearrange("p h t -> p (h t)"), ["OAT"], ["outd"], "OAT")
        NOP(["outd"])
        return finish(nc, S)
    I32 = mybir.dt.int32
    S.barrier()
    M.release(m_glob)
    w2 = M.alloc("w2", [128, 8, 1024], BF16)
    gluw = M.alloc("gluw", [128, 4, 512], BF16)
    glub = M.alloc("glub", [128, 4], F32)
    ta = M.alloc("ta", [128, 2048], F32)
    tb = M.alloc("tb", [128, 2048], F32)
    te = M.alloc("te", [128, 16, 128], F32)
    tf = M.alloc("tf", [128, 16, 128], F32)
    Bbd = M.alloc("Bbd", [128, 4, 2, 512], BF16)
    Ccat = M.alloc("Ccat", [128, 2, 16, 128], BF16)
    Ddiag = M.alloc("Ddiag", [128, 4, 128], BF16)
    dcol = M.alloc("dcol", [128, 4], F32)
    negtri = M.alloc("negtri", [128, 128], BF16)
    CcatN = M.alloc("CcatN", [128, 16, 128], BF16)
    l128 = M.alloc("l128", [128, 6, 16], F32)
    cre = [M.alloc("cre", [128, 16], F32) for _ in range(2)]
    cim = [M.alloc("cim", [128, 16], F32) for _ in range(2)]
    m_work2 = M.mark()
    stg = [M.alloc("stg", [128, 2048], F32) for _ in range(2)]
    load_weight(w2, w_in_d, 0, 8, 2048, 1024, stg, True, "w2")
    load_weight(gluw, gw_d, 0, 4, 0, 512, stg, False, "gluw")
    S.barrier()
    M.release(m_work2)
    T1 = M.alloc("T1", [128, 2048], F32)
    T2 = M.alloc("T2", [128, 2048], F32)
    T3 = M.alloc("T3", [128, 2048], F32)
    T4 = M.alloc("T4", [128, 2048], F32)
    TI = M.alloc("TI", [128, 2048], I32)
    dtb = M.alloc("dtb", [128, 32], F32)
    s2 = M.alloc("s2", [128, 8, 16], F32)
    s3 = M.alloc("s3", [128, 14, 256], F32)
    s3i = M.alloc("s3i", [128, 256], I32)
    dt3 = M.alloc("dt3", [128, 4], F32)

    DMA("sp", glub[:], dap(gb_d, 0, [[1, 128], [128, 4]]), (), ["glub"], "glub", slow=True)

    def frac_sincos(y, tmp, ti, sin_out, cos_out, key):
        CP("dve", ti, y, [key + "y"], [key + "ti"])
        CP("dve", tmp, ti, [key + "ti"], [key + "tmp"])
        TT("dve", tmp, y, tmp, ALU.subtract, [key + "y", key + "tmp"], [key + "tmp"])
        ACT(sin_out, tmp, AF.Sin, [key + "tmp"], [key + "sin"], scale=TWO_PI)
        TS("dve", y, y, 0.25, None, ALU.add, None, [key + "y"], [key + "y"])
        CP("dve", ti, y, [key + "y"], [key + "ti"])
        CP("dve", tmp, ti, [key + "ti"], [key + "tmp"])
        TT("dve", tmp, y, tmp, ALU.subtract, [key + "y", key + "tmp"], [key + "tmp"])
        ACT(cos_out, tmp, AF.Sin, [key + "tmp"], [key + "cos"], scale=TWO_PI)

    DMA("sp", T1[:], dap(are_d, 0, [[0, 128], [1, 2048]]), (), ["T1"], "T1")
    DMA("sp", T2[:], dap(aim_d, 0, [[0, 128], [1, 2048]]), (), ["T2"], "T2")
    DMA("sp", dtb[:], dap(ldt_d, 0, [[0, 128], [1, 32]]), (), ["dtb"], "dtb")
    ACT(dtb[:], dtb[:], AF.Exp, ["dtb"], ["dtb"])
    dtb_b = dtb[:, :].unsqueeze(2).broadcast_to([128, 32, 64])
    TT("dve", T1[:, :].rearrange("p (g q) -> p g q", g=32), T1[:, :].rearrange("p (g q) -> p g q", g=32), dtb_b, ALU.mult, ["T1", "dtb"], ["T1"])
    TT("dve", T2[:, :].rearrange("p (g q) -> p g q", g=32), T2[:, :].rearrange("p (g q) -> p g q", g=32), dtb_b, ALU.mult, ["T2", "dtb"], ["T2"])
    ACT(ta[:], T1[:], AF.Exp, ["T1", "iotap"], ["ta"], scale=iotap[:, 1:2])
    TS("dve", T3[:], T2[:], iotap[:, 0:1], 1.0 / TWO_PI, ALU.mult, ALU.mult, ["T2", "iotap"], ["L1y"])
    frac_sincos(T3[:], T4[:], TI[:], tb[:], T1[:], "L1")
    STT("dve", tb[:], tb[:], -1.0, ta[:], ALU.mult, ALU.mult, ["L1sin", "ta"], ["tb", "L1sin"])
    TT("dve", ta[:], ta[:], T1[:], ALU.mult, ["ta", "L1cos", "tb"], ["ta"])
    A2re, A2im, dt2, m2, th2 = (s2[:, i, :] for i in range(5))
    DMA("sp", A2re, dap(are_d, 0, [[1, 128], [128, 16]]), (), ["A2re"], "A2re", slow=True)
    DMA("sp", A2im, dap(aim_d, 0, [[1, 128], [128, 16]]), (), ["A2im"], "A2im", slow=True)
    for g2 in range(2):
        DMA("sp", s2[64 * g2:64 * g2 + 64, 2, :], dap(ldt_d, g2, [[0, 64], [2, 16]]), (), ["dt2"], "dt2", slow=True)
    ACT(dt2, dt2, AF.Exp, ["dt2"], ["dt2"])
    TT("dve", m2, A2re, dt2, ALU.mult, ["A2re", "dt2"], ["m2"])
    STT("dve", th2, A2im, 1.0 / TWO_PI, dt2, ALU.mult, ALU.mult, ["A2im", "dt2"], ["th2"])
    T3v = T3[:, :].rearrange("p (a t) -> p a t", a=16)
    for pair in range(16):
        ACT(te[:, pair, :], iotat[:], AF.Exp, ["iotat", "m2"], ["te"], scale=s2[:, 3, pair:pair + 1])
        TS("dve", T3v[:, pair, :], iotat[:], s2[:, 4, pair:pair + 1], None, ALU.mult, None, ["iotat", "th2", "L1y", "L1tmp"], ["L2y"])
    frac_sincos(T3[:], T4[:], TI[:], tf[:, :, :].rearrange("p a t -> p (a t)"), T1[:], "L2")
    TT("dve", tf[:, :, :].rearrange("p a t -> p (a t)"), tf[:, :, :].rearrange("p a t -> p (a t)"), te[:, :, :].rearrange("p a t -> p (a t)"),
       ALU.mult, ["L2sin", "te"], ["tf", "L2sin"])
    TT("dve", te[:, :, :].rearrange("p a t -> p (a t)"), te[:, :, :].rearrange("p a t -> p (a t)"), T1[:], ALU.mult, ["te", "L2cos", "tf"], ["te"])
    A3re, A3im, m3, y3, dec3, sin3, cos3, nr3, den3, qre3, qim3, u3a, u3b, tmp3 = (s3[:, i, :] for i in range(14))
    for g8 in range(8):
        DMA("sp", s3[16 * g8:16 * g8 + 16, 0, :].rearrange("p (c q) -> p c q", c=4), dap(are_d, g8 * 64, [[0, 16], [512, 4], [1, 64]]), (), ["A3re"], "A3re")
        DMA("sp", s3[16 * g8:16 * g8 + 16, 1, :].rearrange("p (c q) -> p c q", c=4), dap(aim_d, g8 * 64, [[0, 16], [512, 4], [1, 64]]), (), ["A3im"], "A3im")
        DMA("sp", dt3[16 * g8:16 * g8 + 16, :], dap(ldt_d, g8, [[0, 16], [8, 4]]), (), ["dt3"], "dt3", slow=True)
    ACT(dt3[:], dt3[:], AF.Exp, ["dt3"], ["dt3"])
    dt3_b = dt3[:, :].unsqueeze(2).broadcast_to([128, 4, 64])
    v3 = lambda a: a.rearrange("p (c q) -> p c q", c=4)
    TT("dve", v3(m3), v3(A3re), dt3_b, ALU.mult, ["A3re", "dt3"], ["m3"])
    TT("dve", v3(y3), v3(A3im), dt3_b, ALU.mult, ["A3im", "dt3"], ["L3y"])
    TS("dve", y3, y3, 1.0 / TWO_PI, None, ALU.mult, None, ["L3y"], ["L3y"])
    ACT(dec3, m3, AF.Exp, ["m3"], ["dec3"])
    frac_sincos(y3, tmp3, s3i[:], sin3, cos3, "L3")
    TT("dve", cos3, cos3, dec3, ALU.mult, ["L3cos", "dec3"], ["lbr"])
    TT("dve", sin3, sin3, dec3, ALU.mult, ["L3sin", "dec3"], ["lbi"])
    TS("dve", nr3, cos3, -1.0, None, ALU.add, None, ["lbr"], ["nr3"])
    TT("dve", den3, A3re, A3re, ALU.mult, ["A3re"], ["den3"])
    TT("dve", u3a, A3im, A3im, ALU.mult, ["A3im"], ["u3a"])
    TT("dve", den3, den3, u3a, ALU.add, ["den3", "u3a"], ["den3"])
    RECIP(den3, den3, ["den3"], ["den3"])
    TT("dve", u3a, nr3, A3re, ALU.mult, ["nr3", "A3re", "den3"], ["u3a"])
    TT("dve", u3b, sin3, A3im, ALU.mult, ["lbi", "A3im"], ["u3b"])
    TT("dve", qre3, u3a, u3b, ALU.add, ["u3a", "u3b"], ["qre3"])
    TT("dve", qre3, qre3, den3, ALU.mult, ["qre3", "den3"], ["qre3"])
    TT("dve", u3a, sin3, A3re, ALU.mult, ["lbi", "A3re", "qre3"], ["u3a"])
    TT("dve", u3b, nr3, A3im, ALU.mult, ["nr3", "A3im", "qre3"], ["u3b"])
    TT("dve", qim3, u3a, u3b, ALU.subtract, ["u3a", "u3b"], ["qim3"])
    TT("dve", qim3, qim3, den3, ALU.mult, ["qim3", "den3"], ["qim3"])
    qre_b = v3(qre3).unsqueeze(2).broadcast_to([128, 4, 8, 64])
    qim_b = v3(qim3).unsqueeze(2).broadcast_to([128, 4, 8, 64])
    T1v4 = T1[:, :].rearrange("p (c g q) -> p c g q", c=4, g=8)
    T2v4 = T2[:, :].rearrange("p (c g q) -> p c g q", c=4, g=8)
    TT("dve", T1v4, Bst[:, 0], qre_b, ALU.mult, ["Bst", "qre3", "ta", "te"], ["T1"])
    TT("dve", T2v4, Bst[:, 1], qim_b, ALU.mult, ["Bst", "qim3", "L1y", "L2y"], ["T2"])
    TT("dve", Bbd[:, :, 0, :], T1[:, :].rearrange("p (c x) -> p c x", c=4), T2[:, :].rearrange("p (c x) -> p c x", c=4), ALU.subtract, ["T1", "T2"], ["Bbd"])
    TT("dve", T1v4, Bst[:, 1], qre_b, ALU.mult, ["Bst", "qre3", "Bbd"], ["T1"])
    TT("dve", T2v4, Bst[:, 0], qim_b, ALU.mult, ["Bst", "qim3", "Bbd"], ["T2"])
    TT("dve", Bbd[:, :, 1, :], T1[:, :].rearrange("p (c x) -> p c x", c=4), T2[:, :].rearrange("p (c x) -> p c x", c=4), ALU.add, ["T1", "T2"], ["Bbd"])
    CP("dve", Ccat[:, 0], Cst[:, 0], ["Cst"], ["Ccat"])
    TS("dve", CcatN[:], Cst[:, 0], -1.0, None, ALU.mult, None, ["Cst"], ["CcatN"])
    TS("dve", negtri[:], tri[:], -1.0, None, ALU.mult, None, ["tri"], ["negtri"])
    TT("dve", l128[:, 0, :], te[:, :, 127], te[:, :, 1], ALU.mult, ["te"], ["l128a"])
    TT("dve", l128[:, 1, :], tf[:, :, 127], tf[:, :, 1], ALU.mult, ["tf"], ["l128b"])
    TT("dve", l128[:, 4, :], l128[:, 0, :], l128[:, 1, :], ALU.subtract, ["l128a", "l128b"], ["l128"])
    TT("dve", l128[:, 2, :], te[:, :, 127], tf[:, :, 1], ALU.mult, ["te", "tf"], ["l128c"])
    TT("dve", l128[:, 3, :], tf[:, :, 127], te[:, :, 1], ALU.mult, ["te", "tf"], ["l128d"])
    TT("dve", l128[:, 5, :], l128[:, 2, :], l128[:, 3, :], ALU.add, ["l128c", "l128d"], ["l128"])
    TS("dve", Ccat[:, 1], Cst[:, 1], -1.0, None, ALU.mult, None, ["Cst"], ["Ccat"])
    DMA("sp", dcol[:], dap(dd_d, 0, [[1, 128], [128, 4]]), (), ["dcol"], "dcol", slow=True)
    for ct in range(4):
        TS("dve", Ddiag[:, ct, :], ident[:], dcol[:, ct:ct + 1], None, ALU.mult, None, ["ident", "dcol"], ["Ddiag"])
    MS("dve", cre[0][:], 0.0, ["c0_0", "c0_1", "c0_2", "c0_3"])
    MS("dve", cim[0][:], 0.0, ["c0_0", "c0_1", "c0_2", "c0_3"])
    S.barrier()
    M.release(m_work2)
    hT = M.alloc("hT", [128, 8, 512], BF16)
    xt = [M.alloc("xt", [128, 1024], F32) for _ in range(2)]
    xn = [M.alloc("xn", [128, 1024], BF16) for _ in range(2)]
    uT = M.alloc("uT", [128, 4, 512], BF16)
    gsT = M.alloc("gsT", [128, 4, 512], BF16)
    gT = M.alloc("gT", [128, 4, 512], BF16)
    xs = [M.alloc("xs", [128, 512], F32) for _ in range(2)]
    ys = [M.alloc("ys", [128, 512], F32) for _ in range(2)]
    pa_ = [[M.alloc("pa_", [128, 512], BF16) for _ in range(4)] for _ in range(2)]
    pb_ = [[M.alloc("pb_", [128, 4, 128], BF16) for _ in range(4)] for _ in range(2)]
    sr = [M.alloc("sr", [128, 4, 128], F32) for _ in range(2)]
    si = [M.alloc("si", [128, 4, 128], F32) for _ in range(2)]
    sg = M.alloc("sg", [128, 512], F32)
    xl = M.alloc("xl", [128, 4, 4], F32)

    def ssm_A(st, cc, ct, u):
        tok = slice(cc * 128, (cc + 1) * 128)
        ub = u % 2
        MM(PS[2][:, :], uT[:, ct, tok], Bbd[:, ct, 0, :], True, True, ["uT", "Bbd"], ["ps2"])
        MM(PS[3][:, :], uT[:, ct, tok], Bbd[:, ct, 1, :], True, True, ["uT", "Bbd"], ["ps3"])
        CP("act", xs[ub][:], PS[2][:, :], ["ps2"], [f"xs{ub}"])
        CP("act", ys[ub][:], PS[3][:, :], ["ps3"], [f"ys{ub}"])
        a_ = ta[:, ct * 512:(ct + 1) * 512]
        b_ = tb[:, ct * 512:(ct + 1) * 512]
        p = pa_[ub]
        TT("dve", p[0][:], xs[ub][:], a_, ALU.mult, [f"xs{ub}", "ta"], [f"pa{ub}0"])
        TT("dve", p[1][:], ys[ub][:], b_, ALU.mult, [f"ys{ub}", "tb"], [f"pa{ub}1"])
        TT("dve", p[2][:], ys[ub][:], a_, ALU.mult, [f"ys{ub}", "ta"], [f"pa{ub}2"])
        TT("pool", p[3][:], xs[ub][:], b_, ALU.mult, [f"xs{ub}", "tb"], [f"pa{ub}3"])

    def ssm_A2(st, cc, ct, u):
        ub = u % 2
        p = pa_[ub]
        for pr in range(4):
            cs = slice(pr * 128, (pr + 1) * 128)
            MM(PS[4 + 2 * ub][:, cs], p[0][:, cs], tri[:], True, False, [f"pa{ub}0", "tri"], [f"ps{4 + 2 * ub}"])
            MM(PS[4 + 2 * ub][:, cs], p[1][:, cs], negtri[:], False, True, [f"pa{ub}1", "negtri"], [f"ps{4 + 2 * ub}"])
            MM(PS[5 + 2 * ub][:, cs], p[2][:, cs], tri[:], True, False, [f"pa{ub}2", "tri"], [f"ps{5 + 2 * ub}"])
            MM(PS[5 + 2 * ub][:, cs], p[3][:, cs], tri[:], False, True, [f"pa{ub}3", "tri"], [f"ps{5 + 2 * ub}"])

    def ssm_B(st, cc, ct, u):
        tok = slice(cc * 128, (cc + 1) * 128)
        ci = st * 4 + cc
        ub = u % 2
        cr_in, ci_in = cre[ci % 2], cim[ci % 2]
        cr_out, ci_out = cre[(ci + 1) % 2], cim[(ci + 1) % 2]
        kin, kout = f"c{ci % 2}_{ct}", f"c{(ci + 1) % 2}_{ct}"
        for pr in range(4):
            pair = 4 * ct + pr
            cs = slice(pr * 128, (pr + 1) * 128)
            ACT(sr[ub][:, pr, :], PS[4 + 2 * ub][:, cs], AF.Identity, [f"ps{4 + 2 * ub}", kin], [f"sr{ub}"], bias=cr_in[:, pair:pair + 1])
            ACT(si[ub][:, pr, :], PS[5 + 2 * ub][:, cs], AF.Identity, [f"ps{5 + 2 * ub}", kin], [f"si{ub}"], bias=ci_in[:, pair:pair + 1])
        eh = te[:, 4 * ct:4 * ct + 4, :]
        fh = tf[:, 4 * ct:4 * ct + 4, :]
        q = pb_[ub]
        TT("dve", q[0][:], sr[ub][:], eh, ALU.mult, [f"sr{ub}", "te"], [f"pb{ub}0"])
        TT("pool", q[1][:], si[ub][:], fh, ALU.mult, [f"si{ub}", "tf"], [f"pb{ub}1"])
        TT("dve", q[2][:], si[ub][:], eh, ALU.mult, [f"si{ub}", "te"], [f"pb{ub}2"])
        TT("pool", q[3][:], sr[ub][:], fh, ALU.mult, [f"sr{ub}", "tf"], [f"pb{ub}3"])
        E = l128[:, 4, 4 * ct:4 * ct + 4]
        F_ = l128[:, 5, 4 * ct:4 * ct + 4]
        ps_ = slice(4 * ct, 4 * ct + 4)
        TT("pool", xl[:, 0, :], E, sr[ub][:, :, 127], ALU.mult, ["l128", f"sr{ub}"], ["xl0"])
        TT("pool", xl[:, 1, :], F_, si[ub][:, :, 127], ALU.mult, ["l128", f"si{ub}"], ["xl1"])
        TT("pool", cr_out[:, ps_], xl[:, 0, :], xl[:, 1, :], ALU.subtract, ["xl0", "xl1"], [kout])
        TT("pool", xl[:, 2, :], E, si[ub][:, :, 127], ALU.mult, ["l128", f"si{ub}"], ["xl2"])
        TT("pool", xl[:, 3, :], F_, sr[ub][:, :, 127], ALU.mult, ["l128", f"sr{ub}"], ["xl3"])
        TT("pool", ci_out[:, ps_], xl[:, 2, :], xl[:, 3, :], ALU.add, ["xl2", "xl3"], [kout])

    def ssm_C(st, cc, ct, u):
        tok = slice(cc * 128, (cc + 1) * 128)
        ub = u % 2
        q = pb_[ub]
        ysl = slice(ct * 128, (ct + 1) * 128)
        for pr in range(4):
            pair = 4 * ct + pr
            MM(PS[0][:, ysl], Ccat[:, 0, pair, :], q[0][:, pr, :], pr == 0, False, ["Ccat", f"pb{ub}0"], ["ps0"], fuse=True)
            MM(PS[0][:, ysl], CcatN[:, pair, :], q[1][:, pr, :], False, False, ["CcatN", f"pb{ub}1"], ["ps0"], fuse=True)
            MM(PS[0][:, ysl], Ccat[:, 1, pair, :], q[2][:, pr, :], False, False, ["Ccat", f"pb{ub}2"], ["ps0"], fuse=True)
            MM(PS[0][:, ysl], Ccat[:, 1, pair, :], q[3][:, pr, :], False, False, ["Ccat", f"pb{ub}3"], ["ps0"], fuse=True)
        MM(PS[0][:, ysl], Ddiag[:, ct, :], uT[:, ct, tok], False, True, ["Ddiag", "uT"], ["ps0"], fuse=True)
        if ct == 3:
            ACT(gT[:, :, tok], PS[0][:, :].rearrange("p (c t) -> p c t", c=4), AF.Gelu, ["ps0"], ["gT"])

    for st in range(NST):
        make_hT(st, 1, hT, xt, xn, [0, 1])
        def us_mm(c):
            zb = 1 + (c % 2)
            for kt in range(8):
                MM(PS[zb][:, :], w2[:, kt, c * 128:(c + 1) * 128], hT[:, kt, :], kt == 0, kt == 7, ["w2", "hT"], [f"ps{zb}"], fuse=True)
        us_mm(0)
        for c in range(8):
            if c + 1 < 8:
                us_mm(c + 1)
            zb = 1 + (c % 2)
            if c < 4:
                CP("act", uT[:, c, :], PS[zb][:, :], [f"ps{zb}"], ["uT"])
            else:
                ACT(gsT[:, c - 4, :], PS[zb][:, :], AF.Silu, [f"ps{zb}"], ["gsT"])
        units = [(cc, ct) for cc in range(4) for ct in range(4)]
        NU = len(units)
        for k in range(-2, NU):
            if 0 <= k + 2 < NU:
                ssm_A(st, units[k + 2][0], units[k + 2][1], k + 2)
            if 0 <= k + 1 < NU:
                ssm_A2(st, units[k + 1][0], units[k + 1][1], k + 1)
                ssm_B(st, units[k + 1][0], units[k + 1][1], k + 1)
            if 0 <= k < NU:
                ssm_C(st, units[k][0], units[k][1], k)
        for c in range(4):
            zb = 1 + (c % 2)
            for kt in range(4):
                MM(PS[zb][:, :], gluw[:, kt, c * 128:(c + 1) * 128], gT[:, kt, :], kt == 0, kt == 3, ["gluw", "gT"], [f"ps{zb}"], fuse=True)
            ACT(sg[:], PS[zb][:, :], AF.Sigmoid, [f"ps{zb}", "glub"], ["sg"], bias=glub[:, c:c + 1])
            TT("dve", sg[:], sg[:], gT[:, c, :], ALU.mult, ["sg", "gT"], ["sg"])
            TT("dve", OST[:, c, st * 512:(st + 1) * 512], sg[:], gsT[:, c, :], ALU.mult, ["sg", "gsT"], ["OST", "Bst", "Cst"])
    if dbg == "p2":
        DMA("sp", dbg_d[:, :], OSTm[:, :], ["OST"], ["outd"], "OSTm")
        NOP(["outd"])
        return finish(nc, S)

    S.barrier()
    M.release(m_glob)
    w3 = M.alloc("w3", [128, 8, 2560], BF16)
    pa = M.alloc("pa", [128, 4, 1024], BF16)
    psw = M.alloc("psw", [128, 4, 1024], BF16)
    wo = M.alloc("wo", [128, 8, 1024], BF16)
    mb = M.alloc("mb", [128, 16], F32)
    m_work3 = M.mark()
    stg = [M.alloc("stg", [128, 2048], F32) for _ in range(2)]
    load_weight(w3[:, :, 0:512], w_in_d, 0, 8, 1536, 512, stg, True, "w3")
    load_weight(w3[:, :, 512:2560], w_in_d, 0, 8, 3072, 2048, stg, True, "w3")
    load_weight(pa, pa_d, 0, 4, 0, 1024, stg, False, "pa")
    load_weight(psw, ps_d, 0, 4, 0, 1024, stg, False, "psw")
    load_weight(wo, wo_d, 0, 8, 0, 1024, stg, False, "wo")
    DMA("sp", mb[:], dap(mb_d, 0, [[1, 128], [128, 16]]), (), ["mb"], "mb", slow=True)
    S.barrier()
    M.release(m_work3)
    hT = M.alloc("hT", [128, 8, 512], BF16)
    xt = [M.alloc("xt", [128, 1024], F32) for _ in range(2)]
    xn = [M.alloc("xn", [128, 1024], BF16) for _ in range(2)]
    mT = M.alloc("mT", [128, 8, 512], BF16)
    g1 = M.alloc("g1", [128, 512], F32)
    g2t = M.alloc("g2t", [128, 512], F32)
    m1 = M.alloc("m1", [128, 512], F32)
    xres = M.alloc("xres", [128, 1024], F32)
    ot = [M.alloc("ot", [128, 512], F32) for _ in range(2)]
    outkeys = []
    for st in range(NST):
        stc = slice(st * 512, (st + 1) * 512)
        make_hT(st, 2, hT, xt, xn, [0, 7])
        for h in range(4):
            for kt in range(8):
                MM(PS[1][:, :], w3[:, kt, h * 128:(h + 1) * 128], hT[:, kt, :], kt == 0, kt == 7, ["w3", "hT"], ["ps1"], fuse=True)
            ACT(g1[:], PS[1][:, :], AF.Silu, ["ps1"], ["g1"])
            TT("dve", OAT[:, h, stc], OAT[:, h, stc], g1[:], ALU.mult, ["OAT", "g1"], ["OAT"])
        for c in range(8):
            for h in range(4):
                MM(PS[1][:, :], pa[:, h, c * 128:(c + 1) * 128], OAT[:, h, stc], h == 0, h == 3, ["pa", "OAT"], ["ps1"], fuse=True)
            for kt in range(8):
                MM(PS[2][:, :], w3[:, kt, 512 + c * 128:512 + (c + 1) * 128], hT[:, kt, :], kt == 0, kt == 7, ["w3", "hT"], ["ps2"], fuse=True)
            ACT(g1[:], PS[2][:, :], AF.Sigmoid, ["ps2", "mb"], ["g1"], bias=mb[:, c:c + 1])
            TT("dve", m1[:], PS[1][:, :], g1[:], ALU.mult, ["ps1", "g1"], ["m1"])
            for k in range(4):
                MM(PS[3][:, :], psw[:, k, c * 128:(c + 1) * 128], OST[:, k, stc], k == 0, k == 3, ["psw", "OST"], ["ps3"], fuse=True)
            for kt in range(8):
                MM(PS[4][:, :], w3[:, kt, 1536 + c * 128:1536 + (c + 1) * 128], hT[:, kt, :], kt == 0, kt == 7, ["w3", "hT"], ["ps4"], fuse=True)
            ACT(g2t[:], PS[4][:, :], AF.Sigmoid, ["ps4", "mb"], ["g2t"], bias=mb[:, 8 + c:9 + c])
            TT("dve", g2t[:], PS[3][:, :], g2t[:], ALU.mult, ["ps3", "g2t"], ["g2t"])
            TT("pool", mT[:, c, :], m1[:], g2t[:], ALU.add, ["m1", "g2t"], ["mT"])
        for tt in range(4):
            row = st * 512 + tt * 128
            DMA("sp", xres[:], x_d[row:row + 128, :], (), ["xres"], "xres")
            for half in range(2):
                for kt in range(8):
                    MM(PS[5 + half][:, :], mT[:, kt, tt * 128:(tt + 1) * 128], wo[:, kt, half * 512:(half + 1) * 512], kt == 0, kt == 7,
                       ["mT", "wo"], [f"ps{5 + half}"])
                TT("dve", ot[half][:], PS[5 + half][:, :], xres[:, half * 512:(half + 1) * 512], ALU.add, [f"ps{5 + half}", "xres"], [f"ot{half}"])
                ok = f"outd{st}_{tt}_{half}"
                DMA("sp", out_d[row:row + 128, half * 512:(half + 1) * 512], ot[half][:], [f"ot{half}"], [ok], f"ot{half}")
                outkeys.append(ok)
    NOP(outkeys)
    return finish(nc, S)


def finish(nc, S):
    import contextlib
    with contextlib.ExitStack() as stk:
        S._sem_ctx = {e: stk.enter_context(nc.semaphore(f"s_{e}")) for e in S.ENGS}
        S._dsem = {k: stk.enter_context(nc.semaphore(f"d_{k}")) for k in S.dma_count}
        block = stk.enter_context(nc.Block())
        S.emit(block)
    return nc


_NC_CACHE = {}


def _core_inputs(inputs, b, consts):
    m = {"x": np.ascontiguousarray(inputs["x"][b], dtype=np.float32)}
    for k, v in inputs.items():
        if k == "x":
            continue
        v = np.asarray(v, dtype=np.float32)
        if k == "rel_bias_table":
            m[k] = np.ascontiguousarray(v)
            continue
        v0 = v[0]
        if k in ("w_in", "ssm_glu_w", "proj_attn", "proj_ssm", "w_out"):
            m[k] = np.ascontiguousarray(v0)
        else:
            m[k] = np.ascontiguousarray(v0).reshape(-1)
    m.update(consts)
    return m


def kernel(**inputs):
    if "nc" not in _NC_CACHE:
        _NC_CACHE["nc"] = build_nc()
    nc = _NC_CACHE["nc"]
    consts = host_consts()
    in_maps = [_core_inputs(inputs, b, consts) for b in range(8)]
    res = run_bass_kernel_spmd(nc, in_maps, core_ids=list(range(8)))
    out = np.stack([np.asarray(res.results[b]["out"], dtype=np.float32) for b in range(8)], axis=0)
    return out
```

```python
import math
import numpy as np
import ml_dtypes
import concourse.bass as bass
import concourse.mybir as mybir
from concourse.bass_utils import run_bass_kernel_spmd

F32 = mybir.dt.float32
BF16 = mybir.dt.bfloat16
AF = mybir.ActivationFunctionType
ALU = mybir.AluOpType
AX = mybir.AxisListType


class _Op:
    __slots__ = ("eng", "fn", "idx", "dma", "dkey", "deps", "signal", "sigval", "sem", "fuse")

    def __init__(self, eng, fn, idx, dma, dkey):
        self.eng = eng
        self.fn = fn
        self.idx = idx
        self.dma = dma
        self.dkey = dkey
        self.deps = {}
        self.signal = False
        self.sigval = 0
        self.sem = None
        self.fuse = False


class Sched:
    ENGS = ("pe", "act", "dve", "pool", "sp")
    FUSE_WAITS = True

    def __init__(self, nc):
        self.nc = nc
        self.ops = {e: [] for e in self.ENGS}
        self.last_w = {}
        self.readers = {}
        self.dma_count = {}

    def add(self, eng, fn, reads=(), writes=(), dma=False, dkey=None, fuse=None):
        op = _Op(eng, fn, len(self.ops[eng]), dma, dkey)
        if fuse is None:
            fuse = (eng in ("act", "dve", "pool")) and not dma
        op.fuse = fuse and self.FUSE_WAITS
        reads = tuple(reads) + ("__B__",)
        for k in reads:
            w = self.last_w.get(k)
            if w is not None:
                op.deps[w] = True
        for k in writes:
            w = self.last_w.get(k)
            if w is not None and w not in op.deps:
                op.deps[w] = False
            rd = self.readers.get(k)
            if rd:
                for r in rd.get("eng", {}).values():
                    if r is not op and r not in op.deps:
                        op.deps[r] = False
                for r in rd.get("dma", []):
                    if r is not op and r not in op.deps:
                        op.deps[r] = False
        for k in reads:
            rd = self.readers.setdefault(k, {"eng": {}, "dma": []})
            if dma:
                rd["dma"].append(op)
            else:
                rd["eng"][eng] = op
        for k in writes:
            self.last_w[k] = op
            self.readers[k] = {"eng": {}, "dma": []}
        if dma:
            assert dkey is not None
            n = self.dma_count.get(dkey, 0) + 1
            self.dma_count[dkey] = n
            op.sigval = 16 * n
        self.ops[eng].append(op)
        return op

    def barrier(self):
        t = self._bar_tile
        self.add("dve", lambda e: e.memset(t[:, 0:1], 0.0), writes=("__B__",))

    def _needs_wait(self, op, dep, is_raw):
        if dep.dma:
            return True
        if dep.eng != op.eng:
            return True
        if op.dma:
            return True
        if op.eng == "pe":
            return False
        return is_raw and (op.idx - dep.idx) <= 3

    def emit(self, block, extra_final=None):
        nc = self.nc
        for e in self.ENGS:
            for op in self.ops[e]:
                for dep, is_raw in op.deps.items():
                    if self._needs_wait(op, dep, is_raw) and not dep.dma:
                        dep.signal = True
        sems = {e: self._sem_ctx[e] for e in self.ENGS}
        dsems = self._dsem
        for e in self.ENGS:
            n = 0
            for op in self.ops[e]:
                if op.dma:
                    op.sem = dsems[op.dkey]
                else:
                    op.sem = sems[e]
                    if op.signal:
                        n += 1
                        op.sigval = n

        def run_engine(ename, eng):
            known = {}
            for op in self.ops[ename]:
                need = {}
                for dep, is_raw in op.deps.items():
                    if not self._needs_wait(op, dep, is_raw):
                        continue
                    s = dep.sem
                    v = dep.sigval
                    if need.get(s, 0) < v:
                        need[s] = v
                pend = [(s, v) for s, v in need.items() if known.get(s, 0) < v]
                fused = None
                if op.fuse and pend:
                    fused = pend.pop()
                for s, v in pend:
                    eng.wait_ge(s, v)
                    known[s] = v
                ins = op.fn(eng)
                if fused is not None:
                    ins._wait_ge(fused[0], fused[1])
                    known[fused[0]] = fused[1]
                if op.dma:
                    ins.then_inc(op.sem, 16)
                elif op.signal:
                    ins.then_inc(op.sem, 1)
            if extra_final is not None:
                extra_final(ename, eng, known)

        @block.tensor
        def _(eng):
            run_engine("pe", eng)

        @block.scalar
        def _(eng):
            run_engine("act", eng)

        @block.vector
        def _(eng):
            run_engine("dve", eng)

        @block.gpsimd
        def _(eng):
            run_engine("pool", eng)

        @block.sync
        def _(eng):
            run_engine("sp", eng)


L = 4096
D = 1024
NST = 8
EPS = 1e-6
LAM_INIT = 0.8 - 0.6 * math.exp(-0.3 * 0)
TWO_PI = 2.0 * math.pi
NEG = -30000.0


def _bucket_np(rel):
    nb = 16
    me = 8
    side = np.where(rel > 0, nb, 0)
    n = np.abs(rel)
    nf = np.maximum(n, 1).astype(np.float32)
    large = me + (np.log(nf / np.float32(me)).astype(np.float32) / np.float32(math.log(128 / 8))
                  * np.float32(nb - me)).astype(np.int32)
    large = np.minimum(large, nb - 1)
    return side + np.where(n < me, n, large)


def host_consts():
    c = {}
    c["c_ident"] = np.eye(128, dtype=np.float32).astype(ml_dtypes.bfloat16)
    bo = np.zeros((128, 128), np.float32)
    bo[:64, :64] = 1.0 / 64
    bo[64:, 64:] = 1.0 / 64
    c["c_bones"] = bo.astype(ml_dtypes.bfloat16)
    c["c_J"] = np.ascontiguousarray(np.eye(128, dtype=np.float32)[::-1])
    rel = np.arange(-255, 128)
    b = _bucket_np(rel)
    oh = np.zeros((32, 384), np.float32)
    oh[b, np.arange(383)] = 1.0
    c["c_oh"] = oh
    k = np.arange(128)[:, None]
    q = np.arange(128)[None, :]
    c["c_maskD"] = np.where((k // 64) <= (q // 64), 0.0, NEG).astype(np.float32)
    c["c_iotap"] = np.stack([np.arange(128), -np.arange(128)], 1).astype(np.float32)
    c["c_iotat"] = np.tile(np.arange(128, dtype=np.float32)[None, :], (128, 1))
    c["c_tri"] = (np.arange(128)[:, None] <= np.arange(128)[None, :]).astype(np.float32).astype(ml_dtypes.bfloat16)
    return c


class Mem:
    def __init__(self, nc, start=16640, end=229376):
        self.nc = nc
        self.p = start
        self.end = end
        self.n = 0

    def alloc(self, name, shape, dtype):
        esz = 4 if dtype in (F32, mybir.dt.int32) else 2
        size = int(np.prod(shape[1:])) * esz
        size = (size + 63) // 64 * 64
        assert self.p + size <= self.end, f"SBUF overflow allocating {name}: {self.p}+{size} > {self.end}"
        self.n += 1
        t = self.nc.alloc_sbuf_tensor_at(f"{name}_{self.n}", list(shape), dtype, offset=self.p)
        self.p += size
        return t

    def mark(self):
        return self.p

    def release(self, m):
        self.p = m


def build_nc(dbg=None):
    nc = bass.Bass("TRN2", target_bir_lowering=False)
    S = Sched(nc)
    M = Mem(nc)

    def din(name, shape, dt=F32):
        return nc.dram_tensor(name, list(shape), dt, kind="ExternalInput").ap()

    x_d = din("x", [L, D])
    w_in_d = din("w_in", [D, 5120])
    ng_d = din("norm_gain", [D])
    mb_d = din("merge_gate_b", [2048])
    qg_d = din("q_norm_gain", [64])
    kg_d = din("k_norm_gain", [64])
    lq1_d = din("lambda_q1", [64]); lk1_d = din("lambda_k1", [64])
    lq2_d = din("lambda_q2", [64]); lk2_d = din("lambda_k2", [64])
    sg_d = din("diff_subln_gain", [128])
    rb_d = din("rel_bias_table", [32, 4])
    are_d = din("ssm_A_re", [2048]); aim_d = din("ssm_A_im", [2048])
    ldt_d = din("ssm_log_dt", [32])
    bre_d = din("ssm_B_re", [32 * 64 * 16]); bim_d = din("ssm_B_im", [32 * 64 * 16])
    cre_d = din("ssm_C_re", [32 * 16 * 64]); cim_d = din("ssm_C_im", [32 * 16 * 64])
    dd_d = din("ssm_D", [512])
    gw_d = din("ssm_glu_w", [512, 512])
    gb_d = din("ssm_glu_b", [512])
    pa_d = din("proj_attn", [512, 1024])
    ps_d = din("proj_ssm", [512, 1024])
    wo_d = din("w_out", [1024, 1024])
    c_ident_d = din("c_ident", [128, 128], BF16)
    c_bones_d = din("c_bones", [128, 128], BF16)
    c_J_d = din("c_J", [128, 128])
    c_oh_d = din("c_oh", [32, 384])
    c_maskD_d = din("c_maskD", [128, 128])
    c_iotap_d = din("c_iotap", [128, 2])
    c_iotat_d = din("c_iotat", [128, 128])
    c_tri_d = din("c_tri", [128, 128], BF16)
    out_d = nc.dram_tensor("out", [L, D], F32, kind="ExternalOutput").ap()
    fsc_d = nc.dram_tensor("fscratch", [4, 384], F32).ap()
    dbg_d = None
    if dbg:
        dbg_d = nc.dram_tensor("dbg", [128, 4 * L], BF16, kind="ExternalOutput").ap()

    def dap(t, off, pat):
        return bass.AP(t.tensor, off, [list(p) for p in pat])

    PS = [nc.alloc_psum_tensor(f"psb{i}", [128, 512], F32) for i in range(8)]

    def MM(out, lhsT, rhs, start, stop, r, w, fuse=False):
        S.add("pe", lambda e: e.matmul(out, lhsT=lhsT, rhs=rhs, start=start, stop=stop), r, w, fuse=fuse)

    def TR(out, in_, r, w):
        S.add("pe", lambda e: e.transpose(out=out, in_=in_, identity=ident[:]), tuple(r) + ("ident",), w)

    def ACT(out, in_, func, r, w, bias=0.0, scale=1.0, accum=None):
        if accum is None:
            S.add("act", lambda e: e.activation(out=out, in_=in_, func=func, bias=bias, scale=scale), r, w)
        else:
            S.add("act", lambda e: e.activation(out=out, in_=in_, func=func, bias=bias, scale=scale, accum_out=accum), r, w)

    def TS(eng, out, in0, s1, s2, op0, op1, r, w):
        if s2 is None:
            S.add(eng, lambda e: e.tensor_scalar(out=out, in0=in0, scalar1=s1, scalar2=None, op0=op0), r, w)
        else:
            S.add(eng, lambda e: e.tensor_scalar(out=out, in0=in0, scalar1=s1, scalar2=s2, op0=op0, op1=op1), r, w)

    def TT(eng, out, in0, in1, op, r, w):
        S.add(eng, lambda e: e.tensor_tensor(out=out, in0=in0, in1=in1, op=op), r, w)

    def STT(eng, out, in0, scalar, in1, op0, op1, r, w):
        S.add(eng, lambda e: e.scalar_tensor_tensor(out=out, in0=in0, scalar=scalar, in1=in1, op0=op0, op1=op1), r, w)

    def CP(eng, out, in_, r, w):
        if eng == "act":
            S.add("act", lambda e: e.copy(out=out, in_=in_), r, w)
        else:
            S.add(eng, lambda e: e.tensor_copy(out=out, in_=in_), r, w)

    def RSUM(out, in_, r, w):
        S.add("dve", lambda e: e.reduce_sum(out=out, in_=in_, axis=AX.X), r, w)

    def RECIP(out, in_, r, w):
        S.add("dve", lambda e: e.reciprocal(out=out, in_=in_), r, w)

    def NOP(r):
        S.add("sp", lambda e: e.nop(), r, [])

    def MS(eng, ap, val, w):
        S.add(eng, lambda e: e.memset(ap, val), (), w)

    def DMA(eng, out, in_, r, w, dkey, slow=False):
        if slow:
            S.add(eng, lambda e: e.dma_start(out=out, in_=in_, allow_slow_non_contiguous=True), r, w, dma=True, dkey=dkey)
        else:
            S.add(eng, lambda e: e.dma_start(out=out, in_=in_), r, w, dma=True, dkey=dkey)

    bar = M.alloc("bar", [128, 16], F32)
    S._bar_tile = bar
    ident = M.alloc("ident", [128, 128], BF16)
    bones = M.alloc("bones", [128, 128], BF16)
    tri = M.alloc("tri", [128, 128], BF16)
    iotap = M.alloc("iotap", [128, 2], F32)
    iotat = M.alloc("iotat", [128, 128], F32)
    gcol = M.alloc("gcol", [128, 8], F32)
    rs_all = M.alloc("rs_all", [128, 3 * 32 * 2], F32)
    OAT = M.alloc("OAT", [128, 4, L], BF16)
    OSTm = M.alloc("OSTm", [128, 4 * L], BF16)
    OST = OSTm[:, :].rearrange("p (c t) -> p c t", c=4)
    stag32 = OSTm[:, :].bitcast(F32)
    Bst = stag32[:, 0:4096].rearrange("p (r c g q) -> p r c g q", r=2, c=4, g=8)
    Cst = stag32[:, 4096:8192].rearrange("p (r a c) -> p r a c", r=2, a=16)

    DMA("sp", ident[:], c_ident_d[:, :], (), ["ident"], "ident")
    DMA("sp", bones[:], c_bones_d[:, :], (), ["bones"], "bones")
    DMA("sp", tri[:], c_tri_d[:, :], (), ["tri"], "tri")
    DMA("sp", iotap[:], c_iotap_d[:, :], (), ["iotap"], "iotap")
    DMA("sp", iotat[:], c_iotat_d[:, :], (), ["iotat"], "iotat")
    DMA("sp", gcol[:], dap(ng_d, 0, [[1, 128], [128, 8]]), (), ["gcol"], "gcol", slow=True)

    MS("dve", stag32[:, 0:4096], 0.0, ["Bst"])
    MS("pool", stag32[:, 4096:8192], 0.0, ["Cst"])
    bc_dmas = []
    for g in range(32):
        for ri in range(2):
            bc_dmas.append((g, ri))

    def emit_bc(n):
        for _ in range(n):
            if not bc_dmas:
                return
            g, ri = bc_dmas.pop(0)
            ct, g8 = g // 8, g % 8
            pair, g2 = g // 2, g % 2
            bd = (bre_d, bim_d)[ri]
            cd = (cre_d, cim_d)[ri]
            DMA("sp", Bst[16 * g8:16 * g8 + 16, ri, ct, g8, :], dap(bd, g * 1024, [[1, 16], [16, 64]]),
                (), ["Bst"], "Bst", slow=True)
            DMA("sp", Cst[64 * g2:64 * g2 + 64, ri, pair, (g % 8) * 16:(g % 8) * 16 + 16],
                dap(cd, g * 1024, [[1, 64], [64, 16]]), (), ["Cst"], "Cst", slow=True)

    def load_weight(dst, src_d, row0, nrows_tiles, col0, ncols, stg, fold_gain, keyw):
        i = 0
        for kt in range(nrows_tiles):
            for c0 in range(0, ncols, 2048):
                cw = min(2048, ncols - c0)
                sb = stg[i % 2]
                sk = f"stg{i % 2}"
                DMA("sp", sb[:, 0:cw], src_d[row0 + kt * 128: row0 + (kt + 1) * 128, col0 + c0: col0 + c0 + cw],
                    (), [sk], sk)
                if fold_gain:
                    if i % 2 == 0:
                        ACT(dst[:, kt, c0:c0 + cw], sb[:, 0:cw], AF.Copy, [sk, "gcol"], [keyw], scale=gcol[:, kt:kt + 1])
                    else:
                        TS("dve", dst[:, kt, c0:c0 + cw], sb[:, 0:cw], gcol[:, kt:kt + 1], None, ALU.mult, None, [sk, "gcol"], [keyw])
                else:
                    if i % 2 == 0:
                        CP("act", dst[:, kt, c0:c0 + cw], sb[:, 0:cw], [sk], [keyw])
                    else:
                        CP("dve", dst[:, kt, c0:c0 + cw], sb[:, 0:cw], [sk], [keyw])
                i += 1

    def make_hT(st, phase, hT, xt, xn, tbanks):
        for tt in range(4):
            i = st * 4 + tt
            b = i % len(xt)
            nb_ = i % len(xn)
            tbk = tbanks[i % len(tbanks)]
            psT = PS[tbk][:, 0:512].bitcast(BF16)
            row = st * 512 + tt * 128
            col = (phase * 32 + i) * 2
            ss = rs_all[:, col:col + 1]
            rs = rs_all[:, col + 1:col + 2]
            xk, nk = f"xt{b}", f"xn{nb_}"
            DMA("sp", xt[b][:], x_d[row:row + 128, :], (), [xk], xk)
            ACT(xn[nb_][:], xt[b][:], AF.Square, [xk], [nk, f"ss{col}"], accum=ss)
            ACT(rs, ss, AF.Ln, [f"ss{col}"], [f"rs{col}"], bias=EPS, scale=1.0 / D)
            ACT(rs, rs, AF.Exp, [f"rs{col}"], [f"rs{col}"], scale=-0.5)
            TS("dve", xn[nb_][:], xt[b][:], rs, None, ALU.mult, None, [xk, f"rs{col}"], [nk])
            for kt in range(8):
                TR(psT[:, kt * 128:(kt + 1) * 128], xn[nb_][:, kt * 128:(kt + 1) * 128], [nk], [f"ps{tbk}"])
            CP("dve", hT[:, :, tt * 128:(tt + 1) * 128], psT[:, :].rearrange("p (k t) -> p k t", k=8), [f"ps{tbk}"], ["hT"])

    m_glob = M.mark()
    wq = M.alloc("wq", [128, 8, 1536], BF16)
    KT = M.alloc("KT", [128, 4, L], BF16)
    V = M.alloc("V", [128, 32, 4, 128], BF16)
    ones128 = M.alloc("ones128", [128, 128], BF16)
    sgcol = M.alloc("sgcol", [128, 1], F32)
    c15 = M.alloc("c15", [128, 4], F32)
    bnear = M.alloc("bnear", [128, 2, 2, 4, 128], BF16)
    gq = M.alloc("gq", [128, 2], F32)
    lamt = M.alloc("lamt", [128, 8], F32)
    m_work = M.mark()
    stg = [M.alloc("stg", [128, 2048], F32) for _ in range(2)]
    lamv = M.alloc("lamv", [128, 4, 64], F32)
    tbl = M.alloc("tbl", [32, 4], F32)
    oh = M.alloc("oh", [32, 384], F32)
    fsb = M.alloc("fsb", [4, 384], F32)
    Jt = M.alloc("Jt", [128, 128], F32)
    hank = M.alloc("hank", [128, 4, 256], F32)
    maskD = M.alloc("maskD", [128, 128], F32)
    biasP = M.alloc("biasP", [128, 4, 128], F32)
    biasD = M.alloc("biasD", [128, 4, 128], F32)
    bhi32 = M.alloc("bhi32", [128, 4, 128], F32)

    load_weight(wq, w_in_d, 0, 8, 0, 1536, stg, True, "wq")
    MS("dve", ones128[:], 1.0, ["ones128"])
    DMA("sp", sgcol[:], dap(sg_d, 0, [[1, 128], [1, 1]]), (), ["sgcol"], "sgcol")
    TS("dve", sgcol[:], sgcol[:], 1.0 - LAM_INIT, None, ALU.mult, None, ["sgcol"], ["sgcol"])
    for hlf in range(2):
        DMA("sp", gq[64 * hlf:64 * hlf + 64, 0:1], dap(qg_d, 0, [[1, 64], [1, 1]]), (), ["gq"], "gq")
        DMA("sp", gq[64 * hlf:64 * hlf + 64, 1:2], dap(kg_d, 0, [[1, 64], [1, 1]]), (), ["gq"], "gq")
    TS("dve", gq[:, 0:1], gq[:, 0:1], 0.125, None, ALU.mult, None, ["gq"], ["gq"])
    for i, dd in enumerate((lq1_d, lk1_d, lq2_d, lk2_d)):
        DMA("sp", lamv[:, i, :], dap(dd, 0, [[0, 128], [1, 64]]), (), ["lamv"], "lamv")
    TT("dve", lamv[:, 0, :], lamv[:, 0, :], lamv[:, 1, :], ALU.mult, ["lamv"], ["lamv"])
    TT("dve", lamv[:, 2, :], lamv[:, 2, :], lamv[:, 3, :], ALU.mult, ["lamv"], ["lamv"])
    RSUM(lamt[:, 0:1], lamv[:, 0, :], ["lamv"], ["lamt"])
    RSUM(lamt[:, 1:2], lamv[:, 2, :], ["lamv"], ["lamt"])
    ACT(lamt[:, 2:4], lamt[:, 0:2], AF.Exp, ["lamt"], ["lamt"])
    TT("dve", lamt[:, 4:5], lamt[:, 3:4], lamt[:, 2:3], ALU.subtract, ["lamt"], ["lamt"])
    TS("dve", lamt[:, 5:6], lamt[:, 4:5], -LAM_INIT, None, ALU.add, None, ["lamt"], ["lamt"])
    neglam = lamt[:, 5:6]
    DMA("sp", tbl[:], rb_d[:, :], (), ["tbl"], "tbl")
    DMA("sp", oh[:], c_oh_d[:, :], (), ["oh"], "oh")
    DMA("sp", Jt[:], c_J_d[:, :], (), ["Jt"], "Jt")
    DMA("sp", maskD[:], c_maskD_d[:, :], (), ["maskD"], "maskD")
    DMA("sp", c15[:], dap(rb_d, 15 * 4, [[0, 128], [1, 4]]), (), ["c15"], "c15")
    MM(PS[1][0:4, 0:384], tbl[:], oh[:], True, True, ["tbl", "oh"], ["ps1"])
    CP("dve", fsb[:], PS[1][0:4, 0:384], ["ps1"], ["fsb"])
    DMA("sp", fsc_d[:, :], fsb[:], ["fsb"], ["fsc"], "fsb")
    for h in range(4):
        DMA("sp", hank[:, h, :], dap(fsc_d, h * 384, [[1, 128], [1, 256]]), ["fsc"], ["hank"], "hank")
    for h in range(4):
        MM(PS[2][:, h * 128:(h + 1) * 128], hank[:, h, 0:128], Jt[:], True, True, ["hank", "Jt"], ["ps2"])
        MM(PS[3][:, h * 128:(h + 1) * 128], hank[:, h, 128:256], Jt[:], True, True, ["hank", "Jt"], ["ps3"])
    CP("dve", biasP[:], PS[2][:, :].rearrange("p (h q) -> p h q", h=4), ["ps2"], ["biasP"])
    TT("dve", biasD[:], PS[3][:, :].rearrange("p (h q) -> p h q", h=4),
       maskD[:].unsqueeze(1).broadcast_to([128, 4, 128]), ALU.add, ["ps3", "maskD"], ["biasD"])
    for d_, bt_ in ((0, biasD), (1, biasP)):
        TT("dve", bt_[:], bt_[:], c15[:, :].unsqueeze(2).broadcast_to([128, 4, 128]), ALU.subtract, ["biasD", "biasP", "c15"], ["biasD", "biasP"])
        CP("dve", bnear[:, d_, 0], bt_[:], ["biasD", "biasP"], ["bnear"])
        CP("dve", bhi32[:], bnear[:, d_, 0], ["bnear"], ["bhi32"])
        TT("dve", bnear[:, d_, 1], bt_[:], bhi32[:], ALU.subtract, ["biasD", "biasP", "bhi32"], ["bnear"])
    S.barrier()
    M.release(m_work)
    hT = M.alloc("hT", [128, 8, 512], BF16)
    xt = [M.alloc("xt", [128, 1024], F32) for _ in range(2)]
    xn = [M.alloc("xn", [128, 1024], BF16) for _ in range(2)]
    qT = M.alloc("qT", [128, 4, 512], BF16)
    sq = M.alloc("sq", [128, 512], BF16)
    rstd = M.alloc("rstd", [128, 512], F32)
    PTP = [M.alloc("PTP", [128, 2, 512], BF16) for _ in range(2)]
    PT = [PTP[0][:, 0, :], PTP[0][:, 1, :], PTP[1][:, 0, :], PTP[1][:, 1, :]]
    rcp = rstd
    oraw = [M.alloc("oraw", [128, 512], F32) for _ in range(2)]
    racc = [M.alloc("racc", [128, 2, 512], F32) for _ in range(2)]
    tob = oraw[1]
    rsb = [M.alloc("rsb", [128, 512], BF16) for _ in range(2)]
    sqb = sq
    rs2 = M.alloc("rs2", [128, 512], F32)
    sq2 = [sq, M.alloc("sq2", [128, 512], BF16)]
    rstd2 = [rstd, rs2]
    rkeys = ["rstd0", "rs2"]

    PSB = [3, 4, 7, 6]
    NB = 4
    SKEW = 2
    PAIRED = True
    QPAD = False
    FILL = 0
    deferred = []

    ZB3 = [1, 2, 4]

    def qk_mm(c, st):
        zb = ZB3[c % 3]
        for kt in range(8):
            MM(PS[zb][:, :], wq[:, kt, c * 128:(c + 1) * 128], hT[:, kt, :], kt == 0, kt == 7, ["wq", "hT"], [f"ps{zb}"], fuse=True)

    MB = [0, 3]

    def qk_c1(c, st):
        zb = ZB3[c % 3]
        sqc = sq2[c % 2]
        mb_ = MB[c % 2]
        ACT(sqc[:], PS[zb][:, :], AF.Square, [f"ps{zb}"], [f"sq{c % 2}"])
        MM(PS[mb_][:, :], bones[:], sqc[:], True, True, ["bones", f"sq{c % 2}"], [f"ps{mb_}"])

    def qk_c2(c, st):
        zb = ZB3[c % 3]
        rsc = rstd2[c % 2]
        mb_ = MB[c % 2]
        ACT(rsc[:], PS[mb_][:, :], AF.Ln, [f"ps{mb_}"], [rkeys[c % 2]], bias=EPS)
        ACT(rsc[:], rsc[:], AF.Exp, [rkeys[c % 2]], [rkeys[c % 2]], scale=-0.5)
        if c < 4:
            STT("dve", qT[:, c, :], PS[zb][:, :], gq[:, 0:1], rsc[:], ALU.mult, ALU.mult, [f"ps{zb}", "gq", rkeys[c % 2]], ["qT"])
        else:
            STT("dve", KT[:, c - 4, st * 512:(st + 1) * 512], PS[zb][:, :], gq[:, 1:2], rsc[:], ALU.mult, ALU.mult,
                [f"ps{zb}", "gq", rkeys[c % 2]], [f"KT{st}"])

    for st in range(NST):
        make_hT(st, 0, hT, xt, xn, [0, 3])
        emit_bc(8)
        qk_mm(0, st)
        qk_mm(1, st)
        qk_c1(0, st)
        for c in range(8):
            if c + 2 < 8:
                qk_mm(c + 2, st)
            if c + 1 < 8:
                qk_c1(c + 1, st)
            qk_c2(c, st)
        for tt in range(4):
            blk = st * 4 + tt
            zb = 1 + (tt % 2)
            for kt in range(8):
                MM(PS[zb][:, :], hT[:, kt, tt * 128:(tt + 1) * 128], wq[:, kt, 1024:1536], kt == 0, kt == 7, ["wq", "hT"], [f"ps{zb}"])
            CP("dve", V[:, blk, :, :], PS[zb][:, :].rearrange("p (h d) -> p h d", h=4), [f"ps{zb}"], [f"V{st}"])
        kv_keys = [f"KT{i}" for i in range(st + 1)] + [f"V{i}" for i in range(st + 1)]
        iters = [(h, s, j) for h in range(4) for j in range(4 * st + 4) for s in range(2)]
        NI = len(iters)

        def emit_S(i, st=st, kv_keys=kv_keys, iters=iters):
            h, s, j = iters[i]
            lo = max(0, j - 4 * st)
            cols = slice(lo * 128, 512)
            pb = i % NB
            psS = PS[PSB[pb]]
            near = [(qb, 4 * st + qb - j) for qb in range(lo, 4) if 4 * st + qb - j <= 1]
            MM(psS[:, cols], KT[64 * s:64 * s + 64, h, j * 128:(j + 1) * 128], qT[64 * s:64 * s + 64, h, cols],
               True, not near, kv_keys + ["qT"], [f"ps{PSB[pb]}"], fuse=(j < 4 * st))
            for n_, (qb, d) in enumerate(near):
                qs_ = slice(qb * 128, (qb + 1) * 128)
                MM(psS[:, qs_], ident[:], bnear[:, d, 0, h, :], False, False, ["ident", "bnear"], [f"ps{PSB[pb]}"], fuse=True)
                MM(psS[:, qs_], ident[:], bnear[:, d, 1, h, :], False, n_ == len(near) - 1, ["ident", "bnear"], [f"ps{PSB[pb]}"], fuse=True)

        def fin_A(h, s, st):
            MM(PS[0][:, :], ones128[:], rsb[s][:], True, True, ["ones128", f"rsb{s}"], ["ps0"])
            ACT(rcp[:], PS[0][:, :], AF.Ln, ["ps0"], ["rstd0"])
            ACT(rcp[:], rcp[:], AF.Exp, ["rstd0"], ["rstd0"], scale=-1.0)
            TT("dve", oraw[s][:], oraw[s][:], rcp[:], ALU.mult, [f"oraw{s}", "rstd0"], [f"oraw{s}"])
            if s == 1:
                STT("dve", tob[:], oraw[1][:], neglam, oraw[0][:], ALU.mult, ALU.add, ["oraw0", "oraw1", "lamt"], ["oraw1"])
                ACT(sqb[:], tob[:], AF.Square, ["oraw1"], ["sq0"])

        def fin_B(h, st):
            stc = slice(st * 512, (st + 1) * 512)
            MM(PS[0][:, :], ones128[:], sqb[:], True, True, ["ones128", "sq0"], ["ps0"])
            ACT(rs2[:], PS[0][:, :], AF.Ln, ["ps0"], ["rs2"], bias=EPS, scale=1.0 / 128)
            ACT(rs2[:], rs2[:], AF.Exp, ["rs2"], ["rs2"], scale=-0.5)
            STT("dve", OAT[:, h, stc], tob[:], sgcol[:, 0:1], rs2[:], ALU.mult, ALU.mult, ["oraw1", "sgcol", "rs2"], ["OAT"])

        def emit_rest(i, st=st, kv_keys=kv_keys, iters=iters):
            h, s, j = iters[i]
            lo = max(0, j - 4 * st)
            pb = i % NB
            pk = f"ps{PSB[pb]}"
            psS = PS[PSB[pb]]
            accb = 5 if s == 0 else 2
            if j == 0 and s == 0:
                MS("dve", racc[0][:], 0.0, ["racc0"])
            ACT(PT[pb][:, lo * 128:512], psS[:, lo * 128:512], AF.Exp, [pk, "c15"], [f"PT{pb}"], bias=c15[:, h:h + 1])
            qs = slice(lo * 128, 512)
            last = (j == 4 * st + 3)
            MM(PS[accb][:, qs], V[:, j, h, :], PT[pb][:, qs], j == 0, last, [f"PT{pb}"] + kv_keys, [f"ps{accb}"], fuse=(j < 4 * st))
            for _f in range(FILL):
                MM(PS[6][:, :], ones128[:], wq[:, 0, 0:512], True, True, ["ones128", "wq"], ["ps6"])
            if s == 1:
                e_ = 0
                pbuf = (i // 2) % 2
                TT(("dve", "pool")[e_], racc[e_][:, :, qs], racc[e_][:, :, qs], PTP[pbuf][:, :, qs], ALU.add,
                   [f"racc{e_}", f"PT{pb}", f"PT{pb - 1}"], [f"racc{e_}"])
            if not last:
                return
            CP("act", oraw[s][:], PS[accb][:, :], [f"ps{accb}"], [f"oraw{s}"])
            if s == 1:
                for ss_ in range(2):
                    CP("dve", rsb[ss_][:], racc[0][:, ss_, :], ["racc0"], [f"rsb{ss_}"])
            deferred.append((i + 4, "A", h, s, st))
            if s == 1:
                deferred.append((i + 7, "B", h, s, st))

        def run_deferred(k):
            while deferred and deferred[0][0] <= k:
                _, kind, hh, ss, sst = deferred.pop(0)
                if kind == "A":
                    fin_A(hh, ss, sst)
                else:
                    fin_B(hh, sst)

        for i in range(0, NI + SKEW, 2):
            if PAIRED:
                for ii in (i, i + 1):
                    if ii < NI:
                        emit_S(ii)
                for ii in (i, i + 1):
                    k = ii - SKEW
                    if 0 <= k < NI:
                        emit_rest(k)
                        run_deferred(k)
            else:
                for ii in (i, i + 1):
                    if ii < NI:
                        emit_S(ii)
                    k = ii - SKEW
                    if 0 <= k < NI:
                        emit_rest(k)
                        run_deferred(k)
        run_deferred(10 ** 9)

    if dbg == "p1":
        DMA("sp", dbg_d[:, :], OAT[:, :, :].rearrange("p h t -> p (h t)"), ["OAT"], ["outd"], "OAT")
        NOP(["outd"])
        return finish(nc, S)
    I32 = mybir.dt.int32
    S.barrier()
    M.release(m_glob)
    w2 = M.alloc("w2", [128, 8, 1024], BF16)
    gluw = M.alloc("gluw", [128, 4, 512], BF16)
    glub = M.alloc("glub", [128, 4], F32)
    ta = M.alloc("ta", [128, 2048], F32)
    tb = M.alloc("tb", [128, 2048], F32)
    te = M.alloc("te", [128, 16, 128], F32)
    tf = M.alloc("tf", [128, 16, 128], F32)
    Bbd = M.alloc("Bbd", [128, 4, 2, 512], BF16)
    Ccat = M.alloc("Ccat", [128, 2, 16, 128], BF16)
    Ddiag = M.alloc("Ddiag", [128, 4, 128], BF16)
    dcol = M.alloc("dcol", [128, 4], F32)
    negtri = M.alloc("negtri", [128, 128], BF16)
    CcatN = M.alloc("CcatN", [128, 16, 128], BF16)
    l128 = M.alloc("l128", [128, 6, 16], F32)
    cre = [M.alloc("cre", [128, 16], F32) for _ in range(2)]
    cim = [M.alloc("cim", [128, 16], F32) for _ in range(2)]
    m_work2 = M.mark()
    stg = [M.alloc("stg", [128, 2048], F32) for _ in range(2)]
    load_weight(w2, w_in_d, 0, 8, 2048, 1024, stg, True, "w2")
    load_weight(gluw, gw_d, 0, 4, 0, 512, stg, False, "gluw")
    S.barrier()
    M.release(m_work2)
    T1 = M.alloc("T1", [128, 2048], F32)
    T2 = M.alloc("T2", [128, 2048], F32)
    T3 = M.alloc("T3", [128, 2048], F32)
    T4 = M.alloc("T4", [128, 2048], F32)
    TI = M.alloc("TI", [128, 2048], I32)
    dtb = M.alloc("dtb", [128, 32], F32)
    s2 = M.alloc("s2", [128, 8, 16], F32)
    s3 = M.alloc("s3", [128, 14, 256], F32)
    s3i = M.alloc("s3i", [128, 256], I32)
    dt3 = M.alloc("dt3", [128, 4], F32)

    DMA("sp", glub[:], dap(gb_d, 0, [[1, 128], [128, 4]]), (), ["glub"], "glub", slow=True)

    def frac_sincos(y, tmp, ti, sin_out, cos_out, key):
        CP("dve", ti, y, [key + "y"], [key + "ti"])
        CP("dve", tmp, ti, [key + "ti"], [key + "tmp"])
        TT("dve", tmp, y, tmp, ALU.subtract, [key + "y", key + "tmp"], [key + "tmp"])
        ACT(sin_out, tmp, AF.Sin, [key + "tmp"], [key + "sin"], scale=TWO_PI)
        TS("dve", y, y, 0.25, None, ALU.add, None, [key + "y"], [key + "y"])
        CP("dve", ti, y, [key + "y"], [key + "ti"])
        CP("dve", tmp, ti, [key + "ti"], [key + "tmp"])
        TT("dve", tmp, y, tmp, ALU.subtract, [key + "y", key + "tmp"], [key + "tmp"])
        ACT(cos_out, tmp, AF.Sin, [key + "tmp"], [key + "cos"], scale=TWO_PI)

    DMA("sp", T1[:], dap(are_d, 0, [[0, 128], [1, 2048]]), (), ["T1"], "T1")
    DMA("sp", T2[:], dap(aim_d, 0, [[0, 128], [1, 2048]]), (), ["T2"], "T2")
    DMA("sp", dtb[:], dap(ldt_d, 0, [[0, 128], [1, 32]]), (), ["dtb"], "dtb")
    ACT(dtb[:], dtb[:], AF.Exp, ["dtb"], ["dtb"])
    dtb_b = dtb[:, :].unsqueeze(2).broadcast_to([128, 32, 64])
    TT("dve", T1[:, :].rearrange("p (g q) -> p g q", g=32), T1[:, :].rearrange("p (g q) -> p g q", g=32), dtb_b, ALU.mult, ["T1", "dtb"], ["T1"])
    TT("dve", T2[:, :].rearrange("p (g q) -> p g q", g=32), T2[:, :].rearrange("p (g q) -> p g q", g=32), dtb_b, ALU.mult, ["T2", "dtb"], ["T2"])
    ACT(ta[:], T1[:], AF.Exp, ["T1", "iotap"], ["ta"], scale=iotap[:, 1:2])
    TS("dve", T3[:], T2[:], iotap[:, 0:1], 1.0 / TWO_PI, ALU.mult, ALU.mult, ["T2", "iotap"], ["L1y"])
    frac_sincos(T3[:], T4[:], TI[:], tb[:], T1[:], "L1")
    STT("dve", tb[:], tb[:], -1.0, ta[:], ALU.mult, ALU.mult, ["L1sin", "ta"], ["tb", "L1sin"])
    TT("dve", ta[:], ta[:], T1[:], ALU.mult, ["ta", "L1cos", "tb"], ["ta"])
    A2re, A2im, dt2, m2, th2 = (s2[:, i, :] for i in range(5))
    DMA("sp", A2re, dap(are_d, 0, [[1, 128], [128, 16]]), (), ["A2re"], "A2re", slow=True)
    DMA("sp", A2im, dap(aim_d, 0, [[1, 128], [128, 16]]), (), ["A2im"], "A2im", slow=True)
    for g2 in range(2):
        DMA("sp", s2[64 * g2:64 * g2 + 64, 2, :], dap(ldt_d, g2, [[0, 64], [2, 16]]), (), ["dt2"], "dt2", slow=True)
    ACT(dt2, dt2, AF.Exp, ["dt2"], ["dt2"])
    TT("dve", m2, A2re, dt2, ALU.mult, ["A2re", "dt2"], ["m2"])
    STT("dve", th2, A2im, 1.0 / TWO_PI, dt2, ALU.mult, ALU.mult, ["A2im", "dt2"], ["th2"])
    T3v = T3[:, :].rearrange("p (a t) -> p a t", a=16)
    for pair in range(16):
        ACT(te[:, pair, :], iotat[:], AF.Exp, ["iotat", "m2"], ["te"], scale=s2[:, 3, pair:pair + 1])
        TS("dve", T3v[:, pair, :], iotat[:], s2[:, 4, pair:pair + 1], None, ALU.mult, None, ["iotat", "th2", "L1y", "L1tmp"], ["L2y"])
    frac_sincos(T3[:], T4[:], TI[:], tf[:, :, :].rearrange("p a t -> p (a t)"), T1[:], "L2")
    TT("dve", tf[:, :, :].rearrange("p a t -> p (a t)"), tf[:, :, :].rearrange("p a t -> p (a t)"), te[:, :, :].rearrange("p a t -> p (a t)"),
       ALU.mult, ["L2sin", "te"], ["tf", "L2sin"])
    TT("dve", te[:, :, :].rearrange("p a t -> p (a t)"), te[:, :, :].rearrange("p a t -> p (a t)"), T1[:], ALU.mult, ["te", "L2cos", "tf"], ["te"])
    A3re, A3im, m3, y3, dec3, sin3, cos3, nr3, den3, qre3, qim3, u3a, u3b, tmp3 = (s3[:, i, :] for i in range(14))
    for g8 in range(8):
        DMA("sp", s3[16 * g8:16 * g8 + 16, 0, :].rearrange("p (c q) -> p c q", c=4), dap(are_d, g8 * 64, [[0, 16], [512, 4], [1, 64]]), (), ["A3re"], "A3re")
        DMA("sp", s3[16 * g8:16 * g8 + 16, 1, :].rearrange("p (c q) -> p c q", c=4), dap(aim_d, g8 * 64, [[0, 16], [512, 4], [1, 64]]), (), ["A3im"], "A3im")
        DMA("sp", dt3[16 * g8:16 * g8 + 16, :], dap(ldt_d, g8, [[0, 16], [8, 4]]), (), ["dt3"], "dt3", slow=True)
    ACT(dt3[:], dt3[:], AF.Exp, ["dt3"], ["dt3"])
    dt3_b = dt3[:, :].unsqueeze(2).broadcast_to([128, 4, 64])
    v3 = lambda a: a.rearrange("p (c q) -> p c q", c=4)
    TT("dve", v3(m3), v3(A3re), dt3_b, ALU.mult, ["A3re", "dt3"], ["m3"])
    TT("dve", v3(y3), v3(A3im), dt3_b, ALU.mult, ["A3im", "dt3"], ["L3y"])
    TS("dve", y3, y3, 1.0 / TWO_PI, None, ALU.mult, None, ["L3y"], ["L3y"])
    ACT(dec3, m3, AF.Exp, ["m3"], ["dec3"])
    frac_sincos(y3, tmp3, s3i[:], sin3, cos3, "L3")
    TT("dve", cos3, cos3, dec3, ALU.mult, ["L3cos", "dec3"], ["lbr"])
    TT("dve", sin3, sin3, dec3, ALU.mult, ["L3sin", "dec3"], ["lbi"])
    TS("dve", nr3, cos3, -1.0, None, ALU.add, None, ["lbr"], ["nr3"])
    TT("dve", den3, A3re, A3re, ALU.mult, ["A3re"], ["den3"])
    TT("dve", u3a, A3im, A3im, ALU.mult, ["A3im"], ["u3a"])
    TT("dve", den3, den3, u3a, ALU.add, ["den3", "u3a"], ["den3"])
    RECIP(den3, den3, ["den3"], ["den3"])
    TT("dve", u3a, nr3, A3re, ALU.mult, ["nr3", "A3re", "den3"], ["u3a"])
    TT("dve", u3b, sin3, A3im, ALU.mult, ["lbi", "A3im"], ["u3b"])
    TT("dve", qre3, u3a, u3b, ALU.add, ["u3a", "u3b"], ["qre3"])
    TT("dve", qre3, qre3, den3, ALU.mult, ["qre3", "den3"], ["qre3"])
    TT("dve", u3a, sin3, A3re, ALU.mult, ["lbi", "A3re", "qre3"], ["u3a"])
    TT("dve", u3b, nr3, A3im, ALU.mult, ["nr3", "A3im", "qre3"], ["u3b"])
    TT("dve", qim3, u3a, u3b, ALU.subtract, ["u3a", "u3b"], ["qim3"])
    TT("dve", qim3, qim3, den3, ALU.mult, ["qim3", "den3"], ["qim3"])
    qre_b = v3(qre3).unsqueeze(2).broadcast_to([128, 4, 8, 64])
    qim_b = v3(qim3).unsqueeze(2).broadcast_to([128, 4, 8, 64])
    T1v4 = T1[:, :].rearrange("p (c g q) -> p c g q", c=4, g=8)
    T2v4 = T2[:, :].rearrange("p (c g q) -> p c g q", c=4, g=8)
    TT("dve", T1v4, Bst[:, 0], qre_b, ALU.mult, ["Bst", "qre3", "ta", "te"], ["T1"])
    TT("dve", T2v4, Bst[:, 1], qim_b, ALU.mult, ["Bst", "qim3", "L1y", "L2y"], ["T2"])
    TT("dve", Bbd[:, :, 0, :], T1[:, :].rearrange("p (c x) -> p c x", c=4), T2[:, :].rearrange("p (c x) -> p c x", c=4), ALU.subtract, ["T1", "T2"], ["Bbd"])
    TT("dve", T1v4, Bst[:, 1], qre_b, ALU.mult, ["Bst", "qre3", "Bbd"], ["T1"])
    TT("dve", T2v4, Bst[:, 0], qim_b, ALU.mult, ["Bst", "qim3", "Bbd"], ["T2"])
    TT("dve", Bbd[:, :, 1, :], T1[:, :].rearrange("p (c x) -> p c x", c=4), T2[:, :].rearrange("p (c x) -> p c x", c=4), ALU.add, ["T1", "T2"], ["Bbd"])
    CP("dve", Ccat[:, 0], Cst[:, 0], ["Cst"], ["Ccat"])
    TS("dve", CcatN[:], Cst[:, 0], -1.0, None, ALU.mult, None, ["Cst"], ["CcatN"])
    TS("dve", negtri[:], tri[:], -1.0, None, ALU.mult, None, ["tri"], ["negtri"])
    TT("dve", l128[:, 0, :], te[:, :, 127], te[:, :, 1], ALU.mult, ["te"], ["l128a"])
    TT("dve", l128[:, 1, :], tf[:, :, 127], tf[:, :, 1], ALU.mult, ["tf"], ["l128b"])
    TT("dve", l128[:, 4, :], l128[:, 0, :], l128[:, 1, :], ALU.subtract, ["l128a", "l128b"], ["l128"])
    TT("dve", l128[:, 2, :], te[:, :, 127], tf[:, :, 1], ALU.mult, ["te", "tf"], ["l128c"])
    TT("dve", l128[:, 3, :], tf[:, :, 127], te[:, :, 1], ALU.mult, ["te", "tf"], ["l128d"])
    TT("dve", l128[:, 5, :], l128[:, 2, :], l128[:, 3, :], ALU.add, ["l128c", "l128d"], ["l128"])
    TS("dve", Ccat[:, 1], Cst[:, 1], -1.0, None, ALU.mult, None, ["Cst"], ["Ccat"])
    DMA("sp", dcol[:], dap(dd_d, 0, [[1, 128], [128, 4]]), (), ["dcol"], "dcol", slow=True)
    for ct in range(4):
        TS("dve", Ddiag[:, ct, :], ident[:], dcol[:, ct:ct + 1], None, ALU.mult, None, ["ident", "dcol"], ["Ddiag"])
    MS("dve", cre[0][:], 0.0, ["c0_0", "c0_1", "c0_2", "c0_3"])
    MS("dve", cim[0][:], 0.0, ["c0_0", "c0_1", "c0_2", "c0_3"])
    S.barrier()
    M.release(m_work2)
    hT = M.alloc("hT", [128, 8, 512], BF16)
    xt = [M.alloc("xt", [128, 1024], F32) for _ in range(2)]
    xn = [M.alloc("xn", [128, 1024], BF16) for _ in range(2)]
    uT = M.alloc("uT", [128, 4, 512], BF16)
    gsT = M.alloc("gsT", [128, 4, 512], BF16)
    gT = M.alloc("gT", [128, 4, 512], BF16)
    xs = [M.alloc("xs", [128, 512], F32) for _ in range(2)]
    ys = [M.alloc("ys", [128, 512], F32) for _ in range(2)]
    pa_ = [[M.alloc("pa_", [128, 512], BF16) for _ in range(4)] for _ in range(2)]
    pb_ = [[M.alloc("pb_", [128, 4, 128], BF16) for _ in range(4)] for _ in range(2)]
    sr = [M.alloc("sr", [128, 4, 128], F32) for _ in range(2)]
    si = [M.alloc("si", [128, 4, 128], F32) for _ in range(2)]
    sg = M.alloc("sg", [128, 512], F32)
    xl = M.alloc("xl", [128, 4, 4], F32)

    def ssm_A(st, cc, ct, u):
        tok = slice(cc * 128, (cc + 1) * 128)
        ub = u % 2
        MM(PS[2][:, :], uT[:, ct, tok], Bbd[:, ct, 0, :], True, True, ["uT", "Bbd"], ["ps2"])
        MM(PS[3][:, :], uT[:, ct, tok], Bbd[:, ct, 1, :], True, True, ["uT", "Bbd"], ["ps3"])
        CP("act", xs[ub][:], PS[2][:, :], ["ps2"], [f"xs{ub}"])
        CP("act", ys[ub][:], PS[3][:, :], ["ps3"], [f"ys{ub}"])
        a_ = ta[:, ct * 512:(ct + 1) * 512]
        b_ = tb[:, ct * 512:(ct + 1) * 512]
        p = pa_[ub]
        TT("dve", p[0][:], xs[ub][:], a_, ALU.mult, [f"xs{ub}", "ta"], [f"pa{ub}0"])
        TT("dve", p[1][:], ys[ub][:], b_, ALU.mult, [f"ys{ub}", "tb"], [f"pa{ub}1"])
        TT("dve", p[2][:], ys[ub][:], a_, ALU.mult, [f"ys{ub}", "ta"], [f"pa{ub}2"])
        TT("dve", p[3][:], xs[ub][:], b_, ALU.mult, [f"xs{ub}", "tb"], [f"pa{ub}3"])

    def ssm_A2(st, cc, ct, u):
        ub = u % 2
        p = pa_[ub]
        for pr in range(4):
            cs = slice(pr * 128, (pr + 1) * 128)
            MM(PS[4 + 2 * ub][:, cs], p[0][:, cs], tri[:], True, False, [f"pa{ub}0", "tri"], [f"ps{4 + 2 * ub}"])
            MM(PS[4 + 2 * ub][:, cs], p[1][:, cs], negtri[:], False, True, [f"pa{ub}1", "negtri"], [f"ps{4 + 2 * ub}"])
            MM(PS[5 + 2 * ub][:, cs], p[2][:, cs], tri[:], True, False, [f"pa{ub}2", "tri"], [f"ps{5 + 2 * ub}"])
            MM(PS[5 + 2 * ub][:, cs], p[3][:, cs], tri[:], False, True, [f"pa{ub}3", "tri"], [f"ps{5 + 2 * ub}"])

    def ssm_B(st, cc, ct, u):
        tok = slice(cc * 128, (cc + 1) * 128)
        ci = st * 4 + cc
        ub = u % 2
        cr_in, ci_in = cre[ci % 2], cim[ci % 2]
        cr_out, ci_out = cre[(ci + 1) % 2], cim[(ci + 1) % 2]
        kin, kout = f"c{ci % 2}_{ct}", f"c{(ci + 1) % 2}_{ct}"
        for pr in range(4):
            pair = 4 * ct + pr
            cs = slice(pr * 128, (pr + 1) * 128)
            ACT(sr[ub][:, pr, :], PS[4 + 2 * ub][:, cs], AF.Identity, [f"ps{4 + 2 * ub}", kin], [f"sr{ub}"], bias=cr_in[:, pair:pair + 1])
            ACT(si[ub][:, pr, :], PS[5 + 2 * ub][:, cs], AF.Identity, [f"ps{5 + 2 * ub}", kin], [f"si{ub}"], bias=ci_in[:, pair:pair + 1])
        eh = te[:, 4 * ct:4 * ct + 4, :]
        fh = tf[:, 4 * ct:4 * ct + 4, :]
        q = pb_[ub]
        TT("dve", q[0][:], sr[ub][:], eh, ALU.mult, [f"sr{ub}", "te"], [f"pb{ub}0"])
        TT("dve", q[1][:], si[ub][:], fh, ALU.mult, [f"si{ub}", "tf"], [f"pb{ub}1"])
        TT("dve", q[2][:], si[ub][:], eh, ALU.mult, [f"si{ub}", "te"], [f"pb{ub}2"])
        TT("dve", q[3][:], sr[ub][:], fh, ALU.mult, [f"sr{ub}", "tf"], [f"pb{ub}3"])
        E = l128[:, 4, 4 * ct:4 * ct + 4]
        F_ = l128[:, 5, 4 * ct:4 * ct + 4]
        ps_ = slice(4 * ct, 4 * ct + 4)
        TT("pool", xl[:, 0, :], E, sr[ub][:, :, 127], ALU.mult, ["l128", f"sr{ub}"], ["xl0"])
        TT("pool", xl[:, 1, :], F_, si[ub][:, :, 127], ALU.mult, ["l128", f"si{ub}"], ["xl1"])
        TT("pool", cr_out[:, ps_], xl[:, 0, :], xl[:, 1, :], ALU.subtract, ["xl0", "xl1"], [kout])
        TT("pool", xl[:, 2, :], E, si[ub][:, :, 127], ALU.mult, ["l128", f"si{ub}"], ["xl2"])
        TT("pool", xl[:, 3, :], F_, sr[ub][:, :, 127], ALU.mult, ["l128", f"sr{ub}"], ["xl3"])
        TT("pool", ci_out[:, ps_], xl[:, 2, :], xl[:, 3, :], ALU.add, ["xl2", "xl3"], [kout])

    def ssm_C(st, cc, ct, u):
        tok = slice(cc * 128, (cc + 1) * 128)
        ub = u % 2
        q = pb_[ub]
        ysl = slice(ct * 128, (ct + 1) * 128)
        for pr in range(4):
            pair = 4 * ct + pr
            MM(PS[0][:, ysl], Ccat[:, 0, pair, :], q[0][:, pr, :], pr == 0, False, ["Ccat", f"pb{ub}0"], ["ps0"], fuse=True)
            MM(PS[0][:, ysl], CcatN[:, pair, :], q[1][:, pr, :], False, False, ["CcatN", f"pb{ub}1"], ["ps0"], fuse=True)
            MM(PS[0][:, ysl], Ccat[:, 1, pair, :], q[2][:, pr, :], False, False, ["Ccat", f"pb{ub}2"], ["ps0"], fuse=True)
            MM(PS[0][:, ysl], Ccat[:, 1, pair, :], q[3][:, pr, :], False, False, ["Ccat", f"pb{ub}3"], ["ps0"], fuse=True)
        MM(PS[0][:, ysl], Ddiag[:, ct, :], uT[:, ct, tok], False, True, ["Ddiag", "uT"], ["ps0"], fuse=True)
        if ct == 3:
            ACT(gT[:, :, tok], PS[0][:, :].rearrange("p (c t) -> p c t", c=4), AF.Gelu, ["ps0"], ["gT"])

    for st in range(NST):
        make_hT(st, 1, hT, xt, xn, [0, 1])
        def us_mm(c):
            zb = 1 + (c % 2)
            for kt in range(8):
                MM(PS[zb][:, :], w2[:, kt, c * 128:(c + 1) * 128], hT[:, kt, :], kt == 0, kt == 7, ["w2", "hT"], [f"ps{zb}"], fuse=True)
        us_mm(0)
        for c in range(8):
            if c + 1 < 8:
                us_mm(c + 1)
            zb = 1 + (c % 2)
            if c < 4:
                CP("act", uT[:, c, :], PS[zb][:, :], [f"ps{zb}"], ["uT"])
            else:
                ACT(gsT[:, c - 4, :], PS[zb][:, :], AF.Silu, [f"ps{zb}"], ["gsT"])
        units = [(cc, ct) for cc in range(4) for ct in range(4)]
        NU = len(units)
        for k in range(-2, NU):
            if 0 <= k + 2 < NU:
                ssm_A(st, units[k + 2][0], units[k + 2][1], k + 2)
            if 0 <= k + 1 < NU:
                ssm_A2(st, units[k + 1][0], units[k + 1][1], k + 1)
                ssm_B(st, units[k + 1][0], units[k + 1][1], k + 1)
            if 0 <= k < NU:
                ssm_C(st, units[k][0], units[k][1], k)
        for c in range(4):
            zb = 1 + (c % 2)
            for kt in range(4):
                MM(PS[zb][:, :], gluw[:, kt, c * 128:(c + 1) * 128], gT[:, kt, :], kt == 0, kt == 3, ["gluw", "gT"], [f"ps{zb}"], fuse=True)
            ACT(sg[:], PS[zb][:, :], AF.Sigmoid, [f"ps{zb}", "glub"], ["sg"], bias=glub[:, c:c + 1])
            TT("dve", sg[:], sg[:], gT[:, c, :], ALU.mult, ["sg", "gT"], ["sg"])
            TT("dve", OST[:, c, st * 512:(st + 1) * 512], sg[:], gsT[:, c, :], ALU.mult, ["sg", "gsT"], ["OST", "Bst", "Cst"])
    if dbg == "p2":
        DMA("sp", dbg_d[:, :], OSTm[:, :], ["OST"], ["outd"], "OSTm")
        NOP(["outd"])
        return finish(nc, S)

    S.barrier()
    M.release(m_glob)
    w3 = M.alloc("w3", [128, 8, 2560], BF16)
    pa = M.alloc("pa", [128, 4, 1024], BF16)
    psw = M.alloc("psw", [128, 4, 1024], BF16)
    wo = M.alloc("wo", [128, 8, 1024], BF16)
    mb = M.alloc("mb", [128, 16], F32)
    m_work3 = M.mark()
    stg = [M.alloc("stg", [128, 2048], F32) for _ in range(2)]
    load_weight(w3[:, :, 0:512], w_in_d, 0, 8, 1536, 512, stg, True, "w3")
    load_weight(w3[:, :, 512:2560], w_in_d, 0, 8, 3072, 2048, stg, True, "w3")
    load_weight(pa, pa_d, 0, 4, 0, 1024, stg, False, "pa")
    load_weight(psw, ps_d, 0, 4, 0, 1024, stg, False, "psw")
    load_weight(wo, wo_d, 0, 8, 0, 1024, stg, False, "wo")
    DMA("sp", mb[:], dap(mb_d, 0, [[1, 128], [128, 16]]), (), ["mb"], "mb", slow=True)
    S.barrier()
    M.release(m_work3)
    hT = M.alloc("hT", [128, 8, 512], BF16)
    xt = [M.alloc("xt", [128, 1024], F32) for _ in range(2)]
    xn = [M.alloc("xn", [128, 1024], BF16) for _ in range(2)]
    mT = M.alloc("mT", [128, 8, 512], BF16)
    g1 = M.alloc("g1", [128, 512], F32)
    g2t = M.alloc("g2t", [128, 512], F32)
    m1 = M.alloc("m1", [128, 512], F32)
    xres = M.alloc("xres", [128, 1024], F32)
    ot = [M.alloc("ot", [128, 512], F32) for _ in range(2)]
    outkeys = []
    for st in range(NST):
        stc = slice(st * 512, (st + 1) * 512)
        make_hT(st, 2, hT, xt, xn, [0, 7])
        for h in range(4):
            for kt in range(8):
                MM(PS[1][:, :], w3[:, kt, h * 128:(h + 1) * 128], hT[:, kt, :], kt == 0, kt == 7, ["w3", "hT"], ["ps1"], fuse=True)
            ACT(g1[:], PS[1][:, :], AF.Silu, ["ps1"], ["g1"])
            TT("dve", OAT[:, h, stc], OAT[:, h, stc], g1[:], ALU.mult, ["OAT", "g1"], ["OAT"])
        for c in range(8):
            for h in range(4):
                MM(PS[1][:, :], pa[:, h, c * 128:(c + 1) * 128], OAT[:, h, stc], h == 0, h == 3, ["pa", "OAT"], ["ps1"], fuse=True)
            for kt in range(8):
                MM(PS[2][:, :], w3[:, kt, 512 + c * 128:512 + (c + 1) * 128], hT[:, kt, :], kt == 0, kt == 7, ["w3", "hT"], ["ps2"], fuse=True)
            ACT(g1[:], PS[2][:, :], AF.Sigmoid, ["ps2", "mb"], ["g1"], bias=mb[:, c:c + 1])
            TT("dve", m1[:], PS[1][:, :], g1[:], ALU.mult, ["ps1", "g1"], ["m1"])
            for k in range(4):
                MM(PS[3][:, :], psw[:, k, c * 128:(c + 1) * 128], OST[:, k, stc], k == 0, k == 3, ["psw", "OST"], ["ps3"], fuse=True)
            for kt in range(8):
                MM(PS[4][:, :], w3[:, kt, 1536 + c * 128:1536 + (c + 1) * 128], hT[:, kt, :], kt == 0, kt == 7, ["w3", "hT"], ["ps4"], fuse=True)
            ACT(g2t[:], PS[4][:, :], AF.Sigmoid, ["ps4", "mb"], ["g2t"], bias=mb[:, 8 + c:9 + c])
            TT("dve", g2t[:], PS[3][:, :], g2t[:], ALU.mult, ["ps3", "g2t"], ["g2t"])
            TT("dve", mT[:, c, :], m1[:], g2t[:], ALU.add, ["m1", "g2t"], ["mT"])
        for tt in range(4):
            row = st * 512 + tt * 128
            DMA("sp", xres[:], x_d[row:row + 128, :], (), ["xres"], "xres")
            for half in range(2):
                for kt in range(8):
                    MM(PS[5 + half][:, :], mT[:, kt, tt * 128:(tt + 1) * 128], wo[:, kt, half * 512:(half + 1) * 512], kt == 0, kt == 7,
                       ["mT", "wo"], [f"ps{5 + half}"])
                TT("dve", ot[half][:], PS[5 + half][:, :], xres[:, half * 512:(half + 1) * 512], ALU.add, [f"ps{5 + half}", "xres"], [f"ot{half}"])
                ok = f"outd{st}_{tt}_{half}"
                DMA("sp", out_d[row:row + 128, half * 512:(half + 1) * 512], ot[half][:], [f"ot{half}"], [ok], f"ot{half}")
                outkeys.append(ok)
    NOP(outkeys)
    return finish(nc, S)


def finish(nc, S):
    import contextlib
    with contextlib.ExitStack() as stk:
        S._sem_ctx = {e: stk.enter_context(nc.semaphore(f"s_{e}")) for e in S.ENGS}
        S._dsem = {k: stk.enter_context(nc.semaphore(f"d_{k}")) for k in S.dma_count}
        block = stk.enter_context(nc.Block())
        S.emit(block)
    return nc


_NC_CACHE = {}


def _core_inputs(inputs, b, consts):
    m = {"x": np.ascontiguousarray(inputs["x"][b], dtype=np.float32)}
    for k, v in inputs.items():
        if k == "x":
            continue
        v = np.asarray(v, dtype=np.float32)
        if k == "rel_bias_table":
            m[k] = np.ascontiguousarray(v)
            continue
        v0 = v[0]
        if k in ("w_in", "ssm_glu_w", "proj_attn", "proj_ssm", "w_out"):
            m[k] = np.ascontiguousarray(v0)
        else:
            m[k] = np.ascontiguousarray(v0).reshape(-1)
    m.update(consts)
    return m


def kernel(**inputs):
    if "nc" not in _NC_CACHE:
        _NC_CACHE["nc"] = build_nc()
    nc = _NC_CACHE["nc"]
    consts = host_consts()
    in_maps = [_core_inputs(inputs, b, consts) for b in range(8)]
    res = run_bass_kernel_spmd(nc, in_maps, core_ids=list(range(8)))
    out = np.stack([np.asarray(res.results[b]["out"], dtype=np.float32) for b in range(8)], axis=0)
    return out
```

```python
import math
import numpy as np
import ml_dtypes
import concourse.bass as bass
import concourse.mybir as mybir
from concourse.bass_utils import run_bass_kernel_spmd

F32 = mybir.dt.float32
BF16 = mybir.dt.bfloat16
AF = mybir.ActivationFunctionType
ALU = mybir.AluOpType
AX = mybir.AxisListType


class _Op:
    __slots__ = ("eng", "fn", "idx", "dma", "dkey", "deps", "signal", "sigval", "sem", "fuse")

    def __init__(self, eng, fn, idx, dma, dkey):
        self.eng = eng
        self.fn = fn
        self.idx = idx
        self.dma = dma
        self.dkey = dkey
        self.deps = {}
        self.signal = False
        self.sigval = 0
        self.sem = None
        self.fuse = False


class Sched:
    ENGS = ("pe", "act", "dve", "pool", "sp")
    FUSE_WAITS = True

    def __init__(self, nc):
        self.nc = nc
        self.ops = {e: [] for e in self.ENGS}
        self.last_w = {}
        self.readers = {}
        self.dma_count = {}

    def add(self, eng, fn, reads=(), writes=(), dma=False, dkey=None, fuse=None):
        op = _Op(eng, fn, len(self.ops[eng]), dma, dkey)
        if fuse is None:
            fuse = (eng in ("act", "dve", "pool")) and not dma
        op.fuse = fuse and self.FUSE_WAITS
        reads = tuple(reads) + ("__B__",)
        for k in reads:
            w = self.last_w.get(k)
            if w is not None:
                op.deps[w] = True
        for k in writes:
            w = self.last_w.get(k)
            if w is not None and w not in op.deps:
                op.deps[w] = False
            rd = self.readers.get(k)
            if rd:
                for r in rd.get("eng", {}).values():
                    if r is not op and r not in op.deps:
                        op.deps[r] = False
                for r in rd.get("dma", []):
                    if r is not op and r not in op.deps:
                        op.deps[r] = False
        for k in reads:
            rd = self.readers.setdefault(k, {"eng": {}, "dma": []})
            if dma:
                rd["dma"].append(op)
            else:
                rd["eng"][eng] = op
        for k in writes:
            self.last_w[k] = op
            self.readers[k] = {"eng": {}, "dma": []}
        if dma:
            assert dkey is not None
            n = self.dma_count.get(dkey, 0) + 1
            self.dma_count[dkey] = n
            op.sigval = 16 * n
        self.ops[eng].append(op)
        return op

    def barrier(self):
        t = self._bar_tile
        self.add("dve", lambda e: e.memset(t[:, 0:1], 0.0), writes=("__B__",))

    def _needs_wait(self, op, dep, is_raw):
        if dep.dma:
            return True
        if dep.eng != op.eng:
            return True
        if op.dma:
            return True
        if op.eng == "pe":
            return False
        return is_raw and (op.idx - dep.idx) <= 3

    def emit(self, block, extra_final=None):
        nc = self.nc
        for e in self.ENGS:
            for op in self.ops[e]:
                for dep, is_raw in op.deps.items():
                    if self._needs_wait(op, dep, is_raw) and not dep.dma:
                        dep.signal = True
        sems = {e: self._sem_ctx[e] for e in self.ENGS}
        dsems = self._dsem
        for e in self.ENGS:
            n = 0
            for op in self.ops[e]:
                if op.dma:
                    op.sem = dsems[op.dkey]
                else:
                    op.sem = sems[e]
                    if op.signal:
                        n += 1
                        op.sigval = n

        def run_engine(ename, eng):
            known = {}
            for op in self.ops[ename]:
                need = {}
                for dep, is_raw in op.deps.items():
                    if not self._needs_wait(op, dep, is_raw):
                        continue
                    s = dep.sem
                    v = dep.sigval
                    if need.get(s, 0) < v:
                        need[s] = v
                pend = [(s, v) for s, v in need.items() if known.get(s, 0) < v]
                fused = None
                if op.fuse and pend:
                    fused = pend.pop()
                for s, v in pend:
                    eng.wait_ge(s, v)
                    known[s] = v
                ins = op.fn(eng)
                if fused is not None:
                    ins._wait_ge(fused[0], fused[1])
                    known[fused[0]] = fused[1]
                if op.dma:
                    ins.then_inc(op.sem, 16)
                elif op.signal:
                    ins.then_inc(op.sem, 1)
            if extra_final is not None:
                extra_final(ename, eng, known)

        @block.tensor
        def _(eng):
            run_engine("pe", eng)

        @block.scalar
        def _(eng):
            run_engine("act", eng)

        @block.vector
        def _(eng):
            run_engine("dve", eng)

        @block.gpsimd
        def _(eng):
            run_engine("pool", eng)

        @block.sync
        def _(eng):
            run_engine("sp", eng)


L = 4096
D = 1024
NST = 8
EPS = 1e-6
LAM_INIT = 0.8 - 0.6 * math.exp(-0.3 * 0)
TWO_PI = 2.0 * math.pi
NEG = -30000.0


def _bucket_np(rel):
    nb = 16
    me = 8
    side = np.where(rel > 0, nb, 0)
    n = np.abs(rel)
    nf = np.maximum(n, 1).astype(np.float32)
    large = me + (np.log(nf / np.float32(me)).astype(np.float32) / np.float32(math.log(128 / 8))
                  * np.float32(nb - me)).astype(np.int32)
    large = np.minimum(large, nb - 1)
    return side + np.where(n < me, n, large)


def host_consts():
    c = {}
    c["c_ident"] = np.eye(128, dtype=np.float32).astype(ml_dtypes.bfloat16)
    bo = np.zeros((128, 128), np.float32)
    bo[:64, :64] = 1.0 / 64
    bo[64:, 64:] = 1.0 / 64
    c["c_bones"] = bo.astype(ml_dtypes.bfloat16)
    c["c_J"] = np.ascontiguousarray(np.eye(128, dtype=np.float32)[::-1])
    rel = np.arange(-255, 128)
    b = _bucket_np(rel)
    oh = np.zeros((32, 384), np.float32)
    oh[b, np.arange(383)] = 1.0
    c["c_oh"] = oh
    k = np.arange(128)[:, None]
    q = np.arange(128)[None, :]
    c["c_maskD"] = np.where((k // 64) <= (q // 64), 0.0, NEG).astype(np.float32)
    c["c_iotap"] = np.stack([np.arange(128), -np.arange(128)], 1).astype(np.float32)
    c["c_iotat"] = np.tile(np.arange(128, dtype=np.float32)[None, :], (128, 1))
    c["c_tri"] = (np.arange(128)[:, None] <= np.arange(128)[None, :]).astype(np.float32).astype(ml_dtypes.bfloat16)
    return c


class Mem:
    def __init__(self, nc, start=16640, end=229376):
        self.nc = nc
        self.p = start
        self.end = end
        self.n = 0

    def alloc(self, name, shape, dtype):
        esz = 4 if dtype in (F32, mybir.dt.int32) else 2
        size = int(np.prod(shape[1:])) * esz
        size = (size + 63) // 64 * 64
        assert self.p + size <= self.end, f"SBUF overflow allocating {name}: {self.p}+{size} > {self.end}"
        self.n += 1
        t = self.nc.alloc_sbuf_tensor_at(f"{name}_{self.n}", list(shape), dtype, offset=self.p)
        self.p += size
        return t

    def mark(self):
        return self.p

    def release(self, m):
        self.p = m


def build_nc(dbg=None):
    nc = bass.Bass("TRN2", target_bir_lowering=False)
    S = Sched(nc)
    M = Mem(nc)

    def din(name, shape, dt=F32):
        return nc.dram_tensor(name, list(shape), dt, kind="ExternalInput").ap()

    x_d = din("x", [L, D])
    w_in_d = din("w_in", [D, 5120])
    ng_d = din("norm_gain", [D])
    mb_d = din("merge_gate_b", [2048])
    qg_d = din("q_norm_gain", [64])
    kg_d = din("k_norm_gain", [64])
    lq1_d = din("lambda_q1", [64]); lk1_d = din("lambda_k1", [64])
    lq2_d = din("lambda_q2", [64]); lk2_d = din("lambda_k2", [64])
    sg_d = din("diff_subln_gain", [128])
    rb_d = din("rel_bias_table", [32, 4])
    are_d = din("ssm_A_re", [2048]); aim_d = din("ssm_A_im", [2048])
    ldt_d = din("ssm_log_dt", [32])
    bre_d = din("ssm_B_re", [32 * 64 * 16]); bim_d = din("ssm_B_im", [32 * 64 * 16])
    cre_d = din("ssm_C_re", [32 * 16 * 64]); cim_d = din("ssm_C_im", [32 * 16 * 64])
    dd_d = din("ssm_D", [512])
    gw_d = din("ssm_glu_w", [512, 512])
    gb_d = din("ssm_glu_b", [512])
    pa_d = din("proj_attn", [512, 1024])
    ps_d = din("proj_ssm", [512, 1024])
    wo_d = din("w_out", [1024, 1024])
    c_ident_d = din("c_ident", [128, 128], BF16)
    c_bones_d = din("c_bones", [128, 128], BF16)
    c_J_d = din("c_J", [128, 128])
    c_oh_d = din("c_oh", [32, 384])
    c_maskD_d = din("c_maskD", [128, 128])
    c_iotap_d = din("c_iotap", [128, 2])
    c_iotat_d = din("c_iotat", [128, 128])
    c_tri_d = din("c_tri", [128, 128], BF16)
    out_d = nc.dram_tensor("out", [L, D], F32, kind="ExternalOutput").ap()
    fsc_d = nc.dram_tensor("fscratch", [4, 384], F32).ap()
    dbg_d = None
    if dbg:
        dbg_d = nc.dram_tensor("dbg", [128, 4 * L], BF16, kind="ExternalOutput").ap()

    def dap(t, off, pat):
        return bass.AP(t.tensor, off, [list(p) for p in pat])

    PS = [nc.alloc_psum_tensor(f"psb{i}", [128, 512], F32) for i in range(8)]

    def MM(out, lhsT, rhs, start, stop, r, w, fuse=False):
        S.add("pe", lambda e: e.matmul(out, lhsT=lhsT, rhs=rhs, start=start, stop=stop), r, w, fuse=fuse)

    def TR(out, in_, r, w):
        S.add("pe", lambda e: e.transpose(out=out, in_=in_, identity=ident[:]), tuple(r) + ("ident",), w)

    def ACT(out, in_, func, r, w, bias=0.0, scale=1.0, accum=None):
        if accum is None:
            S.add("act", lambda e: e.activation(out=out, in_=in_, func=func, bias=bias, scale=scale), r, w)
        else:
            S.add("act", lambda e: e.activation(out=out, in_=in_, func=func, bias=bias, scale=scale, accum_out=accum), r, w)

    def TS(eng, out, in0, s1, s2, op0, op1, r, w):
        if s2 is None:
            S.add(eng, lambda e: e.tensor_scalar(out=out, in0=in0, scalar1=s1, scalar2=None, op0=op0), r, w)
        else:
            S.add(eng, lambda e: e.tensor_scalar(out=out, in0=in0, scalar1=s1, scalar2=s2, op0=op0, op1=op1), r, w)

    def TT(eng, out, in0, in1, op, r, w):
        S.add(eng, lambda e: e.tensor_tensor(out=out, in0=in0, in1=in1, op=op), r, w)

    def STT(eng, out, in0, scalar, in1, op0, op1, r, w):
        S.add(eng, lambda e: e.scalar_tensor_tensor(out=out, in0=in0, scalar=scalar, in1=in1, op0=op0, op1=op1), r, w)

    def CP(eng, out, in_, r, w):
        if eng == "act":
            S.add("act", lambda e: e.copy(out=out, in_=in_), r, w)
        else:
            S.add(eng, lambda e: e.tensor_copy(out=out, in_=in_), r, w)

    def RSUM(out, in_, r, w):
        S.add("dve", lambda e: e.reduce_sum(out=out, in_=in_, axis=AX.X), r, w)

    def RECIP(out, in_, r, w):
        S.add("dve", lambda e: e.reciprocal(out=out, in_=in_), r, w)

    def NOP(r):
        S.add("sp", lambda e: e.nop(), r, [])

    def MS(eng, ap, val, w):
        S.add(eng, lambda e: e.memset(ap, val), (), w)

    def DMA(eng, out, in_, r, w, dkey, slow=False):
        if slow:
            S.add(eng, lambda e: e.dma_start(out=out, in_=in_, allow_slow_non_contiguous=True), r, w, dma=True, dkey=dkey)
        else:
            S.add(eng, lambda e: e.dma_start(out=out, in_=in_), r, w, dma=True, dkey=dkey)

    bar = M.alloc("bar", [128, 16], F32)
    S._bar_tile = bar
    ident = M.alloc("ident", [128, 128], BF16)
    bones = M.alloc("bones", [128, 128], BF16)
    tri = M.alloc("tri", [128, 128], BF16)
    iotap = M.alloc("iotap", [128, 2], F32)
    iotat = M.alloc("iotat", [128, 128], F32)
    gcol = M.alloc("gcol", [128, 8], F32)
    rs_all = M.alloc("rs_all", [128, 3 * 32 * 2], F32)
    OAT = M.alloc("OAT", [128, 4, L], BF16)
    OSTm = M.alloc("OSTm", [128, 4 * L], BF16)
    OST = OSTm[:, :].rearrange("p (c t) -> p c t", c=4)
    stag32 = OSTm[:, :].bitcast(F32)
    Bst = stag32[:, 0:4096].rearrange("p (r c g q) -> p r c g q", r=2, c=4, g=8)
    Cst = stag32[:, 4096:8192].rearrange("p (r a c) -> p r a c", r=2, a=16)

    DMA("sp", ident[:], c_ident_d[:, :], (), ["ident"], "ident")
    DMA("sp", bones[:], c_bones_d[:, :], (), ["bones"], "bones")
    DMA("sp", tri[:], c_tri_d[:, :], (), ["tri"], "tri")
    DMA("sp", iotap[:], c_iotap_d[:, :], (), ["iotap"], "iotap")
    DMA("sp", iotat[:], c_iotat_d[:, :], (), ["iotat"], "iotat")
    DMA("sp", gcol[:], dap(ng_d, 0, [[1, 128], [128, 8]]), (), ["gcol"], "gcol", slow=True)

    MS("dve", stag32[:, 0:4096], 0.0, ["Bst"])
    MS("pool", stag32[:, 4096:8192], 0.0, ["Cst"])
    bc_dmas = []
    for g in range(32):
        for ri in range(2):
            bc_dmas.append((g, ri))

    def emit_bc(n):
        for _ in range(n):
            if not bc_dmas:
                return
            g, ri = bc_dmas.pop(0)
            ct, g8 = g // 8, g % 8
            pair, g2 = g // 2, g % 2
            bd = (bre_d, bim_d)[ri]
            cd = (cre_d, cim_d)[ri]
            DMA("sp", Bst[16 * g8:16 * g8 + 16, ri, ct, g8, :], dap(bd, g * 1024, [[1, 16], [16, 64]]),
                (), ["Bst"], "Bst", slow=True)
            DMA("sp", Cst[64 * g2:64 * g2 + 64, ri, pair, (g % 8) * 16:(g % 8) * 16 + 16],
                dap(cd, g * 1024, [[1, 64], [64, 16]]), (), ["Cst"], "Cst", slow=True)

    def load_weight(dst, src_d, row0, nrows_tiles, col0, ncols, stg, fold_gain, keyw):
        i = 0
        for kt in range(nrows_tiles):
            for c0 in range(0, ncols, 2048):
                cw = min(2048, ncols - c0)
                sb = stg[i % 2]
                sk = f"stg{i % 2}"
                DMA("sp", sb[:, 0:cw], src_d[row0 + kt * 128: row0 + (kt + 1) * 128, col0 + c0: col0 + c0 + cw],
                    (), [sk], sk)
                if fold_gain:
                    if i % 2 == 0:
                        ACT(dst[:, kt, c0:c0 + cw], sb[:, 0:cw], AF.Copy, [sk, "gcol"], [keyw], scale=gcol[:, kt:kt + 1])
                    else:
                        TS("dve", dst[:, kt, c0:c0 + cw], sb[:, 0:cw], gcol[:, kt:kt + 1], None, ALU.mult, None, [sk, "gcol"], [keyw])
                else:
                    if i % 2 == 0:
                        CP("act", dst[:, kt, c0:c0 + cw], sb[:, 0:cw], [sk], [keyw])
                    else:
                        CP("dve", dst[:, kt, c0:c0 + cw], sb[:, 0:cw], [sk], [keyw])
                i += 1

    def hT_A(st, tt, phase, hT, xt, xn):
        i = st * 4 + tt
        b = i % len(xt)
        nb_ = i % len(xn)
        row = st * 512 + tt * 128
        col = (phase * 32 + i) * 2
        ss = rs_all[:, col:col + 1]
        rs = rs_all[:, col + 1:col + 2]
        xk, nk = f"xt{b}", f"xn{nb_}"
        DMA("sp", xt[b][:], x_d[row:row + 128, :], (), [xk], xk)
        ACT(xn[nb_][:], xt[b][:], AF.Square, [xk], [nk, f"ss{col}"], accum=ss)
        ACT(rs, ss, AF.Ln, [f"ss{col}"], [f"rs{col}"], bias=EPS, scale=1.0 / D)
        ACT(rs, rs, AF.Exp, [f"rs{col}"], [f"rs{col}"], scale=-0.5)
        TS("dve", xn[nb_][:], xt[b][:], rs, None, ALU.mult, None, [xk, f"rs{col}"], [nk])

    def hT_B(st, tt, phase, hT, xt, xn, tbanks):
        i = st * 4 + tt
        nb_ = i % len(xn)
        nk = f"xn{nb_}"
        tbk = tbanks[i % len(tbanks)]
        psT = PS[tbk][:, 0:512].bitcast(BF16)
        for kt in range(8):
            TR(psT[:, kt * 128:(kt + 1) * 128], xn[nb_][:, kt * 128:(kt + 1) * 128], [nk], [f"ps{tbk}"])
        CP("dve", hT[:, :, tt * 128:(tt + 1) * 128], psT[:, :].rearrange("p (k t) -> p k t", k=8), [f"ps{tbk}"], ["hT"])

    def make_hT(st, phase, hT, xt, xn, tbanks):
        for tt in range(4):
            hT_A(st, tt, phase, hT, xt, xn)
            hT_B(st, tt, phase, hT, xt, xn, tbanks)

    m_glob = M.mark()
    wq = M.alloc("wq", [128, 8, 1536], BF16)
    KT = M.alloc("KT", [128, 4, L], BF16)
    V = M.alloc("V", [128, 32, 4, 128], BF16)
    ones128 = M.alloc("ones128", [128, 128], BF16)
    sgcol = M.alloc("sgcol", [128, 1], F32)
    c15 = M.alloc("c15", [128, 4], F32)
    bnear = M.alloc("bnear", [128, 2, 2, 4, 128], BF16)
    gq = M.alloc("gq", [128, 2], F32)
    lamt = M.alloc("lamt", [128, 8], F32)
    m_work = M.mark()
    stg = [M.alloc("stg", [128, 2048], F32) for _ in range(2)]
    lamv = M.alloc("lamv", [128, 4, 64], F32)
    tbl = M.alloc("tbl", [32, 4], F32)
    oh = M.alloc("oh", [32, 384], F32)
    fsb = M.alloc("fsb", [4, 384], F32)
    Jt = M.alloc("Jt", [128, 128], F32)
    hank = M.alloc("hank", [128, 4, 256], F32)
    maskD = M.alloc("maskD", [128, 128], F32)
    biasP = M.alloc("biasP", [128, 4, 128], F32)
    biasD = M.alloc("biasD", [128, 4, 128], F32)
    bhi32 = M.alloc("bhi32", [128, 4, 128], F32)

    load_weight(wq, w_in_d, 0, 8, 0, 1536, stg, True, "wq")
    MS("dve", ones128[:], 1.0, ["ones128"])
    DMA("sp", sgcol[:], dap(sg_d, 0, [[1, 128], [1, 1]]), (), ["sgcol"], "sgcol")
    TS("dve", sgcol[:], sgcol[:], 1.0 - LAM_INIT, None, ALU.mult, None, ["sgcol"], ["sgcol"])
    for hlf in range(2):
        DMA("sp", gq[64 * hlf:64 * hlf + 64, 0:1], dap(qg_d, 0, [[1, 64], [1, 1]]), (), ["gq"], "gq")
        DMA("sp", gq[64 * hlf:64 * hlf + 64, 1:2], dap(kg_d, 0, [[1, 64], [1, 1]]), (), ["gq"], "gq")
    TS("dve", gq[:, 0:1], gq[:, 0:1], 0.125, None, ALU.mult, None, ["gq"], ["gq"])
    for i, dd in enumerate((lq1_d, lk1_d, lq2_d, lk2_d)):
        DMA("sp", lamv[:, i, :], dap(dd, 0, [[0, 128], [1, 64]]), (), ["lamv"], "lamv")
    TT("dve", lamv[:, 0, :], lamv[:, 0, :], lamv[:, 1, :], ALU.mult, ["lamv"], ["lamv"])
    TT("dve", lamv[:, 2, :], lamv[:, 2, :], lamv[:, 3, :], ALU.mult, ["lamv"], ["lamv"])
    RSUM(lamt[:, 0:1], lamv[:, 0, :], ["lamv"], ["lamt"])
    RSUM(lamt[:, 1:2], lamv[:, 2, :], ["lamv"], ["lamt"])
    ACT(lamt[:, 2:4], lamt[:, 0:2], AF.Exp, ["lamt"], ["lamt"])
    TT("dve", lamt[:, 4:5], lamt[:, 3:4], lamt[:, 2:3], ALU.subtract, ["lamt"], ["lamt"])
    TS("dve", lamt[:, 5:6], lamt[:, 4:5], -LAM_INIT, None, ALU.add, None, ["lamt"], ["lamt"])
    neglam = lamt[:, 5:6]
    DMA("sp", tbl[:], rb_d[:, :], (), ["tbl"], "tbl")
    DMA("sp", oh[:], c_oh_d[:, :], (), ["oh"], "oh")
    DMA("sp", Jt[:], c_J_d[:, :], (), ["Jt"], "Jt")
    DMA("sp", maskD[:], c_maskD_d[:, :], (), ["maskD"], "maskD")
    DMA("sp", c15[:], dap(rb_d, 15 * 4, [[0, 128], [1, 4]]), (), ["c15"], "c15")
    MM(PS[1][0:4, 0:384], tbl[:], oh[:], True, True, ["tbl", "oh"], ["ps1"])
    CP("dve", fsb[:], PS[1][0:4, 0:384], ["ps1"], ["fsb"])
    DMA("sp", fsc_d[:, :], fsb[:], ["fsb"], ["fsc"], "fsb")
    for h in range(4):
        DMA("sp", hank[:, h, :], dap(fsc_d, h * 384, [[1, 128], [1, 256]]), ["fsc"], ["hank"], "hank")
    for h in range(4):
        MM(PS[2][:, h * 128:(h + 1) * 128], hank[:, h, 0:128], Jt[:], True, True, ["hank", "Jt"], ["ps2"])
        MM(PS[3][:, h * 128:(h + 1) * 128], hank[:, h, 128:256], Jt[:], True, True, ["hank", "Jt"], ["ps3"])
    CP("dve", biasP[:], PS[2][:, :].rearrange("p (h q) -> p h q", h=4), ["ps2"], ["biasP"])
    TT("dve", biasD[:], PS[3][:, :].rearrange("p (h q) -> p h q", h=4),
       maskD[:].unsqueeze(1).broadcast_to([128, 4, 128]), ALU.add, ["ps3", "maskD"], ["biasD"])
    for d_, bt_ in ((0, biasD), (1, biasP)):
        TT("dve", bt_[:], bt_[:], c15[:, :].unsqueeze(2).broadcast_to([128, 4, 128]), ALU.subtract, ["biasD", "biasP", "c15"], ["biasD", "biasP"])
        CP("dve", bnear[:, d_, 0], bt_[:], ["biasD", "biasP"], ["bnear"])
        CP("dve", bhi32[:], bnear[:, d_, 0], ["bnear"], ["bhi32"])
        TT("dve", bnear[:, d_, 1], bt_[:], bhi32[:], ALU.subtract, ["biasD", "biasP", "bhi32"], ["bnear"])
    S.barrier()
    M.release(m_work)
    hT = M.alloc("hT", [128, 8, 512], BF16)
    xt = [M.alloc("xt", [128, 1024], F32) for _ in range(2)]
    xn = [M.alloc("xn", [128, 1024], BF16) for _ in range(2)]
    qT = M.alloc("qT", [128, 4, 512], BF16)
    sq = M.alloc("sq", [128, 512], BF16)
    rstd = M.alloc("rstd", [128, 512], F32)
    PTP = [M.alloc("PTP", [128, 2, 512], BF16) for _ in range(2)]
    PT = [PTP[0][:, 0, :], PTP[0][:, 1, :], PTP[1][:, 0, :], PTP[1][:, 1, :]]
    rcp = rstd
    oraw = [M.alloc("oraw", [128, 512], F32) for _ in range(2)]
    racc = [M.alloc("racc", [128, 2, 512], F32) for _ in range(2)]
    tob = oraw[1]
    rsb = [M.alloc("rsb", [128, 512], BF16) for _ in range(2)]
    sqb = sq
    rs2 = M.alloc("rs2", [128, 512], F32)
    sq2 = [sq, M.alloc("sq2", [128, 512], BF16)]
    rstd2 = [rstd, rs2]
    rkeys = ["rstd0", "rs2"]

    PSB = [3, 4, 7, 6]
    NB = 4
    SKEW = 2
    PAIRED = True
    QPAD = False
    FILL = 0
    deferred = []

    ZB3 = [1, 2, 4]

    def qk_mm(c, st):
        zb = ZB3[c % 3]
        for kt in range(8):
            MM(PS[zb][:, :], wq[:, kt, c * 128:(c + 1) * 128], hT[:, kt, :], kt == 0, kt == 7, ["wq", "hT"], [f"ps{zb}"], fuse=True)

    MB = [0, 3]

    def qk_c1(c, st):
        zb = ZB3[c % 3]
        sqc = sq2[c % 2]
        mb_ = MB[c % 2]
        ACT(sqc[:], PS[zb][:, :], AF.Square, [f"ps{zb}"], [f"sq{c % 2}"])
        MM(PS[mb_][:, :], bones[:], sqc[:], True, True, ["bones", f"sq{c % 2}"], [f"ps{mb_}"])

    def qk_c2(c, st):
        zb = ZB3[c % 3]
        rsc = rstd2[c % 2]
        mb_ = MB[c % 2]
        ACT(rsc[:], PS[mb_][:, :], AF.Ln, [f"ps{mb_}"], [rkeys[c % 2]], bias=EPS)
        ACT(rsc[:], rsc[:], AF.Exp, [rkeys[c % 2]], [rkeys[c % 2]], scale=-0.5)
        if c < 4:
            STT("dve", qT[:, c, :], PS[zb][:, :], gq[:, 0:1], rsc[:], ALU.mult, ALU.mult, [f"ps{zb}", "gq", rkeys[c % 2]], ["qT"])
        else:
            STT("dve", KT[:, c - 4, st * 512:(st + 1) * 512], PS[zb][:, :], gq[:, 1:2], rsc[:], ALU.mult, ALU.mult,
                [f"ps{zb}", "gq", rkeys[c % 2]], [f"KT{st}"])

    for st in range(NST):
        if st == 0:
            make_hT(st, 0, hT, xt, xn, [0, 3])
        emit_bc(8)
        qk_mm(0, st)
        qk_mm(1, st)
        qk_c1(0, st)
        for c in range(8):
            if c + 2 < 8:
                qk_mm(c + 2, st)
            if c + 1 < 8:
                qk_c1(c + 1, st)
            qk_c2(c, st)
        for tt in range(4):
            blk = st * 4 + tt
            zb = 1 + (tt % 2)
            for kt in range(8):
                MM(PS[zb][:, :], hT[:, kt, tt * 128:(tt + 1) * 128], wq[:, kt, 1024:1536], kt == 0, kt == 7, ["wq", "hT"], [f"ps{zb}"])
            CP("dve", V[:, blk, :, :], PS[zb][:, :].rearrange("p (h d) -> p h d", h=4), [f"ps{zb}"], [f"V{st}"])
        kv_keys = [f"KT{i}" for i in range(st + 1)] + [f"V{i}" for i in range(st + 1)]
        iters = [(h, s, j) for h in range(4) for j in range(4 * st + 4) for s in range(2)]
        NI = len(iters)

        def emit_S(i, st=st, kv_keys=kv_keys, iters=iters):
            h, s, j = iters[i]
            lo = max(0, j - 4 * st)
            cols = slice(lo * 128, 512)
            pb = i % NB
            psS = PS[PSB[pb]]
            near = [(qb, 4 * st + qb - j) for qb in range(lo, 4) if 4 * st + qb - j <= 1]
            MM(psS[:, cols], KT[64 * s:64 * s + 64, h, j * 128:(j + 1) * 128], qT[64 * s:64 * s + 64, h, cols],
               True, not near, kv_keys + ["qT"], [f"ps{PSB[pb]}"], fuse=(j < 4 * st))
            for n_, (qb, d) in enumerate(near):
                qs_ = slice(qb * 128, (qb + 1) * 128)
                MM(psS[:, qs_], ident[:], bnear[:, d, 0, h, :], False, False, ["ident", "bnear"], [f"ps{PSB[pb]}"], fuse=True)
                MM(psS[:, qs_], ident[:], bnear[:, d, 1, h, :], False, n_ == len(near) - 1, ["ident", "bnear"], [f"ps{PSB[pb]}"], fuse=True)

        def fin_A(h, s, st):
            MM(PS[0][:, :], ones128[:], rsb[s][:], True, True, ["ones128", f"rsb{s}"], ["ps0"])
            ACT(rcp[:], PS[0][:, :], AF.Ln, ["ps0"], ["rstd0"])
            ACT(rcp[:], rcp[:], AF.Exp, ["rstd0"], ["rstd0"], scale=-1.0)
            TT("dve", oraw[s][:], oraw[s][:], rcp[:], ALU.mult, [f"oraw{s}", "rstd0"], [f"oraw{s}"])
            if s == 1:
                STT("dve", tob[:], oraw[1][:], neglam, oraw[0][:], ALU.mult, ALU.add, ["oraw0", "oraw1", "lamt"], ["oraw1"])
                ACT(sqb[:], tob[:], AF.Square, ["oraw1"], ["sq0"])

        def fin_B(h, st):
            stc = slice(st * 512, (st + 1) * 512)
            MM(PS[0][:, :], ones128[:], sqb[:], True, True, ["ones128", "sq0"], ["ps0"])
            ACT(rs2[:], PS[0][:, :], AF.Ln, ["ps0"], ["rs2"], bias=EPS, scale=1.0 / 128)
            ACT(rs2[:], rs2[:], AF.Exp, ["rs2"], ["rs2"], scale=-0.5)
            STT("dve", OAT[:, h, stc], tob[:], sgcol[:, 0:1], rs2[:], ALU.mult, ALU.mult, ["oraw1", "sgcol", "rs2"], ["OAT"])

        def emit_rest(i, st=st, kv_keys=kv_keys, iters=iters):
            h, s, j = iters[i]
            lo = max(0, j - 4 * st)
            pb = i % NB
            pk = f"ps{PSB[pb]}"
            psS = PS[PSB[pb]]
            accb = 5 if s == 0 else 2
            if j == 0 and s == 0:
                MS("dve", racc[0][:], 0.0, ["racc0"])
            ACT(PT[pb][:, lo * 128:512], psS[:, lo * 128:512], AF.Exp, [pk, "c15"], [f"PT{pb}"], bias=c15[:, h:h + 1])
            qs = slice(lo * 128, 512)
            last = (j == 4 * st + 3)
            MM(PS[accb][:, qs], V[:, j, h, :], PT[pb][:, qs], j == 0, last, [f"PT{pb}"] + kv_keys, [f"ps{accb}"], fuse=(j < 4 * st))
            for _f in range(FILL):
                MM(PS[6][:, :], ones128[:], wq[:, 0, 0:512], True, True, ["ones128", "wq"], ["ps6"])
            if s == 1:
                e_ = 0
                pbuf = (i // 2) % 2
                TT(("dve", "pool")[e_], racc[e_][:, :, qs], racc[e_][:, :, qs], PTP[pbuf][:, :, qs], ALU.add,
                   [f"racc{e_}", f"PT{pb}", f"PT{pb - 1}"], [f"racc{e_}"])
            if not last:
                return
            CP("act", oraw[s][:], PS[accb][:, :], [f"ps{accb}"], [f"oraw{s}"])
            if s == 1:
                for ss_ in range(2):
                    CP("dve", rsb[ss_][:], racc[0][:, ss_, :], ["racc0"], [f"rsb{ss_}"])
            deferred.append((i + 4, "A", h, s, st))
            if s == 1:
                deferred.append((i + 7, "B", h, s, st))

        def run_deferred(k):
            while deferred and deferred[0][0] <= k:
                _, kind, hh, ss, sst = deferred.pop(0)
                if kind == "A":
                    fin_A(hh, ss, sst)
                else:
                    fin_B(hh, sst)

        for i in range(0, NI + SKEW, 2):
            if PAIRED:
                for ii in (i, i + 1):
                    if ii < NI:
                        emit_S(ii)
                for ii in (i, i + 1):
                    k = ii - SKEW
                    if 0 <= k < NI:
                        emit_rest(k)
                        run_deferred(k)
                if st + 1 < NST:
                    for tt_ in range(4):
                        if i == 2 * ((tt_ * NI // 4) // 2):
                            hT_A(st + 1, tt_, 0, hT, xt, xn)
                        if i == 2 * ((tt_ * NI // 4 + 7) // 2):
                            hT_B(st + 1, tt_, 0, hT, xt, xn, [0])
            else:
                for ii in (i, i + 1):
                    if ii < NI:
                        emit_S(ii)
                    k = ii - SKEW
                    if 0 <= k < NI:
                        emit_rest(k)
                        run_deferred(k)
        run_deferred(10 ** 9)

    if dbg == "p1":
        DMA("sp", dbg_d[:, :], OAT[:, :, :].rearrange("p h t -> p (h t)"), ["OAT"], ["outd"], "OAT")
        NOP(["outd"])
        return finish(nc, S)
    I32 = mybir.dt.int32
    S.barrier()
    M.release(m_glob)
    w2 = M.alloc("w2", [128, 8, 1024], BF16)
    gluw = M.alloc("gluw", [128, 4, 512], BF16)
    glub = M.alloc("glub", [128, 4], F32)
    ta = M.alloc("ta", [128, 2048], F32)
    tb = M.alloc("tb", [128, 2048], F32)
    te = M.alloc("te", [128, 16, 128], F32)
    tf = M.alloc("tf", [128, 16, 128], F32)
    Bbd = M.alloc("Bbd", [128, 4, 2, 512], BF16)
    Ccat = M.alloc("Ccat", [128, 2, 16, 128], BF16)
    Ddiag = M.alloc("Ddiag", [128, 4, 128], BF16)
    dcol = M.alloc("dcol", [128, 4], F32)
    negtri = M.alloc("negtri", [128, 128], BF16)
    CcatN = M.alloc("CcatN", [128, 16, 128], BF16)
    l128 = M.alloc("l128", [128, 6, 16], F32)
    cre = [M.alloc("cre", [128, 16], F32) for _ in range(2)]
    cim = [M.alloc("cim", [128, 16], F32) for _ in range(2)]
    m_work2 = M.mark()
    stg = [M.alloc("stg", [128, 2048], F32) for _ in range(2)]
    load_weight(w2, w_in_d, 0, 8, 2048, 1024, stg, True, "w2")
    load_weight(gluw, gw_d, 0, 4, 0, 512, stg, False, "gluw")
    S.barrier()
    M.release(m_work2)
    T1 = M.alloc("T1", [128, 2048], F32)
    T2 = M.alloc("T2", [128, 2048], F32)
    T3 = M.alloc("T3", [128, 2048], F32)
    T4 = M.alloc("T4", [128, 2048], F32)
    TI = M.alloc("TI", [128, 2048], I32)
    dtb = M.alloc("dtb", [128, 32], F32)
    s2 = M.alloc("s2", [128, 8, 16], F32)
    s3 = M.alloc("s3", [128, 14, 256], F32)
    s3i = M.alloc("s3i", [128, 256], I32)
    dt3 = M.alloc("dt3", [128, 4], F32)

    DMA("sp", glub[:], dap(gb_d, 0, [[1, 128], [128, 4]]), (), ["glub"], "glub", slow=True)

    def frac_sincos(y, tmp, ti, sin_out, cos_out, key):
        CP("dve", ti, y, [key + "y"], [key + "ti"])
        CP("dve", tmp, ti, [key + "ti"], [key + "tmp"])
        TT("dve", tmp, y, tmp, ALU.subtract, [key + "y", key + "tmp"], [key + "tmp"])
        ACT(sin_out, tmp, AF.Sin, [key + "tmp"], [key + "sin"], scale=TWO_PI)
        TS("dve", y, y, 0.25, None, ALU.add, None, [key + "y"], [key + "y"])
        CP("dve", ti, y, [key + "y"], [key + "ti"])
        CP("dve", tmp, ti, [key + "ti"], [key + "tmp"])
        TT("dve", tmp, y, tmp, ALU.subtract, [key + "y", key + "tmp"], [key + "tmp"])
        ACT(cos_out, tmp, AF.Sin, [key + "tmp"], [key + "cos"], scale=TWO_PI)

    DMA("sp", T1[:], dap(are_d, 0, [[0, 128], [1, 2048]]), (), ["T1"], "T1")
    DMA("sp", T2[:], dap(aim_d, 0, [[0, 128], [1, 2048]]), (), ["T2"], "T2")
    DMA("sp", dtb[:], dap(ldt_d, 0, [[0, 128], [1, 32]]), (), ["dtb"], "dtb")
    ACT(dtb[:], dtb[:], AF.Exp, ["dtb"], ["dtb"])
    dtb_b = dtb[:, :].unsqueeze(2).broadcast_to([128, 32, 64])
    TT("dve", T1[:, :].rearrange("p (g q) -> p g q", g=32), T1[:, :].rearrange("p (g q) -> p g q", g=32), dtb_b, ALU.mult, ["T1", "dtb"], ["T1"])
    TT("dve", T2[:, :].rearrange("p (g q) -> p g q", g=32), T2[:, :].rearrange("p (g q) -> p g q", g=32), dtb_b, ALU.mult, ["T2", "dtb"], ["T2"])
    ACT(ta[:], T1[:], AF.Exp, ["T1", "iotap"], ["ta"], scale=iotap[:, 1:2])
    TS("dve", T3[:], T2[:], iotap[:, 0:1], 1.0 / TWO_PI, ALU.mult, ALU.mult, ["T2", "iotap"], ["L1y"])
    frac_sincos(T3[:], T4[:], TI[:], tb[:], T1[:], "L1")
    STT("dve", tb[:], tb[:], -1.0, ta[:], ALU.mult, ALU.mult, ["L1sin", "ta"], ["tb", "L1sin"])
    TT("dve", ta[:], ta[:], T1[:], ALU.mult, ["ta", "L1cos", "tb"], ["ta"])
    A2re, A2im, dt2, m2, th2 = (s2[:, i, :] for i in range(5))
    DMA("sp", A2re, dap(are_d, 0, [[1, 128], [128, 16]]), (), ["A2re"], "A2re", slow=True)
    DMA("sp", A2im, dap(aim_d, 0, [[1, 128], [128, 16]]), (), ["A2im"], "A2im", slow=True)
    for g2 in range(2):
        DMA("sp", s2[64 * g2:64 * g2 + 64, 2, :], dap(ldt_d, g2, [[0, 64], [2, 16]]), (), ["dt2"], "dt2", slow=True)
    ACT(dt2, dt2, AF.Exp, ["dt2"], ["dt2"])
    TT("dve", m2, A2re, dt2, ALU.mult, ["A2re", "dt2"], ["m2"])
    STT("dve", th2, A2im, 1.0 / TWO_PI, dt2, ALU.mult, ALU.mult, ["A2im", "dt2"], ["th2"])
    T3v = T3[:, :].rearrange("p (a t) -> p a t", a=16)
    for pair in range(16):
        ACT(te[:, pair, :], iotat[:], AF.Exp, ["iotat", "m2"], ["te"], scale=s2[:, 3, pair:pair + 1])
        TS("dve", T3v[:, pair, :], iotat[:], s2[:, 4, pair:pair + 1], None, ALU.mult, None, ["iotat", "th2", "L1y", "L1tmp"], ["L2y"])
    frac_sincos(T3[:], T4[:], TI[:], tf[:, :, :].rearrange("p a t -> p (a t)"), T1[:], "L2")
    TT("dve", tf[:, :, :].rearrange("p a t -> p (a t)"), tf[:, :, :].rearrange("p a t -> p (a t)"), te[:, :, :].rearrange("p a t -> p (a t)"),
       ALU.mult, ["L2sin", "te"], ["tf", "L2sin"])
    TT("dve", te[:, :, :].rearrange("p a t -> p (a t)"), te[:, :, :].rearrange("p a t -> p (a t)"), T1[:], ALU.mult, ["te", "L2cos", "tf"], ["te"])
    A3re, A3im, m3, y3, dec3, sin3, cos3, nr3, den3, qre3, qim3, u3a, u3b, tmp3 = (s3[:, i, :] for i in range(14))
    for g8 in range(8):
        DMA("sp", s3[16 * g8:16 * g8 + 16, 0, :].rearrange("p (c q) -> p c q", c=4), dap(are_d, g8 * 64, [[0, 16], [512, 4], [1, 64]]), (), ["A3re"], "A3re")
        DMA("sp", s3[16 * g8:16 * g8 + 16, 1, :].rearrange("p (c q) -> p c q", c=4), dap(aim_d, g8 * 64, [[0, 16], [512, 4], [1, 64]]), (), ["A3im"], "A3im")
        DMA("sp", dt3[16 * g8:16 * g8 + 16, :], dap(ldt_d, g8, [[0, 16], [8, 4]]), (), ["dt3"], "dt3", slow=True)
    ACT(dt3[:], dt3[:], AF.Exp, ["dt3"], ["dt3"])
    dt3_b = dt3[:, :].unsqueeze(2).broadcast_to([128, 4, 64])
    v3 = lambda a: a.rearrange("p (c q) -> p c q", c=4)
    TT("dve", v3(m3), v3(A3re), dt3_b, ALU.mult, ["A3re", "dt3"], ["m3"])
    TT("dve", v3(y3), v3(A3im), dt3_b, ALU.mult, ["A3im", "dt3"], ["L3y"])
    TS("dve", y3, y3, 1.0 / TWO_PI, None, ALU.mult, None, ["L3y"], ["L3y"])
    ACT(dec3, m3, AF.Exp, ["m3"], ["dec3"])
    frac_sincos(y3, tmp3, s3i[:], sin3, cos3, "L3")
    TT("dve", cos3, cos3, dec3, ALU.mult, ["L3cos", "dec3"], ["lbr"])
    TT("dve", sin3, sin3, dec3, ALU.mult, ["L3sin", "dec3"], ["lbi"])
    TS("dve", nr3, cos3, -1.0, None, ALU.add, None, ["lbr"], ["nr3"])
    TT("dve", den3, A3re, A3re, ALU.mult, ["A3re"], ["den3"])
    TT("dve", u3a, A3im, A3im, ALU.mult, ["A3im"], ["u3a"])
    TT("dve", den3, den3, u3a, ALU.add, ["den3", "u3a"], ["den3"])
    RECIP(den3, den3, ["den3"], ["den3"])
    TT("dve", u3a, nr3, A3re, ALU.mult, ["nr3", "A3re", "den3"], ["u3a"])
    TT("dve", u3b, sin3, A3im, ALU.mult, ["lbi", "A3im"], ["u3b"])
    TT("dve", qre3, u3a, u3b, ALU.add, ["u3a", "u3b"], ["qre3"])
    TT("dve", qre3, qre3, den3, ALU.mult, ["qre3", "den3"], ["qre3"])
    TT("dve", u3a, sin3, A3re, ALU.mult, ["lbi", "A3re", "qre3"], ["u3a"])
    TT("dve", u3b, nr3, A3im, ALU.mult, ["nr3", "A3im", "qre3"], ["u3b"])
    TT("dve", qim3, u3a, u3b, ALU.subtract, ["u3a", "u3b"], ["qim3"])
    TT("dve", qim3, qim3, den3, ALU.mult, ["qim3", "den3"], ["qim3"])
    qre_b = v3(qre3).unsqueeze(2).broadcast_to([128, 4, 8, 64])
    qim_b = v3(qim3).unsqueeze(2).broadcast_to([128, 4, 8, 64])
    T1v4 = T1[:, :].rearrange("p (c g q) -> p c g q", c=4, g=8)
    T2v4 = T2[:, :].rearrange("p (c g q) -> p c g q", c=4, g=8)
    TT("dve", T1v4, Bst[:, 0], qre_b, ALU.mult, ["Bst", "qre3", "ta", "te"], ["T1"])
    TT("dve", T2v4, Bst[:, 1], qim_b, ALU.mult, ["Bst", "qim3", "L1y", "L2y"], ["T2"])
    TT("dve", Bbd[:, :, 0, :], T1[:, :].rearrange("p (c x) -> p c x", c=4), T2[:, :].rearrange("p (c x) -> p c x", c=4), ALU.subtract, ["T1", "T2"], ["Bbd"])
    TT("dve", T1v4, Bst[:, 1], qre_b, ALU.mult, ["Bst", "qre3", "Bbd"], ["T1"])
    TT("dve", T2v4, Bst[:, 0], qim_b, ALU.mult, ["Bst", "qim3", "Bbd"], ["T2"])
    TT("dve", Bbd[:, :, 1, :], T1[:, :].rearrange("p (c x) -> p c x", c=4), T2[:, :].rearrange("p (c x) -> p c x", c=4), ALU.add, ["T1", "T2"], ["Bbd"])
    CP("dve", Ccat[:, 0], Cst[:, 0], ["Cst"], ["Ccat"])
    TS("dve", CcatN[:], Cst[:, 0], -1.0, None, ALU.mult, None, ["Cst"], ["CcatN"])
    TS("dve", negtri[:], tri[:], -1.0, None, ALU.mult, None, ["tri"], ["negtri"])
    TT("dve", l128[:, 0, :], te[:, :, 127], te[:, :, 1], ALU.mult, ["te"], ["l128a"])
    TT("dve", l128[:, 1, :], tf[:, :, 127], tf[:, :, 1], ALU.mult, ["tf"], ["l128b"])
    TT("dve", l128[:, 4, :], l128[:, 0, :], l128[:, 1, :], ALU.subtract, ["l128a", "l128b"], ["l128"])
    TT("dve", l128[:, 2, :], te[:, :, 127], tf[:, :, 1], ALU.mult, ["te", "tf"], ["l128c"])
    TT("dve", l128[:, 3, :], tf[:, :, 127], te[:, :, 1], ALU.mult, ["te", "tf"], ["l128d"])
    TT("dve", l128[:, 5, :], l128[:, 2, :], l128[:, 3, :], ALU.add, ["l128c", "l128d"], ["l128"])
    TS("dve", Ccat[:, 1], Cst[:, 1], -1.0, None, ALU.mult, None, ["Cst"], ["Ccat"])
    DMA("sp", dcol[:], dap(dd_d, 0, [[1, 128], [128, 4]]), (), ["dcol"], "dcol", slow=True)
    for ct in range(4):
        TS("dve", Ddiag[:, ct, :], ident[:], dcol[:, ct:ct + 1], None, ALU.mult, None, ["ident", "dcol"], ["Ddiag"])
    MS("dve", cre[0][:], 0.0, ["c0_0", "c0_1", "c0_2", "c0_3"])
    MS("dve", cim[0][:], 0.0, ["c0_0", "c0_1", "c0_2", "c0_3"])
    S.barrier()
    M.release(m_work2)
    hT = M.alloc("hT", [128, 8, 512], BF16)
    xt = [M.alloc("xt", [128, 1024], F32) for _ in range(2)]
    xn = [M.alloc("xn", [128, 1024], BF16) for _ in range(2)]
    uT = M.alloc("uT", [128, 4, 512], BF16)
    gsT = M.alloc("gsT", [128, 4, 512], BF16)
    gT = M.alloc("gT", [128, 4, 512], BF16)
    xs = [M.alloc("xs", [128, 512], F32) for _ in range(2)]
    ys = [M.alloc("ys", [128, 512], F32) for _ in range(2)]
    pa_ = [[M.alloc("pa_", [128, 512], BF16) for _ in range(4)] for _ in range(2)]
    pb_ = [[M.alloc("pb_", [128, 4, 128], BF16) for _ in range(4)] for _ in range(2)]
    sr = [M.alloc("sr", [128, 4, 128], F32) for _ in range(2)]
    si = [M.alloc("si", [128, 4, 128], F32) for _ in range(2)]
    sg = M.alloc("sg", [128, 512], F32)
    xl = M.alloc("xl", [128, 4, 4], F32)

    def ssm_A(st, cc, ct, u):
        tok = slice(cc * 128, (cc + 1) * 128)
        ub = u % 2
        MM(PS[2][:, :], uT[:, ct, tok], Bbd[:, ct, 0, :], True, True, ["uT", "Bbd"], ["ps2"])
        MM(PS[3][:, :], uT[:, ct, tok], Bbd[:, ct, 1, :], True, True, ["uT", "Bbd"], ["ps3"])
        CP("act", xs[ub][:], PS[2][:, :], ["ps2"], [f"xs{ub}"])
        CP("act", ys[ub][:], PS[3][:, :], ["ps3"], [f"ys{ub}"])
        a_ = ta[:, ct * 512:(ct + 1) * 512]
        b_ = tb[:, ct * 512:(ct + 1) * 512]
        p = pa_[ub]
        TT("dve", p[0][:], xs[ub][:], a_, ALU.mult, [f"xs{ub}", "ta"], [f"pa{ub}0"])
        TT("dve", p[1][:], ys[ub][:], b_, ALU.mult, [f"ys{ub}", "tb"], [f"pa{ub}1"])
        TT("dve", p[2][:], ys[ub][:], a_, ALU.mult, [f"ys{ub}", "ta"], [f"pa{ub}2"])
        TT("dve", p[3][:], xs[ub][:], b_, ALU.mult, [f"xs{ub}", "tb"], [f"pa{ub}3"])

    def ssm_A2(st, cc, ct, u):
        ub = u % 2
        p = pa_[ub]
        for pr in range(4):
            cs = slice(pr * 128, (pr + 1) * 128)
            MM(PS[4 + 2 * ub][:, cs], p[0][:, cs], tri[:], True, False, [f"pa{ub}0", "tri"], [f"ps{4 + 2 * ub}"])
            MM(PS[4 + 2 * ub][:, cs], p[1][:, cs], negtri[:], False, True, [f"pa{ub}1", "negtri"], [f"ps{4 + 2 * ub}"])
            MM(PS[5 + 2 * ub][:, cs], p[2][:, cs], tri[:], True, False, [f"pa{ub}2", "tri"], [f"ps{5 + 2 * ub}"])
            MM(PS[5 + 2 * ub][:, cs], p[3][:, cs], tri[:], False, True, [f"pa{ub}3", "tri"], [f"ps{5 + 2 * ub}"])

    def ssm_B(st, cc, ct, u):
        tok = slice(cc * 128, (cc + 1) * 128)
        ci = st * 4 + cc
        ub = u % 2
        cr_in, ci_in = cre[ci % 2], cim[ci % 2]
        cr_out, ci_out = cre[(ci + 1) % 2], cim[(ci + 1) % 2]
        kin, kout = f"c{ci % 2}_{ct}", f"c{(ci + 1) % 2}_{ct}"
        for pr in range(4):
            pair = 4 * ct + pr
            cs = slice(pr * 128, (pr + 1) * 128)
            ACT(sr[ub][:, pr, :], PS[4 + 2 * ub][:, cs], AF.Identity, [f"ps{4 + 2 * ub}", kin], [f"sr{ub}"], bias=cr_in[:, pair:pair + 1])
            ACT(si[ub][:, pr, :], PS[5 + 2 * ub][:, cs], AF.Identity, [f"ps{5 + 2 * ub}", kin], [f"si{ub}"], bias=ci_in[:, pair:pair + 1])
        eh = te[:, 4 * ct:4 * ct + 4, :]
        fh = tf[:, 4 * ct:4 * ct + 4, :]
        q = pb_[ub]
        TT("dve", q[0][:], sr[ub][:], eh, ALU.mult, [f"sr{ub}", "te"], [f"pb{ub}0"])
        TT("dve", q[1][:], si[ub][:], fh, ALU.mult, [f"si{ub}", "tf"], [f"pb{ub}1"])
        TT("dve", q[2][:], si[ub][:], eh, ALU.mult, [f"si{ub}", "te"], [f"pb{ub}2"])
        TT("dve", q[3][:], sr[ub][:], fh, ALU.mult, [f"sr{ub}", "tf"], [f"pb{ub}3"])
        E = l128[:, 4, 4 * ct:4 * ct + 4]
        F_ = l128[:, 5, 4 * ct:4 * ct + 4]
        ps_ = slice(4 * ct, 4 * ct + 4)
        TT("pool", xl[:, 0, :], E, sr[ub][:, :, 127], ALU.mult, ["l128", f"sr{ub}"], ["xl0"])
        TT("pool", xl[:, 1, :], F_, si[ub][:, :, 127], ALU.mult, ["l128", f"si{ub}"], ["xl1"])
        TT("pool", cr_out[:, ps_], xl[:, 0, :], xl[:, 1, :], ALU.subtract, ["xl0", "xl1"], [kout])
        TT("pool", xl[:, 2, :], E, si[ub][:, :, 127], ALU.mult, ["l128", f"si{ub}"], ["xl2"])
        TT("pool", xl[:, 3, :], F_, sr[ub][:, :, 127], ALU.mult, ["l128", f"sr{ub}"], ["xl3"])
        TT("pool", ci_out[:, ps_], xl[:, 2, :], xl[:, 3, :], ALU.add, ["xl2", "xl3"], [kout])

    def ssm_C(st, cc, ct, u):
        tok = slice(cc * 128, (cc + 1) * 128)
        ub = u % 2
        q = pb_[ub]
        ysl = slice(ct * 128, (ct + 1) * 128)
        for pr in range(4):
            pair = 4 * ct + pr
            MM(PS[0][:, ysl], Ccat[:, 0, pair, :], q[0][:, pr, :], pr == 0, False, ["Ccat", f"pb{ub}0"], ["ps0"], fuse=True)
            MM(PS[0][:, ysl], CcatN[:, pair, :], q[1][:, pr, :], False, False, ["CcatN", f"pb{ub}1"], ["ps0"], fuse=True)
            MM(PS[0][:, ysl], Ccat[:, 1, pair, :], q[2][:, pr, :], False, False, ["Ccat", f"pb{ub}2"], ["ps0"], fuse=True)
            MM(PS[0][:, ysl], Ccat[:, 1, pair, :], q[3][:, pr, :], False, False, ["Ccat", f"pb{ub}3"], ["ps0"], fuse=True)
        MM(PS[0][:, ysl], Ddiag[:, ct, :], uT[:, ct, tok], False, True, ["Ddiag", "uT"], ["ps0"], fuse=True)
        if ct == 3:
            ACT(gT[:, :, tok], PS[0][:, :].rearrange("p (c t) -> p c t", c=4), AF.Gelu, ["ps0"], ["gT"])

    for st in range(NST):
        if st == 0:
            make_hT(st, 1, hT, xt, xn, [0, 1])
        def us_mm(c):
            zb = 1 + (c % 2)
            for kt in range(8):
                MM(PS[zb][:, :], w2[:, kt, c * 128:(c + 1) * 128], hT[:, kt, :], kt == 0, kt == 7, ["w2", "hT"], [f"ps{zb}"], fuse=True)
        us_mm(0)
        for c in range(8):
            if c + 1 < 8:
                us_mm(c + 1)
            zb = 1 + (c % 2)
            if c < 4:
                CP("act", uT[:, c, :], PS[zb][:, :], [f"ps{zb}"], ["uT"])
            else:
                ACT(gsT[:, c - 4, :], PS[zb][:, :], AF.Silu, [f"ps{zb}"], ["gsT"])
        units = [(cc, ct) for cc in range(4) for ct in range(4)]
        NU = len(units)
        for k in range(-2, NU):
            if 0 <= k + 2 < NU:
                ssm_A(st, units[k + 2][0], units[k + 2][1], k + 2)
            if 0 <= k + 1 < NU:
                ssm_A2(st, units[k + 1][0], units[k + 1][1], k + 1)
                ssm_B(st, units[k + 1][0], units[k + 1][1], k + 1)
            if 0 <= k < NU:
                ssm_C(st, units[k][0], units[k][1], k)
                if st + 1 < NST:
                    if k % 4 == 0:
                        hT_A(st + 1, k // 4, 1, hT, xt, xn)
                    if k % 4 == 2:
                        hT_B(st + 1, k // 4, 1, hT, xt, xn, [1])
        for c in range(4):
            zb = 1 + (c % 2)
            for kt in range(4):
                MM(PS[zb][:, :], gluw[:, kt, c * 128:(c + 1) * 128], gT[:, kt, :], kt == 0, kt == 3, ["gluw", "gT"], [f"ps{zb}"], fuse=True)
            ACT(sg[:], PS[zb][:, :], AF.Sigmoid, [f"ps{zb}", "glub"], ["sg"], bias=glub[:, c:c + 1])
            TT("dve", sg[:], sg[:], gT[:, c, :], ALU.mult, ["sg", "gT"], ["sg"])
            TT("dve", OST[:, c, st * 512:(st + 1) * 512], sg[:], gsT[:, c, :], ALU.mult, ["sg", "gsT"], ["OST", "Bst", "Cst"])
    if dbg == "p2":
        DMA("sp", dbg_d[:, :], OSTm[:, :], ["OST"], ["outd"], "OSTm")
        NOP(["outd"])
        return finish(nc, S)

    S.barrier()
    M.release(m_glob)
    w3 = M.alloc("w3", [128, 8, 2560], BF16)
    pa = M.alloc("pa", [128, 4, 1024], BF16)
    psw = M.alloc("psw", [128, 4, 1024], BF16)
    wo = M.alloc("wo", [128, 8, 1024], BF16)
    mb = M.alloc("mb", [128, 16], F32)
    m_work3 = M.mark()
    stg = [M.alloc("stg", [128, 2048], F32) for _ in range(2)]
    load_weight(w3[:, :, 0:512], w_in_d, 0, 8, 1536, 512, stg, True, "w3")
    load_weight(w3[:, :, 512:2560], w_in_d, 0, 8, 3072, 2048, stg, True, "w3")
    load_weight(pa, pa_d, 0, 4, 0, 1024, stg, False, "pa")
    load_weight(psw, ps_d, 0, 4, 0, 1024, stg, False, "psw")
    load_weight(wo, wo_d, 0, 8, 0, 1024, stg, False, "wo")
    DMA("sp", mb[:], dap(mb_d, 0, [[1, 128], [128, 16]]), (), ["mb"], "mb", slow=True)
    S.barrier()
    M.release(m_work3)
    hT = M.alloc("hT", [128, 8, 512], BF16)
    xt = [M.alloc("xt", [128, 1024], F32) for _ in range(2)]
    xn = [M.alloc("xn", [128, 1024], BF16) for _ in range(2)]
    mT = M.alloc("mT", [128, 8, 512], BF16)
    g1 = M.alloc("g1", [128, 512], F32)
    g2t = M.alloc("g2t", [128, 512], F32)
    m1 = M.alloc("m1", [128, 512], F32)
    xres = M.alloc("xres", [128, 1024], F32)
    ot = [M.alloc("ot", [128, 512], F32) for _ in range(2)]
    outkeys = []
    for st in range(NST):
        stc = slice(st * 512, (st + 1) * 512)
        if st == 0:
            make_hT(st, 2, hT, xt, xn, [0, 7])
        for h in range(4):
            for kt in range(8):
                MM(PS[1][:, :], w3[:, kt, h * 128:(h + 1) * 128], hT[:, kt, :], kt == 0, kt == 7, ["w3", "hT"], ["ps1"], fuse=True)
            ACT(g1[:], PS[1][:, :], AF.Silu, ["ps1"], ["g1"])
            TT("dve", OAT[:, h, stc], OAT[:, h, stc], g1[:], ALU.mult, ["OAT", "g1"], ["OAT"])
        for c in range(8):
            for h in range(4):
                MM(PS[1][:, :], pa[:, h, c * 128:(c + 1) * 128], OAT[:, h, stc], h == 0, h == 3, ["pa", "OAT"], ["ps1"], fuse=True)
            for kt in range(8):
                MM(PS[2][:, :], w3[:, kt, 512 + c * 128:512 + (c + 1) * 128], hT[:, kt, :], kt == 0, kt == 7, ["w3", "hT"], ["ps2"], fuse=True)
            ACT(g1[:], PS[2][:, :], AF.Sigmoid, ["ps2", "mb"], ["g1"], bias=mb[:, c:c + 1])
            TT("dve", m1[:], PS[1][:, :], g1[:], ALU.mult, ["ps1", "g1"], ["m1"])
            for k in range(4):
                MM(PS[3][:, :], psw[:, k, c * 128:(c + 1) * 128], OST[:, k, stc], k == 0, k == 3, ["psw", "OST"], ["ps3"], fuse=True)
            for kt in range(8):
                MM(PS[4][:, :], w3[:, kt, 1536 + c * 128:1536 + (c + 1) * 128], hT[:, kt, :], kt == 0, kt == 7, ["w3", "hT"], ["ps4"], fuse=True)
            ACT(g2t[:], PS[4][:, :], AF.Sigmoid, ["ps4", "mb"], ["g2t"], bias=mb[:, 8 + c:9 + c])
            TT("dve", g2t[:], PS[3][:, :], g2t[:], ALU.mult, ["ps3", "g2t"], ["g2t"])
            TT("dve", mT[:, c, :], m1[:], g2t[:], ALU.add, ["m1", "g2t"], ["mT"])
        for tt in range(4):
            row = st * 512 + tt * 128
            if st + 1 < NST:
                hT_A(st + 1, tt, 2, hT, xt, xn)
            DMA("sp", xres[:], x_d[row:row + 128, :], (), ["xres"], "xres")
            for half in range(2):
                for kt in range(8):
                    MM(PS[5 + half][:, :], mT[:, kt, tt * 128:(tt + 1) * 128], wo[:, kt, half * 512:(half + 1) * 512], kt == 0, kt == 7,
                       ["mT", "wo"], [f"ps{5 + half}"])
                TT("dve", ot[half][:], PS[5 + half][:, :], xres[:, half * 512:(half + 1) * 512], ALU.add, [f"ps{5 + half}", "xres"], [f"ot{half}"])
                ok = f"outd{st}_{tt}_{half}"
                DMA("sp", out_d[row:row + 128, half * 512:(half + 1) * 512], ot[half][:], [f"ot{half}"], [ok], f"ot{half}")
                outkeys.append(ok)
            if st + 1 < NST:
                hT_B(st + 1, tt, 2, hT, xt, xn, [0, 7])
    NOP(outkeys)
    return finish(nc, S)


def finish(nc, S):
    import contextlib
    with contextlib.ExitStack() as stk:
        S._sem_ctx = {e: stk.enter_context(nc.semaphore(f"s_{e}")) for e in S.ENGS}
        S._dsem = {k: stk.enter_context(nc.semaphore(f"d_{k}")) for k in S.dma_count}
        block = stk.enter_context(nc.Block())
        S.emit(block)
    return nc


_NC_CACHE = {}


def _core_inputs(inputs, b, consts):
    m = {"x": np.ascontiguousarray(inputs["x"][b], dtype=np.float32)}
    for k, v in inputs.items():
        if k == "x":
            continue
        v = np.asarray(v, dtype=np.float32)
        if k == "rel_bias_table":
            m[k] = np.ascontiguousarray(v)
            continue
        v0 = v[0]
        if k in ("w_in", "ssm_glu_w", "proj_attn", "proj_ssm", "w_out"):
            m[k] = np.ascontiguousarray(v0)
        else:
            m[k] = np.ascontiguousarray(v0).reshape(-1)
    m.update(consts)
    return m


def kernel(**inputs):
    if "nc" not in _NC_CACHE:
        _NC_CACHE["nc"] = build_nc()
    nc = _NC_CACHE["nc"]
    consts = host_consts()
    in_maps = [_core_inputs(inputs, b, consts) for b in range(8)]
    res = run_bass_kernel_spmd(nc, in_maps, core_ids=list(range(8)))
    out = np.stack([np.asarray(res.results[b]["out"], dtype=np.float32) for b in range(8)], axis=0)
    return out
```
